# Optimizing a Trainium2 kernel written in Bass

```python
import math
import jax, jax.numpy as jnp
from jax import lax
import numpy as np

D_MODEL = 1024
BATCH = 8
SEQ = 2048
DEPTH = 2
DEC_BATCH = 128
DEC_SEQ = 1
PAST_LEN = 16384
PAGE_SIZE = 128

GROUP_W = D_MODEL // 4
D_MIX = 4 * GROUP_W
RET_HEADS = 4
RET_DH = GROUP_W // RET_HEADS
ML_HEADS = 4
ML_DH = GROUP_W // ML_HEADS
S5_CH = 16
S5_GROUPS = GROUP_W // S5_CH
S5_P = 64
MEM_LEN = 256
XA_HEADS = 4
XA_DH = GROUP_W // XA_HEADS
CHUNK = 128
ROPE_BASE = 10000.0
EPS = 1e-6
NEG_INF = -1e30
DT_MIN = 1e-3
DT_MAX = 1e-1
SPLIT_SIZES = (GROUP_W, GROUP_W, GROUP_W, GROUP_W,
               GROUP_W, GROUP_W, GROUP_W, GROUP_W, GROUP_W, ML_HEADS, ML_HEADS,
               GROUP_W, GROUP_W,
               GROUP_W, GROUP_W)
D_IN = sum(SPLIT_SIZES)
SPLIT_IDX = tuple(int(i) for i in np.cumsum(SPLIT_SIZES)[:-1])

kernel_name = "hybrid_ret_mlstm_s5_memxattn_step"


def _rms_norm(x, w):
    xf = x.astype(jnp.float32)
    y = xf * lax.rsqrt(jnp.mean(xf * xf, axis=-1, keepdims=True) + EPS)
    return (y * w.astype(jnp.float32)).astype(x.dtype)


def _head_norm(h, gain):
    mu = jnp.mean(h, axis=-1, keepdims=True)
    var = jnp.mean(jnp.square(h - mu), axis=-1, keepdims=True)
    y = (h - mu) * lax.rsqrt(var + EPS)
    return y.reshape(h.shape[0], h.shape[1], -1) * gain.astype(jnp.float32)


def _rope(x, pos):
    half = x.shape[-1] // 2
    inv = ROPE_BASE ** (-jnp.arange(half, dtype=jnp.float32) / half)
    ang = pos.astype(jnp.float32)[:, None] * inv[None, :]
    cos = jnp.cos(ang)[None, :, None, :]
    sin = jnp.sin(ang)[None, :, None, :]
    x1, x2 = x[..., :half], x[..., half:]
    return jnp.concatenate([x1 * cos - x2 * sin, x1 * sin + x2 * cos], axis=-1)


def _chunk_len(T):
    return CHUNK if T % CHUNK == 0 else T


def _to_chunks(t, nc, L):
    B, T, H, d = t.shape
    return t.reshape(B, nc, L, H, d).transpose(1, 0, 3, 2, 4)


def _from_chunks(h, B, T):
    nc, _, H, L, d = h.shape
    return h.transpose(1, 0, 3, 2, 4).reshape(B, T, H, d)


def _retention(q, k, v, s0):
    B, T, H, d = q.shape
    L = _chunk_len(T)
    nc = T // L
    lg = jnp.log1p(-jnp.power(2.0, -5.0 - jnp.arange(H, dtype=jnp.float32)))[:, None]
    idx = jnp.arange(L, dtype=jnp.float32)
    diff = idx[:, None] - idx[None, :]
    decay = jnp.where(diff >= 0, jnp.exp(lg[:, :, None] * jnp.maximum(diff, 0.0)), 0.0)
    q_decay = jnp.exp(lg * (idx + 1.0))[..., None]
    k_decay = jnp.exp(lg * (L - 1.0 - idx))[..., None]
    chunk_decay = jnp.exp(lg * L)[..., None]
    qc, kc, vc = (_to_chunks(t, nc, L) for t in (q, k, v))

    def step(s, inp):
        qi, ki, vi = inp
        inner = jnp.einsum('bhld,bhmd->bhlm', qi, ki) * decay
        o = (jnp.einsum('bhlm,bhme->bhle', inner, vi)
             + jnp.einsum('bhld,bhde->bhle', qi * q_decay, s))
        s = s * chunk_decay + jnp.einsum('bhld,bhle->bhde', ki * k_decay, vi)
        return s, o

    s, o = lax.scan(step, s0, (qc, kc, vc))
    return _from_chunks(o, B, T), s


def _mlstm(q, k, v, ig, lf, c0, n0, m0):
    B, T, H, d = q.shape
    L = _chunk_len(T)
    nc = T // L
    qc, kc, vc = (_to_chunks(t, nc, L) for t in (q, k, v))
    ic = ig.reshape(B, nc, L, H).transpose(1, 0, 3, 2)
    fc = lf.reshape(B, nc, L, H).transpose(1, 0, 3, 2)
    causal = jnp.tril(jnp.ones((L, L), dtype=bool))

    def step(carry, inp):
        c, n, m = carry
        qi, ki, vi, ii, fi = inp
        b = jnp.cumsum(fi, axis=-1)
        a = b + m[..., None]
        dlog = jnp.where(causal, b[..., :, None] - b[..., None, :] + ii[..., None, :], NEG_INF)
        mt = jnp.maximum(a, jnp.max(dlog, axis=-1))
        w_intra = jnp.exp(dlog - mt[..., None])
        w_state = jnp.exp(a - mt)
        s = jnp.einsum('bhld,bhmd->bhlm', qi, ki) * w_intra
        num = (jnp.einsum('bhlm,bhme->bhle', s, vi)
               + w_state[..., None] * jnp.einsum('bhed,bhld->bhle', c, qi))
        den = jnp.sum(s, axis=-1) + w_state * jnp.einsum('bhd,bhld->bhl', n, qi)
        h = num / jnp.maximum(jnp.abs(den), jnp.exp(-mt))[..., None]
        wl, wsl = w_intra[..., -1, :], w_state[..., -1]
        c = wsl[..., None, None] * c + jnp.einsum('bhm,bhme,bhmd->bhed', wl, vi, ki)
        n = wsl[..., None] * n + jnp.einsum('bhm,bhmd->bhd', wl, ki)
        return (c, n, mt[..., -1]), h

    (c, n, m), h = lax.scan(step, (c0, n0, m0), (qc, kc, vc, ic, fc))
    return _from_chunks(h, B, T), c, n, m


def _s5(u, x0_re, x0_im, a_re, a_im, log_dt, b_re, b_im, c_re, c_im, d_skip):
    f32 = jnp.float32
    a_re, a_im, b_re, b_im, c_re, c_im = (t.astype(f32) for t in (a_re, a_im, b_re, b_im, c_re, c_im))
    B, T, _ = u.shape
    ug = u.reshape(B, T, S5_GROUPS, S5_CH)
    dt = jnp.exp(log_dt.astype(f32))[:, None]
    mag = jnp.exp(a_re * dt)
    ab_re, ab_im = mag * jnp.cos(a_im * dt), mag * jnp.sin(a_im * dt)
    den = a_re * a_re + a_im * a_im
    nr, ni = ab_re - 1.0, ab_im
    f_re = (nr * a_re + ni * a_im) / den
    f_im = (ni * a_re - nr * a_im) / den
    bb_re = f_re[..., None] * b_re - f_im[..., None] * b_im
    bb_im = f_re[..., None] * b_im + f_im[..., None] * b_re
    bu_re = jnp.einsum('gpc,btgc->btgp', bb_re, ug)
    bu_im = jnp.einsum('gpc,btgc->btgp', bb_im, ug)
    x0_re, x0_im = x0_re.astype(f32), x0_im.astype(f32)
    bu_re = bu_re.at[:, 0].add(ab_re * x0_re - ab_im * x0_im)
    bu_im = bu_im.at[:, 0].add(ab_re * x0_im + ab_im * x0_re)
    A_re = jnp.broadcast_to(ab_re, bu_re.shape)
    A_im = jnp.broadcast_to(ab_im, bu_im.shape)

    def combine(e1, e2):
        a1r, a1i, b1r, b1i = e1
        a2r, a2i, b2r, b2i = e2
        return (a2r * a1r - a2i * a1i, a2r * a1i + a2i * a1r,
                a2r * b1r - a2i * b1i + b2r, a2r * b1i + a2i * b1r + b2i)

    _, _, xr, xi = lax.associative_scan(combine, (A_re, A_im, bu_re, bu_im), axis=1)
    y = jnp.einsum('gcp,btgp->btgc', c_re, xr) - jnp.einsum('gcp,btgp->btgc', c_im, xi)
    y = y.reshape(B, T, -1) + d_skip.astype(f32) * u
    return y, xr[:, -1], xi[:, -1]


def _mem_attend(q, mk, mv):
    s = jnp.einsum('bthd,bmhd->bhtm', q, mk) * (q.shape[-1] ** -0.5)
    p = jax.nn.softmax(s, axis=-1)
    return jnp.einsum('bhtm,bmhd->bthd', p, mv)


def _mixer_layer(x, pos, ret_s, ml_c, ml_n, ml_m, s5_re, s5_im, mem_k, mem_v,
                 norm_w, w_in, ret_gn, ml_b_i, ml_b_f, ml_gn,
                 s5_a_re, s5_a_im, s5_log_dt, s5_b_re, s5_b_im, s5_c_re, s5_c_im, s5_d, s5_w_glu,
                 w_out):
    f32 = jnp.float32
    B, T, _ = x.shape
    hn = _rms_norm(x, norm_w)
    proj = jnp.einsum('btd,de->bte', hn, w_in)
    (r_q, r_k, r_v, r_g, m_q, m_k, m_v, m_o, m_g, m_i, m_f,
     s_u, s_g, a_q, a_g) = jnp.split(proj, SPLIT_IDX, axis=-1)

    def heads(t, H):
        return t.astype(f32).reshape(B, T, H, -1)

    rq = _rope(heads(r_q, RET_HEADS), pos)
    rk = _rope(heads(r_k, RET_HEADS), pos) * (RET_DH ** -0.5)
    ro, ret_new = _retention(rq, rk, heads(r_v, RET_HEADS), ret_s.astype(f32))
    ret_out = _head_norm(ro, ret_gn) * jax.nn.silu(r_g.astype(f32))

    ig = m_i.astype(f32) + ml_b_i.astype(f32)
    lf = jax.nn.log_sigmoid(m_f.astype(f32) + ml_b_f.astype(f32))
    mh, c_new, n_new, m_new = _mlstm(heads(m_q, ML_HEADS), heads(m_k, ML_HEADS) * (ML_DH ** -0.5),
                                     heads(m_v, ML_HEADS), ig, lf,
                                     ml_c.astype(f32), ml_n.astype(f32), ml_m.astype(f32))
    mh = mh * jax.nn.sigmoid(heads(m_o, ML_HEADS))
    ml_out = _head_norm(mh, ml_gn) * jax.nn.silu(m_g.astype(f32))

    sy, s5r_new, s5i_new = _s5(s_u.astype(f32), s5_re, s5_im, s5_a_re, s5_a_im, s5_log_dt,
                               s5_b_re, s5_b_im, s5_c_re, s5_c_im, s5_d)
    sy = jax.nn.gelu(sy)
    sy = sy * jax.nn.sigmoid(jnp.einsum('btc,ce->bte', sy, s5_w_glu.astype(f32)))
    s5_out = sy * jax.nn.silu(s_g.astype(f32))

    xa = _mem_attend(heads(a_q, XA_HEADS), mem_k.astype(f32), mem_v.astype(f32)).reshape(B, T, -1)
    xa_out = xa * jax.nn.silu(a_g.astype(f32))

    mix = jnp.concatenate([ret_out, ml_out, s5_out, xa_out], axis=-1).astype(x.dtype)
    y = x + jnp.einsum('bte,ed->btd', mix, w_out)
    return y, ret_new, c_new, n_new, m_new, s5r_new, s5i_new


def setup_inputs(seed: int = 0) -> dict:
    key = jax.random.key(seed)
    ks = iter(jax.random.split(key, 40))
    f32 = jnp.float32
    G, P = S5_GROUPS, S5_P

    def nrm(shape, scale=1.0):
        return scale * jax.random.normal(next(ks), shape, f32)

    n_idx = jnp.arange(P, dtype=f32)
    inp = {}
    inp["x_prompt"] = nrm((BATCH, SEQ, D_MODEL))
    inp["x_sample"] = nrm((DEC_BATCH, DEC_SEQ, D_MODEL))
    inp["mem_prompt"] = nrm((BATCH, MEM_LEN, D_MODEL))
    inp["state_ret"] = nrm((DEPTH, DEC_BATCH, RET_HEADS, RET_DH, RET_DH), 0.3)
    inp["state_mlstm_c"] = nrm((DEPTH, DEC_BATCH, ML_HEADS, ML_DH, ML_DH), 0.3)
    inp["state_mlstm_n"] = nrm((DEPTH, DEC_BATCH, ML_HEADS, ML_DH), 0.3)
    inp["state_mlstm_m"] = jax.random.uniform(next(ks), (DEPTH, DEC_BATCH, ML_HEADS), f32, 0.0, 3.0)
    inp["state_s5_re"] = nrm((DEPTH, DEC_BATCH, G, P), 0.1)
    inp["state_s5_im"] = nrm((DEPTH, DEC_BATCH, G, P), 0.1)
    inp["cache_mem_k"] = nrm((DEPTH, DEC_BATCH, MEM_LEN, XA_HEADS, XA_DH))
    inp["cache_mem_v"] = nrm((DEPTH, DEC_BATCH, MEM_LEN, XA_HEADS, XA_DH))
    inp["norm_w"] = 1.0 + nrm((DEPTH, D_MODEL), 0.02)
    inp["w_in"] = nrm((DEPTH, D_MODEL, D_IN), D_MODEL ** -0.5)
    inp["ret_gn"] = 1.0 + nrm((DEPTH, GROUP_W), 0.02)
    inp["ml_b_i"] = nrm((DEPTH, ML_HEADS), 0.1)
    inp["ml_b_f"] = jnp.linspace(3.0, 6.0, ML_HEADS, dtype=f32) + nrm((DEPTH, ML_HEADS), 0.1)
    inp["ml_gn"] = 1.0 + nrm((DEPTH, GROUP_W), 0.02)
    inp["s5_a_re"] = -0.5 * jnp.exp(nrm((DEPTH, G, P), 0.01))
    inp["s5_a_im"] = jnp.pi * n_idx + nrm((DEPTH, G, P), 0.01)
    inp["s5_log_dt"] = jax.random.uniform(next(ks), (DEPTH, G), f32, math.log(DT_MIN), math.log(DT_MAX))
    inp["s5_b_re"] = nrm((DEPTH, G, P, S5_CH), (2 * S5_CH) ** -0.5)
    inp["s5_b_im"] = nrm((DEPTH, G, P, S5_CH), (2 * S5_CH) ** -0.5)
    inp["s5_c_re"] = nrm((DEPTH, G, S5_CH, P), P ** -0.5)
    inp["s5_c_im"] = nrm((DEPTH, G, S5_CH, P), P ** -0.5)
    inp["s5_d"] = nrm((DEPTH, GROUP_W))
    inp["s5_w_glu"] = nrm((DEPTH, GROUP_W, GROUP_W), GROUP_W ** -0.5)
    inp["w_mem_k"] = nrm((DEPTH, D_MODEL, GROUP_W), D_MODEL ** -0.5)
    inp["w_mem_v"] = nrm((DEPTH, D_MODEL, GROUP_W), D_MODEL ** -0.5)
    inp["w_out"] = nrm((DEPTH, D_MIX, D_MODEL), D_MIX ** -0.5)
    inp["final_norm_w"] = 1.0 + nrm((D_MODEL,), 0.02)
    return inp


def reference(x_prompt, x_sample, mem_prompt, state_ret, state_mlstm_c, state_mlstm_n, state_mlstm_m,
              state_s5_re, state_s5_im, cache_mem_k, cache_mem_v,
              norm_w, w_in, ret_gn, ml_b_i, ml_b_f, ml_gn,
              s5_a_re, s5_a_im, s5_log_dt, s5_b_re, s5_b_im, s5_c_re, s5_c_im, s5_d, s5_w_glu,
              w_mem_k, w_mem_v, w_out, final_norm_w):
    f32 = jnp.float32
    Bp, Tp, _ = x_prompt.shape
    Bs, Ts, _ = x_sample.shape
    pos_p = jnp.arange(Tp, dtype=jnp.int32)
    pos_s = PAST_LEN + jnp.arange(Ts, dtype=jnp.int32)
    z_ret = jnp.zeros((Bp,) + state_ret.shape[2:], f32)
    z_c = jnp.zeros((Bp,) + state_mlstm_c.shape[2:], f32)
    z_n = jnp.zeros((Bp,) + state_mlstm_n.shape[2:], f32)
    z_m = jnp.zeros((Bp,) + state_mlstm_m.shape[2:], f32)
    z_sr = jnp.zeros((Bp,) + state_s5_re.shape[2:], f32)
    z_si = jnp.zeros((Bp,) + state_s5_im.shape[2:], f32)

    hp, hs = x_prompt, x_sample
    st_p = [[] for _ in range(6)]
    st_s = [[] for _ in range(6)]
    mk_list, mv_list = [], []
    for l in range(DEPTH):
        w = (norm_w[l], w_in[l], ret_gn[l], ml_b_i[l], ml_b_f[l], ml_gn[l],
             s5_a_re[l], s5_a_im[l], s5_log_dt[l], s5_b_re[l], s5_b_im[l], s5_c_re[l], s5_c_im[l],
             s5_d[l], s5_w_glu[l], w_out[l])
        mk_p = jnp.einsum('bmd,de->bme', mem_prompt, w_mem_k[l]).reshape(Bp, -1, XA_HEADS, XA_DH)
        mv_p = jnp.einsum('bmd,de->bme', mem_prompt, w_mem_v[l]).reshape(Bp, -1, XA_HEADS, XA_DH)
        mk_list.append(mk_p)
        mv_list.append(mv_p)
        hp, *sp = _mixer_layer(hp, pos_p, z_ret, z_c, z_n, z_m, z_sr, z_si, mk_p, mv_p, *w)
        hs, *ss = _mixer_layer(hs, pos_s, state_ret[l], state_mlstm_c[l], state_mlstm_n[l],
                               state_mlstm_m[l], state_s5_re[l], state_s5_im[l],
                               cache_mem_k[l], cache_mem_v[l], *w)
        for i in range(6):
            st_p[i].append(sp[i])
            st_s[i].append(ss[i])

    y_prompt = _rms_norm(hp, final_norm_w)
    y_sample = _rms_norm(hs, final_norm_w)
    dts = (state_ret.dtype, state_mlstm_c.dtype, state_mlstm_n.dtype, state_mlstm_m.dtype,
           state_s5_re.dtype, state_s5_im.dtype)
    P_ = [jnp.stack(st_p[i]).astype(dts[i]) for i in range(6)]
    S_ = [jnp.stack(st_s[i]).astype(dts[i]) for i in range(6)]
    memk_p = jnp.stack(mk_list)
    memv_p = jnp.stack(mv_list)
    return (y_prompt, y_sample, P_[0], S_[0], P_[1], S_[1], P_[2], S_[2], P_[3], S_[3],
            P_[4], S_[4], P_[5], S_[5], memk_p, memv_p)
```

```python
import contextlib
import numpy as np
import concourse.bass as bass
import concourse.mybir as mybir
from concourse.ap import AP
from concourse.bass_utils import run_bass_kernel_spmd

F32 = mybir.dt.float32
BF16 = mybir.dt.bfloat16
I32 = mybir.dt.int32
F32R = mybir.dt.float32r
AF = mybir.ActivationFunctionType
ALU = mybir.AluOpType
AX = mybir.AxisListType

NCORES = 8
T = 2048
TB = 512
NBLK = T // TB
NS = 16
DEPTH = 2
DIN = 3336
EPS = 1e-6
PAST = 16384.0
TWO_PI = 6.283185307179586
C1 = 6.28125
C2 = TWO_PI - C1


class Tok:
    __slots__ = ("w", "r", "name", "excl")

    def __init__(self, name=""):
        self.w = None
        self.r = []
        self.name = name
        self.excl = False


class Op:
    __slots__ = ("eng", "fn", "deps", "idx", "signal", "sem", "semval", "dma", "cost", "pos", "tab")

    def __init__(self, eng, fn, deps, idx, dma):
        self.cost = 100.0
        self.pos = idx
        self.tab = None
        self.eng = eng
        self.fn = fn
        self.deps = deps
        self.idx = idx
        self.signal = False
        self.sem = None
        self.semval = 0
        self.dma = dma


ENGS = ("sync", "scalar", "vector", "gpsimd", "tensor")
DMA_POOL = 20
WCHUNKS = [(0, 512), (512, 1024), (1024, 1536), (1536, 2048), (2048, 2312), (2824, 3336)]


class _Rec:
    def __init__(self):
        self.call = None

    def __getattr__(self, name):
        def f(*a, **kw):
            self.call = (name, a, kw)
            return self
        return f


def _est_cost(eng, name, a_, kw_, dma):
    try:
        out = kw_.get("out", None)
        if out is None:
            out = a_[0]
        shp = out.shape
        n = 1
        for d in shp[1:]:
            n *= int(d)
    except Exception:
        n = 128
    if dma:
        return 2000.0 + n * int(shp[0]) * 4 / 80.0
    if eng == "tensor":
        f32 = False
        try:
            f32 = (kw_.get("lhsT", None) is not None and kw_["lhsT"].dtype == F32)
        except Exception:
            pass
        return 60.0 + n * (4.0 if f32 else 1.0) / 1.7
    if eng == "vector":
        return 100.0 + n * (2.0 if name == "tensor_tensor_scan" else 1.0) / 0.78
    if eng == "scalar":
        return 230.0 + n / 1.2
    if eng == "gpsimd":
        return 150.0 + n / 0.5
    return 50.0


SCHED = True
_TABSET = {AF.Exp: "e", AF.Ln: "e", AF.Silu: "s", AF.Sin: "s", AF.Sigmoid: "g", AF.Tanh: "s"}
TAB_PEN = 0.0


def _list_schedule(ops):
    n = len(ops)
    succ = [[] for _ in range(n)]
    indeg = [0] * n
    for o in ops:
        for d in o.deps:
            succ[d].append(o.idx)
            indeg[o.idx] += 1
    bl = [0.0] * n
    for i in range(n - 1, -1, -1):
        m = 0.0
        for s_ in succ[i]:
            if bl[s_] > m:
                m = bl[s_]
        bl[i] = m + ops[i].cost + 100.0
    efree = {e: 0.0 for e in ENGS}
    fin = [0.0] * n
    rdy_t = [0.0] * n
    ready = [i for i in range(n) if indeg[i] == 0]
    order = []
    cur_tab = [None]
    WINDOW = 6000
    next_unsched = 0
    done = [False] * n
    while ready:
        best = None
        bk = None
        lim = next_unsched + WINDOW
        for i in ready:
            if i > lim:
                continue
            o = ops[i]
            st = efree[o.eng]
            if rdy_t[i] > st:
                st = rdy_t[i]
            if o.tab is not None and o.tab != cur_tab[0]:
                st = st + TAB_PEN
            key = (st, -bl[i], i)
            if bk is None or key < bk:
                bk = key
                best = i
        if best is None:
            best = min(ready)
            o = ops[best]
            bk = (max(efree[o.eng], rdy_t[best]),)
        ready.remove(best)
        o = ops[best]
        st = bk[0]
        if o.tab is not None:
            cur_tab[0] = o.tab
        if o.dma:
            efree[o.eng] = st + 60.0
            fin[best] = st + o.cost
        else:
            efree[o.eng] = st + o.cost
            fin[best] = st + o.cost
        order.append(best)
        done[best] = True
        while next_unsched < n and done[next_unsched]:
            next_unsched += 1
        for s_ in succ[best]:
            t = fin[best] + (110.0 if ops[s_].eng == o.eng else 220.0)
            if t > rdy_t[s_]:
                rdy_t[s_] = t
            indeg[s_] -= 1
            if indeg[s_] == 0:
                ready.append(s_)
    assert len(order) == n
    return order, max(fin)


class Prog:
    def __init__(self, nc):
        self.nc = nc
        self.ops = []
        self.out_ops = []

    @staticmethod
    def _flat(ts):
        out = []
        for t in ts:
            if isinstance(t, (list, tuple)):
                out.extend(Prog._flat(t))
            else:
                out.append(t)
        return out

    def op(self, eng, fn, r=(), w=(), dma=False):
        r = Prog._flat(r)
        w = Prog._flat(w)
        idx = len(self.ops)
        deps = set()
        for t in r:
            if t.w is not None:
                deps.add(t.w)
            if t.excl:
                for ri_ in t.r:
                    if self.ops[ri_].eng != eng:
                        deps.add(ri_)
        for t in w:
            if t.w is not None:
                deps.add(t.w)
            deps.update(t.r)
        rec = _Rec()
        fn(rec)
        name, a_, kw_ = rec.call
        o = Op(eng, (lambda e, name=name, a_=a_, kw_=kw_: getattr(e, name)(*a_, **kw_)), deps, idx, dma)
        o.cost = _est_cost(eng, name, a_, kw_, dma)
        if eng == "scalar" and name == "activation":
            o.tab = _TABSET.get(kw_.get("func", None), None)
        self.ops.append(o)
        for t in r:
            t.r.append(idx)
        for t in w:
            t.w = idx
            t.r = []
        return idx

    def v(self, fn, r=(), w=()):
        return self.op("vector", fn, r, w)

    def a(self, fn, r=(), w=()):
        return self.op("scalar", fn, r, w)

    def g(self, fn, r=(), w=()):
        return self.op("gpsimd", fn, r, w)

    def t(self, fn, r=(), w=()):
        return self.op("tensor", fn, r, w)

    def dma(self, q, out, in_, r=(), w=(), is_output=False):
        idx = self.op(q, lambda e, out=out, in_=in_: e.dma_start(out=out, in_=in_), r, w, dma=True)
        if is_output:
            self.out_ops.append(idx)
        return idx

    def emit(self):
        nc = self.nc
        ops = self.ops
        fin = Op("sync", lambda e: e.nop(), set(self.out_ops), len(ops), False)
        ops.append(fin)
        byidx = ops
        if SCHED:
            order, mk = _list_schedule(ops)
            self.est_makespan = mk
            ops = [byidx[i] for i in order]
            for p_, o in enumerate(ops):
                o.pos = p_
        with contextlib.ExitStack() as es:
            esem = {e: es.enter_context(nc.semaphore("s_" + e)) for e in ENGS}
            pools = {q: [es.enter_context(nc.semaphore("d_%s_%d" % (q, i))) for i in range(DMA_POOL)]
                     for q in ("sync", "gpsimd")}
            dcount = {"sync": 0, "gpsimd": 0}
            last_use = {}
            for o in ops:
                if o.dma:
                    i = dcount[o.eng]
                    dcount[o.eng] += 1
                    sem = pools[o.eng][i % DMA_POOL]
                    u = i // DMA_POOL
                    o.sem = sem
                    o.semval = 16 * (u + 1)
                    o.signal = True
                    key = (o.eng, i % DMA_POOL)
                    if key in last_use:
                        o.deps.add(last_use[key])
                    last_use[key] = o.idx
            for o in ops:
                for d in o.deps:
                    p = byidx[d]
                    if p.eng == "tensor" and o.eng == "tensor" and not p.dma:
                        continue
                    p.signal = True
            cnt = {e: 0 for e in ENGS}
            for o in ops:
                if not o.dma and o.signal:
                    cnt[o.eng] += 1
                    o.sem = esem[o.eng]
                    o.semval = cnt[o.eng]
            per = {e: [o for o in ops if o.eng == e] for e in ENGS}

            def run(eng, lst):
                waited = {}
                for o in lst:
                    need = {}
                    for d in o.deps:
                        p = byidx[d]
                        if p.eng == "tensor" and o.eng == "tensor" and not p.dma:
                            continue
                        k = id(p.sem)
                        if k not in need or need[k][1] < p.semval:
                            need[k] = (p.sem, p.semval)
                    for k, (sem, val) in need.items():
                        if waited.get(k, 0) >= val:
                            continue
                        eng.wait_ge(sem, val)
                        waited[k] = val
                    ins = o.fn(eng)
                    if o.signal:
                        ins.then_inc(o.sem, 16 if o.dma else 1)

            with nc.Block() as block:
                @block.sync
                def _(e):
                    run(e, per["sync"])

                @block.scalar
                def _(e):
                    run(e, per["scalar"])

                @block.vector
                def _(e):
                    run(e, per["vector"])

                @block.gpsimd
                def _(e):
                    run(e, per["gpsimd"])

                @block.tensor
                def _(e):
                    run(e, per["tensor"])


class Buf:
    def __init__(self, t, name=""):
        self.t = t
        self.tok = Tok(name)

    def __getitem__(self, k):
        return self.t[k]


def _c(a):
    return np.ascontiguousarray(a, dtype=np.float32)


def _consts():
    c = {}
    c["ident"] = np.eye(128, dtype=np.float32)
    c["maskT"] = np.triu(np.ones((128, 128), np.float32))
    bd = np.zeros((128, 128), np.float32)
    bd[:64, :64] = 1
    bd[64:, 64:] = 1
    c["bd2"] = bd
    pm = np.zeros((128, 128), np.float32)
    for h2 in range(2):
        for d in range(32):
            pm[h2 * 64 + d + 32, h2 * 64 + d] = -1.0
            pm[h2 * 64 + d, h2 * 64 + d + 32] = 1.0
    c["pm"] = pm
    inv = (10000.0 ** (-np.arange(32, dtype=np.float32) / 32)).astype(np.float32)
    c["invf"] = np.tile(inv, 4).reshape(128, 1).astype(np.float32)
    c["invrow"] = np.tile(inv.reshape(1, 32), (128, 1)).astype(np.float32)
    lg = np.log1p(-np.power(np.float32(2.0), -5.0 - np.arange(4, dtype=np.float32))).astype(np.float32)
    lgc = np.zeros((128, 2), np.float32)
    for i in range(2):
        for h2 in range(2):
            lgc[h2 * 64:(h2 + 1) * 64, i] = lg[2 * i + h2]
    c["lgcol"] = lgc
    c["l1"] = np.tile(np.arange(1, 129, dtype=np.float32).reshape(1, 128), (128, 1))
    ex = np.zeros((8, 4, 128), np.float32)
    for i in range(2):
        for h2 in range(2):
            ex[2 * i + h2, i * 2 + 0, h2 * 64:(h2 + 1) * 64] = 1.0
            ex[4 + 2 * i + h2, i * 2 + 1, h2 * 64:(h2 + 1) * 64] = 1.0
    c["gexp"] = ex.reshape(8, 512)
    sel = np.zeros((16, 4, 64), np.float32)
    for hh in range(4):
        for n in range(16):
            sel[n, hh, 16 * hh + n] = 1.0
    c["sel"] = sel.reshape(16, 256)
    sel2 = np.zeros((64, 16), np.float32)
    mpair = np.zeros((64, 2, 2), np.float32)
    lg64 = np.zeros((64, 1), np.float32)
    for hh in range(4):
        for n in range(16):
            sel2[16 * hh + n, n] = 1.0
            mpair[16 * hh + n, hh // 2, hh % 2] = 1.0
            lg64[16 * hh + n, 0] = lg[hh]
    c["sel2"] = sel2
    c["mpair"] = mpair.reshape(64, 4)
    c["lg64"] = lg64
    c["kgrid"] = np.tile(np.arange(9, dtype=np.float32).reshape(1, 9), (128, 1))
    return c


CONST_SHAPES = {"ident": [128, 128], "maskT": [128, 128], "bd2": [128, 128], "pm": [128, 128],
                "invf": [128, 1], "invrow": [128, 32], "lgcol": [128, 2], "l1": [128, 128],
                "gexp": [8, 512], "kgrid": [128, 9],
                "sel": [16, 256], "sel2": [64, 16], "mpair": [64, 4], "lg64": [64, 1]}


def build(stage=99):
    import os
    STAGE = int(os.environ.get('KSTAGE', '99'))
    KSUB = int(os.environ.get('KSUB', '99'))
    KMASK = int(os.environ.get('KMASK', '63'))
    SAMPLE = int(os.environ.get('KSAMPLE', '99'))
    nc = bass.Bass("TRN2", target_bir_lowering=False)
    P = Prog(nc)
    es = contextlib.ExitStack()

    def din(name, shape):
        return nc.dram_tensor(name, list(shape), F32, kind="ExternalInput").ap()

    def dout(name, shape):
        return nc.dram_tensor(name, list(shape), F32, kind="ExternalOutput").ap()

    def sb(name, shape, dt=F32):
        return Buf(es.enter_context(nc.sbuf_tensor("s_" + name, list(shape), dt)), name)

    d_xT = din("xT", [1024, T])
    d_memT = din("memT", [1024, 256])
    d_win = din("w_in", [DEPTH, 1024, DIN])
    d_wout = din("w_out", [DEPTH, 1024, 1024])
    d_wmk = din("w_mem_k", [DEPTH, 1024, 256])
    d_wmv = din("w_mem_v", [DEPTH, 1024, 256])
    d_normw = din("normw", [128, DEPTH * 8])
    d_fnw = din("fnw", [128, 8])
    d_gn = din("gn", [128, DEPTH * 4])
    d_bif = din("bif", [8, DEPTH])
    dc = {k: din("c_" + k, s) for k, s in CONST_SHAPES.items()}

    o_yT = dout("o_yT", [1024, T])
    o_memkv = dout("o_memkv", [DEPTH, 256, 512])
    o_ret = dout("o_ret", [DEPTH, 4, 64, 64])
    o_mlc = dout("o_mlc", [DEPTH, 4, 64, 64])
    o_mln = dout("o_mln", [DEPTH, 4, 64])
    o_mlm = dout("o_mlm", [DEPTH, 4])

    xTv = d_xT.rearrange("(k p) t -> p k t", p=128)
    yTv = o_yT.rearrange("(k p) t -> p k t", p=128)

    banks = [Buf(es.enter_context(nc.psum_tensor("ps%d" % i, [128, 512], F32)), "ps%d" % i) for i in range(8)]
    for b_ in banks:
        b_.tok.excl = True
    bank_rr = [0]
    rot = [[0, 1, 2, 3, 4, 5]]

    def psum():
        r_ = rot[0]
        b = banks[r_[bank_rr[0] % len(r_)]]
        bank_rr[0] += 1
        return b

    cst = {}
    for k, s in CONST_SHAPES.items():
        cst[k] = sb("k_" + k, s)
        P.dma("sync", cst[k].t[:], dc[k], w=[cst[k].tok])
    ident_b = sb("ident_b", [128, 128], BF16)
    P.v(lambda e: e.tensor_copy(out=ident_b[:], in_=cst["ident"][:]), r=[cst["ident"].tok], w=[ident_b.tok])
    zeros_b = sb("zeros_b", [128, 128], BF16)
    P.v(lambda e: e.memset(zeros_b[:], 0.0), w=[zeros_b.tok])
    ones_b = sb("ones_b", [128, 128], BF16)
    P.v(lambda e: e.memset(ones_b[:], 1.0), w=[ones_b.tok])
    onespad = sb("onespad", [128, 2, 128], BF16)
    P.v(lambda e: e.memset(onespad[:], 0.0), w=[onespad.tok])
    for h2 in range(2):
        P.v(lambda e, h2=h2: e.memset(onespad[:, h2, 64 * h2:64 * h2 + 64], 1.0), w=[onespad.tok])
    avg = sb("avg", [128, 128])
    P.v(lambda e: e.tensor_scalar(out=avg[:], in0=cst["bd2"][:], scalar1=1.0 / 64, scalar2=None, op0=ALU.mult),
        r=[cst["bd2"].tok], w=[avg.tok])
    avg_b = sb("avg_b", [128, 128], BF16)
    P.v(lambda e: e.tensor_copy(out=avg_b[:], in_=avg[:]), r=[avg.tok], w=[avg_b.tok])
    normw = sb("normw", [128, DEPTH * 8])
    P.dma("sync", normw[:], d_normw, w=[normw.tok])
    fnw = sb("fnw", [128, 8])
    P.dma("sync", fnw[:], d_fnw, w=[fnw.tok])
    gn = sb("gn", [128, DEPTH * 4])
    P.dma("sync", gn[:], d_gn, w=[gn.tok])
    bif = sb("bif", [8, DEPTH])
    P.dma("sync", bif[:], d_bif, w=[bif.tok])

    decq = sb("decq", [128, 2, 128])
    deck = sb("deck", [128, 2, 128])
    gL = sb("gL", [128, 2])
    for i in range(2):
        P.a(lambda e, i=i: e.activation(out=decq[:, i, :], in_=cst["l1"][:], func=AF.Exp, scale=cst["lgcol"][:, i:i + 1]),
            r=[cst["l1"].tok, cst["lgcol"].tok], w=[decq.tok])
    P.v(lambda e: e.reciprocal(out=deck[:], in_=decq[:]), r=[decq.tok], w=[deck.tok])
    P.v(lambda e: e.tensor_scalar(out=deck[:], in0=deck[:], scalar1=0.125, scalar2=None, op0=ALU.mult),
        r=[deck.tok], w=[deck.tok])
    P.v(lambda e: e.tensor_copy(out=gL[:], in_=decq[:, :, 127]), r=[decq.tok], w=[gL.tok])

    bd2g = sb("bd2g", [128, 2, 128])
    for i in range(2):
        P.v(lambda e, i=i: e.tensor_scalar(out=bd2g[:, i, :], in0=cst["bd2"][:], scalar1=gL[:, i:i + 1], scalar2=None, op0=ALU.mult),
            r=[cst["bd2"].tok, gL.tok], w=[bd2g.tok])

    xT = sb("xT", [128, 8, T])
    xtok = [Tok("xT%d" % b) for b in range(NBLK)]
    for b in range(NBLK):
        P.dma("sync", xT[:, :, b * TB:(b + 1) * TB], xTv[:, :, b * TB:(b + 1) * TB], w=[xtok[b]])
    hnT = sb("hnT", [128, 8, T], BF16)
    htok = [Tok("hnT%d" % b) for b in range(NBLK)]

    F = [sb("F%d" % i, [128, TB]) for i in range(11)]
    P0 = sb("P0", [128, 2, TB], BF16)
    P1 = sb("P1", [128, 2, TB], BF16)
    P2 = sb("P2", [128, 2, TB], BF16)
    Pf = sb("Pf", [128, 2, TB])
    VA = sb("VA", [128, 4, 2, 2, 128], BF16)
    ET = sb("ET", [128, 4, TB], BF16)
    mixp = sb("mixp", [128, 2, TB], BF16)
    mixp.tok = [Tok("mixp0"), Tok("mixp1")]
    ET.tok = [Tok("ET%d" % k_) for k_ in range(4)]
    Pf.tok = [Tok("Pf0"), Tok("Pf1")]
    ncs = sb("ncs", [128, 2, 4])
    sq2 = [sb("sq%d" % i, [128, TB], BF16) for i in range(2)]

    NW = 3
    wch = [sb("wch%d" % i, [128, 8, 512], BF16) for i in range(NW)]
    wrr = [0]
    wop = [sb("wop%d" % i, [128, 2, 1024], BF16) for i in range(1)]
    worr = [0]

    def wload(parts):
        b = wch[wrr[0] % NW]
        wrr[0] += 1
        for (s2, a0, a1, off) in parts:
            P.dma("gpsimd", b.t[:, :, off:off + (a1 - a0)], s2.rearrange("(k p) c -> p k c", p=128)[:, :, a0:a1], w=[b.tok])
        return b

    def woload(l, m):
        b = wop[0]
        worr[0] += 1
        P.dma("gpsimd", b.t[:], d_wout[l][m * 256:(m + 1) * 256, :].rearrange("(k p) c -> p k c", p=128), w=[b.tok])
        return b

    mkT = sb("mkT", [128, 2, 256], BF16)
    mvpad = sb("mvpad", [128, 2, 2, 2, 128], BF16)
    P.g(lambda e: e.memset(mvpad[:], 0.0), w=[mvpad.tok])

    def mem_kv(l):
        wb = wload([(d_wmk[l], 0, 256, 0), (d_wmv[l], 0, 256, 256)])
        mslot = wch[wrr[0] % NW]
        wrr[0] += 1
        memT = Buf(mslot.t[:].rearrange("p a b -> p (a b)")[:, 0:2048].rearrange("p (k m) -> p k m", k=8), "memT")
        memT.tok = mslot.tok
        P.dma("gpsimd", memT.t, d_memT.rearrange("(k p) m -> p k m", p=128), w=[memT.tok])
        for mt in range(2):
            ps = psum()
            for kt in range(8):
                P.t(lambda e, ps=ps, kt=kt, mt=mt: e.matmul(ps[:, :], lhsT=memT[:, kt, mt * 128:(mt + 1) * 128],
                                                            rhs=wb[:, kt, :], start=(kt == 0), stop=(kt == 7)),
                    r=[memT.tok, wb.tok], w=[ps.tok])
            fo = F[mt]
            P.a(lambda e, ps=ps, fo=fo: e.activation(out=fo[:], in_=ps[:, :], func=AF.Copy), r=[ps.tok], w=[fo.tok])
            dst = AP(mvpad.t, mt * 512 + 0, [[1024, 128], [256, 2], [192, 2], [1, 64]])
            P.v(lambda e, ps=ps, dst=dst: e.tensor_copy(out=dst, in_=ps[:, 256:512].rearrange("p (i h e) -> p i h e", i=2, h=2)),
                r=[ps.tok], w=[mvpad.tok])
            P.dma("sync", o_memkv[l, mt * 128:(mt + 1) * 128, :], fo[:], r=[fo.tok], is_output=True)
        for i in range(2):
            ps = psum()
            for kt in range(8):
                P.t(lambda e, ps=ps, kt=kt, i=i: e.matmul(ps[:, 0:256], lhsT=wb[:, kt, i * 128:(i + 1) * 128],
                                                          rhs=memT[:, kt, :], start=(kt == 0), stop=(kt == 7)),
                    r=[memT.tok, wb.tok], w=[ps.tok])
            P.v(lambda e, ps=ps, i=i: e.tensor_copy(out=mkT[:, i, :], in_=ps[:, 0:256]), r=[ps.tok], w=[mkT.tok])

    def rmsnorm(srcs, rtoks, ntok, dsts, wtoks, wcol):
        rs = F[10]
        ps = psum()
        for kt in range(8):
            sq = sq2[kt % 2]
            P.a(lambda e, kt=kt, sq=sq: e.activation(out=sq[:, 0:ntok], in_=srcs(kt), func=AF.Square), r=rtoks, w=[sq.tok])
            P.t(lambda e, kt=kt, sq=sq: e.matmul(ps[:, 0:ntok], lhsT=ones_b[:, :], rhs=sq[:, 0:ntok],
                                                 start=(kt == 0), stop=(kt == 7)), r=[sq.tok, ones_b.tok], w=[ps.tok])
        P.a(lambda e: e.activation(out=rs[:, 0:ntok], in_=ps[:, 0:ntok], func=AF.Ln, scale=1.0 / 1024, bias=EPS),
            r=[ps.tok], w=[rs.tok])
        P.a(lambda e: e.activation(out=rs[:, 0:ntok], in_=rs[:, 0:ntok], func=AF.Exp, scale=-0.5), r=[rs.tok], w=[rs.tok])
        for kt in range(8):
            P.v(lambda e, kt=kt: e.scalar_tensor_tensor(out=dsts(kt), in0=srcs(kt), scalar=wcol(kt), in1=rs[:, 0:ntok],
                                                        op0=ALU.mult, op1=ALU.mult), r=rtoks + [rs.tok, normw.tok, fnw.tok], w=wtoks(kt))

    def sincos(ang, atok, nfi, nff, ntok, sin_o, stok, cos_o, ctok):
        P.v(lambda e: e.tensor_scalar(out=nfi, in0=ang, scalar1=1.0 / TWO_PI, scalar2=None, op0=ALU.mult), r=[atok], w=[ntok])
        P.v(lambda e: e.tensor_copy(out=cos_o, in_=nfi), r=[ntok], w=[ctok])
        P.v(lambda e: e.scalar_tensor_tensor(out=ang, in0=cos_o, scalar=-C1, in1=ang, op0=ALU.mult, op1=ALU.add), r=[ctok, atok], w=[atok])
        P.v(lambda e: e.scalar_tensor_tensor(out=ang, in0=cos_o, scalar=-C2, in1=ang, op0=ALU.mult, op1=ALU.add), r=[ctok, atok], w=[atok])
        P.v(lambda e: e.tensor_scalar(out=ang, in0=ang, scalar1=3.1415925, scalar2=-3.1415925, op0=ALU.min, op1=ALU.max), r=[atok], w=[atok])
        P.a(lambda e: e.activation(out=sin_o, in_=ang, func=AF.Sin), r=[atok], w=[stok])
        P.a(lambda e: e.activation(out=ang, in_=ang, func=AF.Abs), r=[atok], w=[atok])
        P.a(lambda e: e.activation(out=cos_o, in_=ang, func=AF.Sin, scale=-1.0, bias=1.5707963), r=[atok], w=[ctok])

    def rope_tables(blk):
        angb, nf, cosb, sinb = F[0], F[1], F[2], F[3]
        t0 = float(blk * TB)
        for c in range(4):
            P.v(lambda e, c=c: e.tensor_scalar(out=angb[:, c * 128:(c + 1) * 128], in0=cst["l1"][:], scalar1=t0 + c * 128 - 1.0, scalar2=cst["invf"][:, 0:1],
                                               op0=ALU.add, op1=ALU.mult), r=[cst["l1"].tok, cst["invf"].tok], w=[angb.tok])
        sincos(angb[:], angb.tok, nf.t[:].bitcast(I32), nf[:], nf.tok, sinb[:], sinb.tok, cosb[:], cosb.tok)

    def fm_group(wb, c0, blk, m=128):
        ps = psum()
        for kt in range(8):
            P.t(lambda e, kt=kt: e.matmul(ps[0:m, :], lhsT=wb[:, kt, c0:c0 + m], rhs=hnT[:, kt, blk * TB:(blk + 1) * TB],
                                          start=(kt == 0), stop=(kt == 7)), r=[wb.tok, htok[blk]], w=[ps.tok])
        return ps

    def tm_group(wb, c0, blk, c, n=256):
        ps = psum()
        ts = slice(blk * TB + c * 128, blk * TB + (c + 1) * 128)
        for kt in range(8):
            P.t(lambda e, kt=kt: e.matmul(ps[:, 0:n], lhsT=hnT[:, kt, ts], rhs=wb[:, kt, c0:c0 + n],
                                          start=(kt == 0), stop=(kt == 7)), r=[wb.tok, htok[blk]], w=[ps.tok])
        return ps

    def rope_evac(ps, psr, i, dst, dec):
        cosb, sinb = F[2], F[3]
        t1, t2 = F[4 + i], F[6 + i] if False else F[6]
        P.v(lambda e: e.tensor_tensor(out=t1[:], in0=ps[:, :], in1=cosb[:], op=ALU.mult), r=[ps.tok, cosb.tok], w=[t1.tok])
        P.v(lambda e: e.tensor_tensor(out=t2[:], in0=psr[:, :], in1=sinb[:], op=ALU.mult), r=[psr.tok, sinb.tok], w=[t2.tok])
        P.v(lambda e: e.tensor_tensor(out=t1[:], in0=t1[:], in1=t2[:], op=ALU.add), r=[t1.tok, t2.tok], w=[t1.tok])
        decb = AP(dec.t, i * 128, [[256, 128], [0, 4], [1, 128]])
        P.v(lambda e: e.tensor_tensor(out=dst[:, i, :].rearrange("p (c l) -> p c l", c=4),
                                      in0=t1[:].rearrange("p (c l) -> p c l", c=4), in1=decb, op=ALU.mult),
            r=[t1.tok, dec.tok], w=[dst.tok])

    innT = [sb("innT%d" % i, [128, 2, 128], BF16) for i in range(2)]
    irr = [0]
    kTM = [sb("kTM%d" % i, [128, 128], BF16) for i in range(2)]
    krr = [0]
    tts = [sb("tt%d" % i, [128, 256]) for i in range(2)]
    ttr = [0]

    def head_norm_gate(src, gate, gtok, gcol, dst, cen=None, sq32=None, dtok=None):
        cen = F[8] if cen is None else cen
        sq32 = F[9] if sq32 is None else sq32
        ps = psum()
        P.t(lambda e: e.matmul(ps[:, :], lhsT=avg[:], rhs=src[:], start=True, stop=True), r=[avg.tok, src.tok], w=[ps.tok])
        P.v(lambda e: e.tensor_tensor(out=cen[:], in0=src[:], in1=ps[:, :], op=ALU.subtract), r=[src.tok, ps.tok], w=[cen.tok])
        sqb = sq32.t[:].bitcast(BF16)[:, 0:TB]
        P.a(lambda e: e.activation(out=sqb, in_=cen[:], func=AF.Square), r=[cen.tok], w=[sq32.tok])
        ps2 = psum()
        P.t(lambda e: e.matmul(ps2[:, :], lhsT=avg_b[:], rhs=sqb, start=True, stop=True), r=[avg_b.tok, sq32.tok], w=[ps2.tok])
        P.a(lambda e: e.activation(out=sq32[:], in_=ps2[:, :], func=AF.Ln, bias=EPS), r=[ps2.tok], w=[sq32.tok])
        P.a(lambda e: e.activation(out=sq32[:], in_=sq32[:], func=AF.Exp, scale=-0.5), r=[sq32.tok], w=[sq32.tok])
        P.v(lambda e: e.tensor_tensor(out=cen[:], in0=cen[:], in1=sq32[:], op=ALU.mult), r=[cen.tok, sq32.tok], w=[cen.tok])
        P.v(lambda e: e.scalar_tensor_tensor(out=dst, in0=cen[:], scalar=gcol, in1=gate, op0=ALU.mult, op1=ALU.mult),
            r=[cen.tok, gn.tok, gtok], w=[mixp.tok if dtok is None else dtok])

    def outproj_part(wo, blk):
        t0 = blk * TB
        for dt_ in range(8):
            ps = psum()
            for kt in range(2):
                P.t(lambda e, ps=ps, kt=kt, dt_=dt_: e.matmul(ps[:, :], lhsT=wo[:, kt, dt_ * 128:(dt_ + 1) * 128], rhs=mixp[:, kt, :],
                                                              start=(kt == 0), stop=(kt == 1)), r=[wo.tok, mixp.tok], w=[ps.tok])
            P.v(lambda e, ps=ps, dt_=dt_: e.tensor_tensor(out=xT[:, dt_, t0:t0 + TB], in0=xT[:, dt_, t0:t0 + TB], in1=ps[:, :], op=ALU.add),
                r=[ps.tok, xtok[blk]], w=[xtok[blk]])

    Sret = [sb("Sret%d" % i, [128, 128]) for i in range(2)]
    Sret_b = [sb("Sretb%d" % i, [128, 128], BF16) for i in range(2)]

    def ret_phase(l):
        W = d_win[l]
        wA = wload([(W, 0, 512, 0)])
        wB = wload([(W, 512, 1024, 0)])
        wo = woload(l, 0)
        wR = wch[wrr[0] % NW]
        wrr[0] += 1
        src1 = AP(wA.t, 32, [[4096, 128], [512, 8], [64, 8], [1, 32]])
        src0 = AP(wA.t, 0, [[4096, 128], [512, 8], [64, 8], [1, 32]])
        dst0 = AP(wR.t, 0, [[4096, 128], [512, 8], [64, 8], [1, 32]])
        dst1 = AP(wR.t, 32, [[4096, 128], [512, 8], [64, 8], [1, 32]])
        P.g(lambda e: e.tensor_scalar(out=dst0, in0=src1, scalar1=-1.0, scalar2=None, op0=ALU.mult), r=[wA.tok], w=[wR.tok])
        P.g(lambda e: e.tensor_copy(out=dst1, in_=src0), r=[wA.tok], w=[wR.tok])
        rq, rk, rg, rvpad, oT = P0, P1, P2, VA, F[7]
        P.g(lambda e: e.memset(VA[:], 0.0), w=[VA.tok])
        for i in range(2):
            P.v(lambda e, i=i: e.memset(Sret[i][:], 0.0), w=[Sret[i].tok])
            P.v(lambda e, i=i: e.memset(Sret_b[i][:], 0.0), w=[Sret_b[i].tok])
        for blk in range(NBLK):
            rope_tables(blk)
            for i in range(2):
                rope_evac(fm_group(wA, i * 128, blk), fm_group(wR, i * 128, blk), i, rq, decq)
            for i in range(2):
                rope_evac(fm_group(wA, 256 + i * 128, blk), fm_group(wR, 256 + i * 128, blk), i, rk, deck)
            for c in range(4):
                ps = tm_group(wB, 0, blk, c)
                dst = AP(rvpad.t, c * 512, [[2048, 128], [256, 2], [192, 2], [1, 64]])
                P.a(lambda e, ps=ps, dst=dst: e.activation(out=dst, in_=ps[:, 0:256].rearrange("p (i h e) -> p i h e", i=2, h=2),
                                                           func=AF.Copy), r=[ps.tok], w=[rvpad.tok])
            for i in range(2):
                ps = fm_group(wB, 256 + i * 128, blk)
                P.a(lambda e, ps=ps, i=i: e.activation(out=rg[:, i, :], in_=ps[:, :], func=AF.Silu), r=[ps.tok], w=[rg.tok])
            mask4 = AP(cst["maskT"].t, 0, [[128, 128], [0, 4], [1, 128]])
            for i in range(2):
                pso = banks[6 + i]
                P.t(lambda e, pso=pso: e.matmul(pso[:, :], lhsT=zeros_b[:, 0:128], rhs=hnT[:, 0, 0:512], start=True, stop=False), r=[zeros_b.tok, htok[0]], w=[pso.tok])
                for h2 in range(2):
                    ps = psum()
                    hs = slice(64 * h2, 64 * h2 + 64)
                    for c in range(4):
                        cs = slice(c * 128, (c + 1) * 128)
                        P.t(lambda e, ps=ps, hs=hs, cs=cs, i=i: e.matmul(ps[:, cs], lhsT=rk[hs, i, cs], rhs=rq[hs, i, cs], start=True, stop=True),
                            r=[rk.tok, rq.tok], w=[ps.tok])
                    P.v(lambda e, ps=ps, i=i, h2=h2: e.tensor_tensor(out=ET[:, i * 2 + h2, :].rearrange("p (c l) -> p c l", c=4), in0=ps[:, :].rearrange("p (c l) -> p c l", c=4),
                                                                     in1=mask4, op=ALU.mult), r=[ps.tok, cst["maskT"].tok], w=[ET.tok[i * 2 + h2]])
                for c in range(4):
                    cs = slice(c * 128, (c + 1) * 128)
                    for h2 in range(2):
                        P.t(lambda e, c=c, i=i, h2=h2, cs=cs, pso=pso: e.matmul(pso[:, cs], lhsT=rvpad[:, c, i, h2, :], rhs=ET[:, i * 2 + h2, cs], start=False, stop=False),
                            r=[rvpad.tok, ET.tok[i * 2 + h2]], w=[pso.tok])
                pst = psum()
                pstb = pst.t[:].bitcast(BF16)
                for c in range(4):
                    cs = slice(c * 128, (c + 1) * 128)
                    P.t(lambda e, pstb=pstb, i=i, cs=cs: e.transpose(pstb[:, cs], rk[:, i, cs], ident_b[:]), r=[rk.tok, ident_b.tok], w=[pst.tok])
                ktm = sq2[i]
                P.a(lambda e, pstb=pstb, ktm=ktm: e.activation(out=ktm[:], in_=pstb[:, 0:512], func=AF.Copy), r=[pst.tok], w=[ktm.tok])
                psu = psum()
                for c in range(4):
                    cs = slice(c * 128, (c + 1) * 128)
                    vsl = AP(rvpad.t, c * 512 + i * 256, [[2048, 128], [192, 2], [1, 64]])
                    P.t(lambda e, psu=psu, ktm=ktm, vsl=vsl, cs=cs: e.matmul(psu[:, cs].rearrange("p (h e) -> p h e", h=2), lhsT=ktm[:, cs], rhs=vsl, start=True, stop=True),
                        r=[ktm.tok, rvpad.tok], w=[psu.tok])
                tt = F[4 + i]
                bdg = AP(bd2g.t, i * 128, [[256, 128], [0, 4], [1, 128]])
                P.v(lambda e, psu=psu, tt=tt, bdg=bdg: e.tensor_tensor(out=tt[:].rearrange("p (c x) -> p c x", c=4), in0=psu[:, :].rearrange("p (c x) -> p c x", c=4), in1=bdg, op=ALU.mult),
                    r=[psu.tok, bd2g.tok], w=[tt.tok])
            for c in range(4):
                cs = slice(c * 128, (c + 1) * 128)
                for i in range(2):
                    pso = banks[6 + i]
                    tt = F[4 + i]
                    P.t(lambda e, i=i, cs=cs, pso=pso, c=c: e.matmul(pso[:, cs], lhsT=Sret_b[i][:], rhs=rq[:, i, cs], start=False, stop=(c == 3)),
                        r=[Sret_b[i].tok, rq.tok], w=[pso.tok])
                    P.v(lambda e, i=i, tt=tt, cs=cs: e.scalar_tensor_tensor(out=Sret_b[i][:], in0=Sret[i][:], scalar=gL[:, i:i + 1], in1=tt[:, cs], op0=ALU.mult, op1=ALU.add),
                        r=[Sret[i].tok, gL.tok, tt.tok], w=[Sret_b[i].tok])
                    P.v(lambda e, i=i, tt=tt, cs=cs: e.scalar_tensor_tensor(out=Sret[i][:], in0=Sret[i][:], scalar=gL[:, i:i + 1], in1=tt[:, cs], op0=ALU.mult, op1=ALU.add),
                        r=[Sret[i].tok, gL.tok, tt.tok], w=[Sret[i].tok])
            for i in range(2):
                pso = banks[6 + i]
                oTi, ceni, sqi = (F[7], F[8], F[9]) if i == 0 else (F[0], F[1], F[2])
                P.a(lambda e, pso=pso, oTi=oTi: e.activation(out=oTi[:], in_=pso[:, :], func=AF.Copy), r=[pso.tok], w=[oTi.tok])
                head_norm_gate(oTi, rg[:, i, :], rg.tok, gn[:, l * 4 + i:l * 4 + i + 1], mixp[:, i, :], ceni, sqi, mixp.tok[i])
            outproj_part(wo, blk)
        for i in range(2):
            for h2 in range(2):
                P.dma("sync", o_ret[l, 2 * i + h2], Sret[i][64 * h2:64 * h2 + 64, 64 * h2:64 * h2 + 64],
                      r=[Sret[i].tok], is_output=True)
        if SAMPLE >= 1:
            ret_sample(l, wA, wB, wo)

    Cml = [sb("Cml%d" % i, [128, 256]) for i in range(2)]
    Cml_b = [sb("Cmlb%d" % i, [128, 256], BF16) for i in range(2)]
    Bcar = [sb("Bcar%d" % i, [128, 1]) for i in range(2)]
    Gcar = [sb("Gcar%d" % i, [128, 1]) for i in range(2)]
    E1t = sb("E1", [128, 128])
    mkt = [sb("mkt%d" % i, [128, 128], BF16) for i in range(2)]
    mkrr = [0]
    gs_col = sb("gs_col", [128, 4])
    ngs_col = sb("ngs_col", [128, 4])
    mlm_o = sb("mlm_o", [128, 2])

    def ml_phase(l):
        W = d_win[l]
        wC = wload([(W, 1024, 1536, 0)])
        wD = wload([(W, 1536, 2048, 0)])
        wE = wload([(W, 2048, 2312, 0)])
        wo = woload(l, 1)
        mq, mo, mg, mk_, mvp = P0, P1, P2, Pf, VA
        gates8 = Buf(F[6].t[0:8, :], 'gates8')
        gates8.tok = F[6].tok
        LFt, IGt, d0s, oT = F[0], F[1], F[5], F[7]
        ones_col = cst["l1"][:, 0:1].to_broadcast([128, TB])
        rot[0] = [0, 1, 2, 3]
        mask4 = AP(cst["maskT"].t, 0, [[128, 128], [0, 4], [1, 128]])
        for i in range(2):
            P.v(lambda e, i=i: e.memset(Cml[i][:], 0.0), w=[Cml[i].tok])
            P.v(lambda e, i=i: e.memset(Cml_b[i][:], 0.0), w=[Cml_b[i].tok])
            P.v(lambda e, i=i: e.memset(Bcar[i][:], 0.0), w=[Bcar[i].tok])
            P.v(lambda e, i=i: e.memset(Gcar[i][:], 0.0), w=[Gcar[i].tok])
        for blk in range(NBLK):
            for i in range(2 if KMASK & 1 else 0):
                ps = fm_group(wC, i * 128, blk)
                P.a(lambda e, ps=ps, i=i: e.activation(out=mq[:, i, :], in_=ps[:, :], func=AF.Copy), r=[ps.tok], w=[mq.tok])
            for i in range(2 if KMASK & 2 else 0):
                ps = fm_group(wC, 256 + i * 128, blk)
                P.a(lambda e, ps=ps, i=i: e.activation(out=mk_[:, i, :], in_=ps[:, :], func=AF.Copy), r=[ps.tok], w=[Pf.tok[i]])
            for c in range(4 if KMASK & 4 else 0):
                ps = tm_group(wD, 0, blk, c)
                dst = AP(mvp.t, c * 512, [[2048, 128], [256, 2], [192, 2], [1, 64]])
                P.a(lambda e, ps=ps, dst=dst: e.activation(out=dst, in_=ps[:, 0:256].rearrange("p (i h e) -> p i h e", i=2, h=2),
                                                           func=AF.Copy), r=[ps.tok], w=[mvp.tok])
            for i in range(2 if KMASK & 8 else 0):
                ps = fm_group(wD, 256 + i * 128, blk)
                P.a(lambda e, ps=ps, i=i: e.activation(out=mo[:, i, :], in_=ps[:, :], func=AF.Tanh, scale=0.5), r=[ps.tok], w=[mo.tok])
            for i in range(2 if KMASK & 16 else 0):
                ps = fm_group(wE, i * 128, blk)
                P.a(lambda e, ps=ps, i=i: e.activation(out=mg[:, i, :], in_=ps[:, :], func=AF.Silu), r=[ps.tok], w=[mg.tok])
            if KMASK & 32:
                ps = fm_group(wE, 256, blk, m=8)
                P.a(lambda e, ps=ps: e.activation(out=gates8[:, :], in_=ps[0:8, :], func=AF.Identity, bias=bif[:, l:l + 1]),
                    r=[ps.tok, bif.tok], w=[gates8.tok])
            for i in range(2 if KSUB >= 2 else 0):
                Gt = F[2] if i == 0 else F[4]
                MTt = F[3] if i == 0 else F[10]
                rt = Gt
                psi = psum()
                P.t(lambda e, psi=psi, i=i: e.matmul(psi[:, :], lhsT=cst["gexp"][:, (2 * i) * 128:(2 * i + 1) * 128], rhs=gates8[:, :],
                                                     start=True, stop=True), r=[cst["gexp"].tok, gates8.tok], w=[psi.tok])
                psf = psum()
                P.t(lambda e, psf=psf, i=i: e.matmul(psf[:, :], lhsT=cst["gexp"][:, (2 * i + 1) * 128:(2 * i + 2) * 128], rhs=gates8[:, :],
                                                     start=True, stop=True), r=[cst["gexp"].tok, gates8.tok], w=[psf.tok])
                P.a(lambda e, psf=psf: e.activation(out=LFt[:], in_=psf[:, :], func=AF.Exp, scale=-1.0), r=[psf.tok], w=[LFt.tok])
                P.a(lambda e: e.activation(out=LFt[:], in_=LFt[:], func=AF.Ln, bias=1.0), r=[LFt.tok], w=[LFt.tok])
                P.v(lambda e: e.tensor_scalar(out=LFt[:], in0=LFt[:], scalar1=-1.0, scalar2=None, op0=ALU.mult), r=[LFt.tok], w=[LFt.tok])
                P.v(lambda e, i=i: e.tensor_tensor_scan(out=LFt[:], data0=ones_col, data1=LFt[:], initial=Bcar[i][:, 0:1],
                                                        op0=ALU.mult, op1=ALU.add), r=[LFt.tok, Bcar[i].tok, cst["l1"].tok], w=[LFt.tok])
                P.v(lambda e, psi=psi: e.tensor_tensor(out=IGt[:], in0=psi[:, :], in1=LFt[:], op=ALU.subtract), r=[psi.tok, LFt.tok], w=[IGt.tok])
                P.v(lambda e, i=i: e.tensor_tensor_scan(out=Gt[:], data0=ones_col, data1=IGt[:], initial=Gcar[i][:, 0:1],
                                                        op0=ALU.mult, op1=ALU.max), r=[IGt.tok, Gcar[i].tok, cst["l1"].tok], w=[Gt.tok])
                P.v(lambda e, i=i: e.tensor_copy(out=gs_col[:, 0:1], in_=Gcar[i][:, 0:1]), r=[Gcar[i].tok], w=[gs_col.tok])
                P.v(lambda e: e.tensor_copy(out=gs_col[:, 1:4], in_=AP(Gt.t, 127, [[TB, 128], [128, 3]])), r=[Gt.tok], w=[gs_col.tok])
                P.v(lambda e: e.tensor_scalar(out=ngs_col[:], in0=gs_col[:], scalar1=-1.0, scalar2=None, op0=ALU.mult), r=[gs_col.tok], w=[ngs_col.tok])
                P.v(lambda e: e.tensor_tensor(out=MTt[:], in0=LFt[:], in1=Gt[:], op=ALU.add), r=[LFt.tok, Gt.tok], w=[MTt.tok])
                if blk == NBLK - 1:
                    P.v(lambda e, i=i: e.tensor_copy(out=mlm_o[:, i:i + 1], in_=MTt[:, TB - 1:TB]), r=[MTt.tok], w=[mlm_o.tok])
                P.a(lambda e: e.activation(out=MTt[:], in_=MTt[:], func=AF.Exp, scale=-1.0), r=[MTt.tok], w=[MTt.tok])
                P.v(lambda e, i=i: e.tensor_copy(out=Bcar[i][:, 0:1], in_=LFt[:, TB - 1:TB]), r=[LFt.tok], w=[Bcar[i].tok])
                P.v(lambda e, i=i: e.tensor_copy(out=Gcar[i][:, 0:1], in_=Gt[:, TB - 1:TB]), r=[Gt.tok], w=[Gcar[i].tok])
                psn = banks[6] if i == 0 else banks[4]
                psd = banks[7] if i == 0 else banks[5]
                if KSUB < 3:
                    continue
                for c in range(4):
                    cs = slice(c * 128, (c + 1) * 128)
                    P.a(lambda e, c=c, cs=cs: e.activation(out=IGt[:, cs], in_=IGt[:, cs], func=AF.Exp, bias=ngs_col[:, c:c + 1]),
                        r=[IGt.tok, ngs_col.tok], w=[IGt.tok])
                    P.a(lambda e, c=c, cs=cs, Gt=Gt: e.activation(out=Gt[:, cs], in_=Gt[:, cs], func=AF.Exp, scale=-1.0, bias=gs_col[:, c:c + 1]),
                        r=[Gt.tok, gs_col.tok], w=[Gt.tok])
                ktil = mixp[:, i, :]
                mtk = mixp.tok[i]
                P.v(lambda e, i=i, ktil=ktil: e.scalar_tensor_tensor(out=ktil, in0=mk_[:, i, :], scalar=0.125, in1=IGt[:], op0=ALU.mult, op1=ALU.mult),
                    r=[Pf.tok[i], IGt.tok], w=[mtk])
                for bk in (psn, psd):
                    P.t(lambda e, bk=bk: e.matmul(bk[:, :], lhsT=zeros_b[:, 0:128], rhs=hnT[:, 0, 0:512], start=True, stop=False), r=[zeros_b.tok, htok[0]], w=[bk.tok])
                for h2 in range(2):
                    ps = psum()
                    hs = slice(64 * h2, 64 * h2 + 64)
                    ek = 2 * i + h2
                    for c in range(4):
                        cs = slice(c * 128, (c + 1) * 128)
                        P.t(lambda e, ps=ps, hs=hs, cs=cs, i=i: e.matmul(ps[:, cs], lhsT=mixp[hs, i, cs], rhs=mq[hs, i, cs], start=True, stop=True),
                            r=[mtk, mq.tok], w=[ps.tok])
                    P.v(lambda e, ps=ps, ek=ek: e.tensor_tensor(out=ET[:, ek, :].rearrange("p (c l) -> p c l", c=4), in0=ps[:, :].rearrange("p (c l) -> p c l", c=4),
                                                                in1=mask4, op=ALU.mult), r=[ps.tok, cst["maskT"].tok], w=[ET.tok[ek]])
                for c in range(4):
                    cs = slice(c * 128, (c + 1) * 128)
                    for h2 in range(2):
                        ek = 2 * i + h2
                        P.t(lambda e, c=c, i=i, h2=h2, cs=cs, ek=ek, psn=psn: e.matmul(psn[:, cs], lhsT=mvp[:, c, i, h2, :], rhs=ET[:, ek, cs], start=False, stop=False),
                            r=[mvp.tok, ET.tok[ek]], w=[psn.tok])
                        P.t(lambda e, h2=h2, cs=cs, ek=ek, psd=psd: e.matmul(psd[:, cs], lhsT=onespad[:, h2, :], rhs=ET[:, ek, cs], start=False, stop=False),
                            r=[onespad.tok, ET.tok[ek]], w=[psd.tok])
                pst = psum()
                pstb = pst.t[:].bitcast(BF16)
                for c in range(4):
                    cs = slice(c * 128, (c + 1) * 128)
                    P.t(lambda e, pstb=pstb, i=i, cs=cs: e.transpose(pstb[:, cs], mixp[:, i, cs], ident_b[:]), r=[mtk, ident_b.tok], w=[pst.tok])
                ktm = sq2[i]
                P.a(lambda e, pstb=pstb, ktm=ktm: e.activation(out=ktm[:], in_=pstb[:, 0:512], func=AF.Copy), r=[pst.tok], w=[ktm.tok])
                psuA = psum()
                psuB = psum()
                for c in range(4):
                    cs = slice(c * 128, (c + 1) * 128)
                    vsl = AP(mvp.t, c * 512 + i * 256, [[2048, 128], [192, 2], [1, 64]])
                    P.t(lambda e, ktm=ktm, vsl=vsl, cs=cs, psuA=psuA: e.matmul(psuA[:, cs].rearrange("p (h e) -> p h e", h=2), lhsT=ktm[:, cs], rhs=vsl, start=True, stop=True),
                        r=[ktm.tok, mvp.tok], w=[psuA.tok])
                    P.t(lambda e, ktm=ktm, cs=cs, c=c, psuB=psuB: e.matmul(psuB[:, c:c + 1], lhsT=ktm[:, cs], rhs=ones_b[:, 0:1], start=True, stop=True),
                        r=[ktm.tok, ones_b.tok], w=[psuB.tok])
                ttA = Pf[:, i, :]
                for c in range(4):
                    cs = slice(c * 128, (c + 1) * 128)
                    rcol = rt[:, c * 128 + 127:c * 128 + 128]
                    P.v(lambda e, cs=cs, rcol=rcol, psuA=psuA, i=i: e.scalar_tensor_tensor(out=Pf[:, i, cs], in0=psuA[:, cs], scalar=rcol, in1=cst["bd2"][:], op0=ALU.mult, op1=ALU.mult),
                        r=[psuA.tok, rt.tok, cst["bd2"].tok, mtk], w=[Pf.tok[i]])
                rcols = AP(rt.t, 127, [[TB, 128], [128, 4]])
                P.v(lambda e, psuB=psuB, i=i: e.tensor_tensor(out=ncs[:, i, :], in0=psuB[:, 0:4], in1=rcols, op=ALU.mult), r=[psuB.tok, rt.tok], w=[ncs.tok])
                for c in range(4):
                    cs = slice(c * 128, (c + 1) * 128)
                    P.t(lambda e, i=i, cs=cs, psn=psn: e.matmul(psn[:, cs], lhsT=Cml_b[i][:, 0:128], rhs=mq[:, i, cs], start=False, stop=(cs.stop == 512)),
                        r=[Cml_b[i].tok, mq.tok], w=[psn.tok])
                    P.t(lambda e, i=i, cs=cs, psd=psd: e.matmul(psd[:, cs], lhsT=Cml_b[i][:, 128:256], rhs=mq[:, i, cs], start=False, stop=(cs.stop == 512)),
                        r=[Cml_b[i].tok, mq.tok], w=[psd.tok])
                    tt = tts[ttr[0] % 2]
                    ttr[0] += 1
                    rcol = rt[:, c * 128 + 127:c * 128 + 128]
                    P.v(lambda e, tt=tt, c=c, i=i: e.tensor_scalar(out=tt[:, 0:128], in0=cst["bd2"][:], scalar1=ncs[:, i, c:c + 1], scalar2=None, op0=ALU.mult),
                        r=[ncs.tok, cst["bd2"].tok], w=[tt.tok])
                    P.v(lambda e, i=i, cs=cs, rcol=rcol: e.scalar_tensor_tensor(out=Cml_b[i][:, 0:128], in0=Cml[i][:, 0:128], scalar=rcol, in1=Pf[:, i, cs], op0=ALU.mult, op1=ALU.add),
                        r=[Cml[i].tok, rt.tok, Pf.tok[i]], w=[Cml_b[i].tok])
                    P.v(lambda e, tt=tt, i=i, rcol=rcol: e.scalar_tensor_tensor(out=Cml_b[i][:, 128:256], in0=Cml[i][:, 128:256], scalar=rcol, in1=tt[:, 0:128], op0=ALU.mult, op1=ALU.add),
                        r=[Cml[i].tok, rt.tok, tt.tok], w=[Cml_b[i].tok])
                    P.v(lambda e, i=i, cs=cs, rcol=rcol: e.scalar_tensor_tensor(out=Cml[i][:, 0:128], in0=Cml[i][:, 0:128], scalar=rcol, in1=Pf[:, i, cs], op0=ALU.mult, op1=ALU.add),
                        r=[Cml[i].tok, rt.tok, Pf.tok[i]], w=[Cml[i].tok])
                    P.v(lambda e, tt=tt, i=i, rcol=rcol: e.scalar_tensor_tensor(out=Cml[i][:, 128:256], in0=Cml[i][:, 128:256], scalar=rcol, in1=tt[:, 0:128], op0=ALU.mult, op1=ALU.add),
                        r=[Cml[i].tok, rt.tok, tt.tok], w=[Cml[i].tok])
                d0i, oTi, ceni, sqi = (F[5], F[7], F[8], F[9]) if i == 0 else (F[0], F[1], F[6], F[0])
                P.v(lambda e, psd=psd, d0i=d0i: e.tensor_tensor(out=d0i[:], in0=psd[:, :], in1=rt[:], op=ALU.mult), r=[psd.tok, rt.tok], w=[d0i.tok])
                P.a(lambda e, psn=psn, oTi=oTi: e.activation(out=oTi[:], in_=psn[:, :], func=AF.Copy), r=[psn.tok], w=[oTi.tok])
                P.a(lambda e, d0i=d0i: e.activation(out=d0i[:], in_=d0i[:], func=AF.Abs), r=[d0i.tok], w=[d0i.tok])
                P.v(lambda e, d0i=d0i: e.tensor_tensor(out=d0i[:], in0=d0i[:], in1=MTt[:], op=ALU.max), r=[d0i.tok, MTt.tok], w=[d0i.tok])
                P.a(lambda e, d0i=d0i: e.activation(out=d0i[:], in_=d0i[:], func=AF.Ln), r=[d0i.tok], w=[d0i.tok])
                P.a(lambda e, d0i=d0i: e.activation(out=d0i[:], in_=d0i[:], func=AF.Exp, scale=-1.0), r=[d0i.tok], w=[d0i.tok])
                P.v(lambda e, d0i=d0i: e.scalar_tensor_tensor(out=d0i[:], in0=d0i[:], scalar=0.5, in1=rt[:], op0=ALU.mult, op1=ALU.mult), r=[d0i.tok, rt.tok], w=[d0i.tok])
                P.v(lambda e, d0i=d0i, oTi=oTi: e.tensor_tensor(out=oTi[:], in0=oTi[:], in1=d0i[:], op=ALU.mult), r=[oTi.tok, d0i.tok], w=[oTi.tok])
                P.v(lambda e, i=i, oTi=oTi: e.scalar_tensor_tensor(out=oTi[:], in0=mo[:, i, :], scalar=1.0, in1=oTi[:], op0=ALU.add, op1=ALU.mult), r=[oTi.tok, mo.tok], w=[oTi.tok])
                head_norm_gate(oTi, mg[:, i, :], mg.tok, gn[:, l * 4 + 2 + i:l * 4 + 2 + i + 1], mixp[:, i, :], ceni, sqi, mixp.tok[i])
            if KSUB >= 3:
                outproj_part(wo, blk)
        for i in range(2 if KSUB >= 4 else 0):
            for h2 in range(2):
                hs = slice(64 * h2, 64 * h2 + 64)
                P.dma("sync", o_mlc[l, 2 * i + h2], Cml[i][hs, 64 * h2:64 * h2 + 64], r=[Cml[i].tok], is_output=True)
                P.dma("sync", o_mln[l, 2 * i + h2].rearrange("(a b) -> a b", b=1), Cml[i][hs, 128 + 64 * h2:128 + 64 * h2 + 1],
                      r=[Cml[i].tok], is_output=True)
                P.dma("sync", o_mlm[l, 2 * i + h2:2 * i + h2 + 1].rearrange("(a b) -> a b", b=1), mlm_o[64 * h2:64 * h2 + 1, i:i + 1],
                      r=[mlm_o.tok], is_output=True)
        rot[0] = [0, 1, 2, 3, 4, 5]
        if SAMPLE >= 2:
            ml_sample(l, wC, wD, wE, wo)

    def xa_phase(l):
        W = d_win[l]
        wG = wload([(W, 2824, 3336, 0)])
        wo = woload(l, 3)
        aq, ag = P0, P1
        ett = [[Buf(ET.t[:, k, :], "ET%d" % k) for k in range(4)],
               [Buf(F[1 + k].t[:].bitcast(BF16)[:, 0:TB], "ETb%d" % k) for k in range(4)]]
        for k in range(4):
            ett[0][k].tok = ET.tok[k]
            ett[1][k].tok = F[1 + k].tok
        for blk in range(NBLK):
            for i in range(2):
                ps = fm_group(wG, i * 128, blk)
                P.a(lambda e, ps=ps, i=i: e.activation(out=aq[:, i, :], in_=ps[:, :], func=AF.Copy), r=[ps.tok], w=[aq.tok])
            for i in range(2):
                ps = fm_group(wG, 256 + i * 128, blk)
                P.a(lambda e, ps=ps, i=i: e.activation(out=ag[:, i, :], in_=ps[:, :], func=AF.Silu), r=[ps.tok], w=[ag.tok])
            for i in range(2):
                for h2 in range(2):
                    hs = slice(64 * h2, 64 * h2 + 64)
                    for mt in range(2):
                        ps = psum()
                        P.t(lambda e, ps=ps, hs=hs, mt=mt, i=i: e.matmul(ps[:, :], lhsT=mkT[hs, i, mt * 128:(mt + 1) * 128], rhs=aq[hs, i, :],
                                                                         start=True, stop=True), r=[mkT.tok, aq.tok], w=[ps.tok])
                        et = ett[i][h2 * 2 + mt]
                        P.a(lambda e, ps=ps, et=et: e.activation(out=et.t, in_=ps[:, :], func=AF.Exp, scale=0.125), r=[ps.tok], w=[et.tok])
                pso = banks[6] if i == 0 else psum()
                psd = banks[7] if i == 0 else psum()
                recd = F[0] if i == 0 else F[5]
                n = 0
                for h2 in range(2):
                    for mt in range(2):
                        et = ett[i][h2 * 2 + mt]
                        P.t(lambda e, h2=h2, mt=mt, i=i, n=n, et=et, pso=pso: e.matmul(pso[:, :], lhsT=mvpad[:, mt, i, h2, :], rhs=et.t,
                                                                                     start=(n == 0), stop=(n == 3)), r=[mvpad.tok, et.tok], w=[pso.tok])
                        n += 1
                n = 0
                for h2 in range(2):
                    for mt in range(2):
                        et = ett[i][h2 * 2 + mt]
                        P.t(lambda e, h2=h2, mt=mt, n=n, et=et, psd=psd: e.matmul(psd[:, :], lhsT=onespad[:, h2, :], rhs=et.t,
                                                                                start=(n == 0), stop=(n == 3)), r=[onespad.tok, et.tok], w=[psd.tok])
                        n += 1
                P.a(lambda e, psd=psd, recd=recd: e.activation(out=recd[:], in_=psd[:, :], func=AF.Ln), r=[psd.tok], w=[recd.tok])
                P.a(lambda e, recd=recd: e.activation(out=recd[:], in_=recd[:], func=AF.Exp, scale=-1.0), r=[recd.tok], w=[recd.tok])
                P.v(lambda e, pso=pso, recd=recd: e.tensor_tensor(out=recd[:], in0=pso[:, :], in1=recd[:], op=ALU.mult), r=[pso.tok, recd.tok], w=[recd.tok])
                P.v(lambda e, i=i, recd=recd: e.tensor_tensor(out=mixp[:, i, :], in0=recd[:], in1=ag[:, i, :], op=ALU.mult),
                    r=[recd.tok, ag.tok], w=[mixp.tok[i]])
            outproj_part(wo, blk)
        if SAMPLE >= 3:
            xa_sample(l, wG, wo)


    d_s5par = din("s5par", [128, DEPTH * 3 * 8])
    d_s5b = din("s5b", [DEPTH, 128, 2 * 8 * 16])
    d_s5c = din("s5c", [DEPTH, 128, 2 * 8 * 16])
    d_s5d = din("s5d", [128, DEPTH * 2])
    d_wglu = din("w_glu", [DEPTH, 256, 256])
    o_s5 = dout("o_s5", [DEPTH, 128, 2, 8])
    s5par = sb("s5par", [128, DEPTH, 3, 8])
    s5b = sb("s5b", [128, 2, 8, 16])
    s5c = sb("s5c", [128, 2, 8, 16])
    s5d = sb("s5d", [128, DEPTH, 2])
    P.dma("sync", s5par[:], d_s5par.rearrange("p (l w g) -> p l w g", l=DEPTH, w=3), w=[s5par.tok])
    P.dma("sync", s5d[:], d_s5d.rearrange("p (l f) -> p l f", l=DEPTH), w=[s5d.tok])
    sA = sb("sA", [128, 6, 8])
    sK = sb("sK", [128, 5, 8, 9])
    sKi = sb("sKi", [128, 8, 9], I32)
    sBB = sb("sBB", [128, 2, 8, 16])
    sE = sb("sE", [128, 4, 9, 16])
    sT = Buf(F[9].t[:, 0:288].rearrange("p (a k c) -> p a k c", a=2, k=9), "sT")
    sT.tok = F[9].tok
    s5o = sb("s5o", [128, 2, 8])
    d_sx0 = din("sx0", [DEPTH, 128, 2 * 8 * NS])
    o_s5s = dout("o_s5s", [DEPTH, 128, 2 * 8 * NS])
    xs0 = sb("xs0", [128, 2, 8, NS])
    xs0b = sb("xs0b", [128, 2, 8, NS], BF16)
    usT = sb("usT", [128, 2, NS], BF16)
    sgs = sb("sgs", [128, 2, NS])
    ysacc = sb("ysacc", [128, 2, NS])
    sst = sb("sst", [128, 4, NS])

    def s5_phase(l):
        W = d_win[l]
        wF = wload([(W, 2312, 2824, 0)])
        wo = woload(l, 2)
        P.dma("sync", s5b[:], d_s5b[l].rearrange("p (r g c) -> p r g c", r=2, g=8), w=[s5b.tok])
        P.dma("sync", s5c[:], d_s5c[l].rearrange("p (r g c) -> p r g c", r=2, g=8), w=[s5c.tok])
        uT = [ET, VA]
        pfb = Pf.t[:].rearrange("p a b -> p (a b)").bitcast(BF16)

        def sgp(ft, blk):
            ix = ft * 4 + blk
            if ix < 4:
                return pfb[:, ix * TB:(ix + 1) * TB], Pf.tok
            if ix < 6:
                return P2[:, ix - 4, :], P2.tok
            return sq2[ix - 6][:, :], sq2[ix - 6].tok
        uTap = [ET.t[:].rearrange("p a b -> p (a b)"), VA.t[:].rearrange("p a b c d -> p (a b c d)")]
        for blk in range(NBLK):
            for ft in range(2):
                ps = fm_group(wF, ft * 128, blk)
                dsti = AP(uT[ft].t, blk * 64, [[T, 128], [1, 64], [256, 8]])
                P.a(lambda e, ps=ps, dsti=dsti: e.activation(out=dsti, in_=ps[:, :].rearrange("p (j s) -> p j s", s=8), func=AF.Copy),
                    r=[ps.tok], w=[uT[ft].tok])
            for ft in range(2):
                ps = fm_group(wF, 256 + ft * 128, blk)
                sga, sgtok = sgp(ft, blk)
                P.a(lambda e, ps=ps, sga=sga: e.activation(out=sga, in_=ps[:, :], func=AF.Silu), r=[ps.tok], w=[sgtok])
        if SAMPLE >= 4:
            for ft in range(2):
                ps = fm_sample(wF, ft * 128)
                P.a(lambda e, ps=ps, ft=ft: e.activation(out=usT[:, ft, :], in_=ps[:, 0:NS], func=AF.Copy), r=[ps.tok], w=[usT.tok])
                ps = fm_sample(wF, 256 + ft * 128)
                P.a(lambda e, ps=ps, ft=ft: e.activation(out=sgs[:, ft, :], in_=ps[:, 0:NS], func=AF.Silu), r=[ps.tok], w=[sgs.tok])
            P.dma("sync", xs0[:].rearrange("p a b c -> p (a b c)"), d_sx0[l], w=[xs0.tok])
            P.v(lambda e: e.tensor_copy(out=xs0b[:], in_=xs0[:]), r=[xs0.tok], w=[xs0b.tok])
            P.v(lambda e: e.memset(ysacc[:], 0.0), w=[ysacc.tok])
        are, aim, ldt = s5par[:, l, 0, :], s5par[:, l, 1, :], s5par[:, l, 2, :]
        dtc, ardt, th, fre, fim, tm8 = (sA[:, j, :] for j in range(6))
        P.a(lambda e: e.activation(out=dtc, in_=ldt, func=AF.Exp), r=[s5par.tok], w=[sA.tok])
        P.v(lambda e: e.tensor_tensor(out=ardt, in0=are, in1=dtc, op=ALU.mult), r=[s5par.tok, sA.tok], w=[sA.tok])
        P.v(lambda e: e.tensor_tensor(out=th, in0=aim, in1=dtc, op=ALU.mult), r=[s5par.tok, sA.tok], w=[sA.tok])
        P.v(lambda e: e.tensor_scalar(out=tm8, in0=th, scalar1=8.0, scalar2=None, op0=ALU.mult), r=[sA.tok], w=[sA.tok])
        kg = AP(cst["kgrid"].t, 0, [[9, 128], [0, 8], [1, 9]])
        arg, ang, Pr, Pi, scr = (sK[:, j, :, :] for j in range(5))
        P.v(lambda e: e.tensor_tensor(out=arg, in0=AP(sA.t, 8, [[48, 128], [1, 8], [0, 9]]), in1=kg, op=ALU.mult), r=[sA.tok, cst["kgrid"].tok], w=[sK.tok])
        P.a(lambda e: e.activation(out=arg, in_=arg, func=AF.Exp), r=[sK.tok], w=[sK.tok])
        P.v(lambda e: e.tensor_tensor(out=ang, in0=AP(sA.t, 16, [[48, 128], [1, 8], [0, 9]]), in1=kg, op=ALU.mult), r=[sA.tok, cst["kgrid"].tok], w=[sK.tok])
        sincos(ang, sK.tok, sKi[:], scr, sKi.tok, Pi, sK.tok, Pr, sK.tok)
        P.v(lambda e: e.tensor_tensor(out=Pr, in0=Pr, in1=arg, op=ALU.mult), r=[sK.tok], w=[sK.tok])
        P.v(lambda e: e.tensor_tensor(out=Pi, in0=Pi, in1=arg, op=ALU.mult), r=[sK.tok], w=[sK.tok])
        nr, ni, den = scr[:, :, 0], scr[:, :, 1], scr[:, :, 2]
        t_a, t_b = scr[:, :, 3], scr[:, :, 4]
        P.v(lambda e: e.tensor_scalar(out=nr, in0=sK[:, 2, :, 1], scalar1=-1.0, scalar2=None, op0=ALU.add), r=[sK.tok], w=[sK.tok])
        P.v(lambda e: e.tensor_copy(out=ni, in_=sK[:, 3, :, 1]), r=[sK.tok], w=[sK.tok])
        P.v(lambda e: e.tensor_tensor(out=den, in0=are, in1=are, op=ALU.mult), r=[s5par.tok], w=[sK.tok])
        P.v(lambda e: e.tensor_tensor(out=t_a, in0=aim, in1=aim, op=ALU.mult), r=[s5par.tok], w=[sK.tok])
        P.v(lambda e: e.tensor_tensor(out=den, in0=den, in1=t_a, op=ALU.add), r=[sK.tok], w=[sK.tok])
        P.v(lambda e: e.reciprocal(out=den, in_=den), r=[sK.tok], w=[sK.tok])
        P.v(lambda e: e.tensor_tensor(out=t_a, in0=nr, in1=are, op=ALU.mult), r=[sK.tok, s5par.tok], w=[sK.tok])
        P.v(lambda e: e.tensor_tensor(out=t_b, in0=ni, in1=aim, op=ALU.mult), r=[sK.tok, s5par.tok], w=[sK.tok])
        P.v(lambda e: e.tensor_tensor(out=t_a, in0=t_a, in1=t_b, op=ALU.add), r=[sK.tok], w=[sK.tok])
        P.v(lambda e: e.tensor_tensor(out=fre, in0=t_a, in1=den, op=ALU.mult), r=[sK.tok], w=[sA.tok])
        P.v(lambda e: e.tensor_tensor(out=t_a, in0=ni, in1=are, op=ALU.mult), r=[sK.tok, s5par.tok], w=[sK.tok])
        P.v(lambda e: e.tensor_tensor(out=t_b, in0=nr, in1=aim, op=ALU.mult), r=[sK.tok, s5par.tok], w=[sK.tok])
        P.v(lambda e: e.tensor_tensor(out=t_a, in0=t_a, in1=t_b, op=ALU.subtract), r=[sK.tok], w=[sK.tok])
        P.v(lambda e: e.tensor_tensor(out=fim, in0=t_a, in1=den, op=ALU.mult), r=[sK.tok], w=[sA.tok])
        freb = AP(sA.t, 24, [[48, 128], [1, 8], [0, 16]])
        fimb = AP(sA.t, 32, [[48, 128], [1, 8], [0, 16]])
        bre, bim = s5b[:, 0, :, :], s5b[:, 1, :, :]
        bbre, bbim = sBB[:, 0, :, :], sBB[:, 1, :, :]
        tq = F[9].t[:, 0:128].rearrange("p (g c) -> p g c", g=8)
        P.v(lambda e: e.tensor_tensor(out=bbre, in0=bre, in1=freb, op=ALU.mult), r=[s5b.tok, sA.tok], w=[sBB.tok])
        P.v(lambda e: e.tensor_tensor(out=tq, in0=bim, in1=fimb, op=ALU.mult), r=[s5b.tok, sA.tok], w=[sT.tok])
        P.v(lambda e: e.tensor_tensor(out=bbre, in0=bbre, in1=tq, op=ALU.subtract), r=[sBB.tok, sT.tok], w=[sBB.tok])
        P.v(lambda e: e.tensor_tensor(out=bbim, in0=bim, in1=freb, op=ALU.mult), r=[s5b.tok, sA.tok], w=[sBB.tok])
        P.v(lambda e: e.tensor_tensor(out=tq, in0=bre, in1=fimb, op=ALU.mult), r=[s5b.tok, sA.tok], w=[sT.tok])
        P.v(lambda e: e.tensor_tensor(out=bbim, in0=bbim, in1=tq, op=ALU.add), r=[sBB.tok, sT.tok], w=[sBB.tok])

        W0, W1b, WZ = wch[(wrr[0]) % NW], wch[(wrr[0] + 1) % NW], wF
        w0f = W0.t[:].rearrange("p a b -> p (a b)")
        w1f = W1b.t[:].rearrange("p a b -> p (a b)")
        wzf = WZ.t[:].rearrange("p a b -> p (a b)")
        ptok = [Tok("ptwA"), Tok("ptwB")]
        ytok = [W1b.tok, WZ.tok]
        ptw = [w0f[:, 0:2048], w0f[:, 2048:4096]]
        ymw = [w1f[:, 0:2304], wzf[:, 0:2304]]
        PTp = [[ptw[p_][:, 0:1024].rearrange("p (k c) -> p k c", k=8), ptw[p_][:, 1024:2048].rearrange("p (k c) -> p k c", k=8)] for p_ in range(2)]
        Ymp = [[ymw[p_][:, 0:1152].rearrange("p (k c) -> p k c", k=9), ymw[p_][:, 1152:2304].rearrange("p (k c) -> p k c", k=9)] for p_ in range(2)]
        wglu = Buf(w1f[:, 2304:2816].rearrange("p (k c) -> p k c", k=2), "wglu")
        wglu.tok = W1b.tok
        P.dma("gpsimd", wglu.t, d_wglu[l].rearrange("(k p) c -> p k c", p=128), w=[wglu.tok])
        first_use = [True, True]
        xprev = P0
        BDm = P1.t[:].rearrange("p a b -> p (a b)")[:, 0:1024].rearrange("p (k c) -> p k c", k=8)
        rot[0] = [6, 7]
        yacc = banks[0:4]
        bdacc = banks[4:6]
        cs_t, an_t, vt_t, wt_t, xt_t, t1_t, t2_t, nf_t = F[0], F[1], F[2], F[3], F[4], F[5], F[6], F[7]
        for ft in range(2):
            for bk in list(yacc) + list(bdacc):
                P.t(lambda e, bk=bk: e.matmul(bk[:, :], lhsT=zeros_b[:, 0:128], rhs=hnT[:, 0, 0:512], start=True, stop=False), r=[zeros_b.tok, htok[0]], w=[bk.tok])
            for pp in range(4):
                pi_ = ft * 4 + pp
                par = pi_ % 2
                PT, W1, Ym = PTp[par], PTp[par], Ymp[par]
                PTK, YTK = ptok[par], ytok[par]
                Prb = AP(sK.t, 2 * 72 + pi_ * 9, [[360, 128], [1, 9], [0, 16]])
                Pib = AP(sK.t, 3 * 72 + pi_ * 9, [[360, 128], [1, 9], [0, 16]])
                bbr = AP(sBB.t, pi_ * 16, [[256, 128], [0, 9], [1, 16]])
                bbi = AP(sBB.t, 128 + pi_ * 16, [[256, 128], [0, 9], [1, 16]])
                crb = AP(s5c.t, 0 * 128 + pi_ * 16, [[256, 128], [0, 9], [1, 16]])
                cib = AP(s5c.t, 1 * 128 + pi_ * 16, [[256, 128], [0, 9], [1, 16]])
                Ere, Eim, CAre, CAim = (sE[:, j, :, :] for j in range(4))
                ta, tb = sT[:, 0, :, :], sT[:, 1, :, :]
                for (o_, x1, y1, x2, y2, op_) in ((Ere, Prb, bbr, Pib, bbi, ALU.subtract), (Eim, Prb, bbi, Pib, bbr, ALU.add),
                                                   (CAre, Prb, crb, Pib, cib, ALU.subtract), (CAim, Pib, crb, Prb, cib, ALU.add)):
                    P.v(lambda e, o_=o_, x1=x1, y1=y1: e.tensor_tensor(out=o_, in0=x1, in1=y1, op=ALU.mult), r=[sK.tok, sBB.tok, s5c.tok], w=[sE.tok])
                    P.v(lambda e, x2=x2, y2=y2: e.tensor_tensor(out=ta, in0=x2, in1=y2, op=ALU.mult), r=[sK.tok, sBB.tok, s5c.tok], w=[sT.tok])
                    P.v(lambda e, o_=o_, op_=op_: e.tensor_tensor(out=o_, in0=o_, in1=ta, op=op_), r=[sE.tok, sT.tok], w=[sE.tok])
                P.g(lambda e, par=par: e.memset(ptw[par], 0.0), w=[PTK] + ([W0.tok] if first_use[par] else []))
                first_use[par] = False
                P.g(lambda e, par=par: e.memset(ymw[par], 0.0), w=[YTK])
                for g2 in range(2):
                    rs_ = slice(64 * g2, 64 * g2 + 64)
                    c0 = (2 * pp + g2) * 16
                    for ri in range(2):
                        P.v(lambda e, rs_=rs_, c0=c0, ri=ri: e.tensor_copy(out=PT[ri][rs_, :, c0:c0 + 16], in_=sE[rs_, ri, 0:8, :]), r=[sE.tok], w=[PTK])
                    P.v(lambda e, rs_=rs_, c0=c0: e.tensor_copy(out=Ym[0][rs_, :, c0:c0 + 16], in_=sE[rs_, 2, :, :]), r=[sE.tok], w=[YTK])
                    P.v(lambda e, rs_=rs_, c0=c0: e.tensor_scalar(out=Ym[1][rs_, :, c0:c0 + 16], in0=sE[rs_, 3, :, :], scalar1=-1.0, scalar2=None, op0=ALU.mult),
                        r=[sE.tok], w=[YTK])
                for k in range(8):
                    bk = bdacc[k // 4]
                    for ri in range(2):
                        P.t(lambda e, k=k, ri=ri, bk=bk, pp=pp: e.matmul(bk[:, (k % 4) * 128:(k % 4) * 128 + 128], lhsT=PT[ri][:, 0, :], rhs=Ym[ri][:, k, :],
                                                                         start=False, stop=(pp == 3 and ri == 1 and k % 4 == 3)),
                            r=[PTK, YTK], w=[bk.tok])
                for ri in range(2):
                    pst = psum()
                    pstb = pst.t[:].bitcast(BF16)
                    for k in range(8):
                        P.t(lambda e, pstb=pstb, ri=ri, k=k: e.transpose(pstb[:, k * 128:(k + 1) * 128], PT[ri][:, k, :], ident_b[:]),
                            r=[PTK, ident_b.tok], w=[pst.tok])
                    P.a(lambda e, pstb=pstb, ri=ri: e.activation(out=W1[ri], in_=pstb[:, :].rearrange("p (k c) -> p k c", k=8), func=AF.Copy),
                        r=[pst.tok], w=[PTK])
                if SAMPLE >= 4:
                    pvs = psum()
                    for ri in range(2):
                        P.t(lambda e, ri=ri, pvs=pvs, ft=ft: e.matmul(pvs[:, ri * NS:(ri + 1) * NS], lhsT=W1[ri][:, 0, :], rhs=usT[:, ft, :], start=True, stop=True),
                            r=[PTK, usT.tok], w=[pvs.tok])
                    pr1 = sK[:, 2, pi_, 1:2]
                    pi1 = sK[:, 3, pi_, 1:2]
                    x0r, x0i = xs0[:, 0, pi_, :], xs0[:, 1, pi_, :]
                    ta_, tb_ = sst[:, 0, :], sst[:, 1, :]
                    P.v(lambda e: e.tensor_scalar(out=ta_, in0=x0r, scalar1=pr1, scalar2=None, op0=ALU.mult), r=[xs0.tok, sK.tok], w=[sst.tok])
                    P.v(lambda e: e.tensor_scalar(out=tb_, in0=x0i, scalar1=pi1, scalar2=None, op0=ALU.mult), r=[xs0.tok, sK.tok], w=[sst.tok])
                    P.v(lambda e: e.tensor_tensor(out=ta_, in0=ta_, in1=tb_, op=ALU.subtract), r=[sst.tok], w=[sst.tok])
                    P.v(lambda e, pvs=pvs: e.tensor_tensor(out=sst[:, 2, :], in0=ta_, in1=pvs[:, 0:NS], op=ALU.add), r=[sst.tok, pvs.tok], w=[sst.tok])
                    P.v(lambda e: e.tensor_scalar(out=ta_, in0=x0i, scalar1=pr1, scalar2=None, op0=ALU.mult), r=[xs0.tok, sK.tok], w=[sst.tok])
                    P.v(lambda e: e.tensor_scalar(out=tb_, in0=x0r, scalar1=pi1, scalar2=None, op0=ALU.mult), r=[xs0.tok, sK.tok], w=[sst.tok])
                    P.v(lambda e: e.tensor_tensor(out=ta_, in0=ta_, in1=tb_, op=ALU.add), r=[sst.tok], w=[sst.tok])
                    P.v(lambda e, pvs=pvs, pi_=pi_: e.tensor_tensor(out=xs0[:, 1, pi_, :], in0=ta_, in1=pvs[:, NS:2 * NS], op=ALU.add), r=[sst.tok, pvs.tok], w=[xs0.tok])
                    P.v(lambda e, pi_=pi_: e.tensor_copy(out=xs0[:, 0, pi_, :], in_=sst[:, 2, :]), r=[sst.tok], w=[xs0.tok])
                    pys = psum()
                    for ri in range(2):
                        P.t(lambda e, ri=ri, pys=pys, pi_=pi_: e.matmul(pys[:, 0:NS], lhsT=Ym[ri][:, 1, :], rhs=xs0b[:, ri, pi_, :], start=(ri == 0), stop=(ri == 1)),
                            r=[YTK, xs0b.tok], w=[pys.tok])
                    P.v(lambda e, pys=pys, ft=ft: e.tensor_tensor(out=ysacc[:, ft, :], in0=ysacc[:, ft, :], in1=pys[:, 0:NS], op=ALU.add), r=[pys.tok, ysacc.tok], w=[ysacc.tok])
                psv = psum()
                for ri in range(2):
                    for s_ in range(8):
                        usl = AP(uT[ft].t, s_ * 256, [[T, 128], [1, 256]])
                        P.t(lambda e, ri=ri, s_=s_, usl=usl, psv=psv: e.matmul(psv[:, ri * 256:(ri + 1) * 256], lhsT=W1[ri][:, 7 - s_, :], rhs=usl,
                                                                              start=(s_ == 0), stop=(s_ == 7)), r=[PTK, uT[ft].tok], w=[psv.tok])
                P.v(lambda e, pi_=pi_: e.tensor_scalar(out=an_t[:, 0:128], in0=cst["l1"][:], scalar1=-1.0, scalar2=sA[:, 5, pi_:pi_ + 1], op0=ALU.add, op1=ALU.mult),
                    r=[cst["l1"].tok, sA.tok], w=[an_t.tok])
                P.v(lambda e, pi_=pi_: e.tensor_scalar(out=an_t[:, 128:256], in0=cst["l1"][:], scalar1=127.0, scalar2=sA[:, 5, pi_:pi_ + 1], op0=ALU.add, op1=ALU.mult),
                    r=[cst["l1"].tok, sA.tok], w=[an_t.tok])
                cb, sb_ = cs_t[:, 0:256], cs_t[:, 256:512]
                sincos(an_t[:, 0:256], an_t.tok, nf_t.t[:, 0:256].bitcast(I32), nf_t[:, 0:256], nf_t.tok, sb_, cs_t.tok, cb, cs_t.tok)
                vr, vi = psv[:, 0:256], psv[:, 256:512]
                vtr, vti = vt_t[:, 0:256], vt_t[:, 256:512]
                t1, t2 = t1_t[:, 0:256], t2_t[:, 0:256]
                P.v(lambda e: e.tensor_tensor(out=t1, in0=vr, in1=cb, op=ALU.mult), r=[psv.tok, cs_t.tok], w=[t1_t.tok])
                P.v(lambda e: e.tensor_tensor(out=t2, in0=vi, in1=sb_, op=ALU.mult), r=[psv.tok, cs_t.tok], w=[t2_t.tok])
                P.v(lambda e: e.tensor_tensor(out=vtr, in0=t1, in1=t2, op=ALU.add), r=[t1_t.tok, t2_t.tok], w=[vt_t.tok])
                P.v(lambda e: e.tensor_tensor(out=t1, in0=vi, in1=cb, op=ALU.mult), r=[psv.tok, cs_t.tok], w=[t1_t.tok])
                P.v(lambda e: e.tensor_tensor(out=t2, in0=vr, in1=sb_, op=ALU.mult), r=[psv.tok, cs_t.tok], w=[t2_t.tok])
                P.v(lambda e: e.tensor_tensor(out=vti, in0=t1, in1=t2, op=ALU.subtract), r=[t1_t.tok, t2_t.tok], w=[vt_t.tok])
                rho = AP(sK.t, pi_ * 9 + 8, [[360, 128], [0, 256]])
                wr_, wi_ = wt_t[:, 0:256], wt_t[:, 256:512]
                P.v(lambda e: e.tensor_tensor_scan(out=wr_, data0=rho, data1=vtr, initial=0.0, op0=ALU.mult, op1=ALU.add), r=[sK.tok, vt_t.tok], w=[wt_t.tok])
                P.v(lambda e: e.tensor_tensor_scan(out=wi_, data0=rho, data1=vti, initial=0.0, op0=ALU.mult, op1=ALU.add), r=[sK.tok, vt_t.tok], w=[wt_t.tok])
                xr_, xi_ = xt_t[:, 0:256], xt_t[:, 256:512]
                P.v(lambda e: e.tensor_tensor(out=t1, in0=wr_, in1=cb, op=ALU.mult), r=[wt_t.tok, cs_t.tok], w=[t1_t.tok])
                P.v(lambda e: e.tensor_tensor(out=t2, in0=wi_, in1=sb_, op=ALU.mult), r=[wt_t.tok, cs_t.tok], w=[t2_t.tok])
                P.v(lambda e: e.tensor_tensor(out=xr_, in0=t1, in1=t2, op=ALU.subtract), r=[t1_t.tok, t2_t.tok], w=[xt_t.tok])
                P.v(lambda e: e.tensor_tensor(out=t1, in0=wr_, in1=sb_, op=ALU.mult), r=[wt_t.tok, cs_t.tok], w=[t1_t.tok])
                P.v(lambda e: e.tensor_tensor(out=t2, in0=wi_, in1=cb, op=ALU.mult), r=[wt_t.tok, cs_t.tok], w=[t2_t.tok])
                P.v(lambda e: e.tensor_tensor(out=xi_, in0=t1, in1=t2, op=ALU.add), r=[t1_t.tok, t2_t.tok], w=[xt_t.tok])
                P.v(lambda e, pi_=pi_: e.tensor_copy(out=s5o[:, :, pi_], in_=AP(xt_t.t, 255, [[TB, 128], [256, 2]])), r=[xt_t.tok], w=[s5o.tok])
                P.v(lambda e: e.memset(xprev[:, :, 0:1], 0.0), w=[xprev.tok])
                P.a(lambda e: e.activation(out=xprev[:, :, 1:256], in_=AP(xt_t.t, 0, [[TB, 128], [256, 2], [1, 255]]), func=AF.Copy), r=[xt_t.tok], w=[xprev.tok])
                for t8 in range(8):
                    ya = yacc[t8 // 2]
                    for ri in range(2):
                        P.t(lambda e, t8=t8, ri=ri, ya=ya, pp=pp: e.matmul(ya[:, (t8 % 2) * 256:(t8 % 2) * 256 + 256], lhsT=Ym[ri][:, t8 + 1, :], rhs=xprev[:, ri, 0:256],
                                                                           start=False, stop=False), r=[YTK, xprev.tok], w=[ya.tok])
            for k in range(8):
                bk = bdacc[k // 4]
                if k == 0:
                    P.v(lambda e, bk=bk, ft=ft: e.scalar_tensor_tensor(out=BDm[:, 0, :], in0=cst["ident"][:], scalar=s5d[:, l, ft:ft + 1], in1=bk[:, 0:128],
                                                                       op0=ALU.mult, op1=ALU.add), r=[bk.tok, cst["ident"].tok, s5d.tok], w=[P1.tok])
                else:
                    P.v(lambda e, bk=bk, k=k: e.tensor_copy(out=BDm[:, k, :], in_=bk[:, (k % 4) * 128:(k % 4) * 128 + 128]), r=[bk.tok], w=[P1.tok])
            if SAMPLE >= 4:
                pbs = psum()
                P.t(lambda e, pbs=pbs, ft=ft: e.matmul(pbs[:, 0:NS], lhsT=BDm[:, 0, :], rhs=usT[:, ft, :], start=True, stop=True), r=[P1.tok, usT.tok], w=[pbs.tok])
                P.v(lambda e, pbs=pbs, ft=ft: e.tensor_tensor(out=ysacc[:, ft, :], in0=ysacc[:, ft, :], in1=pbs[:, 0:NS], op=ALU.add), r=[pbs.tok, ysacc.tok], w=[ysacc.tok])
            for t8 in range(8):
                ya = yacc[t8 // 2]
                for s_ in range(t8 + 1):
                    usl = AP(uT[ft].t, s_ * 256, [[T, 128], [1, 256]])
                    P.t(lambda e, t8=t8, s_=s_, ya=ya, usl=usl: e.matmul(ya[:, (t8 % 2) * 256:(t8 % 2) * 256 + 256], lhsT=BDm[:, t8 - s_, :], rhs=usl,
                                                                         start=False, stop=(s_ == t8 and t8 % 2 == 1)), r=[P1.tok, uT[ft].tok], w=[ya.tok])
            for t8 in range(8):
                ya = yacc[t8 // 2]
                ysl = ya[:, (t8 % 2) * 256:(t8 % 2) * 256 + 256]
                g1, g2_ = t1_t[:, 0:256], t2_t[:, 0:256]
                P.a(lambda e, ysl=ysl: e.activation(out=g1, in_=ysl, func=AF.Square), r=[ya.tok], w=[t1_t.tok])
                P.v(lambda e: e.tensor_scalar(out=g1, in0=g1, scalar1=0.044715, scalar2=1.0, op0=ALU.mult, op1=ALU.add), r=[t1_t.tok], w=[t1_t.tok])
                P.v(lambda e, ysl=ysl: e.tensor_tensor(out=g1, in0=g1, in1=ysl, op=ALU.mult), r=[t1_t.tok, ya.tok], w=[t1_t.tok])
                P.a(lambda e: e.activation(out=g2_, in_=g1, func=AF.Tanh, scale=0.79788456), r=[t1_t.tok], w=[t2_t.tok])
                usl = AP(uT[ft].t, t8 * 256, [[T, 128], [1, 256]])
                P.v(lambda e, ysl=ysl, usl=usl: e.scalar_tensor_tensor(out=usl, in0=g2_, scalar=1.0, in1=ysl, op0=ALU.add, op1=ALU.mult), r=[t2_t.tok, ya.tok], w=[uT[ft].tok])
        rot[0] = [0, 1, 2, 3, 4, 5]
        P.g(lambda e: e.memset(w0f[:, 0:2], 0.0), r=ptok, w=[W0.tok] + ptok)
        P.dma("sync", o_s5[l], s5o[:], r=[s5o.tok], is_output=True)
        for blk in range(NBLK):
            bs = slice(blk * TB, (blk + 1) * TB)
            for fo_ in range(2):
                ps = psum()
                for fi_ in range(2):
                    rsj = AP(uT[fi_].t, blk * 64, [[T, 128], [256, 8], [1, 64]])
                    P.t(lambda e, ps=ps, fi_=fi_, fo_=fo_, rsj=rsj: e.matmul(ps[:, :].rearrange("p (s j) -> p s j", s=8), lhsT=wglu[:, fi_, fo_ * 128:(fo_ + 1) * 128], rhs=rsj,
                                                                             start=(fi_ == 0), stop=(fi_ == 1)), r=[wglu.tok, uT[fi_].tok], w=[ps.tok])
                sg_ = F[10]
                sgn = AP(sg_.t, 0, [[TB, 128], [1, 8], [8, 64]])
                P.a(lambda e, ps=ps, sgn=sgn: e.activation(out=sgn, in_=ps[:, :].rearrange("p (s j) -> p s j", s=8), func=AF.Tanh, scale=0.25), r=[ps.tok], w=[sg_.tok])
                syn = AP(uT[fo_].t, blk * 64, [[T, 128], [1, 64], [256, 8]])
                P.v(lambda e, syn=syn: e.scalar_tensor_tensor(out=sg_[:].rearrange("p (j s) -> p j s", s=8), in0=sg_[:].rearrange("p (j s) -> p j s", s=8), scalar=1.0, in1=syn,
                                                            op0=ALU.add, op1=ALU.mult), r=[sg_.tok, uT[fo_].tok], w=[sg_.tok])
                sga, sgtok = sgp(fo_, blk)
                P.v(lambda e, fo_=fo_, sga=sga: e.scalar_tensor_tensor(out=mixp[:, fo_, :], in0=sg_[:], scalar=0.25, in1=sga, op0=ALU.mult, op1=ALU.mult),
                    r=[sg_.tok, sgtok], w=[mixp.tok[fo_]])
            outproj_part(wo, blk)
        if SAMPLE >= 4:
            P.dma("sync", o_s5s[l], xs0[:].rearrange("p a b c -> p (a b c)"), r=[xs0.tok], is_output=True)
            ysf = ysacc[:].rearrange("p a n -> p (a n)")
            g1 = sst[:, 0:2, :].rearrange("p a n -> p (a n)")
            g2_ = sst[:, 2:4, :].rearrange("p a n -> p (a n)")
            P.a(lambda e: e.activation(out=g1, in_=ysf, func=AF.Square), r=[ysacc.tok], w=[sst.tok])
            P.v(lambda e: e.tensor_scalar(out=g1, in0=g1, scalar1=0.044715, scalar2=1.0, op0=ALU.mult, op1=ALU.add), r=[sst.tok], w=[sst.tok])
            P.v(lambda e: e.tensor_tensor(out=g1, in0=g1, in1=ysf, op=ALU.mult), r=[sst.tok, ysacc.tok], w=[sst.tok])
            P.a(lambda e: e.activation(out=g2_, in_=g1, func=AF.Tanh, scale=0.79788456), r=[sst.tok], w=[sst.tok])
            P.v(lambda e: e.scalar_tensor_tensor(out=ysf, in0=g2_, scalar=1.0, in1=ysf, op0=ALU.add, op1=ALU.mult), r=[sst.tok, ysacc.tok], w=[ysacc.tok])
            P.v(lambda e: e.tensor_copy(out=usT[:], in_=ysacc[:]), r=[ysacc.tok], w=[usT.tok])
            for fo_ in range(2):
                ps = psum()
                for fi_ in range(2):
                    P.t(lambda e, ps=ps, fi_=fi_, fo_=fo_: e.matmul(ps[:, 0:NS], lhsT=wglu[:, fi_, fo_ * 128:(fo_ + 1) * 128], rhs=usT[:, fi_, :],
                                                                    start=(fi_ == 0), stop=(fi_ == 1)), r=[wglu.tok, usT.tok], w=[ps.tok])
                P.a(lambda e, ps=ps, fo_=fo_: e.activation(out=sst[:, fo_, :], in_=ps[:, 0:NS], func=AF.Tanh, scale=0.25), r=[ps.tok], w=[sst.tok])
                P.v(lambda e, fo_=fo_: e.scalar_tensor_tensor(out=sst[:, fo_, :], in0=sst[:, fo_, :], scalar=1.0, in1=ysacc[:, fo_, :], op0=ALU.add, op1=ALU.mult),
                    r=[sst.tok, ysacc.tok], w=[sst.tok])
                P.v(lambda e, fo_=fo_: e.scalar_tensor_tensor(out=mixs[:, fo_, :], in0=sst[:, fo_, :], scalar=0.25, in1=sgs[:, fo_, :], op0=ALU.mult, op1=ALU.mult),
                    r=[sst.tok, sgs.tok], w=[mixs.tok])
            outproj_s(wo)


    d_xsT = din("xsT", [1024, NS])
    d_sret = din("sret", [DEPTH, 64, 4096])
    d_gns = din("gns", [64, DEPTH * 2 * 64])
    o_ysT = dout("o_ysT", [1024, NS])
    o_sret = dout("o_sret", [DEPTH, 64, 4096])
    xsT = sb("xsT", [128, 8, NS])
    P.dma("sync", xsT[:], d_xsT.rearrange("(k p) n -> p k n", p=128), w=[xsT.tok])
    hnsT = sb("hnsT", [128, 8, NS], BF16)
    gns = sb("gns", [64, DEPTH * 2 * 64])
    P.dma("sync", gns[:], d_gns, w=[gns.tok])
    qk4 = sb("qk4", [64, 5, 64])
    ropes = sb("ropes", [64, 3, 32])
    ropei = sb("ropei", [64, 32], I32)
    sm = sb("sm", [64, 16])
    osn = sb("osn", [64, 4, 64])
    xpad = sb("xpad", [64, 2, 64])
    mixs = sb("mixs", [128, 2, NS], BF16)
    gam = sb("gam", [64, 1])
    P.a(lambda e: e.activation(out=gam[:], in_=cst["lg64"][:], func=AF.Exp), r=[cst["lg64"].tok], w=[gam.tok])
    P.v(lambda e: e.tensor_scalar(out=ropes[:, 2, :], in0=cst["invrow"][0:64, :], scalar1=PAST, scalar2=None, op0=ALU.mult),
        r=[cst["invrow"].tok], w=[ropes.tok])
    sincos(ropes[:, 2, :], ropes.tok, ropei[:], ropei.t[:].bitcast(F32), ropei.tok, ropes[:, 1, :], ropes.tok, ropes[:, 0, :], ropes.tok)

    def tm_sample(wb, c0, n):
        ps = psum()
        for kt in range(8):
            P.t(lambda e, kt=kt: e.matmul(ps[0:NS, 0:n], lhsT=hnsT[:, kt, :], rhs=wb[:, kt, c0:c0 + n], start=(kt == 0), stop=(kt == 7)),
                r=[wb.tok, hnsT.tok], w=[ps.tok])
        return ps

    def fm_sample(wb, c0):
        ps = psum()
        for kt in range(8):
            P.t(lambda e, kt=kt: e.matmul(ps[:, 0:NS], lhsT=wb[:, kt, c0:c0 + 128], rhs=hnsT[:, kt, :], start=(kt == 0), stop=(kt == 7)),
                r=[wb.tok, hnsT.tok], w=[ps.tok])
        return ps

    def to_hn(pr_ap, prtok, nsec, dst, dtok):
        ps = psum()
        for h in range(4):
            rhs = AP(pr_ap.tensor, pr_ap.offset + h * 64, [list(pr_ap.ap[0]), [256, nsec], [1, 64]])
            P.t(lambda e, h=h, rhs=rhs: e.matmul(ps[0:64, 0:nsec * 64].rearrange("p (s d) -> p s d", s=nsec), lhsT=cst["sel"][:, h * 64:(h + 1) * 64], rhs=rhs,
                                                 start=(h == 0), stop=(h == 3)), r=[cst["sel"].tok, prtok], w=[ps.tok])
        P.a(lambda e: e.activation(out=dst, in_=ps[0:64, 0:nsec * 64].rearrange("p (s d) -> p s d", s=nsec), func=AF.Copy), r=[ps.tok], w=[dtok])

    def rope_s(x4, xtok, secs):
        cosb = AP(ropes.t, 0, [[96, 64], [0, secs], [1, 32]])
        sinb = AP(ropes.t, 32, [[96, 64], [0, secs], [1, 32]])
        x1, x2 = x4[:, 0:secs, 0:32], x4[:, 0:secs, 32:64]
        ta = osn[:, 0, :].rearrange("p (s d) -> p s d", s=2)[:, 0:secs, :]
        tb = osn[:, 1, :].rearrange("p (s d) -> p s d", s=2)[:, 0:secs, :]
        tc = osn[:, 2, :].rearrange("p (s d) -> p s d", s=2)[:, 0:secs, :]
        P.v(lambda e: e.tensor_tensor(out=ta, in0=x1, in1=cosb, op=ALU.mult), r=[xtok, ropes.tok], w=[osn.tok])
        P.v(lambda e: e.tensor_tensor(out=tb, in0=x2, in1=sinb, op=ALU.mult), r=[xtok, ropes.tok], w=[osn.tok])
        P.v(lambda e: e.tensor_tensor(out=ta, in0=ta, in1=tb, op=ALU.subtract), r=[osn.tok], w=[osn.tok])
        P.v(lambda e: e.tensor_tensor(out=tb, in0=x1, in1=sinb, op=ALU.mult), r=[xtok, ropes.tok], w=[osn.tok])
        P.v(lambda e: e.tensor_tensor(out=tc, in0=x2, in1=cosb, op=ALU.mult), r=[xtok, ropes.tok], w=[osn.tok])
        P.v(lambda e: e.tensor_tensor(out=x2, in0=tb, in1=tc, op=ALU.add), r=[osn.tok], w=[xtok])
        P.v(lambda e: e.tensor_copy(out=x1, in_=ta), r=[osn.tok], w=[xtok])

    def headnorm_s(o_ap, g_ap, gn_ap):
        mean, var = sm[:, 0:1], sm[:, 1:2]
        cen = osn[:, 1, :]
        P.v(lambda e: e.tensor_reduce(out=mean, in_=o_ap, axis=AX.X, op=ALU.add), r=[osn.tok], w=[sm.tok])
        P.v(lambda e: e.tensor_scalar(out=mean, in0=mean, scalar1=-1.0 / 64, scalar2=None, op0=ALU.mult), r=[sm.tok], w=[sm.tok])
        P.v(lambda e: e.tensor_scalar(out=cen, in0=o_ap, scalar1=mean, scalar2=None, op0=ALU.add), r=[osn.tok, sm.tok], w=[osn.tok])
        P.v(lambda e: e.tensor_tensor(out=osn[:, 2, :], in0=cen, in1=cen, op=ALU.mult), r=[osn.tok], w=[osn.tok])
        P.v(lambda e: e.tensor_reduce(out=var, in_=osn[:, 2, :], axis=AX.X, op=ALU.add), r=[osn.tok], w=[sm.tok])
        P.a(lambda e: e.activation(out=var, in_=var, func=AF.Ln, scale=1.0 / 64, bias=EPS), r=[sm.tok], w=[sm.tok])
        P.a(lambda e: e.activation(out=var, in_=var, func=AF.Exp, scale=-0.5), r=[sm.tok], w=[sm.tok])
        P.v(lambda e: e.scalar_tensor_tensor(out=cen, in0=cen, scalar=var, in1=gn_ap, op0=ALU.mult, op1=ALU.mult), r=[osn.tok, sm.tok, gns.tok], w=[osn.tok])
        P.v(lambda e: e.tensor_tensor(out=cen, in0=cen, in1=g_ap, op=ALU.mult), r=[osn.tok], w=[osn.tok])
        return cen

    def place_mix(y_ap):
        for i in range(2):
            yb = AP(y_ap.tensor, y_ap.offset, [list(y_ap.ap[0]), [0, 2], [1, 64]])
            mk2 = AP(cst["mpair"].t, i * 2, [[4, 64], [1, 2], [0, 64]])
            P.v(lambda e, yb=yb, mk2=mk2: e.tensor_tensor(out=xpad[:], in0=yb, in1=mk2, op=ALU.mult), r=[osn.tok, cst["mpair"].tok], w=[xpad.tok])
            ps = psum()
            P.t(lambda e, ps=ps: e.matmul(ps[:, 0:NS], lhsT=xpad[:].rearrange("p a b -> p (a b)"), rhs=cst["sel2"][:], start=True, stop=True),
                r=[xpad.tok, cst["sel2"].tok], w=[ps.tok])
            P.a(lambda e, ps=ps, i=i: e.activation(out=mixs[:, i, :], in_=ps[:, 0:NS], func=AF.Copy), r=[ps.tok], w=[mixs.tok])

    def outproj_s(wo):
        ps = psum()
        for dt_ in range(8):
            for kt in range(2):
                P.t(lambda e, kt=kt, dt_=dt_: e.matmul(ps[:, dt_ * NS:(dt_ + 1) * NS], lhsT=wo[:, kt, dt_ * 128:(dt_ + 1) * 128], rhs=mixs[:, kt, :],
                                                       start=(kt == 0), stop=(kt == 1)), r=[wo.tok, mixs.tok], w=[ps.tok])
        P.v(lambda e: e.tensor_tensor(out=xsT[:].rearrange("p k n -> p (k n)"), in0=xsT[:].rearrange("p k n -> p (k n)"), in1=ps[:, 0:8 * NS], op=ALU.add),
            r=[ps.tok, xsT.tok], w=[xsT.tok])

    def ret_sample(l, wA, wB, wo):
        pr = Pf.t[:].rearrange("p a b -> p (a b)")
        ps = tm_sample(wA, 0, 512)
        P.a(lambda e: e.activation(out=pr[0:NS, 0:512], in_=ps[0:NS, 0:512], func=AF.Copy), r=[ps.tok], w=[Pf.tok])
        ps = tm_sample(wB, 0, 512)
        P.a(lambda e: e.activation(out=pr[0:NS, 512:1024], in_=ps[0:NS, 0:512], func=AF.Copy), r=[ps.tok], w=[Pf.tok])
        to_hn(pr[0:NS, :], Pf.tok, 4, qk4[:, 0:4, :], qk4.tok)
        rope_s(qk4, qk4.tok, 2)
        q, k, v, g = (qk4[:, j, :] for j in range(4))
        P.v(lambda e: e.tensor_scalar(out=k, in0=k, scalar1=0.125, scalar2=None, op0=ALU.mult), r=[qk4.tok], w=[qk4.tok])
        P.a(lambda e: e.activation(out=osn[:, 3, :], in_=g, func=AF.Silu), r=[qk4.tok], w=[osn.tok])
        o = osn[:, 0, :]
        for j in range(8):
            S = F[j]
            P.dma("sync", S[0:64, :], d_sret[l][:, j * 512:(j + 1) * 512], w=[S.tok])
            tmpk = F[8 + j % 2]
            kb = AP(qk4.t, 1 * 64 + j * 8, [[320, 64], [1, 8], [0, 64]])
            vb = AP(qk4.t, 2 * 64, [[320, 64], [0, 8], [1, 64]])
            qb = AP(qk4.t, 0 * 64 + j * 8, [[320, 64], [0, 64], [1, 8]])
            P.v(lambda e, tmpk=tmpk, kb=kb, vb=vb: e.tensor_tensor(out=tmpk[0:64, :].rearrange("p (d x) -> p d x", d=8), in0=kb, in1=vb, op=ALU.mult),
                r=[qk4.tok], w=[tmpk.tok])
            P.v(lambda e, S=S, tmpk=tmpk: e.scalar_tensor_tensor(out=S[0:64, :], in0=S[0:64, :], scalar=gam[:, 0:1], in1=tmpk[0:64, :], op0=ALU.mult, op1=ALU.add),
                r=[S.tok, tmpk.tok, gam.tok], w=[S.tok])
            P.dma("sync", o_sret[l][:, j * 512:(j + 1) * 512], S[0:64, :], r=[S.tok], is_output=True)
            Sv = AP(S.t, 0, [[TB, 64], [1, 64], [64, 8]])
            P.v(lambda e, tmpk=tmpk, Sv=Sv, qb=qb: e.tensor_tensor(out=tmpk[0:64, :].rearrange("p (x d) -> p x d", d=8), in0=Sv, in1=qb, op=ALU.mult),
                r=[S.tok, qk4.tok], w=[tmpk.tok])
            if j == 0:
                P.v(lambda e, tmpk=tmpk: e.tensor_reduce(out=o, in_=tmpk[0:64, :].rearrange("p (x d) -> p x d", d=8), axis=AX.X, op=ALU.add),
                    r=[tmpk.tok], w=[osn.tok])
            else:
                P.v(lambda e, tmpk=tmpk: e.tensor_reduce(out=osn[:, 2, :], in_=tmpk[0:64, :].rearrange("p (x d) -> p x d", d=8), axis=AX.X, op=ALU.add),
                    r=[tmpk.tok], w=[osn.tok])
                P.v(lambda e: e.tensor_tensor(out=o, in0=o, in1=osn[:, 2, :], op=ALU.add), r=[osn.tok], w=[osn.tok])
        y = headnorm_s(o, osn[:, 3, :], gns[:, (l * 2 + 0) * 64:(l * 2 + 1) * 64])
        place_mix(y)
        outproj_s(wo)


    d_smlc = din("smlc", [DEPTH, 64, 4096])
    d_smln = din("smln", [DEPTH, 64, 64])
    d_smlm = din("smlm", [DEPTH, 64, 1])
    d_bifs = din("bifs", [64, DEPTH * 2])
    o_smlc = dout("o_smlc", [DEPTH, 64, 4096])
    o_smln = dout("o_smln", [DEPTH, 64, 64])
    o_smlm = dout("o_smlm", [DEPTH, 64, 1])
    bifs = sb("bifs", [64, DEPTH * 2])
    P.dma("sync", bifs[:], d_bifs, w=[bifs.tok])
    n0s = sb("n0s", [64, 2, 64])

    def ml_sample(l, wC, wD, wE, wo):
        pr = Pf.t[:].rearrange("p a b -> p (a b)")
        pr2 = P2.t[:].rearrange("p a b -> p (a b)").bitcast(F32)
        ps = tm_sample(wC, 0, 512)
        P.a(lambda e: e.activation(out=pr[0:NS, 0:512], in_=ps[0:NS, 0:512], func=AF.Copy), r=[ps.tok], w=[Pf.tok])
        ps = tm_sample(wD, 0, 512)
        P.a(lambda e: e.activation(out=pr[0:NS, 512:1024], in_=ps[0:NS, 0:512], func=AF.Copy), r=[ps.tok], w=[Pf.tok])
        ps = tm_sample(wE, 0, 264)
        P.a(lambda e: e.activation(out=pr2[0:NS, 0:264], in_=ps[0:NS, 0:264], func=AF.Copy), r=[ps.tok], w=[P2.tok])
        to_hn(pr[0:NS, :], Pf.tok, 4, qk4[:, 0:4, :], qk4.tok)
        to_hn(pr2[0:NS, 0:256], P2.tok, 1, qk4[:, 4:5, :], qk4.tok)
        psg = psum()
        for h in range(4):
            rhs = AP(P2.t, 0, [[1024, NS], [8, 2]]).bitcast(F32) if False else AP(pr2.tensor, pr2.offset + 256 + h, [list(pr2.ap[0])[0:1] + [NS], [4, 2]])
            P.t(lambda e, h=h, rhs=rhs: e.matmul(psg[0:64, 0:2], lhsT=cst["sel"][:, h * 64:(h + 1) * 64], rhs=rhs, start=(h == 0), stop=(h == 3)),
                r=[cst["sel"].tok, P2.tok], w=[psg.tok])
        q, k, v, og, g = (qk4[:, j, :] for j in range(5))
        ig, fz, lf, a_, mt_, wi, ws, emt, qk_, nq, den, scol, rc = (sm[:, j:j + 1] for j in range(2, 15))
        P.v(lambda e: e.tensor_tensor(out=sm[:, 2:4], in0=psg[0:64, 0:2], in1=bifs[:, l * 2:l * 2 + 2], op=ALU.add), r=[psg.tok, bifs.tok], w=[sm.tok])
        P.dma("sync", n0s[:, 0, :], d_smln[l], w=[n0s.tok])
        P.dma("sync", sm[:, 15:16], d_smlm[l], w=[sm.tok])
        m0 = sm[:, 15:16]
        P.a(lambda e: e.activation(out=lf, in_=fz, func=AF.Exp, scale=-1.0), r=[sm.tok], w=[sm.tok])
        P.a(lambda e: e.activation(out=lf, in_=lf, func=AF.Ln, bias=1.0), r=[sm.tok], w=[sm.tok])
        P.v(lambda e: e.tensor_tensor(out=a_, in0=m0, in1=lf, op=ALU.subtract), r=[sm.tok], w=[sm.tok])
        P.v(lambda e: e.tensor_tensor(out=mt_, in0=a_, in1=ig, op=ALU.max), r=[sm.tok], w=[sm.tok])
        P.v(lambda e: e.tensor_tensor(out=wi, in0=ig, in1=mt_, op=ALU.subtract), r=[sm.tok], w=[sm.tok])
        P.a(lambda e: e.activation(out=wi, in_=wi, func=AF.Exp), r=[sm.tok], w=[sm.tok])
        P.v(lambda e: e.tensor_tensor(out=ws, in0=a_, in1=mt_, op=ALU.subtract), r=[sm.tok], w=[sm.tok])
        P.a(lambda e: e.activation(out=ws, in_=ws, func=AF.Exp), r=[sm.tok], w=[sm.tok])
        P.a(lambda e: e.activation(out=emt, in_=mt_, func=AF.Exp, scale=-1.0), r=[sm.tok], w=[sm.tok])
        P.dma("sync", o_smlm[l], mt_, r=[sm.tok], is_output=True)
        P.v(lambda e: e.tensor_scalar(out=k, in0=k, scalar1=0.125, scalar2=None, op0=ALU.mult), r=[qk4.tok], w=[qk4.tok])
        P.v(lambda e: e.tensor_tensor(out=osn[:, 2, :], in0=q, in1=k, op=ALU.mult), r=[qk4.tok], w=[osn.tok])
        P.v(lambda e: e.tensor_reduce(out=qk_, in_=osn[:, 2, :], axis=AX.X, op=ALU.add), r=[osn.tok], w=[sm.tok])
        P.v(lambda e: e.tensor_tensor(out=osn[:, 2, :], in0=q, in1=n0s[:, 0, :], op=ALU.mult), r=[qk4.tok, n0s.tok], w=[osn.tok])
        P.v(lambda e: e.tensor_reduce(out=nq, in_=osn[:, 2, :], axis=AX.X, op=ALU.add), r=[osn.tok], w=[sm.tok])
        P.v(lambda e: e.tensor_tensor(out=scol, in0=qk_, in1=wi, op=ALU.mult), r=[sm.tok], w=[sm.tok])
        P.v(lambda e: e.tensor_tensor(out=den, in0=nq, in1=ws, op=ALU.mult), r=[sm.tok], w=[sm.tok])
        P.v(lambda e: e.tensor_tensor(out=den, in0=den, in1=scol, op=ALU.add), r=[sm.tok], w=[sm.tok])
        P.a(lambda e: e.activation(out=den, in_=den, func=AF.Abs), r=[sm.tok], w=[sm.tok])
        P.v(lambda e: e.tensor_tensor(out=den, in0=den, in1=emt, op=ALU.max), r=[sm.tok], w=[sm.tok])
        P.v(lambda e: e.reciprocal(out=rc, in_=den), r=[sm.tok], w=[sm.tok])
        P.v(lambda e: e.tensor_scalar(out=n0s[:, 1, :], in0=k, scalar1=wi, scalar2=None, op0=ALU.mult), r=[qk4.tok, sm.tok], w=[n0s.tok])
        P.v(lambda e: e.scalar_tensor_tensor(out=n0s[:, 1, :], in0=n0s[:, 0, :], scalar=ws, in1=n0s[:, 1, :], op0=ALU.mult, op1=ALU.add),
            r=[n0s.tok, sm.tok], w=[n0s.tok])
        P.dma("sync", o_smln[l], n0s[:, 1, :], r=[n0s.tok], is_output=True)
        vp = osn[:, 3, :]
        P.v(lambda e: e.tensor_scalar(out=vp, in0=v, scalar1=wi, scalar2=None, op0=ALU.mult), r=[qk4.tok, sm.tok], w=[osn.tok])
        Cq = osn[:, 0, :]
        for j in range(8):
            C = F[j]
            P.dma("sync", C[0:64, :], d_smlc[l][:, j * 512:(j + 1) * 512], w=[C.tok])
            tmpk = F[8 + j % 2]
            qb = AP(qk4.t, 0, [[320, 64], [0, 8], [1, 64]])
            P.v(lambda e, tmpk=tmpk, C=C, qb=qb: e.tensor_tensor(out=tmpk[0:64, :].rearrange("p (x d) -> p x d", x=8), in0=C[0:64, :].rearrange("p (x d) -> p x d", x=8),
                                                                 in1=qb, op=ALU.mult), r=[C.tok, qk4.tok], w=[tmpk.tok])
            P.v(lambda e, tmpk=tmpk, j=j: e.tensor_reduce(out=Cq[:, j * 8:(j + 1) * 8], in_=tmpk[0:64, :].rearrange("p (x d) -> p x d", x=8), axis=AX.X, op=ALU.add),
                r=[tmpk.tok], w=[osn.tok])
            vb = AP(osn.t, 3 * 64 + j * 8, [[256, 64], [1, 8], [0, 64]])
            kb = AP(qk4.t, 1 * 64, [[320, 64], [0, 8], [1, 64]])
            P.v(lambda e, tmpk=tmpk, vb=vb, kb=kb: e.tensor_tensor(out=tmpk[0:64, :].rearrange("p (x d) -> p x d", x=8), in0=vb, in1=kb, op=ALU.mult),
                r=[osn.tok, qk4.tok], w=[tmpk.tok])
            P.v(lambda e, C=C, tmpk=tmpk: e.scalar_tensor_tensor(out=C[0:64, :], in0=C[0:64, :], scalar=ws, in1=tmpk[0:64, :], op0=ALU.mult, op1=ALU.add),
                r=[C.tok, tmpk.tok, sm.tok], w=[C.tok])
            P.dma("sync", o_smlc[l][:, j * 512:(j + 1) * 512], C[0:64, :], r=[C.tok], is_output=True)
        P.v(lambda e: e.tensor_scalar(out=osn[:, 2, :], in0=v, scalar1=scol, scalar2=None, op0=ALU.mult), r=[qk4.tok, sm.tok], w=[osn.tok])
        P.v(lambda e: e.scalar_tensor_tensor(out=Cq, in0=Cq, scalar=ws, in1=osn[:, 2, :], op0=ALU.mult, op1=ALU.add), r=[osn.tok, sm.tok], w=[osn.tok])
        P.a(lambda e: e.activation(out=osn[:, 2, :], in_=og, func=AF.Tanh, scale=0.5), r=[qk4.tok], w=[osn.tok])
        P.v(lambda e: e.tensor_scalar(out=osn[:, 2, :], in0=osn[:, 2, :], scalar1=0.5, scalar2=0.5, op0=ALU.mult, op1=ALU.add), r=[osn.tok], w=[osn.tok])
        P.v(lambda e: e.scalar_tensor_tensor(out=Cq, in0=Cq, scalar=rc, in1=osn[:, 2, :], op0=ALU.mult, op1=ALU.mult), r=[osn.tok, sm.tok], w=[osn.tok])
        P.a(lambda e: e.activation(out=osn[:, 3, :], in_=g, func=AF.Silu), r=[qk4.tok, osn.tok], w=[osn.tok])
        y = headnorm_s(Cq, osn[:, 3, :], gns[:, (l * 2 + 1) * 64:(l * 2 + 2) * 64])
        place_mix(y)
        outproj_s(wo)

    d_ck = din("ck", [DEPTH, NS, 256, 256])
    d_cv = din("cv", [DEPTH, NS, 256, 256])
    gsx = sb("gsx", [128, 2, NS])
    ones32 = sb("ones32", [128, 128])
    P.v(lambda e: e.memset(ones32[:], 1.0), w=[ones32.tok])

    def xa_sample(l, wG, wo):
        pr = Pf.t[:].rearrange("p a b -> p (a b)")
        ps = tm_sample(wG, 0, 256)
        P.a(lambda e: e.activation(out=pr[0:NS, 0:256], in_=ps[0:NS, 0:256], func=AF.Copy), r=[ps.tok], w=[Pf.tok])
        for i in range(2):
            ps = fm_sample(wG, 256 + i * 128)
            P.a(lambda e, ps=ps, i=i: e.activation(out=gsx[:, i, :], in_=ps[:, 0:NS], func=AF.Silu), r=[ps.tok], w=[gsx.tok])
        scT = F[10]
        for n in range(NS):
            qm = F[9]
            Kn = F[n % 4]
            P.dma("sync", Kn[:].rearrange("p (mt c) -> p mt c", mt=2), d_ck[l, n].rearrange("(mt p) c -> p mt c", p=128), w=[Kn.tok])
            P.v(lambda e, n=n: e.tensor_scalar(out=qm[0:NS, 0:256], in0=pr[0:NS, 0:256], scalar1=cst["ident"][0:NS, n:n + 1], scalar2=None, op0=ALU.mult),
                r=[Pf.tok, cst["ident"].tok], w=[qm.tok])
            psq = psum()
            P.t(lambda e, psq=psq: e.matmul(psq[:, 0:256], lhsT=ones32[0:NS, :], rhs=qm[0:NS, 0:256], start=True, stop=True), r=[ones32.tok, qm.tok], w=[psq.tok])
            tmpk = F[8]
            qrb = AP(psq.t, 0, [[512, 128], [0, 2], [1, 256]])
            P.v(lambda e, Kn=Kn, qrb=qrb: e.tensor_tensor(out=tmpk[:].rearrange("p (mt c) -> p mt c", mt=2), in0=Kn[:].rearrange("p (mt c) -> p mt c", mt=2), in1=qrb, op=ALU.mult),
                r=[Kn.tok, psq.tok], w=[tmpk.tok])
            dst = AP(scT.t, n * 4, [[TB, 128], [64, 2], [1, 4]])
            P.v(lambda e, dst=dst: e.tensor_reduce(out=dst, in_=tmpk[:].rearrange("p (mt h d) -> p mt h d", mt=2, h=4), axis=AX.X, op=ALU.add),
                r=[tmpk.tok], w=[scT.tok])
        pss = psum()
        for mt in range(2):
            P.t(lambda e, mt=mt: e.transpose(pss[0:64, mt * 128:(mt + 1) * 128], scT[:, mt * 64:(mt + 1) * 64], cst["ident"][:]),
                r=[scT.tok, cst["ident"].tok], w=[pss.tok])
        mx, sme, nb = sm[:, 0:1], sm[:, 1:2], sm[:, 2:3]
        Pm = F[9]
        P.v(lambda e: e.tensor_reduce(out=mx, in_=pss[0:64, 0:256], axis=AX.X, op=ALU.max), r=[pss.tok], w=[sm.tok])
        P.v(lambda e: e.tensor_scalar(out=nb, in0=mx, scalar1=-0.125, scalar2=None, op0=ALU.mult), r=[sm.tok], w=[sm.tok])
        P.a(lambda e: e.activation(out=Pm[0:64, 0:256], in_=pss[0:64, 0:256], func=AF.Exp, scale=0.125, bias=nb), r=[pss.tok, sm.tok], w=[Pm.tok])
        P.v(lambda e: e.tensor_reduce(out=sme, in_=Pm[0:64, 0:256], axis=AX.X, op=ALU.add), r=[Pm.tok], w=[sm.tok])
        P.v(lambda e: e.reciprocal(out=sme, in_=sme), r=[sm.tok], w=[sm.tok])
        P.v(lambda e: e.tensor_scalar(out=Pm[0:64, 0:256], in0=Pm[0:64, 0:256], scalar1=sme, scalar2=None, op0=ALU.mult), r=[Pm.tok, sm.tok], w=[Pm.tok])
        pst = psum()
        for mt in range(2):
            P.t(lambda e, mt=mt: e.transpose(pst[:, mt * 64:(mt + 1) * 64], Pm[0:64, mt * 128:(mt + 1) * 128], cst["ident"][0:64, 0:64]),
                r=[Pm.tok, cst["ident"].tok], w=[pst.tok])
        PTs = F[10]
        P.a(lambda e: e.activation(out=PTs[:, 128:256], in_=pst[:, 0:128], func=AF.Copy), r=[pst.tok], w=[PTs.tok])
        psx = psum()
        for n in range(NS):
            Vn = F[4 + n % 4]
            P.dma("sync", Vn[:].rearrange("p (mt c) -> p mt c", mt=2), d_cv[l, n].rearrange("(mt p) c -> p mt c", p=128), w=[Vn.tok])
            tmpk = F[8]
            pb = AP(PTs.t, 128 + n * 4, [[TB, 128], [64, 2], [1, 4], [0, 64]])
            P.v(lambda e, Vn=Vn, pb=pb: e.tensor_tensor(out=tmpk[:].rearrange("p (mt h d) -> p mt h d", mt=2, h=4), in0=Vn[:].rearrange("p (mt h d) -> p mt h d", mt=2, h=4),
                                                        in1=pb, op=ALU.mult), r=[Vn.tok, PTs.tok], w=[tmpk.tok])
            for i in range(2):
                for mt in range(2):
                    P.t(lambda e, i=i, mt=mt, n=n: e.matmul(psx[:, i * NS + n:i * NS + n + 1], lhsT=tmpk[:, mt * 256 + i * 128:mt * 256 + (i + 1) * 128], rhs=ones32[:, 0:1],
                                                            start=(mt == 0), stop=(mt == 1)), r=[tmpk.tok, ones32.tok], w=[psx.tok])
        P.v(lambda e: e.tensor_tensor(out=mixs[:].rearrange("p a n -> p (a n)"), in0=psx[:, 0:2 * NS], in1=gsx[:].rearrange("p a n -> p (a n)"), op=ALU.mult),
            r=[psx.tok, gsx.tok], w=[mixs.tok])
        outproj_s(wo)

    for l in range(DEPTH):
        for blk in range(NBLK):
            t0 = blk * TB
            rmsnorm(lambda kt: xT[:, kt, t0:t0 + TB], [xtok[blk]], TB, lambda kt: hnT[:, kt, t0:t0 + TB], lambda kt: [htok[blk]],
                    lambda kt: normw[:, l * 8 + kt:l * 8 + kt + 1])
        rmsnorm(lambda kt: xsT[:, kt, :], [xsT.tok], NS, lambda kt: hnsT[:, kt, :], lambda kt: [hnsT.tok],
                lambda kt: normw[:, l * 8 + kt:l * 8 + kt + 1])
        if STAGE >= 2:
            mem_kv(l)
        if STAGE >= 3:
            ret_phase(l)
        if STAGE >= 4:
            ml_phase(l)
        if STAGE >= 5:
            xa_phase(l)
        if STAGE >= 6:
            s5_phase(l)
    for blk in range(NBLK):
        t0 = blk * TB
        for kt in range(8):
            pass
        yo = [F[0], F[1], F[2], F[3], F[4], F[5], F[6], F[7]]
        rmsnorm(lambda kt: xT[:, kt, t0:t0 + TB], [xtok[blk]], TB, lambda kt: yo[kt][:], lambda kt: [yo[kt].tok], lambda kt: fnw[:, kt:kt + 1])
        for kt in range(8):
            P.dma("sync" if kt % 2 == 0 else "gpsimd", yTv[:, kt, t0:t0 + TB], yo[kt][:], r=[yo[kt].tok], is_output=True)

    yos = F[8]
    rmsnorm(lambda kt: xsT[:, kt, :], [xsT.tok], NS, lambda kt: yos[:, kt * NS:(kt + 1) * NS], lambda kt: [yos.tok], lambda kt: fnw[:, kt:kt + 1])
    P.dma("sync", o_ysT.rearrange("(k p) n -> p k n", p=128), yos[:, 0:8 * NS].rearrange("p (k n) -> p k n", k=8), r=[yos.tok], is_output=True)

    P.emit()
    es.close()
    return nc


_NC = None


def make_in_maps(inp, cores=range(NCORES)):
    cs = _consts()
    f = {k: np.asarray(v) for k, v in inp.items()}
    in_maps = []
    for b in cores:
        m = {}
        m["xT"] = _c(f["x_prompt"][b].T)
        m["memT"] = _c(f["mem_prompt"][b].T)
        m["w_in"] = _c(f["w_in"])
        m["w_out"] = _c(f["w_out"])
        m["w_mem_k"] = _c(f["w_mem_k"])
        m["w_mem_v"] = _c(f["w_mem_v"])
        m["normw"] = _c(f["norm_w"].reshape(DEPTH, 8, 128).transpose(2, 0, 1).reshape(128, DEPTH * 8))
        m["fnw"] = _c(f["final_norm_w"].reshape(8, 128).T)
        gnc = np.zeros((128, DEPTH, 2, 2), np.float32)
        for l in range(DEPTH):
            gnc[:, l, 0, :] = f["ret_gn"][l].reshape(2, 128).T
            gnc[:, l, 1, :] = f["ml_gn"][l].reshape(2, 128).T
        m["gn"] = _c(gnc.reshape(128, DEPTH * 4))
        m["bif"] = _c(np.concatenate([f["ml_b_i"].T, f["ml_b_f"].T], axis=0))
        def pairlay(a):
            a = np.asarray(a)
            L = a.shape[0]
            rest = a.shape[3:]
            a = a.reshape((L, 8, 2, 64) + rest)
            a = np.moveaxis(a, (2, 3), (0, 1))
            return a.reshape((128, L, 8) + rest)
        are = pairlay(f["s5_a_re"])
        aim = pairlay(f["s5_a_im"])
        ldt = pairlay(np.repeat(f["s5_log_dt"][:, :, None], 64, axis=2))
        m["s5par"] = _c(np.stack([are, aim, ldt], axis=2).reshape(128, -1))
        bre = pairlay(f["s5_b_re"])
        bim = pairlay(f["s5_b_im"])
        m["s5b"] = _c(np.stack([bre, bim], axis=2).transpose(1, 0, 2, 3, 4).reshape(DEPTH, 128, -1))
        cre = pairlay(np.swapaxes(f["s5_c_re"], 2, 3))
        cim = pairlay(np.swapaxes(f["s5_c_im"], 2, 3))
        m["s5c"] = _c(np.stack([cre, cim], axis=2).transpose(1, 0, 2, 3, 4).reshape(DEPTH, 128, -1))
        m["s5d"] = _c(f["s5_d"].reshape(DEPTH, 2, 128).transpose(2, 0, 1).reshape(128, -1))
        m["w_glu"] = _c(f["s5_w_glu"])
        ns = slice(NS * b, NS * (b + 1))
        m["xsT"] = _c(f["x_sample"][ns, 0, :].T)
        m["sret"] = _c(f["state_ret"][:, ns].transpose(0, 2, 1, 3, 4).reshape(DEPTH, 64, 4096))
        gs = np.zeros((4, NS, DEPTH, 2, 64), np.float32)
        for l in range(DEPTH):
            gs[:, :, l, 0, :] = f["ret_gn"][l].reshape(4, 1, 64)
            gs[:, :, l, 1, :] = f["ml_gn"][l].reshape(4, 1, 64)
        m["gns"] = _c(gs.reshape(64, -1))
        m["smlc"] = _c(f["state_mlstm_c"][:, ns].transpose(0, 2, 1, 3, 4).reshape(DEPTH, 64, 4096))
        m["smln"] = _c(f["state_mlstm_n"][:, ns].transpose(0, 2, 1, 3).reshape(DEPTH, 64, 64))
        m["smlm"] = _c(f["state_mlstm_m"][:, ns].transpose(0, 2, 1).reshape(DEPTH, 64, 1))
        bs_ = np.zeros((4, NS, DEPTH, 2), np.float32)
        for l in range(DEPTH):
            bs_[:, :, l, 0] = f["ml_b_i"][l].reshape(4, 1)
            bs_[:, :, l, 1] = f["ml_b_f"][l].reshape(4, 1)
        m["bifs"] = _c(bs_.reshape(64, -1))
        def spair(a):
            a = np.asarray(a)[:, ns]
            a = a.reshape(DEPTH, NS, 8, 2, 64).transpose(0, 3, 4, 2, 1)
            return a.reshape(DEPTH, 128, 8, NS)
        m["sx0"] = _c(np.stack([spair(f["state_s5_re"]), spair(f["state_s5_im"])], axis=2).reshape(DEPTH, 128, -1))
        m["ck"] = _c(f["cache_mem_k"][:, ns].reshape(DEPTH, NS, 256, 256))
        m["cv"] = _c(f["cache_mem_v"][:, ns].reshape(DEPTH, NS, 256, 256))
        for k, v in cs.items():
            m["c_" + k] = _c(v)
        in_maps.append(m)
    return in_maps


def kernel(**inp):
    global _NC
    if _NC is None:
        _NC = build()
    nc = _NC
    in_maps = make_in_maps(inp)
    res = run_bass_kernel_spmd(nc, in_maps, core_ids=list(range(NCORES)))
    return assemble(res.results)


def assemble(R):
    y_prompt = np.stack([R[b]["o_yT"].T for b in range(NCORES)]).astype(np.float32)
    memkv = np.stack([R[b]["o_memkv"] for b in range(NCORES)], axis=1)
    memk = np.ascontiguousarray(memkv[..., 0:256]).reshape(DEPTH, NCORES, 256, 4, 64)
    memv = np.ascontiguousarray(memkv[..., 256:512]).reshape(DEPTH, NCORES, 256, 4, 64)
    ret_p = np.stack([R[b]["o_ret"] for b in range(NCORES)], axis=1)
    mlc_p = np.stack([R[b]["o_mlc"].transpose(0, 1, 3, 2) for b in range(NCORES)], axis=1)
    mln_p = np.stack([R[b]["o_mln"] for b in range(NCORES)], axis=1)
    mlm_p = np.stack([R[b]["o_mlm"] for b in range(NCORES)], axis=1)
    s5 = np.stack([R[b]["o_s5"] for b in range(NCORES)], axis=1)
    s5 = s5.reshape(DEPTH, NCORES, 2, 64, 2, 8).transpose(0, 1, 4, 5, 2, 3).reshape(DEPTH, NCORES, 2, 16, 64)
    s5re_p = np.ascontiguousarray(s5[:, :, 0])
    s5im_p = np.ascontiguousarray(s5[:, :, 1])
    y_sample = np.concatenate([R[b]["o_ysT"].T for b in range(NCORES)], axis=0).reshape(NCORES * NS, 1, 1024)
    ret_s = np.concatenate([R[b]["o_sret"].reshape(DEPTH, 4, NS, 64, 64).transpose(0, 2, 1, 3, 4) for b in range(NCORES)], axis=1)
    mlc_s = np.concatenate([R[b]["o_smlc"].reshape(DEPTH, 4, NS, 64, 64).transpose(0, 2, 1, 3, 4) for b in range(NCORES)], axis=1)
    mln_s = np.concatenate([R[b]["o_smln"].reshape(DEPTH, 4, NS, 64).transpose(0, 2, 1, 3) for b in range(NCORES)], axis=1)
    mlm_s = np.concatenate([R[b]["o_smlm"].reshape(DEPTH, 4, NS).transpose(0, 2, 1) for b in range(NCORES)], axis=1)
    def unsp(b):
        a = R[b]["o_s5s"].reshape(DEPTH, 2, 64, 2, 8, NS).transpose(3, 0, 5, 4, 1, 2)
        return a.reshape(2, DEPTH, NS, 16, 64)
    s5s = np.concatenate([unsp(b) for b in range(NCORES)], axis=2)
    z = lambda *s: np.zeros(s, np.float32)
    outs = (y_prompt, y_sample, ret_p, ret_s, np.ascontiguousarray(mlc_p), mlc_s,
            mln_p, mln_s, mlm_p, mlm_s, s5re_p, s5s[0],
            s5im_p, s5s[1], memk, memv)
    return tuple(np.ascontiguousarray(o, dtype=np.float32) for o in outs)
```

```python
import contextlib
import numpy as np
import concourse.bass as bass
import concourse.mybir as mybir
from concourse.ap import AP
from concourse.bass_utils import run_bass_kernel_spmd

F32 = mybir.dt.float32
BF16 = mybir.dt.bfloat16
I32 = mybir.dt.int32
F32R = mybir.dt.float32r
AF = mybir.ActivationFunctionType
ALU = mybir.AluOpType
AX = mybir.AxisListType

NCORES = 8
T = 2048
TB = 512
NBLK = T // TB
NS = 16
DEPTH = 2
DIN = 3336
EPS = 1e-6
PAST = 16384.0
TWO_PI = 6.283185307179586
C1 = 6.28125
C2 = TWO_PI - C1


class Tok:
    __slots__ = ("w", "r", "name", "excl")

    def __init__(self, name=""):
        self.w = None
        self.r = []
        self.name = name
        self.excl = False


class Op:
    __slots__ = ("eng", "fn", "deps", "idx", "signal", "sem", "semval", "dma", "cost", "pos", "tab")

    def __init__(self, eng, fn, deps, idx, dma):
        self.cost = 100.0
        self.pos = idx
        self.tab = None
        self.eng = eng
        self.fn = fn
        self.deps = deps
        self.idx = idx
        self.signal = False
        self.sem = None
        self.semval = 0
        self.dma = dma


ENGS = ("sync", "scalar", "vector", "gpsimd", "tensor")
DMA_POOL = 20
WCHUNKS = [(0, 512), (512, 1024), (1024, 1536), (1536, 2048), (2048, 2312), (2824, 3336)]


class _Rec:
    def __init__(self):
        self.call = None

    def __getattr__(self, name):
        def f(*a, **kw):
            self.call = (name, a, kw)
            return self
        return f


def _est_cost(eng, name, a_, kw_, dma):
    try:
        out = kw_.get("out", None)
        if out is None:
            out = a_[0]
        shp = out.shape
        n = 1
        for d in shp[1:]:
            n *= int(d)
    except Exception:
        n = 128
    if dma:
        return 2000.0 + n * int(shp[0]) * 4 / 80.0
    if eng == "tensor":
        f32 = False
        try:
            f32 = (kw_.get("lhsT", None) is not None and kw_["lhsT"].dtype == F32)
        except Exception:
            pass
        return 60.0 + n * (4.0 if f32 else 1.0) / 1.7
    if eng == "vector":
        return 100.0 + n * (2.0 if name == "tensor_tensor_scan" else 1.0) / 0.78
    if eng == "scalar":
        return 230.0 + n / 1.2
    if eng == "gpsimd":
        return 150.0 + n / 0.5
    return 50.0


SCHED = True
_TABSET = {AF.Exp: "e", AF.Ln: "e", AF.Silu: "s", AF.Sin: "s", AF.Sigmoid: "g", AF.Tanh: "s"}
TAB_PEN = 0.0


def _list_schedule(ops):
    n = len(ops)
    succ = [[] for _ in range(n)]
    indeg = [0] * n
    for o in ops:
        for d in o.deps:
            succ[d].append(o.idx)
            indeg[o.idx] += 1
    bl = [0.0] * n
    for i in range(n - 1, -1, -1):
        m = 0.0
        for s_ in succ[i]:
            if bl[s_] > m:
                m = bl[s_]
        bl[i] = m + ops[i].cost + 100.0
    efree = {e: 0.0 for e in ENGS}
    fin = [0.0] * n
    rdy_t = [0.0] * n
    ready = [i for i in range(n) if indeg[i] == 0]
    order = []
    cur_tab = [None]
    WINDOW = 6000
    next_unsched = 0
    done = [False] * n
    while ready:
        best = None
        bk = None
        lim = next_unsched + WINDOW
        for i in ready:
            if i > lim:
                continue
            o = ops[i]
            st = efree[o.eng]
            if rdy_t[i] > st:
                st = rdy_t[i]
            if o.tab is not None and o.tab != cur_tab[0]:
                st = st + TAB_PEN
            key = (st, -bl[i], i)
            if bk is None or key < bk:
                bk = key
                best = i
        if best is None:
            best = min(ready)
            o = ops[best]
            bk = (max(efree[o.eng], rdy_t[best]),)
        ready.remove(best)
        o = ops[best]
        st = bk[0]
        if o.tab is not None:
            cur_tab[0] = o.tab
        if o.dma:
            efree[o.eng] = st + 60.0
            fin[best] = st + o.cost
        else:
            efree[o.eng] = st + o.cost
            fin[best] = st + o.cost
        order.append(best)
        done[best] = True
        while next_unsched < n and done[next_unsched]:
            next_unsched += 1
        for s_ in succ[best]:
            t = fin[best] + (110.0 if ops[s_].eng == o.eng else 220.0)
            if t > rdy_t[s_]:
                rdy_t[s_] = t
            indeg[s_] -= 1
            if indeg[s_] == 0:
                ready.append(s_)
    assert len(order) == n
    return order, max(fin)


class Prog:
    def __init__(self, nc):
        self.nc = nc
        self.ops = []
        self.out_ops = []

    @staticmethod
    def _flat(ts):
        out = []
        for t in ts:
            if isinstance(t, (list, tuple)):
                out.extend(Prog._flat(t))
            else:
                out.append(t)
        return out

    def op(self, eng, fn, r=(), w=(), dma=False):
        r = Prog._flat(r)
        w = Prog._flat(w)
        idx = len(self.ops)
        deps = set()
        for t in r:
            if t.w is not None:
                deps.add(t.w)
            if t.excl:
                for ri_ in t.r:
                    if self.ops[ri_].eng != eng:
                        deps.add(ri_)
        for t in w:
            if t.w is not None:
                deps.add(t.w)
            deps.update(t.r)
        rec = _Rec()
        fn(rec)
        name, a_, kw_ = rec.call
        o = Op(eng, (lambda e, name=name, a_=a_, kw_=kw_: getattr(e, name)(*a_, **kw_)), deps, idx, dma)
        o.cost = _est_cost(eng, name, a_, kw_, dma)
        if eng == "scalar" and name == "activation":
            o.tab = _TABSET.get(kw_.get("func", None), None)
        self.ops.append(o)
        for t in r:
            t.r.append(idx)
        for t in w:
            t.w = idx
            t.r = []
        return idx

    def v(self, fn, r=(), w=()):
        return self.op("vector", fn, r, w)

    def a(self, fn, r=(), w=()):
        return self.op("scalar", fn, r, w)

    def g(self, fn, r=(), w=()):
        return self.op("gpsimd", fn, r, w)

    def t(self, fn, r=(), w=()):
        return self.op("tensor", fn, r, w)

    def dma(self, q, out, in_, r=(), w=(), is_output=False):
        idx = self.op(q, lambda e, out=out, in_=in_: e.dma_start(out=out, in_=in_), r, w, dma=True)
        if is_output:
            self.out_ops.append(idx)
        return idx

    def emit(self):
        nc = self.nc
        ops = self.ops
        fin = Op("sync", lambda e: e.nop(), set(self.out_ops), len(ops), False)
        ops.append(fin)
        byidx = ops
        if SCHED:
            order, mk = _list_schedule(ops)
            self.est_makespan = mk
            ops = [byidx[i] for i in order]
            for p_, o in enumerate(ops):
                o.pos = p_
        with contextlib.ExitStack() as es:
            esem = {e: es.enter_context(nc.semaphore("s_" + e)) for e in ENGS}
            pools = {q: [es.enter_context(nc.semaphore("d_%s_%d" % (q, i))) for i in range(DMA_POOL)]
                     for q in ("sync", "gpsimd")}
            dcount = {"sync": 0, "gpsimd": 0}
            last_use = {}
            for o in ops:
                if o.dma:
                    i = dcount[o.eng]
                    dcount[o.eng] += 1
                    sem = pools[o.eng][i % DMA_POOL]
                    u = i // DMA_POOL
                    o.sem = sem
                    o.semval = 16 * (u + 1)
                    o.signal = True
                    key = (o.eng, i % DMA_POOL)
                    if key in last_use:
                        o.deps.add(last_use[key])
                    last_use[key] = o.idx
            for o in ops:
                for d in o.deps:
                    p = byidx[d]
                    if p.eng == "tensor" and o.eng == "tensor" and not p.dma:
                        continue
                    p.signal = True
            cnt = {e: 0 for e in ENGS}
            for o in ops:
                if not o.dma and o.signal:
                    cnt[o.eng] += 1
                    o.sem = esem[o.eng]
                    o.semval = cnt[o.eng]
            per = {e: [o for o in ops if o.eng == e] for e in ENGS}

            def run(eng, lst):
                waited = {}
                for o in lst:
                    need = {}
                    for d in o.deps:
                        p = byidx[d]
                        if p.eng == "tensor" and o.eng == "tensor" and not p.dma:
                            continue
                        k = id(p.sem)
                        if k not in need or need[k][1] < p.semval:
                            need[k] = (p.sem, p.semval)
                    for k, (sem, val) in need.items():
                        if waited.get(k, 0) >= val:
                            continue
                        eng.wait_ge(sem, val)
                        waited[k] = val
                    ins = o.fn(eng)
                    if o.signal:
                        ins.then_inc(o.sem, 16 if o.dma else 1)

            with nc.Block() as block:
                @block.sync
                def _(e):
                    run(e, per["sync"])

                @block.scalar
                def _(e):
                    run(e, per["scalar"])

                @block.vector
                def _(e):
                    run(e, per["vector"])

                @block.gpsimd
                def _(e):
                    run(e, per["gpsimd"])

                @block.tensor
                def _(e):
                    run(e, per["tensor"])


class Buf:
    def __init__(self, t, name=""):
        self.t = t
        self.tok = Tok(name)

    def __getitem__(self, k):
        return self.t[k]


def _c(a):
    return np.ascontiguousarray(a, dtype=np.float32)


def _consts():
    c = {}
    c["ident"] = np.eye(128, dtype=np.float32)
    c["maskT"] = np.triu(np.ones((128, 128), np.float32))
    bd = np.zeros((128, 128), np.float32)
    bd[:64, :64] = 1
    bd[64:, 64:] = 1
    c["bd2"] = bd
    pm = np.zeros((128, 128), np.float32)
    for h2 in range(2):
        for d in range(32):
            pm[h2 * 64 + d + 32, h2 * 64 + d] = -1.0
            pm[h2 * 64 + d, h2 * 64 + d + 32] = 1.0
    c["pm"] = pm
    inv = (10000.0 ** (-np.arange(32, dtype=np.float32) / 32)).astype(np.float32)
    c["invf"] = np.tile(inv, 4).reshape(128, 1).astype(np.float32)
    c["invrow"] = np.tile(inv.reshape(1, 32), (128, 1)).astype(np.float32)
    lg = np.log1p(-np.power(np.float32(2.0), -5.0 - np.arange(4, dtype=np.float32))).astype(np.float32)
    lgc = np.zeros((128, 2), np.float32)
    for i in range(2):
        for h2 in range(2):
            lgc[h2 * 64:(h2 + 1) * 64, i] = lg[2 * i + h2]
    c["lgcol"] = lgc
    c["l1"] = np.tile(np.arange(1, 129, dtype=np.float32).reshape(1, 128), (128, 1))
    ex = np.zeros((8, 4, 128), np.float32)
    for i in range(2):
        for h2 in range(2):
            ex[2 * i + h2, i * 2 + 0, h2 * 64:(h2 + 1) * 64] = 1.0
            ex[4 + 2 * i + h2, i * 2 + 1, h2 * 64:(h2 + 1) * 64] = 1.0
    c["gexp"] = ex.reshape(8, 512)
    sel = np.zeros((16, 4, 64), np.float32)
    for hh in range(4):
        for n in range(16):
            sel[n, hh, 16 * hh + n] = 1.0
    c["sel"] = sel.reshape(16, 256)
    sel2 = np.zeros((64, 16), np.float32)
    mpair = np.zeros((64, 2, 2), np.float32)
    lg64 = np.zeros((64, 1), np.float32)
    for hh in range(4):
        for n in range(16):
            sel2[16 * hh + n, n] = 1.0
            mpair[16 * hh + n, hh // 2, hh % 2] = 1.0
            lg64[16 * hh + n, 0] = lg[hh]
    c["sel2"] = sel2
    c["mpair"] = mpair.reshape(64, 4)
    c["lg64"] = lg64
    c["kgrid"] = np.tile(np.arange(9, dtype=np.float32).reshape(1, 9), (128, 1))
    return c


CONST_SHAPES = {"ident": [128, 128], "maskT": [128, 128], "bd2": [128, 128], "pm": [128, 128],
                "invf": [128, 1], "invrow": [128, 32], "lgcol": [128, 2], "l1": [128, 128],
                "gexp": [8, 512], "kgrid": [128, 9],
                "sel": [16, 256], "sel2": [64, 16], "mpair": [64, 4], "lg64": [64, 1]}


def build(stage=99):
    import os
    STAGE = int(os.environ.get('KSTAGE', '99'))
    KSUB = int(os.environ.get('KSUB', '99'))
    KMASK = int(os.environ.get('KMASK', '63'))
    SAMPLE = int(os.environ.get('KSAMPLE', '99'))
    nc = bass.Bass("TRN2", target_bir_lowering=False)
    P = Prog(nc)
    es = contextlib.ExitStack()

    def din(name, shape):
        return nc.dram_tensor(name, list(shape), F32, kind="ExternalInput").ap()

    def dout(name, shape):
        return nc.dram_tensor(name, list(shape), F32, kind="ExternalOutput").ap()

    def sb(name, shape, dt=F32):
        return Buf(es.enter_context(nc.sbuf_tensor("s_" + name, list(shape), dt)), name)

    d_xT = din("xT", [1024, T])
    d_memT = din("memT", [1024, 256])
    d_win = din("w_in", [DEPTH, 1024, DIN])
    d_wout = din("w_out", [DEPTH, 1024, 1024])
    d_wmk = din("w_mem_k", [DEPTH, 1024, 256])
    d_wmv = din("w_mem_v", [DEPTH, 1024, 256])
    d_normw = din("normw", [128, DEPTH * 8])
    d_fnw = din("fnw", [128, 8])
    d_gn = din("gn", [128, DEPTH * 4])
    d_bif = din("bif", [8, DEPTH])
    dc = {k: din("c_" + k, s) for k, s in CONST_SHAPES.items()}

    o_yT = dout("o_yT", [1024, T])
    o_memkv = dout("o_memkv", [DEPTH, 256, 512])
    o_ret = dout("o_ret", [DEPTH, 4, 64, 64])
    o_mlc = dout("o_mlc", [DEPTH, 4, 64, 64])
    o_mln = dout("o_mln", [DEPTH, 4, 64])
    o_mlm = dout("o_mlm", [DEPTH, 4])

    xTv = d_xT.rearrange("(k p) t -> p k t", p=128)
    yTv = o_yT.rearrange("(k p) t -> p k t", p=128)

    banks = [Buf(es.enter_context(nc.psum_tensor("ps%d" % i, [128, 512], F32)), "ps%d" % i) for i in range(8)]
    for b_ in banks:
        b_.tok.excl = True
    bank_rr = [0]
    rot = [[0, 1, 2, 3, 4, 5]]

    def psum():
        r_ = rot[0]
        b = banks[r_[bank_rr[0] % len(r_)]]
        bank_rr[0] += 1
        return b

    cst = {}
    for k, s in CONST_SHAPES.items():
        cst[k] = sb("k_" + k, s)
        P.dma("sync", cst[k].t[:], dc[k], w=[cst[k].tok])
    ident_b = sb("ident_b", [128, 128], BF16)
    P.v(lambda e: e.tensor_copy(out=ident_b[:], in_=cst["ident"][:]), r=[cst["ident"].tok], w=[ident_b.tok])
    zeros_b = sb("zeros_b", [128, 128], BF16)
    P.v(lambda e: e.memset(zeros_b[:], 0.0), w=[zeros_b.tok])
    ones_b = sb("ones_b", [128, 128], BF16)
    P.v(lambda e: e.memset(ones_b[:], 1.0), w=[ones_b.tok])
    onespad = sb("onespad", [128, 2, 128], BF16)
    P.v(lambda e: e.memset(onespad[:], 0.0), w=[onespad.tok])
    for h2 in range(2):
        P.v(lambda e, h2=h2: e.memset(onespad[:, h2, 64 * h2:64 * h2 + 64], 1.0), w=[onespad.tok])
    avg = sb("avg", [128, 128])
    P.v(lambda e: e.tensor_scalar(out=avg[:], in0=cst["bd2"][:], scalar1=1.0 / 64, scalar2=None, op0=ALU.mult),
        r=[cst["bd2"].tok], w=[avg.tok])
    avg_b = sb("avg_b", [128, 128], BF16)
    P.v(lambda e: e.tensor_copy(out=avg_b[:], in_=avg[:]), r=[avg.tok], w=[avg_b.tok])
    normw = sb("normw", [128, DEPTH * 8])
    P.dma("sync", normw[:], d_normw, w=[normw.tok])
    fnw = sb("fnw", [128, 8])
    P.dma("sync", fnw[:], d_fnw, w=[fnw.tok])
    gn = sb("gn", [128, DEPTH * 4])
    P.dma("sync", gn[:], d_gn, w=[gn.tok])
    bif = sb("bif", [8, DEPTH])
    P.dma("sync", bif[:], d_bif, w=[bif.tok])

    decq = sb("decq", [128, 2, 128])
    deck = sb("deck", [128, 2, 128])
    gL = sb("gL", [128, 2])
    for i in range(2):
        P.a(lambda e, i=i: e.activation(out=decq[:, i, :], in_=cst["l1"][:], func=AF.Exp, scale=cst["lgcol"][:, i:i + 1]),
            r=[cst["l1"].tok, cst["lgcol"].tok], w=[decq.tok])
    P.v(lambda e: e.reciprocal(out=deck[:], in_=decq[:]), r=[decq.tok], w=[deck.tok])
    P.v(lambda e: e.tensor_scalar(out=deck[:], in0=deck[:], scalar1=0.125, scalar2=None, op0=ALU.mult),
        r=[deck.tok], w=[deck.tok])
    P.v(lambda e: e.tensor_copy(out=gL[:], in_=decq[:, :, 127]), r=[decq.tok], w=[gL.tok])

    bd2g = sb("bd2g", [128, 2, 128])
    for i in range(2):
        P.v(lambda e, i=i: e.tensor_scalar(out=bd2g[:, i, :], in0=cst["bd2"][:], scalar1=gL[:, i:i + 1], scalar2=None, op0=ALU.mult),
            r=[cst["bd2"].tok, gL.tok], w=[bd2g.tok])

    xT = sb("xT", [128, 8, T])
    xtok = [Tok("xT%d" % b) for b in range(NBLK)]
    for b in range(NBLK):
        P.dma("sync", xT[:, :, b * TB:(b + 1) * TB], xTv[:, :, b * TB:(b + 1) * TB], w=[xtok[b]])
    hnT = sb("hnT", [128, 8, T], BF16)
    htok = [Tok("hnT%d" % b) for b in range(NBLK)]

    F = [sb("F%d" % i, [128, TB]) for i in range(11)]
    P0 = sb("P0", [128, 2, TB], BF16)
    P1 = sb("P1", [128, 2, TB], BF16)
    P2 = sb("P2", [128, 2, TB], BF16)
    Pf = sb("Pf", [128, 2, TB])
    VA = sb("VA", [128, 4, 2, 2, 128], BF16)
    ET = sb("ET", [128, 4, TB], BF16)
    mixp = sb("mixp", [128, 2, TB], BF16)
    mixp.tok = [Tok("mixp0"), Tok("mixp1")]
    ET.tok = [Tok("ET%d" % k_) for k_ in range(4)]
    Pf.tok = [Tok("Pf0"), Tok("Pf1")]
    ncs = sb("ncs", [128, 2, 4])
    sq2 = [sb("sq%d" % i, [128, TB], BF16) for i in range(2)]

    NW = 3
    wch = [sb("wch%d" % i, [128, 8, 512], BF16) for i in range(NW)]
    wrr = [0]
    wop = [sb("wop%d" % i, [128, 2, 1024], BF16) for i in range(1)]
    worr = [0]

    def wload(parts):
        b = wch[wrr[0] % NW]
        wrr[0] += 1
        for (s2, a0, a1, off) in parts:
            P.dma("gpsimd", b.t[:, :, off:off + (a1 - a0)], s2.rearrange("(k p) c -> p k c", p=128)[:, :, a0:a1], w=[b.tok])
        return b

    def woload(l, m):
        b = wop[0]
        worr[0] += 1
        P.dma("gpsimd", b.t[:], d_wout[l][m * 256:(m + 1) * 256, :].rearrange("(k p) c -> p k c", p=128), w=[b.tok])
        return b

    mkT = sb("mkT", [128, 2, 256], BF16)
    mvpad = sb("mvpad", [128, 2, 2, 2, 128], BF16)
    P.g(lambda e: e.memset(mvpad[:], 0.0), w=[mvpad.tok])

    def mem_kv(l):
        wb = wload([(d_wmk[l], 0, 256, 0), (d_wmv[l], 0, 256, 256)])
        mslot = wch[wrr[0] % NW]
        wrr[0] += 1
        memT = Buf(mslot.t[:].rearrange("p a b -> p (a b)")[:, 0:2048].rearrange("p (k m) -> p k m", k=8), "memT")
        memT.tok = mslot.tok
        P.dma("gpsimd", memT.t, d_memT.rearrange("(k p) m -> p k m", p=128), w=[memT.tok])
        for mt in range(2):
            ps = psum()
            for kt in range(8):
                P.t(lambda e, ps=ps, kt=kt, mt=mt: e.matmul(ps[:, :], lhsT=memT[:, kt, mt * 128:(mt + 1) * 128],
                                                            rhs=wb[:, kt, :], start=(kt == 0), stop=(kt == 7)),
                    r=[memT.tok, wb.tok], w=[ps.tok])
            fo = F[mt]
            P.a(lambda e, ps=ps, fo=fo: e.activation(out=fo[:], in_=ps[:, :], func=AF.Copy), r=[ps.tok], w=[fo.tok])
            dst = AP(mvpad.t, mt * 512 + 0, [[1024, 128], [256, 2], [192, 2], [1, 64]])
            P.v(lambda e, ps=ps, dst=dst: e.tensor_copy(out=dst, in_=ps[:, 256:512].rearrange("p (i h e) -> p i h e", i=2, h=2)),
                r=[ps.tok], w=[mvpad.tok])
            P.dma("sync", o_memkv[l, mt * 128:(mt + 1) * 128, :], fo[:], r=[fo.tok], is_output=True)
        for i in range(2):
            ps = psum()
            for kt in range(8):
                P.t(lambda e, ps=ps, kt=kt, i=i: e.matmul(ps[:, 0:256], lhsT=wb[:, kt, i * 128:(i + 1) * 128],
                                                          rhs=memT[:, kt, :], start=(kt == 0), stop=(kt == 7)),
                    r=[memT.tok, wb.tok], w=[ps.tok])
            P.v(lambda e, ps=ps, i=i: e.tensor_copy(out=mkT[:, i, :], in_=ps[:, 0:256]), r=[ps.tok], w=[mkT.tok])

    def rmsnorm(srcs, rtoks, ntok, dsts, wtoks, wcol):
        rs = F[10]
        ps = psum()
        for kt in range(8):
            sq = sq2[kt % 2]
            P.a(lambda e, kt=kt, sq=sq: e.activation(out=sq[:, 0:ntok], in_=srcs(kt), func=AF.Square), r=rtoks, w=[sq.tok])
            P.t(lambda e, kt=kt, sq=sq: e.matmul(ps[:, 0:ntok], lhsT=ones_b[:, :], rhs=sq[:, 0:ntok],
                                                 start=(kt == 0), stop=(kt == 7)), r=[sq.tok, ones_b.tok], w=[ps.tok])
        P.a(lambda e: e.activation(out=rs[:, 0:ntok], in_=ps[:, 0:ntok], func=AF.Ln, scale=1.0 / 1024, bias=EPS),
            r=[ps.tok], w=[rs.tok])
        P.a(lambda e: e.activation(out=rs[:, 0:ntok], in_=rs[:, 0:ntok], func=AF.Exp, scale=-0.5), r=[rs.tok], w=[rs.tok])
        for kt in range(8):
            P.v(lambda e, kt=kt: e.scalar_tensor_tensor(out=dsts(kt), in0=srcs(kt), scalar=wcol(kt), in1=rs[:, 0:ntok],
                                                        op0=ALU.mult, op1=ALU.mult), r=rtoks + [rs.tok, normw.tok, fnw.tok], w=wtoks(kt))

    def sincos(ang, atok, nfi, nff, ntok, sin_o, stok, cos_o, ctok):
        P.v(lambda e: e.tensor_scalar(out=nfi, in0=ang, scalar1=1.0 / TWO_PI, scalar2=None, op0=ALU.mult), r=[atok], w=[ntok])
        P.v(lambda e: e.tensor_copy(out=cos_o, in_=nfi), r=[ntok], w=[ctok])
        P.v(lambda e: e.scalar_tensor_tensor(out=ang, in0=cos_o, scalar=-C1, in1=ang, op0=ALU.mult, op1=ALU.add), r=[ctok, atok], w=[atok])
        P.v(lambda e: e.scalar_tensor_tensor(out=ang, in0=cos_o, scalar=-C2, in1=ang, op0=ALU.mult, op1=ALU.add), r=[ctok, atok], w=[atok])
        P.v(lambda e: e.tensor_scalar(out=ang, in0=ang, scalar1=3.1415925, scalar2=-3.1415925, op0=ALU.min, op1=ALU.max), r=[atok], w=[atok])
        P.a(lambda e: e.activation(out=sin_o, in_=ang, func=AF.Sin), r=[atok], w=[stok])
        P.a(lambda e: e.activation(out=ang, in_=ang, func=AF.Abs), r=[atok], w=[atok])
        P.a(lambda e: e.activation(out=cos_o, in_=ang, func=AF.Sin, scale=-1.0, bias=1.5707963), r=[atok], w=[ctok])

    def rope_tables(blk):
        angb, nf, cosb, sinb = F[0], F[1], F[2], F[3]
        t0 = float(blk * TB)
        for c in range(4):
            P.v(lambda e, c=c: e.tensor_scalar(out=angb[:, c * 128:(c + 1) * 128], in0=cst["l1"][:], scalar1=t0 + c * 128 - 1.0, scalar2=cst["invf"][:, 0:1],
                                               op0=ALU.add, op1=ALU.mult), r=[cst["l1"].tok, cst["invf"].tok], w=[angb.tok])
        sincos(angb[:], angb.tok, nf.t[:].bitcast(I32), nf[:], nf.tok, sinb[:], sinb.tok, cosb[:], cosb.tok)

    def fm_group(wb, c0, blk, m=128):
        ps = psum()
        for kt in range(8):
            P.t(lambda e, kt=kt: e.matmul(ps[0:m, :], lhsT=wb[:, kt, c0:c0 + m], rhs=hnT[:, kt, blk * TB:(blk + 1) * TB],
                                          start=(kt == 0), stop=(kt == 7)), r=[wb.tok, htok[blk]], w=[ps.tok])
        return ps

    def tm_group(wb, c0, blk, c, n=256):
        ps = psum()
        ts = slice(blk * TB + c * 128, blk * TB + (c + 1) * 128)
        for kt in range(8):
            P.t(lambda e, kt=kt: e.matmul(ps[:, 0:n], lhsT=hnT[:, kt, ts], rhs=wb[:, kt, c0:c0 + n],
                                          start=(kt == 0), stop=(kt == 7)), r=[wb.tok, htok[blk]], w=[ps.tok])
        return ps

    def rope_evac(ps, psr, i, dst, dec):
        cosb, sinb = F[2], F[3]
        t1, t2 = F[4 + i], F[6 + i] if False else F[6]
        P.v(lambda e: e.tensor_tensor(out=t1[:], in0=ps[:, :], in1=cosb[:], op=ALU.mult), r=[ps.tok, cosb.tok], w=[t1.tok])
        P.v(lambda e: e.tensor_tensor(out=t2[:], in0=psr[:, :], in1=sinb[:], op=ALU.mult), r=[psr.tok, sinb.tok], w=[t2.tok])
        P.v(lambda e: e.tensor_tensor(out=t1[:], in0=t1[:], in1=t2[:], op=ALU.add), r=[t1.tok, t2.tok], w=[t1.tok])
        decb = AP(dec.t, i * 128, [[256, 128], [0, 4], [1, 128]])
        P.v(lambda e: e.tensor_tensor(out=dst[:, i, :].rearrange("p (c l) -> p c l", c=4),
                                      in0=t1[:].rearrange("p (c l) -> p c l", c=4), in1=decb, op=ALU.mult),
            r=[t1.tok, dec.tok], w=[dst.tok])

    innT = [sb("innT%d" % i, [128, 2, 128], BF16) for i in range(2)]
    irr = [0]
    kTM = [sb("kTM%d" % i, [128, 128], BF16) for i in range(2)]
    krr = [0]
    tts = [sb("tt%d" % i, [128, 256]) for i in range(2)]
    ttr = [0]

    def head_norm_gate(src, gate, gtok, gcol, dst, cen=None, sq32=None, dtok=None):
        cen = F[8] if cen is None else cen
        sq32 = F[9] if sq32 is None else sq32
        ps = psum()
        P.t(lambda e: e.matmul(ps[:, :], lhsT=avg[:], rhs=src[:], start=True, stop=True), r=[avg.tok, src.tok], w=[ps.tok])
        P.v(lambda e: e.tensor_tensor(out=cen[:], in0=src[:], in1=ps[:, :], op=ALU.subtract), r=[src.tok, ps.tok], w=[cen.tok])
        sqb = sq32.t[:].bitcast(BF16)[:, 0:TB]
        P.a(lambda e: e.activation(out=sqb, in_=cen[:], func=AF.Square), r=[cen.tok], w=[sq32.tok])
        ps2 = psum()
        P.t(lambda e: e.matmul(ps2[:, :], lhsT=avg_b[:], rhs=sqb, start=True, stop=True), r=[avg_b.tok, sq32.tok], w=[ps2.tok])
        P.a(lambda e: e.activation(out=sq32[:], in_=ps2[:, :], func=AF.Ln, bias=EPS), r=[ps2.tok], w=[sq32.tok])
        P.a(lambda e: e.activation(out=sq32[:], in_=sq32[:], func=AF.Exp, scale=-0.5), r=[sq32.tok], w=[sq32.tok])
        P.v(lambda e: e.tensor_tensor(out=cen[:], in0=cen[:], in1=sq32[:], op=ALU.mult), r=[cen.tok, sq32.tok], w=[cen.tok])
        P.v(lambda e: e.scalar_tensor_tensor(out=dst, in0=cen[:], scalar=gcol, in1=gate, op0=ALU.mult, op1=ALU.mult),
            r=[cen.tok, gn.tok, gtok], w=[mixp.tok if dtok is None else dtok])

    def outproj_part(wo, blk):
        t0 = blk * TB
        for dt_ in range(8):
            ps = psum()
            for kt in range(2):
                P.t(lambda e, ps=ps, kt=kt, dt_=dt_: e.matmul(ps[:, :], lhsT=wo[:, kt, dt_ * 128:(dt_ + 1) * 128], rhs=mixp[:, kt, :],
                                                              start=(kt == 0), stop=(kt == 1)), r=[wo.tok, mixp.tok], w=[ps.tok])
            P.v(lambda e, ps=ps, dt_=dt_: e.tensor_tensor(out=xT[:, dt_, t0:t0 + TB], in0=xT[:, dt_, t0:t0 + TB], in1=ps[:, :], op=ALU.add),
                r=[ps.tok, xtok[blk]], w=[xtok[blk]])

    Sret = [sb("Sret%d" % i, [128, 128]) for i in range(2)]
    Sret_b = [sb("Sretb%d" % i, [128, 128], BF16) for i in range(2)]

    def ret_phase(l):
        W = d_win[l]
        wA = wload([(W, 0, 512, 0)])
        wB = wload([(W, 512, 1024, 0)])
        wo = woload(l, 0)
        wR = wch[wrr[0] % NW]
        wrr[0] += 1
        src1 = AP(wA.t, 32, [[4096, 128], [512, 8], [64, 8], [1, 32]])
        src0 = AP(wA.t, 0, [[4096, 128], [512, 8], [64, 8], [1, 32]])
        dst0 = AP(wR.t, 0, [[4096, 128], [512, 8], [64, 8], [1, 32]])
        dst1 = AP(wR.t, 32, [[4096, 128], [512, 8], [64, 8], [1, 32]])
        P.g(lambda e: e.tensor_scalar(out=dst0, in0=src1, scalar1=-1.0, scalar2=None, op0=ALU.mult), r=[wA.tok], w=[wR.tok])
        P.g(lambda e: e.tensor_copy(out=dst1, in_=src0), r=[wA.tok], w=[wR.tok])
        rq, rk, rg, rvpad, oT = P0, P1, P2, VA, F[7]
        P.g(lambda e: e.memset(VA[:], 0.0), w=[VA.tok])
        for i in range(2):
            P.v(lambda e, i=i: e.memset(Sret[i][:], 0.0), w=[Sret[i].tok])
            P.v(lambda e, i=i: e.memset(Sret_b[i][:], 0.0), w=[Sret_b[i].tok])
        for blk in range(NBLK):
            rope_tables(blk)
            for i in range(2):
                rope_evac(fm_group(wA, i * 128, blk), fm_group(wR, i * 128, blk), i, rq, decq)
            for i in range(2):
                rope_evac(fm_group(wA, 256 + i * 128, blk), fm_group(wR, 256 + i * 128, blk), i, rk, deck)
            for c in range(4):
                ps = tm_group(wB, 0, blk, c)
                dst = AP(rvpad.t, c * 512, [[2048, 128], [256, 2], [192, 2], [1, 64]])
                P.a(lambda e, ps=ps, dst=dst: e.activation(out=dst, in_=ps[:, 0:256].rearrange("p (i h e) -> p i h e", i=2, h=2),
                                                           func=AF.Copy), r=[ps.tok], w=[rvpad.tok])
            for i in range(2):
                ps = fm_group(wB, 256 + i * 128, blk)
                P.a(lambda e, ps=ps, i=i: e.activation(out=rg[:, i, :], in_=ps[:, :], func=AF.Silu), r=[ps.tok], w=[rg.tok])
            mask4 = AP(cst["maskT"].t, 0, [[128, 128], [0, 4], [1, 128]])
            for i in range(2):
                pso = banks[6 + i]
                P.t(lambda e, pso=pso: e.matmul(pso[:, :], lhsT=zeros_b[:, 0:128], rhs=hnT[:, 0, 0:512], start=True, stop=False), r=[zeros_b.tok, htok[0]], w=[pso.tok])
                for h2 in range(2):
                    ps = psum()
                    hs = slice(64 * h2, 64 * h2 + 64)
                    for c in range(4):
                        cs = slice(c * 128, (c + 1) * 128)
                        P.t(lambda e, ps=ps, hs=hs, cs=cs, i=i: e.matmul(ps[:, cs], lhsT=rk[hs, i, cs], rhs=rq[hs, i, cs], start=True, stop=True),
                            r=[rk.tok, rq.tok], w=[ps.tok])
                    P.v(lambda e, ps=ps, i=i, h2=h2: e.tensor_tensor(out=ET[:, i * 2 + h2, :].rearrange("p (c l) -> p c l", c=4), in0=ps[:, :].rearrange("p (c l) -> p c l", c=4),
                                                                     in1=mask4, op=ALU.mult), r=[ps.tok, cst["maskT"].tok], w=[ET.tok[i * 2 + h2]])
                for c in range(4):
                    cs = slice(c * 128, (c + 1) * 128)
                    for h2 in range(2):
                        P.t(lambda e, c=c, i=i, h2=h2, cs=cs, pso=pso: e.matmul(pso[:, cs], lhsT=rvpad[:, c, i, h2, :], rhs=ET[:, i * 2 + h2, cs], start=False, stop=False),
                            r=[rvpad.tok, ET.tok[i * 2 + h2]], w=[pso.tok])
                pst = psum()
                pstb = pst.t[:].bitcast(BF16)
                for c in range(4):
                    cs = slice(c * 128, (c + 1) * 128)
                    P.t(lambda e, pstb=pstb, i=i, cs=cs: e.transpose(pstb[:, cs], rk[:, i, cs], ident_b[:]), r=[rk.tok, ident_b.tok], w=[pst.tok])
                ktm = sq2[i]
                P.a(lambda e, pstb=pstb, ktm=ktm: e.activation(out=ktm[:], in_=pstb[:, 0:512], func=AF.Copy), r=[pst.tok], w=[ktm.tok])
                psu = psum()
                for c in range(4):
                    cs = slice(c * 128, (c + 1) * 128)
                    vsl = AP(rvpad.t, c * 512 + i * 256, [[2048, 128], [192, 2], [1, 64]])
                    P.t(lambda e, psu=psu, ktm=ktm, vsl=vsl, cs=cs: e.matmul(psu[:, cs].rearrange("p (h e) -> p h e", h=2), lhsT=ktm[:, cs], rhs=vsl, start=True, stop=True),
                        r=[ktm.tok, rvpad.tok], w=[psu.tok])
                tt = F[4 + i]
                bdg = AP(bd2g.t, i * 128, [[256, 128], [0, 4], [1, 128]])
                P.v(lambda e, psu=psu, tt=tt, bdg=bdg: e.tensor_tensor(out=tt[:].rearrange("p (c x) -> p c x", c=4), in0=psu[:, :].rearrange("p (c x) -> p c x", c=4), in1=bdg, op=ALU.mult),
                    r=[psu.tok, bd2g.tok], w=[tt.tok])
            for c in range(4):
                cs = slice(c * 128, (c + 1) * 128)
                for i in range(2):
                    pso = banks[6 + i]
                    tt = F[4 + i]
                    P.t(lambda e, i=i, cs=cs, pso=pso, c=c: e.matmul(pso[:, cs], lhsT=Sret_b[i][:], rhs=rq[:, i, cs], start=False, stop=(c == 3)),
                        r=[Sret_b[i].tok, rq.tok], w=[pso.tok])
                    P.v(lambda e, i=i, tt=tt, cs=cs: e.scalar_tensor_tensor(out=Sret_b[i][:], in0=Sret[i][:], scalar=gL[:, i:i + 1], in1=tt[:, cs], op0=ALU.mult, op1=ALU.add),
                        r=[Sret[i].tok, gL.tok, tt.tok], w=[Sret_b[i].tok])
                    P.v(lambda e, i=i, tt=tt, cs=cs: e.scalar_tensor_tensor(out=Sret[i][:], in0=Sret[i][:], scalar=gL[:, i:i + 1], in1=tt[:, cs], op0=ALU.mult, op1=ALU.add),
                        r=[Sret[i].tok, gL.tok, tt.tok], w=[Sret[i].tok])
            for i in range(2):
                pso = banks[6 + i]
                oTi, ceni, sqi = (F[7], F[8], F[9]) if i == 0 else (F[0], F[1], F[2])
                P.a(lambda e, pso=pso, oTi=oTi: e.activation(out=oTi[:], in_=pso[:, :], func=AF.Copy), r=[pso.tok], w=[oTi.tok])
                head_norm_gate(oTi, rg[:, i, :], rg.tok, gn[:, l * 4 + i:l * 4 + i + 1], mixp[:, i, :], ceni, sqi, mixp.tok[i])
            outproj_part(wo, blk)
        for i in range(2):
            for h2 in range(2):
                P.dma("sync", o_ret[l, 2 * i + h2], Sret[i][64 * h2:64 * h2 + 64, 64 * h2:64 * h2 + 64],
                      r=[Sret[i].tok], is_output=True)
        if SAMPLE >= 1:
            ret_sample(l, wA, wB, wo)

    Cml = [sb("Cml%d" % i, [128, 256]) for i in range(2)]
    Cml_b = [sb("Cmlb%d" % i, [128, 256], BF16) for i in range(2)]
    Bcar = [sb("Bcar%d" % i, [128, 1]) for i in range(2)]
    Gcar = [sb("Gcar%d" % i, [128, 1]) for i in range(2)]
    E1t = sb("E1", [128, 128])
    mkt = [sb("mkt%d" % i, [128, 128], BF16) for i in range(2)]
    mkrr = [0]
    gs_col = sb("gs_col", [128, 4])
    ngs_col = sb("ngs_col", [128, 4])
    mlm_o = sb("mlm_o", [128, 2])

    def ml_phase(l):
        W = d_win[l]
        wC = wload([(W, 1024, 1536, 0)])
        wD = wload([(W, 1536, 2048, 0)])
        wE = wload([(W, 2048, 2312, 0)])
        wo = woload(l, 1)
        mq, mo, mg, mk_, mvp = P0, P1, P2, Pf, VA
        gates8 = Buf(F[6].t[0:8, :], 'gates8')
        gates8.tok = F[6].tok
        LFt, IGt, d0s, oT = F[0], F[1], F[5], F[7]
        ones_col = cst["l1"][:, 0:1].to_broadcast([128, TB])
        rot[0] = [0, 1, 2, 3, 4, 5]
        mask4 = AP(cst["maskT"].t, 0, [[128, 128], [0, 4], [1, 128]])
        for i in range(2):
            P.v(lambda e, i=i: e.memset(Cml[i][:], 0.0), w=[Cml[i].tok])
            P.v(lambda e, i=i: e.memset(Cml_b[i][:], 0.0), w=[Cml_b[i].tok])
            P.v(lambda e, i=i: e.memset(Bcar[i][:], 0.0), w=[Bcar[i].tok])
            P.v(lambda e, i=i: e.memset(Gcar[i][:], 0.0), w=[Gcar[i].tok])
        for blk in range(NBLK):
            for i in range(2 if KMASK & 1 else 0):
                ps = fm_group(wC, i * 128, blk)
                P.a(lambda e, ps=ps, i=i: e.activation(out=mq[:, i, :], in_=ps[:, :], func=AF.Copy), r=[ps.tok], w=[mq.tok])
            for i in range(2 if KMASK & 2 else 0):
                ps = fm_group(wC, 256 + i * 128, blk)
                P.a(lambda e, ps=ps, i=i: e.activation(out=mk_[:, i, :], in_=ps[:, :], func=AF.Copy), r=[ps.tok], w=[Pf.tok[i]])
            for c in range(4 if KMASK & 4 else 0):
                ps = tm_group(wD, 0, blk, c)
                dst = AP(mvp.t, c * 512, [[2048, 128], [256, 2], [192, 2], [1, 64]])
                P.a(lambda e, ps=ps, dst=dst: e.activation(out=dst, in_=ps[:, 0:256].rearrange("p (i h e) -> p i h e", i=2, h=2),
                                                           func=AF.Copy), r=[ps.tok], w=[mvp.tok])
            for i in range(2 if KMASK & 8 else 0):
                ps = fm_group(wD, 256 + i * 128, blk)
                P.a(lambda e, ps=ps, i=i: e.activation(out=mo[:, i, :], in_=ps[:, :], func=AF.Tanh, scale=0.5), r=[ps.tok], w=[mo.tok])
            for i in range(2 if KMASK & 16 else 0):
                ps = fm_group(wE, i * 128, blk)
                P.a(lambda e, ps=ps, i=i: e.activation(out=mg[:, i, :], in_=ps[:, :], func=AF.Silu), r=[ps.tok], w=[mg.tok])
            if KMASK & 32:
                ps = fm_group(wE, 256, blk, m=8)
                P.a(lambda e, ps=ps: e.activation(out=gates8[:, :], in_=ps[0:8, :], func=AF.Identity, bias=bif[:, l:l + 1]),
                    r=[ps.tok, bif.tok], w=[gates8.tok])
            for i in range(2 if KSUB >= 2 else 0):
                Gt = F[2] if i == 0 else F[4]
                MTt = F[3] if i == 0 else F[10]
                rt = Gt
                psi = psum()
                P.t(lambda e, psi=psi, i=i: e.matmul(psi[:, :], lhsT=cst["gexp"][:, (2 * i) * 128:(2 * i + 1) * 128], rhs=gates8[:, :],
                                                     start=True, stop=True), r=[cst["gexp"].tok, gates8.tok], w=[psi.tok])
                psf = psum()
                P.t(lambda e, psf=psf, i=i: e.matmul(psf[:, :], lhsT=cst["gexp"][:, (2 * i + 1) * 128:(2 * i + 2) * 128], rhs=gates8[:, :],
                                                     start=True, stop=True), r=[cst["gexp"].tok, gates8.tok], w=[psf.tok])
                P.a(lambda e, psf=psf: e.activation(out=LFt[:], in_=psf[:, :], func=AF.Exp, scale=-1.0), r=[psf.tok], w=[LFt.tok])
                P.a(lambda e: e.activation(out=LFt[:], in_=LFt[:], func=AF.Ln, bias=1.0), r=[LFt.tok], w=[LFt.tok])
                P.v(lambda e, i=i: e.tensor_tensor_scan(out=LFt[:], data0=ones_col, data1=LFt[:], initial=Bcar[i][:, 0:1],
                                                        op0=ALU.mult, op1=ALU.subtract), r=[LFt.tok, Bcar[i].tok, cst["l1"].tok], w=[LFt.tok])
                P.v(lambda e, psi=psi: e.tensor_tensor(out=IGt[:], in0=psi[:, :], in1=LFt[:], op=ALU.subtract), r=[psi.tok, LFt.tok], w=[IGt.tok])
                P.v(lambda e, i=i: e.tensor_tensor_scan(out=Gt[:], data0=ones_col, data1=IGt[:], initial=Gcar[i][:, 0:1],
                                                        op0=ALU.mult, op1=ALU.max), r=[IGt.tok, Gcar[i].tok, cst["l1"].tok], w=[Gt.tok])
                P.v(lambda e, i=i: e.tensor_copy(out=gs_col[:, 0:1], in_=Gcar[i][:, 0:1]), r=[Gcar[i].tok], w=[gs_col.tok])
                P.v(lambda e: e.tensor_copy(out=gs_col[:, 1:4], in_=AP(Gt.t, 127, [[TB, 128], [128, 3]])), r=[Gt.tok], w=[gs_col.tok])
                P.v(lambda e: e.tensor_scalar(out=ngs_col[:], in0=gs_col[:], scalar1=-1.0, scalar2=None, op0=ALU.mult), r=[gs_col.tok], w=[ngs_col.tok])
                P.v(lambda e: e.tensor_tensor(out=MTt[:], in0=LFt[:], in1=Gt[:], op=ALU.add), r=[LFt.tok, Gt.tok], w=[MTt.tok])
                if blk == NBLK - 1:
                    P.v(lambda e, i=i: e.tensor_copy(out=mlm_o[:, i:i + 1], in_=MTt[:, TB - 1:TB]), r=[MTt.tok], w=[mlm_o.tok])
                P.a(lambda e: e.activation(out=MTt[:], in_=MTt[:], func=AF.Exp, scale=-1.0), r=[MTt.tok], w=[MTt.tok])
                P.v(lambda e, i=i: e.tensor_copy(out=Bcar[i][:, 0:1], in_=LFt[:, TB - 1:TB]), r=[LFt.tok], w=[Bcar[i].tok])
                P.v(lambda e, i=i: e.tensor_copy(out=Gcar[i][:, 0:1], in_=Gt[:, TB - 1:TB]), r=[Gt.tok], w=[Gcar[i].tok])
                psn = banks[6]
                psd = banks[7]
                if KSUB < 3:
                    continue
                for c in range(4):
                    cs = slice(c * 128, (c + 1) * 128)
                    P.a(lambda e, c=c, cs=cs: e.activation(out=IGt[:, cs], in_=IGt[:, cs], func=AF.Exp, bias=ngs_col[:, c:c + 1]),
                        r=[IGt.tok, ngs_col.tok], w=[IGt.tok])
                    P.a(lambda e, c=c, cs=cs, Gt=Gt: e.activation(out=Gt[:, cs], in_=Gt[:, cs], func=AF.Exp, scale=-1.0, bias=gs_col[:, c:c + 1]),
                        r=[Gt.tok, gs_col.tok], w=[Gt.tok])
                ktil = mixp[:, i, :]
                mtk = mixp.tok[i]
                P.v(lambda e, i=i, ktil=ktil: e.scalar_tensor_tensor(out=ktil, in0=mk_[:, i, :], scalar=0.125, in1=IGt[:], op0=ALU.mult, op1=ALU.mult),
                    r=[Pf.tok[i], IGt.tok], w=[mtk])
                for bk in (psn, psd):
                    P.t(lambda e, bk=bk: e.matmul(bk[:, :], lhsT=zeros_b[:, 0:128], rhs=hnT[:, 0, 0:512], start=True, stop=False), r=[zeros_b.tok, htok[0]], w=[bk.tok])
                for h2 in range(2):
                    ps = psum()
                    hs = slice(64 * h2, 64 * h2 + 64)
                    ek = 2 * i + h2
                    for c in range(4):
                        cs = slice(c * 128, (c + 1) * 128)
                        P.t(lambda e, ps=ps, hs=hs, cs=cs, i=i: e.matmul(ps[:, cs], lhsT=mixp[hs, i, cs], rhs=mq[hs, i, cs], start=True, stop=True),
                            r=[mtk, mq.tok], w=[ps.tok])
                    P.v(lambda e, ps=ps, ek=ek: e.tensor_tensor(out=ET[:, ek, :].rearrange("p (c l) -> p c l", c=4), in0=ps[:, :].rearrange("p (c l) -> p c l", c=4),
                                                                in1=mask4, op=ALU.mult), r=[ps.tok, cst["maskT"].tok], w=[ET.tok[ek]])
                for c in range(4):
                    cs = slice(c * 128, (c + 1) * 128)
                    for h2 in range(2):
                        ek = 2 * i + h2
                        P.t(lambda e, c=c, i=i, h2=h2, cs=cs, ek=ek, psn=psn: e.matmul(psn[:, cs], lhsT=mvp[:, c, i, h2, :], rhs=ET[:, ek, cs], start=False, stop=False),
                            r=[mvp.tok, ET.tok[ek]], w=[psn.tok])
                        P.t(lambda e, h2=h2, cs=cs, ek=ek, psd=psd: e.matmul(psd[:, cs], lhsT=onespad[:, h2, :], rhs=ET[:, ek, cs], start=False, stop=False),
                            r=[onespad.tok, ET.tok[ek]], w=[psd.tok])
                pst = psum()
                pstb = pst.t[:].bitcast(BF16)
                for c in range(4):
                    cs = slice(c * 128, (c + 1) * 128)
                    P.t(lambda e, pstb=pstb, i=i, cs=cs: e.transpose(pstb[:, cs], mixp[:, i, cs], ident_b[:]), r=[mtk, ident_b.tok], w=[pst.tok])
                ktm = sq2[i]
                P.a(lambda e, pstb=pstb, ktm=ktm: e.activation(out=ktm[:], in_=pstb[:, 0:512], func=AF.Copy), r=[pst.tok], w=[ktm.tok])
                psuA = psum()
                psuB = psum()
                for c in range(4):
                    cs = slice(c * 128, (c + 1) * 128)
                    vsl = AP(mvp.t, c * 512 + i * 256, [[2048, 128], [192, 2], [1, 64]])
                    P.t(lambda e, ktm=ktm, vsl=vsl, cs=cs, psuA=psuA: e.matmul(psuA[:, cs].rearrange("p (h e) -> p h e", h=2), lhsT=ktm[:, cs], rhs=vsl, start=True, stop=True),
                        r=[ktm.tok, mvp.tok], w=[psuA.tok])
                    P.t(lambda e, ktm=ktm, cs=cs, c=c, psuB=psuB: e.matmul(psuB[:, c:c + 1], lhsT=ktm[:, cs], rhs=ones_b[:, 0:1], start=True, stop=True),
                        r=[ktm.tok, ones_b.tok], w=[psuB.tok])
                ttA = Pf[:, i, :]
                for c in range(4):
                    cs = slice(c * 128, (c + 1) * 128)
                    rcol = rt[:, c * 128 + 127:c * 128 + 128]
                    P.v(lambda e, cs=cs, rcol=rcol, psuA=psuA, i=i: e.scalar_tensor_tensor(out=Pf[:, i, cs], in0=psuA[:, cs], scalar=rcol, in1=cst["bd2"][:], op0=ALU.mult, op1=ALU.mult),
                        r=[psuA.tok, rt.tok, cst["bd2"].tok, mtk], w=[Pf.tok[i]])
                rcols = AP(rt.t, 127, [[TB, 128], [128, 4]])
                P.v(lambda e, psuB=psuB, i=i: e.tensor_tensor(out=ncs[:, i, :], in0=psuB[:, 0:4], in1=rcols, op=ALU.mult), r=[psuB.tok, rt.tok], w=[ncs.tok])
                for c in range(4):
                    cs = slice(c * 128, (c + 1) * 128)
                    P.t(lambda e, i=i, cs=cs, psn=psn: e.matmul(psn[:, cs], lhsT=Cml_b[i][:, 0:128], rhs=mq[:, i, cs], start=False, stop=(cs.stop == 512)),
                        r=[Cml_b[i].tok, mq.tok], w=[psn.tok])
                    P.t(lambda e, i=i, cs=cs, psd=psd: e.matmul(psd[:, cs], lhsT=Cml_b[i][:, 128:256], rhs=mq[:, i, cs], start=False, stop=(cs.stop == 512)),
                        r=[Cml_b[i].tok, mq.tok], w=[psd.tok])
                    tt = tts[ttr[0] % 2]
                    ttr[0] += 1
                    rcol = rt[:, c * 128 + 127:c * 128 + 128]
                    P.v(lambda e, tt=tt, c=c, i=i: e.tensor_scalar(out=tt[:, 0:128], in0=cst["bd2"][:], scalar1=ncs[:, i, c:c + 1], scalar2=None, op0=ALU.mult),
                        r=[ncs.tok, cst["bd2"].tok], w=[tt.tok])
                    P.v(lambda e, i=i, cs=cs, rcol=rcol: e.scalar_tensor_tensor(out=Cml_b[i][:, 0:128], in0=Cml[i][:, 0:128], scalar=rcol, in1=Pf[:, i, cs], op0=ALU.mult, op1=ALU.add),
                        r=[Cml[i].tok, rt.tok, Pf.tok[i]], w=[Cml_b[i].tok])
                    P.v(lambda e, tt=tt, i=i, rcol=rcol: e.scalar_tensor_tensor(out=Cml_b[i][:, 128:256], in0=Cml[i][:, 128:256], scalar=rcol, in1=tt[:, 0:128], op0=ALU.mult, op1=ALU.add),
                        r=[Cml[i].tok, rt.tok, tt.tok], w=[Cml_b[i].tok])
                    P.v(lambda e, i=i, cs=cs, rcol=rcol: e.scalar_tensor_tensor(out=Cml[i][:, 0:128], in0=Cml[i][:, 0:128], scalar=rcol, in1=Pf[:, i, cs], op0=ALU.mult, op1=ALU.add),
                        r=[Cml[i].tok, rt.tok, Pf.tok[i]], w=[Cml[i].tok])
                    P.v(lambda e, tt=tt, i=i, rcol=rcol: e.scalar_tensor_tensor(out=Cml[i][:, 128:256], in0=Cml[i][:, 128:256], scalar=rcol, in1=tt[:, 0:128], op0=ALU.mult, op1=ALU.add),
                        r=[Cml[i].tok, rt.tok, tt.tok], w=[Cml[i].tok])
                d0i, oTi, ceni, sqi = (F[5], F[7], F[8], F[9]) if i == 0 else (F[0], F[1], F[6], F[0])
                P.v(lambda e, psd=psd, d0i=d0i: e.tensor_tensor(out=d0i[:], in0=psd[:, :], in1=rt[:], op=ALU.mult), r=[psd.tok, rt.tok], w=[d0i.tok])
                P.a(lambda e, psn=psn, oTi=oTi: e.activation(out=oTi[:], in_=psn[:, :], func=AF.Copy), r=[psn.tok], w=[oTi.tok])
                P.a(lambda e, d0i=d0i: e.activation(out=d0i[:], in_=d0i[:], func=AF.Abs), r=[d0i.tok], w=[d0i.tok])
                P.v(lambda e, d0i=d0i: e.tensor_tensor(out=d0i[:], in0=d0i[:], in1=MTt[:], op=ALU.max), r=[d0i.tok, MTt.tok], w=[d0i.tok])
                P.a(lambda e, d0i=d0i: e.activation(out=d0i[:], in_=d0i[:], func=AF.Ln), r=[d0i.tok], w=[d0i.tok])
                P.a(lambda e, d0i=d0i: e.activation(out=d0i[:], in_=d0i[:], func=AF.Exp, scale=-1.0), r=[d0i.tok], w=[d0i.tok])
                P.v(lambda e, d0i=d0i: e.scalar_tensor_tensor(out=d0i[:], in0=d0i[:], scalar=0.5, in1=rt[:], op0=ALU.mult, op1=ALU.mult), r=[d0i.tok, rt.tok], w=[d0i.tok])
                P.v(lambda e, d0i=d0i, oTi=oTi: e.tensor_tensor(out=oTi[:], in0=oTi[:], in1=d0i[:], op=ALU.mult), r=[oTi.tok, d0i.tok], w=[oTi.tok])
                P.v(lambda e, i=i, oTi=oTi: e.scalar_tensor_tensor(out=oTi[:], in0=mo[:, i, :], scalar=1.0, in1=oTi[:], op0=ALU.add, op1=ALU.mult), r=[oTi.tok, mo.tok], w=[oTi.tok])
                head_norm_gate(oTi, mg[:, i, :], mg.tok, gn[:, l * 4 + 2 + i:l * 4 + 2 + i + 1], mixp[:, i, :], ceni, sqi, mixp.tok[i])
            if KSUB >= 3:
                outproj_part(wo, blk)
        for i in range(2 if KSUB >= 4 else 0):
            for h2 in range(2):
                hs = slice(64 * h2, 64 * h2 + 64)
                P.dma("sync", o_mlc[l, 2 * i + h2], Cml[i][hs, 64 * h2:64 * h2 + 64], r=[Cml[i].tok], is_output=True)
                P.dma("sync", o_mln[l, 2 * i + h2].rearrange("(a b) -> a b", b=1), Cml[i][hs, 128 + 64 * h2:128 + 64 * h2 + 1],
                      r=[Cml[i].tok], is_output=True)
                P.dma("sync", o_mlm[l, 2 * i + h2:2 * i + h2 + 1].rearrange("(a b) -> a b", b=1), mlm_o[64 * h2:64 * h2 + 1, i:i + 1],
                      r=[mlm_o.tok], is_output=True)
        rot[0] = [0, 1, 2, 3, 4, 5]
        if SAMPLE >= 2:
            ml_sample(l, wC, wD, wE, wo)

    def xa_phase(l):
        W = d_win[l]
        wG = wload([(W, 2824, 3336, 0)])
        wo = woload(l, 3)
        aq, ag = P0, P1
        ett = [[Buf(ET.t[:, k, :], "ET%d" % k) for k in range(4)],
               [Buf(F[1 + k].t[:].bitcast(BF16)[:, 0:TB], "ETb%d" % k) for k in range(4)]]
        for k in range(4):
            ett[0][k].tok = ET.tok[k]
            ett[1][k].tok = F[1 + k].tok
        for blk in range(NBLK):
            for i in range(2):
                ps = fm_group(wG, i * 128, blk)
                P.a(lambda e, ps=ps, i=i, aq=aq: e.activation(out=aq[:, i, :], in_=ps[:, :], func=AF.Copy), r=[ps.tok], w=[aq.tok])
            for i in range(2):
                ps = fm_group(wG, 256 + i * 128, blk)
                P.a(lambda e, ps=ps, i=i, ag=ag: e.activation(out=ag[:, i, :], in_=ps[:, :], func=AF.Silu), r=[ps.tok], w=[ag.tok])
            for i in range(2):
                for h2 in range(2):
                    hs = slice(64 * h2, 64 * h2 + 64)
                    for mt in range(2):
                        ps = psum()
                        P.t(lambda e, ps=ps, hs=hs, mt=mt, i=i: e.matmul(ps[:, :], lhsT=mkT[hs, i, mt * 128:(mt + 1) * 128], rhs=aq[hs, i, :],
                                                                         start=True, stop=True), r=[mkT.tok, aq.tok], w=[ps.tok])
                        et = ett[i][h2 * 2 + mt]
                        P.a(lambda e, ps=ps, et=et: e.activation(out=et.t, in_=ps[:, :], func=AF.Exp, scale=0.125), r=[ps.tok], w=[et.tok])
                pso = banks[6] if i == 0 else psum()
                psd = banks[7] if i == 0 else psum()
                recd = F[0] if i == 0 else F[5]
                n = 0
                for h2 in range(2):
                    for mt in range(2):
                        et = ett[i][h2 * 2 + mt]
                        P.t(lambda e, h2=h2, mt=mt, i=i, n=n, et=et, pso=pso: e.matmul(pso[:, :], lhsT=mvpad[:, mt, i, h2, :], rhs=et.t,
                                                                                     start=(n == 0), stop=(n == 3)), r=[mvpad.tok, et.tok], w=[pso.tok])
                        n += 1
                n = 0
                for h2 in range(2):
                    for mt in range(2):
                        et = ett[i][h2 * 2 + mt]
                        P.t(lambda e, h2=h2, mt=mt, n=n, et=et, psd=psd: e.matmul(psd[:, :], lhsT=onespad[:, h2, :], rhs=et.t,
                                                                                start=(n == 0), stop=(n == 3)), r=[onespad.tok, et.tok], w=[psd.tok])
                        n += 1
                P.a(lambda e, psd=psd, recd=recd: e.activation(out=recd[:], in_=psd[:, :], func=AF.Ln), r=[psd.tok], w=[recd.tok])
                P.a(lambda e, recd=recd: e.activation(out=recd[:], in_=recd[:], func=AF.Exp, scale=-1.0), r=[recd.tok], w=[recd.tok])
                P.v(lambda e, pso=pso, recd=recd: e.tensor_tensor(out=recd[:], in0=pso[:, :], in1=recd[:], op=ALU.mult), r=[pso.tok, recd.tok], w=[recd.tok])
                P.v(lambda e, i=i, recd=recd: e.tensor_tensor(out=mixp[:, i, :], in0=recd[:], in1=ag[:, i, :], op=ALU.mult),
                    r=[recd.tok, ag.tok], w=[mixp.tok[i]])
            outproj_part(wo, blk)
        if SAMPLE >= 3:
            xa_sample(l, wG, wo)


    d_s5par = din("s5par", [128, DEPTH * 3 * 8])
    d_s5b = din("s5b", [DEPTH, 128, 2 * 8 * 16])
    d_s5c = din("s5c", [DEPTH, 128, 2 * 8 * 16])
    d_s5d = din("s5d", [128, DEPTH * 2])
    d_wglu = din("w_glu", [DEPTH, 256, 256])
    o_s5 = dout("o_s5", [DEPTH, 128, 2, 8])
    s5par = sb("s5par", [128, DEPTH, 3, 8])
    s5b = sb("s5b", [128, 2, 8, 16])
    s5c = sb("s5c", [128, 2, 8, 16])
    s5d = sb("s5d", [128, DEPTH, 2])
    P.dma("sync", s5par[:], d_s5par.rearrange("p (l w g) -> p l w g", l=DEPTH, w=3), w=[s5par.tok])
    P.dma("sync", s5d[:], d_s5d.rearrange("p (l f) -> p l f", l=DEPTH), w=[s5d.tok])
    sA = sb("sA", [128, 6, 8])
    sK = sb("sK", [128, 5, 8, 9])
    sKi = sb("sKi", [128, 8, 9], I32)
    sBB = sb("sBB", [128, 2, 8, 16])
    sE = sb("sE", [128, 4, 9, 16])
    sT = Buf(F[9].t[:, 0:288].rearrange("p (a k c) -> p a k c", a=2, k=9), "sT")
    sT.tok = F[9].tok
    s5o = sb("s5o", [128, 2, 8])
    d_sx0 = din("sx0", [DEPTH, 128, 2 * 8 * NS])
    o_s5s = dout("o_s5s", [DEPTH, 128, 2 * 8 * NS])
    xs0 = sb("xs0", [128, 2, 8, NS])
    xs0b = sb("xs0b", [128, 2, 8, NS], BF16)
    usT = sb("usT", [128, 2, NS], BF16)
    sgs = sb("sgs", [128, 2, NS])
    ysacc = sb("ysacc", [128, 2, NS])
    sst = sb("sst", [128, 4, NS])

    def s5_phase(l):
        W = d_win[l]
        wF = wload([(W, 2312, 2824, 0)])
        wo = woload(l, 2)
        P.dma("sync", s5b[:], d_s5b[l].rearrange("p (r g c) -> p r g c", r=2, g=8), w=[s5b.tok])
        P.dma("sync", s5c[:], d_s5c[l].rearrange("p (r g c) -> p r g c", r=2, g=8), w=[s5c.tok])
        uT = [ET, VA]
        pfb = Pf.t[:].rearrange("p a b -> p (a b)").bitcast(BF16)

        def sgp(ft, blk):
            ix = ft * 4 + blk
            if ix < 4:
                return pfb[:, ix * TB:(ix + 1) * TB], Pf.tok
            if ix < 6:
                return P2[:, ix - 4, :], P2.tok
            return sq2[ix - 6][:, :], sq2[ix - 6].tok
        uTap = [ET.t[:].rearrange("p a b -> p (a b)"), VA.t[:].rearrange("p a b c d -> p (a b c d)")]
        for blk in range(NBLK):
            for ft in range(2):
                ps = fm_group(wF, ft * 128, blk)
                dsti = AP(uT[ft].t, blk * 64, [[T, 128], [1, 64], [256, 8]])
                P.a(lambda e, ps=ps, dsti=dsti: e.activation(out=dsti, in_=ps[:, :].rearrange("p (j s) -> p j s", s=8), func=AF.Copy),
                    r=[ps.tok], w=[uT[ft].tok])
            for ft in range(2):
                ps = fm_group(wF, 256 + ft * 128, blk)
                sga, sgtok = sgp(ft, blk)
                P.a(lambda e, ps=ps, sga=sga: e.activation(out=sga, in_=ps[:, :], func=AF.Silu), r=[ps.tok], w=[sgtok])
        if SAMPLE >= 4:
            for ft in range(2):
                ps = fm_sample(wF, ft * 128)
                P.a(lambda e, ps=ps, ft=ft: e.activation(out=usT[:, ft, :], in_=ps[:, 0:NS], func=AF.Copy), r=[ps.tok], w=[usT.tok])
                ps = fm_sample(wF, 256 + ft * 128)
                P.a(lambda e, ps=ps, ft=ft: e.activation(out=sgs[:, ft, :], in_=ps[:, 0:NS], func=AF.Silu), r=[ps.tok], w=[sgs.tok])
            P.dma("sync", xs0[:].rearrange("p a b c -> p (a b c)"), d_sx0[l], w=[xs0.tok])
            P.v(lambda e: e.tensor_copy(out=xs0b[:], in_=xs0[:]), r=[xs0.tok], w=[xs0b.tok])
            P.v(lambda e: e.memset(ysacc[:], 0.0), w=[ysacc.tok])
        are, aim, ldt = s5par[:, l, 0, :], s5par[:, l, 1, :], s5par[:, l, 2, :]
        dtc, ardt, th, fre, fim, tm8 = (sA[:, j, :] for j in range(6))
        P.a(lambda e: e.activation(out=dtc, in_=ldt, func=AF.Exp), r=[s5par.tok], w=[sA.tok])
        P.v(lambda e: e.tensor_tensor(out=ardt, in0=are, in1=dtc, op=ALU.mult), r=[s5par.tok, sA.tok], w=[sA.tok])
        P.v(lambda e: e.tensor_tensor(out=th, in0=aim, in1=dtc, op=ALU.mult), r=[s5par.tok, sA.tok], w=[sA.tok])
        P.v(lambda e: e.tensor_scalar(out=tm8, in0=th, scalar1=8.0, scalar2=None, op0=ALU.mult), r=[sA.tok], w=[sA.tok])
        kg = AP(cst["kgrid"].t, 0, [[9, 128], [0, 8], [1, 9]])
        arg, ang, Pr, Pi, scr = (sK[:, j, :, :] for j in range(5))
        P.v(lambda e: e.tensor_tensor(out=arg, in0=AP(sA.t, 8, [[48, 128], [1, 8], [0, 9]]), in1=kg, op=ALU.mult), r=[sA.tok, cst["kgrid"].tok], w=[sK.tok])
        P.a(lambda e: e.activation(out=arg, in_=arg, func=AF.Exp), r=[sK.tok], w=[sK.tok])
        P.v(lambda e: e.tensor_tensor(out=ang, in0=AP(sA.t, 16, [[48, 128], [1, 8], [0, 9]]), in1=kg, op=ALU.mult), r=[sA.tok, cst["kgrid"].tok], w=[sK.tok])
        sincos(ang, sK.tok, sKi[:], scr, sKi.tok, Pi, sK.tok, Pr, sK.tok)
        P.v(lambda e: e.tensor_tensor(out=Pr, in0=Pr, in1=arg, op=ALU.mult), r=[sK.tok], w=[sK.tok])
        P.v(lambda e: e.tensor_tensor(out=Pi, in0=Pi, in1=arg, op=ALU.mult), r=[sK.tok], w=[sK.tok])
        nr, ni, den = scr[:, :, 0], scr[:, :, 1], scr[:, :, 2]
        t_a, t_b = scr[:, :, 3], scr[:, :, 4]
        P.v(lambda e: e.tensor_scalar(out=nr, in0=sK[:, 2, :, 1], scalar1=-1.0, scalar2=None, op0=ALU.add), r=[sK.tok], w=[sK.tok])
        P.v(lambda e: e.tensor_copy(out=ni, in_=sK[:, 3, :, 1]), r=[sK.tok], w=[sK.tok])
        P.v(lambda e: e.tensor_tensor(out=den, in0=are, in1=are, op=ALU.mult), r=[s5par.tok], w=[sK.tok])
        P.v(lambda e: e.tensor_tensor(out=t_a, in0=aim, in1=aim, op=ALU.mult), r=[s5par.tok], w=[sK.tok])
        P.v(lambda e: e.tensor_tensor(out=den, in0=den, in1=t_a, op=ALU.add), r=[sK.tok], w=[sK.tok])
        P.v(lambda e: e.reciprocal(out=den, in_=den), r=[sK.tok], w=[sK.tok])
        P.v(lambda e: e.tensor_tensor(out=t_a, in0=nr, in1=are, op=ALU.mult), r=[sK.tok, s5par.tok], w=[sK.tok])
        P.v(lambda e: e.tensor_tensor(out=t_b, in0=ni, in1=aim, op=ALU.mult), r=[sK.tok, s5par.tok], w=[sK.tok])
        P.v(lambda e: e.tensor_tensor(out=t_a, in0=t_a, in1=t_b, op=ALU.add), r=[sK.tok], w=[sK.tok])
        P.v(lambda e: e.tensor_tensor(out=fre, in0=t_a, in1=den, op=ALU.mult), r=[sK.tok], w=[sA.tok])
        P.v(lambda e: e.tensor_tensor(out=t_a, in0=ni, in1=are, op=ALU.mult), r=[sK.tok, s5par.tok], w=[sK.tok])
        P.v(lambda e: e.tensor_tensor(out=t_b, in0=nr, in1=aim, op=ALU.mult), r=[sK.tok, s5par.tok], w=[sK.tok])
        P.v(lambda e: e.tensor_tensor(out=t_a, in0=t_a, in1=t_b, op=ALU.subtract), r=[sK.tok], w=[sK.tok])
        P.v(lambda e: e.tensor_tensor(out=fim, in0=t_a, in1=den, op=ALU.mult), r=[sK.tok], w=[sA.tok])
        freb = AP(sA.t, 24, [[48, 128], [1, 8], [0, 16]])
        fimb = AP(sA.t, 32, [[48, 128], [1, 8], [0, 16]])
        bre, bim = s5b[:, 0, :, :], s5b[:, 1, :, :]
        bbre, bbim = sBB[:, 0, :, :], sBB[:, 1, :, :]
        tq = F[9].t[:, 0:128].rearrange("p (g c) -> p g c", g=8)
        P.v(lambda e: e.tensor_tensor(out=bbre, in0=bre, in1=freb, op=ALU.mult), r=[s5b.tok, sA.tok], w=[sBB.tok])
        P.v(lambda e: e.tensor_tensor(out=tq, in0=bim, in1=fimb, op=ALU.mult), r=[s5b.tok, sA.tok], w=[sT.tok])
        P.v(lambda e: e.tensor_tensor(out=bbre, in0=bbre, in1=tq, op=ALU.subtract), r=[sBB.tok, sT.tok], w=[sBB.tok])
        P.v(lambda e: e.tensor_tensor(out=bbim, in0=bim, in1=freb, op=ALU.mult), r=[s5b.tok, sA.tok], w=[sBB.tok])
        P.v(lambda e: e.tensor_tensor(out=tq, in0=bre, in1=fimb, op=ALU.mult), r=[s5b.tok, sA.tok], w=[sT.tok])
        P.v(lambda e: e.tensor_tensor(out=bbim, in0=bbim, in1=tq, op=ALU.add), r=[sBB.tok, sT.tok], w=[sBB.tok])

        W0, W1b, WZ = wch[(wrr[0]) % NW], wch[(wrr[0] + 1) % NW], wF
        w0f = W0.t[:].rearrange("p a b -> p (a b)")
        w1f = W1b.t[:].rearrange("p a b -> p (a b)")
        wzf = WZ.t[:].rearrange("p a b -> p (a b)")
        ptok = [Tok("ptwA"), Tok("ptwB")]
        ytok = [W1b.tok, WZ.tok]
        ptw = [w0f[:, 0:2048], w0f[:, 2048:4096]]
        ymw = [w1f[:, 0:2304], wzf[:, 0:2304]]
        PTp = [[ptw[p_][:, 0:1024].rearrange("p (k c) -> p k c", k=8), ptw[p_][:, 1024:2048].rearrange("p (k c) -> p k c", k=8)] for p_ in range(2)]
        Ymp = [[ymw[p_][:, 0:1152].rearrange("p (k c) -> p k c", k=9), ymw[p_][:, 1152:2304].rearrange("p (k c) -> p k c", k=9)] for p_ in range(2)]
        wglu = Buf(w1f[:, 2304:2816].rearrange("p (k c) -> p k c", k=2), "wglu")
        wglu.tok = W1b.tok
        P.dma("gpsimd", wglu.t, d_wglu[l].rearrange("(k p) c -> p k c", p=128), w=[wglu.tok])
        first_use = [True, True]
        xprev = P0
        BDm = P1.t[:].rearrange("p a b -> p (a b)")[:, 0:1024].rearrange("p (k c) -> p k c", k=8)
        rot[0] = [6, 7]
        yacc = banks[0:4]
        bdacc = banks[4:6]
        cs_t, an_t, vt_t, wt_t, xt_t, t1_t, t2_t, nf_t = F[0], F[1], F[2], F[3], F[4], F[5], F[6], F[7]
        for ft in range(2):
            for bk in list(yacc) + list(bdacc):
                P.t(lambda e, bk=bk: e.matmul(bk[:, :], lhsT=zeros_b[:, 0:128], rhs=hnT[:, 0, 0:512], start=True, stop=False), r=[zeros_b.tok, htok[0]], w=[bk.tok])
            for pp in range(4):
                pi_ = ft * 4 + pp
                par = pi_ % 2
                PT, W1, Ym = PTp[par], PTp[par], Ymp[par]
                PTK, YTK = ptok[par], ytok[par]
                Prb = AP(sK.t, 2 * 72 + pi_ * 9, [[360, 128], [1, 9], [0, 16]])
                Pib = AP(sK.t, 3 * 72 + pi_ * 9, [[360, 128], [1, 9], [0, 16]])
                bbr = AP(sBB.t, pi_ * 16, [[256, 128], [0, 9], [1, 16]])
                bbi = AP(sBB.t, 128 + pi_ * 16, [[256, 128], [0, 9], [1, 16]])
                crb = AP(s5c.t, 0 * 128 + pi_ * 16, [[256, 128], [0, 9], [1, 16]])
                cib = AP(s5c.t, 1 * 128 + pi_ * 16, [[256, 128], [0, 9], [1, 16]])
                Ere, Eim, CAre, CAim = (sE[:, j, :, :] for j in range(4))
                ta, tb = sT[:, 0, :, :], sT[:, 1, :, :]
                for (o_, x1, y1, x2, y2, op_) in ((Ere, Prb, bbr, Pib, bbi, ALU.subtract), (Eim, Prb, bbi, Pib, bbr, ALU.add),
                                                   (CAre, Prb, crb, Pib, cib, ALU.subtract), (CAim, Pib, crb, Prb, cib, ALU.add)):
                    P.v(lambda e, o_=o_, x1=x1, y1=y1: e.tensor_tensor(out=o_, in0=x1, in1=y1, op=ALU.mult), r=[sK.tok, sBB.tok, s5c.tok], w=[sE.tok])
                    P.v(lambda e, x2=x2, y2=y2: e.tensor_tensor(out=ta, in0=x2, in1=y2, op=ALU.mult), r=[sK.tok, sBB.tok, s5c.tok], w=[sT.tok])
                    P.v(lambda e, o_=o_, op_=op_: e.tensor_tensor(out=o_, in0=o_, in1=ta, op=op_), r=[sE.tok, sT.tok], w=[sE.tok])
                P.g(lambda e, par=par: e.memset(ptw[par], 0.0), w=[PTK] + ([W0.tok] if first_use[par] else []))
                first_use[par] = False
                P.g(lambda e, par=par: e.memset(ymw[par], 0.0), w=[YTK])
                for g2 in range(2):
                    rs_ = slice(64 * g2, 64 * g2 + 64)
                    c0 = (2 * pp + g2) * 16
                    for ri in range(2):
                        P.a(lambda e, rs_=rs_, c0=c0, ri=ri: e.activation(out=PT[ri][rs_, :, c0:c0 + 16], in_=sE[rs_, ri, 0:8, :], func=AF.Copy), r=[sE.tok], w=[PTK])
                    P.a(lambda e, rs_=rs_, c0=c0: e.activation(out=Ym[0][rs_, :, c0:c0 + 16], in_=sE[rs_, 2, :, :], func=AF.Copy), r=[sE.tok], w=[YTK])
                    P.a(lambda e, rs_=rs_, c0=c0: e.activation(out=Ym[1][rs_, :, c0:c0 + 16], in_=sE[rs_, 3, :, :], func=AF.Copy, scale=-1.0), r=[sE.tok], w=[YTK])
                for k in range(8):
                    bk = bdacc[k // 4]
                    for ri in range(2):
                        P.t(lambda e, k=k, ri=ri, bk=bk, pp=pp: e.matmul(bk[:, (k % 4) * 128:(k % 4) * 128 + 128], lhsT=PT[ri][:, 0, :], rhs=Ym[ri][:, k, :],
                                                                         start=False, stop=(pp == 3 and ri == 1 and k % 4 == 3)),
                            r=[PTK, YTK], w=[bk.tok])
                for ri in range(2):
                    pst = psum()
                    pstb = pst.t[:].bitcast(BF16)
                    for k in range(8):
                        P.t(lambda e, pstb=pstb, ri=ri, k=k: e.transpose(pstb[:, k * 128:(k + 1) * 128], PT[ri][:, k, :], ident_b[:]),
                            r=[PTK, ident_b.tok], w=[pst.tok])
                    P.a(lambda e, pstb=pstb, ri=ri: e.activation(out=W1[ri], in_=pstb[:, :].rearrange("p (k c) -> p k c", k=8), func=AF.Copy),
                        r=[pst.tok], w=[PTK])
                if SAMPLE >= 4:
                    pvs = psum()
                    for ri in range(2):
                        P.t(lambda e, ri=ri, pvs=pvs, ft=ft: e.matmul(pvs[:, ri * NS:(ri + 1) * NS], lhsT=W1[ri][:, 0, :], rhs=usT[:, ft, :], start=True, stop=True),
                            r=[PTK, usT.tok], w=[pvs.tok])
                    pr1 = sK[:, 2, pi_, 1:2]
                    pi1 = sK[:, 3, pi_, 1:2]
                    x0r, x0i = xs0[:, 0, pi_, :], xs0[:, 1, pi_, :]
                    ta_, tb_ = sst[:, 0, :], sst[:, 1, :]
                    P.v(lambda e: e.tensor_scalar(out=ta_, in0=x0r, scalar1=pr1, scalar2=None, op0=ALU.mult), r=[xs0.tok, sK.tok], w=[sst.tok])
                    P.v(lambda e: e.tensor_scalar(out=tb_, in0=x0i, scalar1=pi1, scalar2=None, op0=ALU.mult), r=[xs0.tok, sK.tok], w=[sst.tok])
                    P.v(lambda e: e.tensor_tensor(out=ta_, in0=ta_, in1=tb_, op=ALU.subtract), r=[sst.tok], w=[sst.tok])
                    P.v(lambda e, pvs=pvs: e.tensor_tensor(out=sst[:, 2, :], in0=ta_, in1=pvs[:, 0:NS], op=ALU.add), r=[sst.tok, pvs.tok], w=[sst.tok])
                    P.v(lambda e: e.tensor_scalar(out=ta_, in0=x0i, scalar1=pr1, scalar2=None, op0=ALU.mult), r=[xs0.tok, sK.tok], w=[sst.tok])
                    P.v(lambda e: e.tensor_scalar(out=tb_, in0=x0r, scalar1=pi1, scalar2=None, op0=ALU.mult), r=[xs0.tok, sK.tok], w=[sst.tok])
                    P.v(lambda e: e.tensor_tensor(out=ta_, in0=ta_, in1=tb_, op=ALU.add), r=[sst.tok], w=[sst.tok])
                    P.v(lambda e, pvs=pvs, pi_=pi_: e.tensor_tensor(out=xs0[:, 1, pi_, :], in0=ta_, in1=pvs[:, NS:2 * NS], op=ALU.add), r=[sst.tok, pvs.tok], w=[xs0.tok])
                    P.v(lambda e, pi_=pi_: e.tensor_copy(out=xs0[:, 0, pi_, :], in_=sst[:, 2, :]), r=[sst.tok], w=[xs0.tok])
                    pys = psum()
                    for ri in range(2):
                        P.t(lambda e, ri=ri, pys=pys, pi_=pi_: e.matmul(pys[:, 0:NS], lhsT=Ym[ri][:, 1, :], rhs=xs0b[:, ri, pi_, :], start=(ri == 0), stop=(ri == 1)),
                            r=[YTK, xs0b.tok], w=[pys.tok])
                    P.v(lambda e, pys=pys, ft=ft: e.tensor_tensor(out=ysacc[:, ft, :], in0=ysacc[:, ft, :], in1=pys[:, 0:NS], op=ALU.add), r=[pys.tok, ysacc.tok], w=[ysacc.tok])
                psv = psum()
                for ri in range(2):
                    for s_ in range(8):
                        usl = AP(uT[ft].t, s_ * 256, [[T, 128], [1, 256]])
                        P.t(lambda e, ri=ri, s_=s_, usl=usl, psv=psv: e.matmul(psv[:, ri * 256:(ri + 1) * 256], lhsT=W1[ri][:, 7 - s_, :], rhs=usl,
                                                                              start=(s_ == 0), stop=(s_ == 7)), r=[PTK, uT[ft].tok], w=[psv.tok])
                P.v(lambda e, pi_=pi_: e.tensor_scalar(out=an_t[:, 0:128], in0=cst["l1"][:], scalar1=-1.0, scalar2=sA[:, 5, pi_:pi_ + 1], op0=ALU.add, op1=ALU.mult),
                    r=[cst["l1"].tok, sA.tok], w=[an_t.tok])
                P.v(lambda e, pi_=pi_: e.tensor_scalar(out=an_t[:, 128:256], in0=cst["l1"][:], scalar1=127.0, scalar2=sA[:, 5, pi_:pi_ + 1], op0=ALU.add, op1=ALU.mult),
                    r=[cst["l1"].tok, sA.tok], w=[an_t.tok])
                cb, sb_ = cs_t[:, 0:256], cs_t[:, 256:512]
                sincos(an_t[:, 0:256], an_t.tok, nf_t.t[:, 0:256].bitcast(I32), nf_t[:, 0:256], nf_t.tok, sb_, cs_t.tok, cb, cs_t.tok)
                vr, vi = psv[:, 0:256], psv[:, 256:512]
                vtr, vti = vt_t[:, 0:256], vt_t[:, 256:512]
                t1, t2 = t1_t[:, 0:256], t2_t[:, 0:256]
                P.v(lambda e: e.tensor_tensor(out=t1, in0=vr, in1=cb, op=ALU.mult), r=[psv.tok, cs_t.tok], w=[t1_t.tok])
                P.v(lambda e: e.tensor_tensor(out=t2, in0=vi, in1=sb_, op=ALU.mult), r=[psv.tok, cs_t.tok], w=[t2_t.tok])
                P.v(lambda e: e.tensor_tensor(out=vtr, in0=t1, in1=t2, op=ALU.add), r=[t1_t.tok, t2_t.tok], w=[vt_t.tok])
                P.v(lambda e: e.tensor_tensor(out=t1, in0=vi, in1=cb, op=ALU.mult), r=[psv.tok, cs_t.tok], w=[t1_t.tok])
                P.v(lambda e: e.tensor_tensor(out=t2, in0=vr, in1=sb_, op=ALU.mult), r=[psv.tok, cs_t.tok], w=[t2_t.tok])
                P.v(lambda e: e.tensor_tensor(out=vti, in0=t1, in1=t2, op=ALU.subtract), r=[t1_t.tok, t2_t.tok], w=[vt_t.tok])
                rho = AP(sK.t, pi_ * 9 + 8, [[360, 128], [0, 256]])
                wr_, wi_ = wt_t[:, 0:256], wt_t[:, 256:512]
                P.v(lambda e: e.tensor_tensor_scan(out=wr_, data0=rho, data1=vtr, initial=0.0, op0=ALU.mult, op1=ALU.add), r=[sK.tok, vt_t.tok], w=[wt_t.tok])
                P.v(lambda e: e.tensor_tensor_scan(out=wi_, data0=rho, data1=vti, initial=0.0, op0=ALU.mult, op1=ALU.add), r=[sK.tok, vt_t.tok], w=[wt_t.tok])
                xr_, xi_ = xt_t[:, 0:256], xt_t[:, 256:512]
                P.v(lambda e: e.tensor_tensor(out=t1, in0=wr_, in1=cb, op=ALU.mult), r=[wt_t.tok, cs_t.tok], w=[t1_t.tok])
                P.v(lambda e: e.tensor_tensor(out=t2, in0=wi_, in1=sb_, op=ALU.mult), r=[wt_t.tok, cs_t.tok], w=[t2_t.tok])
                P.v(lambda e: e.tensor_tensor(out=xr_, in0=t1, in1=t2, op=ALU.subtract), r=[t1_t.tok, t2_t.tok], w=[xt_t.tok])
                P.v(lambda e: e.tensor_tensor(out=t1, in0=wr_, in1=sb_, op=ALU.mult), r=[wt_t.tok, cs_t.tok], w=[t1_t.tok])
                P.v(lambda e: e.tensor_tensor(out=t2, in0=wi_, in1=cb, op=ALU.mult), r=[wt_t.tok, cs_t.tok], w=[t2_t.tok])
                P.v(lambda e: e.tensor_tensor(out=xi_, in0=t1, in1=t2, op=ALU.add), r=[t1_t.tok, t2_t.tok], w=[xt_t.tok])
                P.v(lambda e, pi_=pi_: e.tensor_copy(out=s5o[:, :, pi_], in_=AP(xt_t.t, 255, [[TB, 128], [256, 2]])), r=[xt_t.tok], w=[s5o.tok])
                P.v(lambda e: e.memset(xprev[:, :, 0:1], 0.0), w=[xprev.tok])
                P.a(lambda e: e.activation(out=xprev[:, :, 1:256], in_=AP(xt_t.t, 0, [[TB, 128], [256, 2], [1, 255]]), func=AF.Copy), r=[xt_t.tok], w=[xprev.tok])
                for t8 in range(8):
                    ya = yacc[t8 // 2]
                    for ri in range(2):
                        P.t(lambda e, t8=t8, ri=ri, ya=ya, pp=pp: e.matmul(ya[:, (t8 % 2) * 256:(t8 % 2) * 256 + 256], lhsT=Ym[ri][:, t8 + 1, :], rhs=xprev[:, ri, 0:256],
                                                                           start=False, stop=False), r=[YTK, xprev.tok], w=[ya.tok])
            for k in range(8):
                bk = bdacc[k // 4]
                if k == 0:
                    P.v(lambda e, bk=bk, ft=ft: e.scalar_tensor_tensor(out=BDm[:, 0, :], in0=cst["ident"][:], scalar=s5d[:, l, ft:ft + 1], in1=bk[:, 0:128],
                                                                       op0=ALU.mult, op1=ALU.add), r=[bk.tok, cst["ident"].tok, s5d.tok], w=[P1.tok])
                else:
                    P.v(lambda e, bk=bk, k=k: e.tensor_copy(out=BDm[:, k, :], in_=bk[:, (k % 4) * 128:(k % 4) * 128 + 128]), r=[bk.tok], w=[P1.tok])
            if SAMPLE >= 4:
                pbs = psum()
                P.t(lambda e, pbs=pbs, ft=ft: e.matmul(pbs[:, 0:NS], lhsT=BDm[:, 0, :], rhs=usT[:, ft, :], start=True, stop=True), r=[P1.tok, usT.tok], w=[pbs.tok])
                P.v(lambda e, pbs=pbs, ft=ft: e.tensor_tensor(out=ysacc[:, ft, :], in0=ysacc[:, ft, :], in1=pbs[:, 0:NS], op=ALU.add), r=[pbs.tok, ysacc.tok], w=[ysacc.tok])
            for t8 in range(8):
                ya = yacc[t8 // 2]
                for s_ in range(t8 + 1):
                    usl = AP(uT[ft].t, s_ * 256, [[T, 128], [1, 256]])
                    P.t(lambda e, t8=t8, s_=s_, ya=ya, usl=usl: e.matmul(ya[:, (t8 % 2) * 256:(t8 % 2) * 256 + 256], lhsT=BDm[:, t8 - s_, :], rhs=usl,
                                                                         start=False, stop=(s_ == t8 and t8 % 2 == 1)), r=[P1.tok, uT[ft].tok], w=[ya.tok])
            for t8 in range(8):
                ya = yacc[t8 // 2]
                ysl = ya[:, (t8 % 2) * 256:(t8 % 2) * 256 + 256]
                g1, g2_ = t1_t[:, 0:256], t2_t[:, 0:256]
                P.a(lambda e, ysl=ysl: e.activation(out=g1, in_=ysl, func=AF.Square), r=[ya.tok], w=[t1_t.tok])
                P.v(lambda e: e.tensor_scalar(out=g1, in0=g1, scalar1=0.044715, scalar2=1.0, op0=ALU.mult, op1=ALU.add), r=[t1_t.tok], w=[t1_t.tok])
                P.v(lambda e, ysl=ysl: e.tensor_tensor(out=g1, in0=g1, in1=ysl, op=ALU.mult), r=[t1_t.tok, ya.tok], w=[t1_t.tok])
                P.a(lambda e: e.activation(out=g2_, in_=g1, func=AF.Tanh, scale=0.79788456), r=[t1_t.tok], w=[t2_t.tok])
                usl = AP(uT[ft].t, t8 * 256, [[T, 128], [1, 256]])
                P.v(lambda e, ysl=ysl, usl=usl: e.scalar_tensor_tensor(out=usl, in0=g2_, scalar=1.0, in1=ysl, op0=ALU.add, op1=ALU.mult), r=[t2_t.tok, ya.tok], w=[uT[ft].tok])
        rot[0] = [0, 1, 2, 3, 4, 5]
        P.g(lambda e: e.memset(w0f[:, 0:2], 0.0), r=ptok, w=[W0.tok] + ptok)
        P.dma("sync", o_s5[l], s5o[:], r=[s5o.tok], is_output=True)
        for blk in range(NBLK):
            bs = slice(blk * TB, (blk + 1) * TB)
            for fo_ in range(2):
                ps = psum()
                for fi_ in range(2):
                    rsj = AP(uT[fi_].t, blk * 64, [[T, 128], [256, 8], [1, 64]])
                    P.t(lambda e, ps=ps, fi_=fi_, fo_=fo_, rsj=rsj: e.matmul(ps[:, :].rearrange("p (s j) -> p s j", s=8), lhsT=wglu[:, fi_, fo_ * 128:(fo_ + 1) * 128], rhs=rsj,
                                                                             start=(fi_ == 0), stop=(fi_ == 1)), r=[wglu.tok, uT[fi_].tok], w=[ps.tok])
                sg_ = F[10]
                sgn = AP(sg_.t, 0, [[TB, 128], [1, 8], [8, 64]])
                P.a(lambda e, ps=ps, sgn=sgn: e.activation(out=sgn, in_=ps[:, :].rearrange("p (s j) -> p s j", s=8), func=AF.Tanh, scale=0.25), r=[ps.tok], w=[sg_.tok])
                syn = AP(uT[fo_].t, blk * 64, [[T, 128], [1, 64], [256, 8]])
                P.v(lambda e, syn=syn: e.scalar_tensor_tensor(out=sg_[:].rearrange("p (j s) -> p j s", s=8), in0=sg_[:].rearrange("p (j s) -> p j s", s=8), scalar=1.0, in1=syn,
                                                            op0=ALU.add, op1=ALU.mult), r=[sg_.tok, uT[fo_].tok], w=[sg_.tok])
                sga, sgtok = sgp(fo_, blk)
                P.v(lambda e, fo_=fo_, sga=sga: e.scalar_tensor_tensor(out=mixp[:, fo_, :], in0=sg_[:], scalar=0.25, in1=sga, op0=ALU.mult, op1=ALU.mult),
                    r=[sg_.tok, sgtok], w=[mixp.tok[fo_]])
            outproj_part(wo, blk)
        if SAMPLE >= 4:
            P.dma("sync", o_s5s[l], xs0[:].rearrange("p a b c -> p (a b c)"), r=[xs0.tok], is_output=True)
            ysf = ysacc[:].rearrange("p a n -> p (a n)")
            g1 = sst[:, 0:2, :].rearrange("p a n -> p (a n)")
            g2_ = sst[:, 2:4, :].rearrange("p a n -> p (a n)")
            P.a(lambda e: e.activation(out=g1, in_=ysf, func=AF.Square), r=[ysacc.tok], w=[sst.tok])
            P.v(lambda e: e.tensor_scalar(out=g1, in0=g1, scalar1=0.044715, scalar2=1.0, op0=ALU.mult, op1=ALU.add), r=[sst.tok], w=[sst.tok])
            P.v(lambda e: e.tensor_tensor(out=g1, in0=g1, in1=ysf, op=ALU.mult), r=[sst.tok, ysacc.tok], w=[sst.tok])
            P.a(lambda e: e.activation(out=g2_, in_=g1, func=AF.Tanh, scale=0.79788456), r=[sst.tok], w=[sst.tok])
            P.v(lambda e: e.scalar_tensor_tensor(out=ysf, in0=g2_, scalar=1.0, in1=ysf, op0=ALU.add, op1=ALU.mult), r=[sst.tok, ysacc.tok], w=[ysacc.tok])
            P.v(lambda e: e.tensor_copy(out=usT[:], in_=ysacc[:]), r=[ysacc.tok], w=[usT.tok])
            for fo_ in range(2):
                ps = psum()
                for fi_ in range(2):
                    P.t(lambda e, ps=ps, fi_=fi_, fo_=fo_: e.matmul(ps[:, 0:NS], lhsT=wglu[:, fi_, fo_ * 128:(fo_ + 1) * 128], rhs=usT[:, fi_, :],
                                                                    start=(fi_ == 0), stop=(fi_ == 1)), r=[wglu.tok, usT.tok], w=[ps.tok])
                P.a(lambda e, ps=ps, fo_=fo_: e.activation(out=sst[:, fo_, :], in_=ps[:, 0:NS], func=AF.Tanh, scale=0.25), r=[ps.tok], w=[sst.tok])
                P.v(lambda e, fo_=fo_: e.scalar_tensor_tensor(out=sst[:, fo_, :], in0=sst[:, fo_, :], scalar=1.0, in1=ysacc[:, fo_, :], op0=ALU.add, op1=ALU.mult),
                    r=[sst.tok, ysacc.tok], w=[sst.tok])
                P.v(lambda e, fo_=fo_: e.scalar_tensor_tensor(out=mixs[:, fo_, :], in0=sst[:, fo_, :], scalar=0.25, in1=sgs[:, fo_, :], op0=ALU.mult, op1=ALU.mult),
                    r=[sst.tok, sgs.tok], w=[mixs.tok])
            outproj_s(wo)


    d_xsT = din("xsT", [1024, NS])
    d_sret = din("sret", [DEPTH, 64, 4096])
    d_gns = din("gns", [64, DEPTH * 2 * 64])
    o_ysT = dout("o_ysT", [1024, NS])
    o_sret = dout("o_sret", [DEPTH, 64, 4096])
    xsT = sb("xsT", [128, 8, NS])
    P.dma("sync", xsT[:], d_xsT.rearrange("(k p) n -> p k n", p=128), w=[xsT.tok])
    hnsT = sb("hnsT", [128, 8, NS], BF16)
    gns = sb("gns", [64, DEPTH * 2 * 64])
    P.dma("sync", gns[:], d_gns, w=[gns.tok])
    qk4 = sb("qk4", [64, 5, 64])
    ropes = sb("ropes", [64, 3, 32])
    ropei = sb("ropei", [64, 32], I32)
    sm = sb("sm", [64, 16])
    osn = sb("osn", [64, 4, 64])
    xpad = sb("xpad", [64, 2, 64])
    mixs = sb("mixs", [128, 2, NS], BF16)
    gam = sb("gam", [64, 1])
    P.a(lambda e: e.activation(out=gam[:], in_=cst["lg64"][:], func=AF.Exp), r=[cst["lg64"].tok], w=[gam.tok])
    P.v(lambda e: e.tensor_scalar(out=ropes[:, 2, :], in0=cst["invrow"][0:64, :], scalar1=PAST, scalar2=None, op0=ALU.mult),
        r=[cst["invrow"].tok], w=[ropes.tok])
    sincos(ropes[:, 2, :], ropes.tok, ropei[:], ropei.t[:].bitcast(F32), ropei.tok, ropes[:, 1, :], ropes.tok, ropes[:, 0, :], ropes.tok)

    def tm_sample(wb, c0, n):
        ps = psum()
        for kt in range(8):
            P.t(lambda e, kt=kt: e.matmul(ps[0:NS, 0:n], lhsT=hnsT[:, kt, :], rhs=wb[:, kt, c0:c0 + n], start=(kt == 0), stop=(kt == 7)),
                r=[wb.tok, hnsT.tok], w=[ps.tok])
        return ps

    def fm_sample(wb, c0):
        ps = psum()
        for kt in range(8):
            P.t(lambda e, kt=kt: e.matmul(ps[:, 0:NS], lhsT=wb[:, kt, c0:c0 + 128], rhs=hnsT[:, kt, :], start=(kt == 0), stop=(kt == 7)),
                r=[wb.tok, hnsT.tok], w=[ps.tok])
        return ps

    def to_hn(pr_ap, prtok, nsec, dst, dtok):
        ps = psum()
        for h in range(4):
            rhs = AP(pr_ap.tensor, pr_ap.offset + h * 64, [list(pr_ap.ap[0]), [256, nsec], [1, 64]])
            P.t(lambda e, h=h, rhs=rhs: e.matmul(ps[0:64, 0:nsec * 64].rearrange("p (s d) -> p s d", s=nsec), lhsT=cst["sel"][:, h * 64:(h + 1) * 64], rhs=rhs,
                                                 start=(h == 0), stop=(h == 3)), r=[cst["sel"].tok, prtok], w=[ps.tok])
        P.a(lambda e: e.activation(out=dst, in_=ps[0:64, 0:nsec * 64].rearrange("p (s d) -> p s d", s=nsec), func=AF.Copy), r=[ps.tok], w=[dtok])

    def rope_s(x4, xtok, secs):
        cosb = AP(ropes.t, 0, [[96, 64], [0, secs], [1, 32]])
        sinb = AP(ropes.t, 32, [[96, 64], [0, secs], [1, 32]])
        x1, x2 = x4[:, 0:secs, 0:32], x4[:, 0:secs, 32:64]
        ta = osn[:, 0, :].rearrange("p (s d) -> p s d", s=2)[:, 0:secs, :]
        tb = osn[:, 1, :].rearrange("p (s d) -> p s d", s=2)[:, 0:secs, :]
        tc = osn[:, 2, :].rearrange("p (s d) -> p s d", s=2)[:, 0:secs, :]
        P.v(lambda e: e.tensor_tensor(out=ta, in0=x1, in1=cosb, op=ALU.mult), r=[xtok, ropes.tok], w=[osn.tok])
        P.v(lambda e: e.tensor_tensor(out=tb, in0=x2, in1=sinb, op=ALU.mult), r=[xtok, ropes.tok], w=[osn.tok])
        P.v(lambda e: e.tensor_tensor(out=ta, in0=ta, in1=tb, op=ALU.subtract), r=[osn.tok], w=[osn.tok])
        P.v(lambda e: e.tensor_tensor(out=tb, in0=x1, in1=sinb, op=ALU.mult), r=[xtok, ropes.tok], w=[osn.tok])
        P.v(lambda e: e.tensor_tensor(out=tc, in0=x2, in1=cosb, op=ALU.mult), r=[xtok, ropes.tok], w=[osn.tok])
        P.v(lambda e: e.tensor_tensor(out=x2, in0=tb, in1=tc, op=ALU.add), r=[osn.tok], w=[xtok])
        P.v(lambda e: e.tensor_copy(out=x1, in_=ta), r=[osn.tok], w=[xtok])

    def headnorm_s(o_ap, g_ap, gn_ap):
        mean, var = sm[:, 0:1], sm[:, 1:2]
        cen = osn[:, 1, :]
        P.v(lambda e: e.tensor_reduce(out=mean, in_=o_ap, axis=AX.X, op=ALU.add), r=[osn.tok], w=[sm.tok])
        P.v(lambda e: e.tensor_scalar(out=mean, in0=mean, scalar1=-1.0 / 64, scalar2=None, op0=ALU.mult), r=[sm.tok], w=[sm.tok])
        P.v(lambda e: e.tensor_scalar(out=cen, in0=o_ap, scalar1=mean, scalar2=None, op0=ALU.add), r=[osn.tok, sm.tok], w=[osn.tok])
        P.v(lambda e: e.tensor_tensor(out=osn[:, 2, :], in0=cen, in1=cen, op=ALU.mult), r=[osn.tok], w=[osn.tok])
        P.v(lambda e: e.tensor_reduce(out=var, in_=osn[:, 2, :], axis=AX.X, op=ALU.add), r=[osn.tok], w=[sm.tok])
        P.a(lambda e: e.activation(out=var, in_=var, func=AF.Ln, scale=1.0 / 64, bias=EPS), r=[sm.tok], w=[sm.tok])
        P.a(lambda e: e.activation(out=var, in_=var, func=AF.Exp, scale=-0.5), r=[sm.tok], w=[sm.tok])
        P.v(lambda e: e.scalar_tensor_tensor(out=cen, in0=cen, scalar=var, in1=gn_ap, op0=ALU.mult, op1=ALU.mult), r=[osn.tok, sm.tok, gns.tok], w=[osn.tok])
        P.v(lambda e: e.tensor_tensor(out=cen, in0=cen, in1=g_ap, op=ALU.mult), r=[osn.tok], w=[osn.tok])
        return cen

    def place_mix(y_ap):
        for i in range(2):
            yb = AP(y_ap.tensor, y_ap.offset, [list(y_ap.ap[0]), [0, 2], [1, 64]])
            mk2 = AP(cst["mpair"].t, i * 2, [[4, 64], [1, 2], [0, 64]])
            P.v(lambda e, yb=yb, mk2=mk2: e.tensor_tensor(out=xpad[:], in0=yb, in1=mk2, op=ALU.mult), r=[osn.tok, cst["mpair"].tok], w=[xpad.tok])
            ps = psum()
            P.t(lambda e, ps=ps: e.matmul(ps[:, 0:NS], lhsT=xpad[:].rearrange("p a b -> p (a b)"), rhs=cst["sel2"][:], start=True, stop=True),
                r=[xpad.tok, cst["sel2"].tok], w=[ps.tok])
            P.a(lambda e, ps=ps, i=i: e.activation(out=mixs[:, i, :], in_=ps[:, 0:NS], func=AF.Copy), r=[ps.tok], w=[mixs.tok])

    def outproj_s(wo):
        ps = psum()
        for dt_ in range(8):
            for kt in range(2):
                P.t(lambda e, kt=kt, dt_=dt_: e.matmul(ps[:, dt_ * NS:(dt_ + 1) * NS], lhsT=wo[:, kt, dt_ * 128:(dt_ + 1) * 128], rhs=mixs[:, kt, :],
                                                       start=(kt == 0), stop=(kt == 1)), r=[wo.tok, mixs.tok], w=[ps.tok])
        P.v(lambda e: e.tensor_tensor(out=xsT[:].rearrange("p k n -> p (k n)"), in0=xsT[:].rearrange("p k n -> p (k n)"), in1=ps[:, 0:8 * NS], op=ALU.add),
            r=[ps.tok, xsT.tok], w=[xsT.tok])

    def ret_sample(l, wA, wB, wo):
        pr = Pf.t[:].rearrange("p a b -> p (a b)")
        ps = tm_sample(wA, 0, 512)
        P.a(lambda e: e.activation(out=pr[0:NS, 0:512], in_=ps[0:NS, 0:512], func=AF.Copy), r=[ps.tok], w=[Pf.tok])
        ps = tm_sample(wB, 0, 512)
        P.a(lambda e: e.activation(out=pr[0:NS, 512:1024], in_=ps[0:NS, 0:512], func=AF.Copy), r=[ps.tok], w=[Pf.tok])
        to_hn(pr[0:NS, :], Pf.tok, 4, qk4[:, 0:4, :], qk4.tok)
        rope_s(qk4, qk4.tok, 2)
        q, k, v, g = (qk4[:, j, :] for j in range(4))
        P.v(lambda e: e.tensor_scalar(out=k, in0=k, scalar1=0.125, scalar2=None, op0=ALU.mult), r=[qk4.tok], w=[qk4.tok])
        P.a(lambda e: e.activation(out=osn[:, 3, :], in_=g, func=AF.Silu), r=[qk4.tok], w=[osn.tok])
        o = osn[:, 0, :]
        for j in range(8):
            S = F[j]
            P.dma("sync", S[0:64, :], d_sret[l][:, j * 512:(j + 1) * 512], w=[S.tok])
            tmpk = F[8 + j % 2]
            kb = AP(qk4.t, 1 * 64 + j * 8, [[320, 64], [1, 8], [0, 64]])
            vb = AP(qk4.t, 2 * 64, [[320, 64], [0, 8], [1, 64]])
            qb = AP(qk4.t, 0 * 64 + j * 8, [[320, 64], [0, 64], [1, 8]])
            P.v(lambda e, tmpk=tmpk, kb=kb, vb=vb: e.tensor_tensor(out=tmpk[0:64, :].rearrange("p (d x) -> p d x", d=8), in0=kb, in1=vb, op=ALU.mult),
                r=[qk4.tok], w=[tmpk.tok])
            P.v(lambda e, S=S, tmpk=tmpk: e.scalar_tensor_tensor(out=S[0:64, :], in0=S[0:64, :], scalar=gam[:, 0:1], in1=tmpk[0:64, :], op0=ALU.mult, op1=ALU.add),
                r=[S.tok, tmpk.tok, gam.tok], w=[S.tok])
            P.dma("sync", o_sret[l][:, j * 512:(j + 1) * 512], S[0:64, :], r=[S.tok], is_output=True)
            Sv = AP(S.t, 0, [[TB, 64], [1, 64], [64, 8]])
            P.v(lambda e, tmpk=tmpk, Sv=Sv, qb=qb: e.tensor_tensor(out=tmpk[0:64, :].rearrange("p (x d) -> p x d", d=8), in0=Sv, in1=qb, op=ALU.mult),
                r=[S.tok, qk4.tok], w=[tmpk.tok])
            if j == 0:
                P.v(lambda e, tmpk=tmpk: e.tensor_reduce(out=o, in_=tmpk[0:64, :].rearrange("p (x d) -> p x d", d=8), axis=AX.X, op=ALU.add),
                    r=[tmpk.tok], w=[osn.tok])
            else:
                P.v(lambda e, tmpk=tmpk: e.tensor_reduce(out=osn[:, 2, :], in_=tmpk[0:64, :].rearrange("p (x d) -> p x d", d=8), axis=AX.X, op=ALU.add),
                    r=[tmpk.tok], w=[osn.tok])
                P.v(lambda e: e.tensor_tensor(out=o, in0=o, in1=osn[:, 2, :], op=ALU.add), r=[osn.tok], w=[osn.tok])
        y = headnorm_s(o, osn[:, 3, :], gns[:, (l * 2 + 0) * 64:(l * 2 + 1) * 64])
        place_mix(y)
        outproj_s(wo)


    d_smlc = din("smlc", [DEPTH, 64, 4096])
    d_smln = din("smln", [DEPTH, 64, 64])
    d_smlm = din("smlm", [DEPTH, 64, 1])
    d_bifs = din("bifs", [64, DEPTH * 2])
    o_smlc = dout("o_smlc", [DEPTH, 64, 4096])
    o_smln = dout("o_smln", [DEPTH, 64, 64])
    o_smlm = dout("o_smlm", [DEPTH, 64, 1])
    bifs = sb("bifs", [64, DEPTH * 2])
    P.dma("sync", bifs[:], d_bifs, w=[bifs.tok])
    n0s = sb("n0s", [64, 2, 64])

    def ml_sample(l, wC, wD, wE, wo):
        pr = Pf.t[:].rearrange("p a b -> p (a b)")
        pr2 = P2.t[:].rearrange("p a b -> p (a b)").bitcast(F32)
        ps = tm_sample(wC, 0, 512)
        P.a(lambda e: e.activation(out=pr[0:NS, 0:512], in_=ps[0:NS, 0:512], func=AF.Copy), r=[ps.tok], w=[Pf.tok])
        ps = tm_sample(wD, 0, 512)
        P.a(lambda e: e.activation(out=pr[0:NS, 512:1024], in_=ps[0:NS, 0:512], func=AF.Copy), r=[ps.tok], w=[Pf.tok])
        ps = tm_sample(wE, 0, 264)
        P.a(lambda e: e.activation(out=pr2[0:NS, 0:264], in_=ps[0:NS, 0:264], func=AF.Copy), r=[ps.tok], w=[P2.tok])
        to_hn(pr[0:NS, :], Pf.tok, 4, qk4[:, 0:4, :], qk4.tok)
        to_hn(pr2[0:NS, 0:256], P2.tok, 1, qk4[:, 4:5, :], qk4.tok)
        psg = psum()
        for h in range(4):
            rhs = AP(P2.t, 0, [[1024, NS], [8, 2]]).bitcast(F32) if False else AP(pr2.tensor, pr2.offset + 256 + h, [list(pr2.ap[0])[0:1] + [NS], [4, 2]])
            P.t(lambda e, h=h, rhs=rhs: e.matmul(psg[0:64, 0:2], lhsT=cst["sel"][:, h * 64:(h + 1) * 64], rhs=rhs, start=(h == 0), stop=(h == 3)),
                r=[cst["sel"].tok, P2.tok], w=[psg.tok])
        q, k, v, og, g = (qk4[:, j, :] for j in range(5))
        ig, fz, lf, a_, mt_, wi, ws, emt, qk_, nq, den, scol, rc = (sm[:, j:j + 1] for j in range(2, 15))
        P.v(lambda e: e.tensor_tensor(out=sm[:, 2:4], in0=psg[0:64, 0:2], in1=bifs[:, l * 2:l * 2 + 2], op=ALU.add), r=[psg.tok, bifs.tok], w=[sm.tok])
        P.dma("sync", n0s[:, 0, :], d_smln[l], w=[n0s.tok])
        P.dma("sync", sm[:, 15:16], d_smlm[l], w=[sm.tok])
        m0 = sm[:, 15:16]
        P.a(lambda e: e.activation(out=lf, in_=fz, func=AF.Exp, scale=-1.0), r=[sm.tok], w=[sm.tok])
        P.a(lambda e: e.activation(out=lf, in_=lf, func=AF.Ln, bias=1.0), r=[sm.tok], w=[sm.tok])
        P.v(lambda e: e.tensor_tensor(out=a_, in0=m0, in1=lf, op=ALU.subtract), r=[sm.tok], w=[sm.tok])
        P.v(lambda e: e.tensor_tensor(out=mt_, in0=a_, in1=ig, op=ALU.max), r=[sm.tok], w=[sm.tok])
        P.v(lambda e: e.tensor_tensor(out=wi, in0=ig, in1=mt_, op=ALU.subtract), r=[sm.tok], w=[sm.tok])
        P.a(lambda e: e.activation(out=wi, in_=wi, func=AF.Exp), r=[sm.tok], w=[sm.tok])
        P.v(lambda e: e.tensor_tensor(out=ws, in0=a_, in1=mt_, op=ALU.subtract), r=[sm.tok], w=[sm.tok])
        P.a(lambda e: e.activation(out=ws, in_=ws, func=AF.Exp), r=[sm.tok], w=[sm.tok])
        P.a(lambda e: e.activation(out=emt, in_=mt_, func=AF.Exp, scale=-1.0), r=[sm.tok], w=[sm.tok])
        P.dma("sync", o_smlm[l], mt_, r=[sm.tok], is_output=True)
        P.v(lambda e: e.tensor_scalar(out=k, in0=k, scalar1=0.125, scalar2=None, op0=ALU.mult), r=[qk4.tok], w=[qk4.tok])
        P.v(lambda e: e.tensor_tensor(out=osn[:, 2, :], in0=q, in1=k, op=ALU.mult), r=[qk4.tok], w=[osn.tok])
        P.v(lambda e: e.tensor_reduce(out=qk_, in_=osn[:, 2, :], axis=AX.X, op=ALU.add), r=[osn.tok], w=[sm.tok])
        P.v(lambda e: e.tensor_tensor(out=osn[:, 2, :], in0=q, in1=n0s[:, 0, :], op=ALU.mult), r=[qk4.tok, n0s.tok], w=[osn.tok])
        P.v(lambda e: e.tensor_reduce(out=nq, in_=osn[:, 2, :], axis=AX.X, op=ALU.add), r=[osn.tok], w=[sm.tok])
        P.v(lambda e: e.tensor_tensor(out=scol, in0=qk_, in1=wi, op=ALU.mult), r=[sm.tok], w=[sm.tok])
        P.v(lambda e: e.tensor_tensor(out=den, in0=nq, in1=ws, op=ALU.mult), r=[sm.tok], w=[sm.tok])
        P.v(lambda e: e.tensor_tensor(out=den, in0=den, in1=scol, op=ALU.add), r=[sm.tok], w=[sm.tok])
        P.a(lambda e: e.activation(out=den, in_=den, func=AF.Abs), r=[sm.tok], w=[sm.tok])
        P.v(lambda e: e.tensor_tensor(out=den, in0=den, in1=emt, op=ALU.max), r=[sm.tok], w=[sm.tok])
        P.v(lambda e: e.reciprocal(out=rc, in_=den), r=[sm.tok], w=[sm.tok])
        P.v(lambda e: e.tensor_scalar(out=n0s[:, 1, :], in0=k, scalar1=wi, scalar2=None, op0=ALU.mult), r=[qk4.tok, sm.tok], w=[n0s.tok])
        P.v(lambda e: e.scalar_tensor_tensor(out=n0s[:, 1, :], in0=n0s[:, 0, :], scalar=ws, in1=n0s[:, 1, :], op0=ALU.mult, op1=ALU.add),
            r=[n0s.tok, sm.tok], w=[n0s.tok])
        P.dma("sync", o_smln[l], n0s[:, 1, :], r=[n0s.tok], is_output=True)
        vp = osn[:, 3, :]
        P.v(lambda e: e.tensor_scalar(out=vp, in0=v, scalar1=wi, scalar2=None, op0=ALU.mult), r=[qk4.tok, sm.tok], w=[osn.tok])
        Cq = osn[:, 0, :]
        for j in range(8):
            C = F[j]
            P.dma("sync", C[0:64, :], d_smlc[l][:, j * 512:(j + 1) * 512], w=[C.tok])
            tmpk = F[8 + j % 2]
            qb = AP(qk4.t, 0, [[320, 64], [0, 8], [1, 64]])
            P.v(lambda e, tmpk=tmpk, C=C, qb=qb: e.tensor_tensor(out=tmpk[0:64, :].rearrange("p (x d) -> p x d", x=8), in0=C[0:64, :].rearrange("p (x d) -> p x d", x=8),
                                                                 in1=qb, op=ALU.mult), r=[C.tok, qk4.tok], w=[tmpk.tok])
            P.v(lambda e, tmpk=tmpk, j=j: e.tensor_reduce(out=Cq[:, j * 8:(j + 1) * 8], in_=tmpk[0:64, :].rearrange("p (x d) -> p x d", x=8), axis=AX.X, op=ALU.add),
                r=[tmpk.tok], w=[osn.tok])
            vb = AP(osn.t, 3 * 64 + j * 8, [[256, 64], [1, 8], [0, 64]])
            kb = AP(qk4.t, 1 * 64, [[320, 64], [0, 8], [1, 64]])
            P.v(lambda e, tmpk=tmpk, vb=vb, kb=kb: e.tensor_tensor(out=tmpk[0:64, :].rearrange("p (x d) -> p x d", x=8), in0=vb, in1=kb, op=ALU.mult),
                r=[osn.tok, qk4.tok], w=[tmpk.tok])
            P.v(lambda e, C=C, tmpk=tmpk: e.scalar_tensor_tensor(out=C[0:64, :], in0=C[0:64, :], scalar=ws, in1=tmpk[0:64, :], op0=ALU.mult, op1=ALU.add),
                r=[C.tok, tmpk.tok, sm.tok], w=[C.tok])
            P.dma("sync", o_smlc[l][:, j * 512:(j + 1) * 512], C[0:64, :], r=[C.tok], is_output=True)
        P.v(lambda e: e.tensor_scalar(out=osn[:, 2, :], in0=v, scalar1=scol, scalar2=None, op0=ALU.mult), r=[qk4.tok, sm.tok], w=[osn.tok])
        P.v(lambda e: e.scalar_tensor_tensor(out=Cq, in0=Cq, scalar=ws, in1=osn[:, 2, :], op0=ALU.mult, op1=ALU.add), r=[osn.tok, sm.tok], w=[osn.tok])
        P.a(lambda e: e.activation(out=osn[:, 2, :], in_=og, func=AF.Tanh, scale=0.5), r=[qk4.tok], w=[osn.tok])
        P.v(lambda e: e.tensor_scalar(out=osn[:, 2, :], in0=osn[:, 2, :], scalar1=0.5, scalar2=0.5, op0=ALU.mult, op1=ALU.add), r=[osn.tok], w=[osn.tok])
        P.v(lambda e: e.scalar_tensor_tensor(out=Cq, in0=Cq, scalar=rc, in1=osn[:, 2, :], op0=ALU.mult, op1=ALU.mult), r=[osn.tok, sm.tok], w=[osn.tok])
        P.a(lambda e: e.activation(out=osn[:, 3, :], in_=g, func=AF.Silu), r=[qk4.tok, osn.tok], w=[osn.tok])
        y = headnorm_s(Cq, osn[:, 3, :], gns[:, (l * 2 + 1) * 64:(l * 2 + 2) * 64])
        place_mix(y)
        outproj_s(wo)

    d_ck = din("ck", [DEPTH, NS, 256, 256])
    d_cv = din("cv", [DEPTH, NS, 256, 256])
    gsx = sb("gsx", [128, 2, NS])
    ones32 = sb("ones32", [128, 128])
    P.v(lambda e: e.memset(ones32[:], 1.0), w=[ones32.tok])

    def xa_sample(l, wG, wo):
        pr = Pf.t[:].rearrange("p a b -> p (a b)")
        ps = tm_sample(wG, 0, 256)
        P.a(lambda e: e.activation(out=pr[0:NS, 0:256], in_=ps[0:NS, 0:256], func=AF.Copy), r=[ps.tok], w=[Pf.tok])
        for i in range(2):
            ps = fm_sample(wG, 256 + i * 128)
            P.a(lambda e, ps=ps, i=i: e.activation(out=gsx[:, i, :], in_=ps[:, 0:NS], func=AF.Silu), r=[ps.tok], w=[gsx.tok])
        scT = F[10]
        for n in range(NS):
            qm = F[9]
            Kn = F[n % 4]
            P.dma("sync", Kn[:].rearrange("p (mt c) -> p mt c", mt=2), d_ck[l, n].rearrange("(mt p) c -> p mt c", p=128), w=[Kn.tok])
            P.v(lambda e, n=n: e.tensor_scalar(out=qm[0:NS, 0:256], in0=pr[0:NS, 0:256], scalar1=cst["ident"][0:NS, n:n + 1], scalar2=None, op0=ALU.mult),
                r=[Pf.tok, cst["ident"].tok], w=[qm.tok])
            psq = psum()
            P.t(lambda e, psq=psq: e.matmul(psq[:, 0:256], lhsT=ones32[0:NS, :], rhs=qm[0:NS, 0:256], start=True, stop=True), r=[ones32.tok, qm.tok], w=[psq.tok])
            tmpk = F[8]
            qrb = AP(psq.t, 0, [[512, 128], [0, 2], [1, 256]])
            P.v(lambda e, Kn=Kn, qrb=qrb: e.tensor_tensor(out=tmpk[:].rearrange("p (mt c) -> p mt c", mt=2), in0=Kn[:].rearrange("p (mt c) -> p mt c", mt=2), in1=qrb, op=ALU.mult),
                r=[Kn.tok, psq.tok], w=[tmpk.tok])
            dst = AP(scT.t, n * 4, [[TB, 128], [64, 2], [1, 4]])
            P.v(lambda e, dst=dst: e.tensor_reduce(out=dst, in_=tmpk[:].rearrange("p (mt h d) -> p mt h d", mt=2, h=4), axis=AX.X, op=ALU.add),
                r=[tmpk.tok], w=[scT.tok])
        pss = psum()
        for mt in range(2):
            P.t(lambda e, mt=mt: e.transpose(pss[0:64, mt * 128:(mt + 1) * 128], scT[:, mt * 64:(mt + 1) * 64], cst["ident"][:]),
                r=[scT.tok, cst["ident"].tok], w=[pss.tok])
        mx, sme, nb = sm[:, 0:1], sm[:, 1:2], sm[:, 2:3]
        Pm = F[9]
        P.v(lambda e: e.tensor_reduce(out=mx, in_=pss[0:64, 0:256], axis=AX.X, op=ALU.max), r=[pss.tok], w=[sm.tok])
        P.v(lambda e: e.tensor_scalar(out=nb, in0=mx, scalar1=-0.125, scalar2=None, op0=ALU.mult), r=[sm.tok], w=[sm.tok])
        P.a(lambda e: e.activation(out=Pm[0:64, 0:256], in_=pss[0:64, 0:256], func=AF.Exp, scale=0.125, bias=nb), r=[pss.tok, sm.tok], w=[Pm.tok])
        P.v(lambda e: e.tensor_reduce(out=sme, in_=Pm[0:64, 0:256], axis=AX.X, op=ALU.add), r=[Pm.tok], w=[sm.tok])
        P.v(lambda e: e.reciprocal(out=sme, in_=sme), r=[sm.tok], w=[sm.tok])
        P.v(lambda e: e.tensor_scalar(out=Pm[0:64, 0:256], in0=Pm[0:64, 0:256], scalar1=sme, scalar2=None, op0=ALU.mult), r=[Pm.tok, sm.tok], w=[Pm.tok])
        pst = psum()
        for mt in range(2):
            P.t(lambda e, mt=mt: e.transpose(pst[:, mt * 64:(mt + 1) * 64], Pm[0:64, mt * 128:(mt + 1) * 128], cst["ident"][0:64, 0:64]),
                r=[Pm.tok, cst["ident"].tok], w=[pst.tok])
        PTs = F[10]
        P.a(lambda e: e.activation(out=PTs[:, 128:256], in_=pst[:, 0:128], func=AF.Copy), r=[pst.tok], w=[PTs.tok])
        psx = psum()
        for n in range(NS):
            Vn = F[4 + n % 4]
            P.dma("sync", Vn[:].rearrange("p (mt c) -> p mt c", mt=2), d_cv[l, n].rearrange("(mt p) c -> p mt c", p=128), w=[Vn.tok])
            tmpk = F[8]
            pb = AP(PTs.t, 128 + n * 4, [[TB, 128], [64, 2], [1, 4], [0, 64]])
            P.v(lambda e, Vn=Vn, pb=pb: e.tensor_tensor(out=tmpk[:].rearrange("p (mt h d) -> p mt h d", mt=2, h=4), in0=Vn[:].rearrange("p (mt h d) -> p mt h d", mt=2, h=4),
                                                        in1=pb, op=ALU.mult), r=[Vn.tok, PTs.tok], w=[tmpk.tok])
            for i in range(2):
                for mt in range(2):
                    P.t(lambda e, i=i, mt=mt, n=n: e.matmul(psx[:, i * NS + n:i * NS + n + 1], lhsT=tmpk[:, mt * 256 + i * 128:mt * 256 + (i + 1) * 128], rhs=ones32[:, 0:1],
                                                            start=(mt == 0), stop=(mt == 1)), r=[tmpk.tok, ones32.tok], w=[psx.tok])
        P.v(lambda e: e.tensor_tensor(out=mixs[:].rearrange("p a n -> p (a n)"), in0=psx[:, 0:2 * NS], in1=gsx[:].rearrange("p a n -> p (a n)"), op=ALU.mult),
            r=[psx.tok, gsx.tok], w=[mixs.tok])
        outproj_s(wo)

    for l in range(DEPTH):
        for blk in range(NBLK):
            t0 = blk * TB
            rmsnorm(lambda kt: xT[:, kt, t0:t0 + TB], [xtok[blk]], TB, lambda kt: hnT[:, kt, t0:t0 + TB], lambda kt: [htok[blk]],
                    lambda kt: normw[:, l * 8 + kt:l * 8 + kt + 1])
        rmsnorm(lambda kt: xsT[:, kt, :], [xsT.tok], NS, lambda kt: hnsT[:, kt, :], lambda kt: [hnsT.tok],
                lambda kt: normw[:, l * 8 + kt:l * 8 + kt + 1])
        if STAGE >= 2:
            mem_kv(l)
        if STAGE >= 3:
            ret_phase(l)
        if STAGE >= 4:
            ml_phase(l)
        if STAGE >= 5:
            xa_phase(l)
        if STAGE >= 6:
            s5_phase(l)
    for blk in range(NBLK):
        t0 = blk * TB
        for kt in range(8):
            pass
        yo = [F[0], F[1], F[2], F[3], F[4], F[5], F[6], F[7]]
        rmsnorm(lambda kt: xT[:, kt, t0:t0 + TB], [xtok[blk]], TB, lambda kt: yo[kt][:], lambda kt: [yo[kt].tok], lambda kt: fnw[:, kt:kt + 1])
        for kt in range(8):
            P.dma("sync" if kt % 2 == 0 else "gpsimd", yTv[:, kt, t0:t0 + TB], yo[kt][:], r=[yo[kt].tok], is_output=True)

    yos = F[8]
    rmsnorm(lambda kt: xsT[:, kt, :], [xsT.tok], NS, lambda kt: yos[:, kt * NS:(kt + 1) * NS], lambda kt: [yos.tok], lambda kt: fnw[:, kt:kt + 1])
    P.dma("sync", o_ysT.rearrange("(k p) n -> p k n", p=128), yos[:, 0:8 * NS].rearrange("p (k n) -> p k n", k=8), r=[yos.tok], is_output=True)

    P.emit()
    es.close()
    return nc


_NC = None


def make_in_maps(inp, cores=range(NCORES)):
    cs = _consts()
    f = {k: np.asarray(v) for k, v in inp.items()}
    in_maps = []
    for b in cores:
        m = {}
        m["xT"] = _c(f["x_prompt"][b].T)
        m["memT"] = _c(f["mem_prompt"][b].T)
        m["w_in"] = _c(f["w_in"])
        m["w_out"] = _c(f["w_out"])
        m["w_mem_k"] = _c(f["w_mem_k"])
        m["w_mem_v"] = _c(f["w_mem_v"])
        m["normw"] = _c(f["norm_w"].reshape(DEPTH, 8, 128).transpose(2, 0, 1).reshape(128, DEPTH * 8))
        m["fnw"] = _c(f["final_norm_w"].reshape(8, 128).T)
        gnc = np.zeros((128, DEPTH, 2, 2), np.float32)
        for l in range(DEPTH):
            gnc[:, l, 0, :] = f["ret_gn"][l].reshape(2, 128).T
            gnc[:, l, 1, :] = f["ml_gn"][l].reshape(2, 128).T
        m["gn"] = _c(gnc.reshape(128, DEPTH * 4))
        m["bif"] = _c(np.concatenate([f["ml_b_i"].T, f["ml_b_f"].T], axis=0))
        def pairlay(a):
            a = np.asarray(a)
            L = a.shape[0]
            rest = a.shape[3:]
            a = a.reshape((L, 8, 2, 64) + rest)
            a = np.moveaxis(a, (2, 3), (0, 1))
            return a.reshape((128, L, 8) + rest)
        are = pairlay(f["s5_a_re"])
        aim = pairlay(f["s5_a_im"])
        ldt = pairlay(np.repeat(f["s5_log_dt"][:, :, None], 64, axis=2))
        m["s5par"] = _c(np.stack([are, aim, ldt], axis=2).reshape(128, -1))
        bre = pairlay(f["s5_b_re"])
        bim = pairlay(f["s5_b_im"])
        m["s5b"] = _c(np.stack([bre, bim], axis=2).transpose(1, 0, 2, 3, 4).reshape(DEPTH, 128, -1))
        cre = pairlay(np.swapaxes(f["s5_c_re"], 2, 3))
        cim = pairlay(np.swapaxes(f["s5_c_im"], 2, 3))
        m["s5c"] = _c(np.stack([cre, cim], axis=2).transpose(1, 0, 2, 3, 4).reshape(DEPTH, 128, -1))
        m["s5d"] = _c(f["s5_d"].reshape(DEPTH, 2, 128).transpose(2, 0, 1).reshape(128, -1))
        m["w_glu"] = _c(f["s5_w_glu"])
        ns = slice(NS * b, NS * (b + 1))
        m["xsT"] = _c(f["x_sample"][ns, 0, :].T)
        m["sret"] = _c(f["state_ret"][:, ns].transpose(0, 2, 1, 3, 4).reshape(DEPTH, 64, 4096))
        gs = np.zeros((4, NS, DEPTH, 2, 64), np.float32)
        for l in range(DEPTH):
            gs[:, :, l, 0, :] = f["ret_gn"][l].reshape(4, 1, 64)
            gs[:, :, l, 1, :] = f["ml_gn"][l].reshape(4, 1, 64)
        m["gns"] = _c(gs.reshape(64, -1))
        m["smlc"] = _c(f["state_mlstm_c"][:, ns].transpose(0, 2, 1, 3, 4).reshape(DEPTH, 64, 4096))
        m["smln"] = _c(f["state_mlstm_n"][:, ns].transpose(0, 2, 1, 3).reshape(DEPTH, 64, 64))
        m["smlm"] = _c(f["state_mlstm_m"][:, ns].transpose(0, 2, 1).reshape(DEPTH, 64, 1))
        bs_ = np.zeros((4, NS, DEPTH, 2), np.float32)
        for l in range(DEPTH):
            bs_[:, :, l, 0] = f["ml_b_i"][l].reshape(4, 1)
            bs_[:, :, l, 1] = f["ml_b_f"][l].reshape(4, 1)
        m["bifs"] = _c(bs_.reshape(64, -1))
        def spair(a):
            a = np.asarray(a)[:, ns]
            a = a.reshape(DEPTH, NS, 8, 2, 64).transpose(0, 3, 4, 2, 1)
            return a.reshape(DEPTH, 128, 8, NS)
        m["sx0"] = _c(np.stack([spair(f["state_s5_re"]), spair(f["state_s5_im"])], axis=2).reshape(DEPTH, 128, -1))
        m["ck"] = _c(f["cache_mem_k"][:, ns].reshape(DEPTH, NS, 256, 256))
        m["cv"] = _c(f["cache_mem_v"][:, ns].reshape(DEPTH, NS, 256, 256))
        for k, v in cs.items():
            m["c_" + k] = _c(v)
        in_maps.append(m)
    return in_maps


def kernel(**inp):
    global _NC
    if _NC is None:
        _NC = build()
    nc = _NC
    in_maps = make_in_maps(inp)
    res = run_bass_kernel_spmd(nc, in_maps, core_ids=list(range(NCORES)))
    return assemble(res.results)


def assemble(R):
    y_prompt = np.stack([R[b]["o_yT"].T for b in range(NCORES)]).astype(np.float32)
    memkv = np.stack([R[b]["o_memkv"] for b in range(NCORES)], axis=1)
    memk = np.ascontiguousarray(memkv[..., 0:256]).reshape(DEPTH, NCORES, 256, 4, 64)
    memv = np.ascontiguousarray(memkv[..., 256:512]).reshape(DEPTH, NCORES, 256, 4, 64)
    ret_p = np.stack([R[b]["o_ret"] for b in range(NCORES)], axis=1)
    mlc_p = np.stack([R[b]["o_mlc"].transpose(0, 1, 3, 2) for b in range(NCORES)], axis=1)
    mln_p = np.stack([R[b]["o_mln"] for b in range(NCORES)], axis=1)
    mlm_p = np.stack([R[b]["o_mlm"] for b in range(NCORES)], axis=1)
    s5 = np.stack([R[b]["o_s5"] for b in range(NCORES)], axis=1)
    s5 = s5.reshape(DEPTH, NCORES, 2, 64, 2, 8).transpose(0, 1, 4, 5, 2, 3).reshape(DEPTH, NCORES, 2, 16, 64)
    s5re_p = np.ascontiguousarray(s5[:, :, 0])
    s5im_p = np.ascontiguousarray(s5[:, :, 1])
    y_sample = np.concatenate([R[b]["o_ysT"].T for b in range(NCORES)], axis=0).reshape(NCORES * NS, 1, 1024)
    ret_s = np.concatenate([R[b]["o_sret"].reshape(DEPTH, 4, NS, 64, 64).transpose(0, 2, 1, 3, 4) for b in range(NCORES)], axis=1)
    mlc_s = np.concatenate([R[b]["o_smlc"].reshape(DEPTH, 4, NS, 64, 64).transpose(0, 2, 1, 3, 4) for b in range(NCORES)], axis=1)
    mln_s = np.concatenate([R[b]["o_smln"].reshape(DEPTH, 4, NS, 64).transpose(0, 2, 1, 3) for b in range(NCORES)], axis=1)
    mlm_s = np.concatenate([R[b]["o_smlm"].reshape(DEPTH, 4, NS).transpose(0, 2, 1) for b in range(NCORES)], axis=1)
    def unsp(b):
        a = R[b]["o_s5s"].reshape(DEPTH, 2, 64, 2, 8, NS).transpose(3, 0, 5, 4, 1, 2)
        return a.reshape(2, DEPTH, NS, 16, 64)
    s5s = np.concatenate([unsp(b) for b in range(NCORES)], axis=2)
    z = lambda *s: np.zeros(s, np.float32)
    outs = (y_prompt, y_sample, ret_p, ret_s, np.ascontiguousarray(mlc_p), mlc_s,
            mln_p, mln_s, mlm_p, mlm_s, s5re_p, s5s[0],
            s5im_p, s5s[1], memk, memv)
    return tuple(np.ascontiguousarray(o, dtype=np.float32) for o in outs)
```

```python
import contextlib
import numpy as np
import concourse.bass as bass
import concourse.mybir as mybir
from concourse.ap import AP
from concourse.bass_utils import run_bass_kernel_spmd

F32 = mybir.dt.float32
BF16 = mybir.dt.bfloat16
I32 = mybir.dt.int32
F32R = mybir.dt.float32r
AF = mybir.ActivationFunctionType
ALU = mybir.AluOpType
AX = mybir.AxisListType

NCORES = 8
T = 2048
TB = 512
NBLK = T // TB
NS = 16
DEPTH = 2
DIN = 3336
EPS = 1e-6
PAST = 16384.0
TWO_PI = 6.283185307179586
C1 = 6.28125
C2 = TWO_PI - C1


class Tok:
    __slots__ = ("w", "r", "name", "excl")

    def __init__(self, name=""):
        self.w = None
        self.r = []
        self.name = name
        self.excl = False


class Op:
    __slots__ = ("eng", "fn", "deps", "idx", "signal", "sem", "semval", "dma", "cost", "pos", "tab")

    def __init__(self, eng, fn, deps, idx, dma):
        self.cost = 100.0
        self.pos = idx
        self.tab = None
        self.eng = eng
        self.fn = fn
        self.deps = deps
        self.idx = idx
        self.signal = False
        self.sem = None
        self.semval = 0
        self.dma = dma


ENGS = ("sync", "scalar", "vector", "gpsimd", "tensor")
DMA_POOL = 20
WCHUNKS = [(0, 512), (512, 1024), (1024, 1536), (1536, 2048), (2048, 2312), (2824, 3336)]


class _Rec:
    def __init__(self):
        self.call = None

    def __getattr__(self, name):
        def f(*a, **kw):
            self.call = (name, a, kw)
            return self
        return f


def _est_cost(eng, name, a_, kw_, dma):
    try:
        out = kw_.get("out", None)
        if out is None:
            out = a_[0]
        shp = out.shape
        n = 1
        for d in shp[1:]:
            n *= int(d)
    except Exception:
        n = 128
    if dma:
        return 2000.0 + n * int(shp[0]) * 4 / 80.0
    if eng == "tensor":
        f32 = False
        try:
            f32 = (kw_.get("lhsT", None) is not None and kw_["lhsT"].dtype == F32)
        except Exception:
            pass
        return 60.0 + n * (4.0 if f32 else 1.0) / 1.7
    if eng == "vector":
        return 100.0 + n * (2.0 if name == "tensor_tensor_scan" else 1.0) / 0.78
    if eng == "scalar":
        return 230.0 + n / 1.2
    if eng == "gpsimd":
        return 150.0 + n / 0.5
    return 50.0


SCHED = True
_TABSET = {AF.Exp: "e", AF.Ln: "e", AF.Silu: "s", AF.Sin: "s", AF.Sigmoid: "g", AF.Tanh: "s"}
TAB_PEN = 0.0


def _list_schedule(ops):
    n = len(ops)
    succ = [[] for _ in range(n)]
    indeg = [0] * n
    for o in ops:
        for d in o.deps:
            succ[d].append(o.idx)
            indeg[o.idx] += 1
    bl = [0.0] * n
    for i in range(n - 1, -1, -1):
        m = 0.0
        for s_ in succ[i]:
            if bl[s_] > m:
                m = bl[s_]
        bl[i] = m + ops[i].cost + 100.0
    efree = {e: 0.0 for e in ENGS}
    fin = [0.0] * n
    rdy_t = [0.0] * n
    ready = [i for i in range(n) if indeg[i] == 0]
    order = []
    cur_tab = [None]
    WINDOW = 6000
    next_unsched = 0
    done = [False] * n
    while ready:
        best = None
        bk = None
        lim = next_unsched + WINDOW
        for i in ready:
            if i > lim:
                continue
            o = ops[i]
            st = efree[o.eng]
            if rdy_t[i] > st:
                st = rdy_t[i]
            if o.tab is not None and o.tab != cur_tab[0]:
                st = st + TAB_PEN
            key = (int(st / 200.0), -bl[i], i)
            if bk is None or key < bk:
                bk = key
                best = i
        if best is None:
            best = min(ready)
            o = ops[best]
            bk = (max(efree[o.eng], rdy_t[best]),)
        ready.remove(best)
        o = ops[best]
        st = max(efree[o.eng], rdy_t[best])
        if o.tab is not None:
            cur_tab[0] = o.tab
        if o.dma:
            efree[o.eng] = st + 60.0
            fin[best] = st + o.cost
        else:
            efree[o.eng] = st + o.cost
            fin[best] = st + o.cost
        order.append(best)
        done[best] = True
        while next_unsched < n and done[next_unsched]:
            next_unsched += 1
        for s_ in succ[best]:
            t = fin[best] + (110.0 if ops[s_].eng == o.eng else 220.0)
            if t > rdy_t[s_]:
                rdy_t[s_] = t
            indeg[s_] -= 1
            if indeg[s_] == 0:
                ready.append(s_)
    assert len(order) == n
    return order, max(fin)


class Prog:
    def __init__(self, nc):
        self.nc = nc
        self.ops = []
        self.out_ops = []

    @staticmethod
    def _flat(ts):
        out = []
        for t in ts:
            if isinstance(t, (list, tuple)):
                out.extend(Prog._flat(t))
            else:
                out.append(t)
        return out

    def op(self, eng, fn, r=(), w=(), dma=False):
        r = Prog._flat(r)
        w = Prog._flat(w)
        idx = len(self.ops)
        deps = set()
        for t in r:
            if t.w is not None:
                deps.add(t.w)
            if t.excl:
                for ri_ in t.r:
                    if self.ops[ri_].eng != eng:
                        deps.add(ri_)
        for t in w:
            if t.w is not None:
                deps.add(t.w)
            deps.update(t.r)
        rec = _Rec()
        fn(rec)
        name, a_, kw_ = rec.call
        o = Op(eng, (lambda e, name=name, a_=a_, kw_=kw_: getattr(e, name)(*a_, **kw_)), deps, idx, dma)
        o.cost = _est_cost(eng, name, a_, kw_, dma)
        if eng == "scalar" and name == "activation":
            o.tab = _TABSET.get(kw_.get("func", None), None)
        self.ops.append(o)
        for t in r:
            t.r.append(idx)
        for t in w:
            t.w = idx
            t.r = []
        return idx

    def v(self, fn, r=(), w=()):
        return self.op("vector", fn, r, w)

    def a(self, fn, r=(), w=()):
        return self.op("scalar", fn, r, w)

    def g(self, fn, r=(), w=()):
        return self.op("gpsimd", fn, r, w)

    def t(self, fn, r=(), w=()):
        return self.op("tensor", fn, r, w)

    def dma(self, q, out, in_, r=(), w=(), is_output=False):
        idx = self.op(q, lambda e, out=out, in_=in_: e.dma_start(out=out, in_=in_), r, w, dma=True)
        if is_output:
            self.out_ops.append(idx)
        return idx

    def emit(self):
        nc = self.nc
        ops = self.ops
        fin = Op("sync", lambda e: e.nop(), set(self.out_ops), len(ops), False)
        ops.append(fin)
        byidx = ops
        if SCHED:
            order, mk = _list_schedule(ops)
            self.est_makespan = mk
            ops = [byidx[i] for i in order]
            for p_, o in enumerate(ops):
                o.pos = p_
        with contextlib.ExitStack() as es:
            esem = {e: es.enter_context(nc.semaphore("s_" + e)) for e in ENGS}
            pools = {q: [es.enter_context(nc.semaphore("d_%s_%d" % (q, i))) for i in range(DMA_POOL)]
                     for q in ("sync", "gpsimd")}
            dcount = {"sync": 0, "gpsimd": 0}
            last_use = {}
            for o in ops:
                if o.dma:
                    i = dcount[o.eng]
                    dcount[o.eng] += 1
                    sem = pools[o.eng][i % DMA_POOL]
                    u = i // DMA_POOL
                    o.sem = sem
                    o.semval = 16 * (u + 1)
                    o.signal = True
                    key = (o.eng, i % DMA_POOL)
                    if key in last_use:
                        o.deps.add(last_use[key])
                    last_use[key] = o.idx
            for o in ops:
                for d in o.deps:
                    p = byidx[d]
                    if p.eng == "tensor" and o.eng == "tensor" and not p.dma:
                        continue
                    p.signal = True
            cnt = {e: 0 for e in ENGS}
            for o in ops:
                if not o.dma and o.signal:
                    cnt[o.eng] += 1
                    o.sem = esem[o.eng]
                    o.semval = cnt[o.eng]
            per = {e: [o for o in ops if o.eng == e] for e in ENGS}

            def run(eng, lst):
                waited = {}
                for o in lst:
                    need = {}
                    for d in o.deps:
                        p = byidx[d]
                        if p.eng == "tensor" and o.eng == "tensor" and not p.dma:
                            continue
                        k = id(p.sem)
                        if k not in need or need[k][1] < p.semval:
                            need[k] = (p.sem, p.semval)
                    for k, (sem, val) in need.items():
                        if waited.get(k, 0) >= val:
                            continue
                        eng.wait_ge(sem, val)
                        waited[k] = val
                    ins = o.fn(eng)
                    if o.signal:
                        ins.then_inc(o.sem, 16 if o.dma else 1)

            with nc.Block() as block:
                @block.sync
                def _(e):
                    run(e, per["sync"])

                @block.scalar
                def _(e):
                    run(e, per["scalar"])

                @block.vector
                def _(e):
                    run(e, per["vector"])

                @block.gpsimd
                def _(e):
                    run(e, per["gpsimd"])

                @block.tensor
                def _(e):
                    run(e, per["tensor"])


class Buf:
    def __init__(self, t, name=""):
        self.t = t
        self.tok = Tok(name)

    def __getitem__(self, k):
        return self.t[k]


def _c(a):
    return np.ascontiguousarray(a, dtype=np.float32)


def _consts():
    c = {}
    c["ident"] = np.eye(128, dtype=np.float32)
    c["maskT"] = np.triu(np.ones((128, 128), np.float32))
    bd = np.zeros((128, 128), np.float32)
    bd[:64, :64] = 1
    bd[64:, 64:] = 1
    c["bd2"] = bd
    pm = np.zeros((128, 128), np.float32)
    for h2 in range(2):
        for d in range(32):
            pm[h2 * 64 + d + 32, h2 * 64 + d] = -1.0
            pm[h2 * 64 + d, h2 * 64 + d + 32] = 1.0
    c["pm"] = pm
    inv = (10000.0 ** (-np.arange(32, dtype=np.float32) / 32)).astype(np.float32)
    c["invf"] = np.tile(inv, 4).reshape(128, 1).astype(np.float32)
    c["invrow"] = np.tile(inv.reshape(1, 32), (128, 1)).astype(np.float32)
    lg = np.log1p(-np.power(np.float32(2.0), -5.0 - np.arange(4, dtype=np.float32))).astype(np.float32)
    lgc = np.zeros((128, 2), np.float32)
    for i in range(2):
        for h2 in range(2):
            lgc[h2 * 64:(h2 + 1) * 64, i] = lg[2 * i + h2]
    c["lgcol"] = lgc
    c["l1"] = np.tile(np.arange(1, 129, dtype=np.float32).reshape(1, 128), (128, 1))
    ex = np.zeros((8, 4, 128), np.float32)
    for i in range(2):
        for h2 in range(2):
            ex[2 * i + h2, i * 2 + 0, h2 * 64:(h2 + 1) * 64] = 1.0
            ex[4 + 2 * i + h2, i * 2 + 1, h2 * 64:(h2 + 1) * 64] = 1.0
    c["gexp"] = ex.reshape(8, 512)
    sel = np.zeros((16, 4, 64), np.float32)
    for hh in range(4):
        for n in range(16):
            sel[n, hh, 16 * hh + n] = 1.0
    c["sel"] = sel.reshape(16, 256)
    sel2 = np.zeros((64, 16), np.float32)
    mpair = np.zeros((64, 2, 2), np.float32)
    lg64 = np.zeros((64, 1), np.float32)
    for hh in range(4):
        for n in range(16):
            sel2[16 * hh + n, n] = 1.0
            mpair[16 * hh + n, hh // 2, hh % 2] = 1.0
            lg64[16 * hh + n, 0] = lg[hh]
    c["sel2"] = sel2
    c["mpair"] = mpair.reshape(64, 4)
    c["lg64"] = lg64
    c["kgrid"] = np.tile(np.arange(9, dtype=np.float32).reshape(1, 9), (128, 1))
    return c


CONST_SHAPES = {"ident": [128, 128], "maskT": [128, 128], "bd2": [128, 128], "pm": [128, 128],
                "invf": [128, 1], "invrow": [128, 32], "lgcol": [128, 2], "l1": [128, 128],
                "gexp": [8, 512], "kgrid": [128, 9],
                "sel": [16, 256], "sel2": [64, 16], "mpair": [64, 4], "lg64": [64, 1]}


def build(stage=99):
    import os
    STAGE = int(os.environ.get('KSTAGE', '99'))
    KSUB = int(os.environ.get('KSUB', '99'))
    KMASK = int(os.environ.get('KMASK', '63'))
    SAMPLE = int(os.environ.get('KSAMPLE', '99'))
    nc = bass.Bass("TRN2", target_bir_lowering=False)
    P = Prog(nc)
    es = contextlib.ExitStack()

    def din(name, shape):
        return nc.dram_tensor(name, list(shape), F32, kind="ExternalInput").ap()

    def dout(name, shape):
        return nc.dram_tensor(name, list(shape), F32, kind="ExternalOutput").ap()

    def sb(name, shape, dt=F32):
        return Buf(es.enter_context(nc.sbuf_tensor("s_" + name, list(shape), dt)), name)

    d_xT = din("xT", [1024, T])
    d_memT = din("memT", [1024, 256])
    d_win = din("w_in", [DEPTH, 1024, DIN])
    d_wout = din("w_out", [DEPTH, 1024, 1024])
    d_wmk = din("w_mem_k", [DEPTH, 1024, 256])
    d_wmv = din("w_mem_v", [DEPTH, 1024, 256])
    d_normw = din("normw", [128, DEPTH * 8])
    d_fnw = din("fnw", [128, 8])
    d_gn = din("gn", [128, DEPTH * 4])
    d_bif = din("bif", [8, DEPTH])
    dc = {k: din("c_" + k, s) for k, s in CONST_SHAPES.items()}

    o_yT = dout("o_yT", [1024, T])
    o_memkv = dout("o_memkv", [DEPTH, 256, 512])
    o_ret = dout("o_ret", [DEPTH, 4, 64, 64])
    o_mlc = dout("o_mlc", [DEPTH, 4, 64, 64])
    o_mln = dout("o_mln", [DEPTH, 4, 64])
    o_mlm = dout("o_mlm", [DEPTH, 4])

    xTv = d_xT.rearrange("(k p) t -> p k t", p=128)
    yTv = o_yT.rearrange("(k p) t -> p k t", p=128)

    banks = [Buf(es.enter_context(nc.psum_tensor("ps%d" % i, [128, 512], F32)), "ps%d" % i) for i in range(8)]
    for b_ in banks:
        b_.tok.excl = True
    bank_rr = [0]
    rot = [[0, 1, 2, 3, 4, 5]]

    def psum():
        r_ = rot[0]
        b = banks[r_[bank_rr[0] % len(r_)]]
        bank_rr[0] += 1
        return b

    cst = {}
    for k, s in CONST_SHAPES.items():
        cst[k] = sb("k_" + k, s)
        P.dma("sync", cst[k].t[:], dc[k], w=[cst[k].tok])
    ident_b = sb("ident_b", [128, 128], BF16)
    P.v(lambda e: e.tensor_copy(out=ident_b[:], in_=cst["ident"][:]), r=[cst["ident"].tok], w=[ident_b.tok])
    zeros_b = sb("zeros_b", [128, 128], BF16)
    P.v(lambda e: e.memset(zeros_b[:], 0.0), w=[zeros_b.tok])
    ones_b = sb("ones_b", [128, 128], BF16)
    P.v(lambda e: e.memset(ones_b[:], 1.0), w=[ones_b.tok])
    onespad = sb("onespad", [128, 2, 128], BF16)
    P.v(lambda e: e.memset(onespad[:], 0.0), w=[onespad.tok])
    for h2 in range(2):
        P.v(lambda e, h2=h2: e.memset(onespad[:, h2, 64 * h2:64 * h2 + 64], 1.0), w=[onespad.tok])
    avg = sb("avg", [128, 128])
    P.v(lambda e: e.tensor_scalar(out=avg[:], in0=cst["bd2"][:], scalar1=1.0 / 64, scalar2=None, op0=ALU.mult),
        r=[cst["bd2"].tok], w=[avg.tok])
    avg_b = sb("avg_b", [128, 128], BF16)
    P.v(lambda e: e.tensor_copy(out=avg_b[:], in_=avg[:]), r=[avg.tok], w=[avg_b.tok])
    normw = sb("normw", [128, DEPTH * 8])
    P.dma("sync", normw[:], d_normw, w=[normw.tok])
    fnw = sb("fnw", [128, 8])
    P.dma("sync", fnw[:], d_fnw, w=[fnw.tok])
    gn = sb("gn", [128, DEPTH * 4])
    P.dma("sync", gn[:], d_gn, w=[gn.tok])
    bif = sb("bif", [8, DEPTH])
    P.dma("sync", bif[:], d_bif, w=[bif.tok])

    decq = sb("decq", [128, 2, 128])
    deck = sb("deck", [128, 2, 128])
    gL = sb("gL", [128, 2])
    for i in range(2):
        P.a(lambda e, i=i: e.activation(out=decq[:, i, :], in_=cst["l1"][:], func=AF.Exp, scale=cst["lgcol"][:, i:i + 1]),
            r=[cst["l1"].tok, cst["lgcol"].tok], w=[decq.tok])
    P.v(lambda e: e.reciprocal(out=deck[:], in_=decq[:]), r=[decq.tok], w=[deck.tok])
    P.v(lambda e: e.tensor_scalar(out=deck[:], in0=deck[:], scalar1=0.125, scalar2=None, op0=ALU.mult),
        r=[deck.tok], w=[deck.tok])
    P.v(lambda e: e.tensor_copy(out=gL[:], in_=decq[:, :, 127]), r=[decq.tok], w=[gL.tok])

    bd2g = sb("bd2g", [128, 2, 128])
    for i in range(2):
        P.v(lambda e, i=i: e.tensor_scalar(out=bd2g[:, i, :], in0=cst["bd2"][:], scalar1=gL[:, i:i + 1], scalar2=None, op0=ALU.mult),
            r=[cst["bd2"].tok, gL.tok], w=[bd2g.tok])

    xT = sb("xT", [128, 8, T])
    xtok = [Tok("xT%d" % b) for b in range(NBLK)]
    for b in range(NBLK):
        P.dma("sync", xT[:, :, b * TB:(b + 1) * TB], xTv[:, :, b * TB:(b + 1) * TB], w=[xtok[b]])
    hnT = sb("hnT", [128, 8, T], BF16)
    htok = [Tok("hnT%d" % b) for b in range(NBLK)]

    F = [sb("F%d" % i, [128, TB]) for i in range(11)]
    P0 = sb("P0", [128, 2, TB], BF16)
    P1 = sb("P1", [128, 2, TB], BF16)
    P2 = sb("P2", [128, 2, TB], BF16)
    Pf = sb("Pf", [128, 2, TB])
    VA = sb("VA", [128, 4, 2, 2, 128], BF16)
    ET = sb("ET", [128, 4, TB], BF16)
    mixp = sb("mixp", [128, 2, TB], BF16)
    mixp.tok = [Tok("mixp0"), Tok("mixp1")]
    ET.tok = [Tok("ET%d" % k_) for k_ in range(4)]
    Pf.tok = [Tok("Pf0"), Tok("Pf1")]
    ncs = sb("ncs", [128, 2, 4])
    sq2 = [sb("sq%d" % i, [128, TB], BF16) for i in range(2)]

    NW = 3
    wch = [sb("wch%d" % i, [128, 8, 512], BF16) for i in range(NW)]
    wrr = [0]
    wop = [sb("wop%d" % i, [128, 2, 1024], BF16) for i in range(1)]
    worr = [0]

    def wload(parts):
        b = wch[wrr[0] % NW]
        wrr[0] += 1
        for (s2, a0, a1, off) in parts:
            P.dma("gpsimd", b.t[:, :, off:off + (a1 - a0)], s2.rearrange("(k p) c -> p k c", p=128)[:, :, a0:a1], w=[b.tok])
        return b

    def woload(l, m):
        b = wop[0]
        worr[0] += 1
        P.dma("gpsimd", b.t[:], d_wout[l][m * 256:(m + 1) * 256, :].rearrange("(k p) c -> p k c", p=128), w=[b.tok])
        return b

    mkT = sb("mkT", [128, 2, 256], BF16)
    mvpad = sb("mvpad", [128, 2, 2, 2, 128], BF16)
    P.g(lambda e: e.memset(mvpad[:], 0.0), w=[mvpad.tok])

    def mem_kv(l):
        wb = wload([(d_wmk[l], 0, 256, 0), (d_wmv[l], 0, 256, 256)])
        mslot = wch[wrr[0] % NW]
        wrr[0] += 1
        memT = Buf(mslot.t[:].rearrange("p a b -> p (a b)")[:, 0:2048].rearrange("p (k m) -> p k m", k=8), "memT")
        memT.tok = mslot.tok
        P.dma("gpsimd", memT.t, d_memT.rearrange("(k p) m -> p k m", p=128), w=[memT.tok])
        for mt in range(2):
            ps = psum()
            for kt in range(8):
                P.t(lambda e, ps=ps, kt=kt, mt=mt: e.matmul(ps[:, :], lhsT=memT[:, kt, mt * 128:(mt + 1) * 128],
                                                            rhs=wb[:, kt, :], start=(kt == 0), stop=(kt == 7)),
                    r=[memT.tok, wb.tok], w=[ps.tok])
            fo = F[mt]
            P.a(lambda e, ps=ps, fo=fo: e.activation(out=fo[:], in_=ps[:, :], func=AF.Copy), r=[ps.tok], w=[fo.tok])
            dst = AP(mvpad.t, mt * 512 + 0, [[1024, 128], [256, 2], [192, 2], [1, 64]])
            P.v(lambda e, ps=ps, dst=dst: e.tensor_copy(out=dst, in_=ps[:, 256:512].rearrange("p (i h e) -> p i h e", i=2, h=2)),
                r=[ps.tok], w=[mvpad.tok])
            P.dma("sync", o_memkv[l, mt * 128:(mt + 1) * 128, :], fo[:], r=[fo.tok], is_output=True)
        for i in range(2):
            ps = psum()
            for kt in range(8):
                P.t(lambda e, ps=ps, kt=kt, i=i: e.matmul(ps[:, 0:256], lhsT=wb[:, kt, i * 128:(i + 1) * 128],
                                                          rhs=memT[:, kt, :], start=(kt == 0), stop=(kt == 7)),
                    r=[memT.tok, wb.tok], w=[ps.tok])
            P.v(lambda e, ps=ps, i=i: e.tensor_copy(out=mkT[:, i, :], in_=ps[:, 0:256]), r=[ps.tok], w=[mkT.tok])

    def rmsnorm(srcs, rtoks, ntok, dsts, wtoks, wcol):
        rs = F[10]
        ps = psum()
        for kt in range(8):
            sq = sq2[kt % 2]
            P.a(lambda e, kt=kt, sq=sq: e.activation(out=sq[:, 0:ntok], in_=srcs(kt), func=AF.Square), r=rtoks, w=[sq.tok])
            P.t(lambda e, kt=kt, sq=sq: e.matmul(ps[:, 0:ntok], lhsT=ones_b[:, :], rhs=sq[:, 0:ntok],
                                                 start=(kt == 0), stop=(kt == 7)), r=[sq.tok, ones_b.tok], w=[ps.tok])
        P.a(lambda e: e.activation(out=rs[:, 0:ntok], in_=ps[:, 0:ntok], func=AF.Ln, scale=1.0 / 1024, bias=EPS),
            r=[ps.tok], w=[rs.tok])
        P.a(lambda e: e.activation(out=rs[:, 0:ntok], in_=rs[:, 0:ntok], func=AF.Exp, scale=-0.5), r=[rs.tok], w=[rs.tok])
        for kt in range(8):
            P.v(lambda e, kt=kt: e.scalar_tensor_tensor(out=dsts(kt), in0=srcs(kt), scalar=wcol(kt), in1=rs[:, 0:ntok],
                                                        op0=ALU.mult, op1=ALU.mult), r=rtoks + [rs.tok, normw.tok, fnw.tok], w=wtoks(kt))

    def sincos(ang, atok, nfi, nff, ntok, sin_o, stok, cos_o, ctok):
        P.v(lambda e: e.tensor_scalar(out=nfi, in0=ang, scalar1=1.0 / TWO_PI, scalar2=None, op0=ALU.mult), r=[atok], w=[ntok])
        P.v(lambda e: e.tensor_copy(out=cos_o, in_=nfi), r=[ntok], w=[ctok])
        P.v(lambda e: e.scalar_tensor_tensor(out=ang, in0=cos_o, scalar=-C1, in1=ang, op0=ALU.mult, op1=ALU.add), r=[ctok, atok], w=[atok])
        P.v(lambda e: e.scalar_tensor_tensor(out=ang, in0=cos_o, scalar=-C2, in1=ang, op0=ALU.mult, op1=ALU.add), r=[ctok, atok], w=[atok])
        P.v(lambda e: e.tensor_scalar(out=ang, in0=ang, scalar1=3.1415925, scalar2=-3.1415925, op0=ALU.min, op1=ALU.max), r=[atok], w=[atok])
        P.a(lambda e: e.activation(out=sin_o, in_=ang, func=AF.Sin), r=[atok], w=[stok])
        P.a(lambda e: e.activation(out=ang, in_=ang, func=AF.Abs), r=[atok], w=[atok])
        P.a(lambda e: e.activation(out=cos_o, in_=ang, func=AF.Sin, scale=-1.0, bias=1.5707963), r=[atok], w=[ctok])

    def rope_tables(blk):
        angb, nf, cosb, sinb = F[0], F[1], F[2], F[3]
        t0 = float(blk * TB)
        for c in range(4):
            P.v(lambda e, c=c: e.tensor_scalar(out=angb[:, c * 128:(c + 1) * 128], in0=cst["l1"][:], scalar1=t0 + c * 128 - 1.0, scalar2=cst["invf"][:, 0:1],
                                               op0=ALU.add, op1=ALU.mult), r=[cst["l1"].tok, cst["invf"].tok], w=[angb.tok])
        sincos(angb[:], angb.tok, nf.t[:].bitcast(I32), nf[:], nf.tok, sinb[:], sinb.tok, cosb[:], cosb.tok)

    def fm_group(wb, c0, blk, m=128):
        ps = psum()
        for kt in range(8):
            P.t(lambda e, kt=kt: e.matmul(ps[0:m, :], lhsT=wb[:, kt, c0:c0 + m], rhs=hnT[:, kt, blk * TB:(blk + 1) * TB],
                                          start=(kt == 0), stop=(kt == 7)), r=[wb.tok, htok[blk]], w=[ps.tok])
        return ps

    def tm_group(wb, c0, blk, c, n=256):
        ps = psum()
        ts = slice(blk * TB + c * 128, blk * TB + (c + 1) * 128)
        for kt in range(8):
            P.t(lambda e, kt=kt: e.matmul(ps[:, 0:n], lhsT=hnT[:, kt, ts], rhs=wb[:, kt, c0:c0 + n],
                                          start=(kt == 0), stop=(kt == 7)), r=[wb.tok, htok[blk]], w=[ps.tok])
        return ps

    def rope_evac(ps, psr, i, dst, dec):
        cosb, sinb = F[2], F[3]
        t1, t2 = F[4 + i], F[6 + i] if False else F[6]
        P.v(lambda e: e.tensor_tensor(out=t1[:], in0=ps[:, :], in1=cosb[:], op=ALU.mult), r=[ps.tok, cosb.tok], w=[t1.tok])
        P.v(lambda e: e.tensor_tensor(out=t2[:], in0=psr[:, :], in1=sinb[:], op=ALU.mult), r=[psr.tok, sinb.tok], w=[t2.tok])
        P.v(lambda e: e.tensor_tensor(out=t1[:], in0=t1[:], in1=t2[:], op=ALU.add), r=[t1.tok, t2.tok], w=[t1.tok])
        decb = AP(dec.t, i * 128, [[256, 128], [0, 4], [1, 128]])
        P.v(lambda e: e.tensor_tensor(out=dst[:, i, :].rearrange("p (c l) -> p c l", c=4),
                                      in0=t1[:].rearrange("p (c l) -> p c l", c=4), in1=decb, op=ALU.mult),
            r=[t1.tok, dec.tok], w=[dst.tok])

    innT = [sb("innT%d" % i, [128, 2, 128], BF16) for i in range(2)]
    irr = [0]
    kTM = [sb("kTM%d" % i, [128, 128], BF16) for i in range(2)]
    krr = [0]
    tts = [sb("tt%d" % i, [128, 256]) for i in range(2)]
    ttr = [0]

    def head_norm_gate(src, gate, gtok, gcol, dst, cen=None, sq32=None, dtok=None):
        cen = F[8] if cen is None else cen
        sq32 = F[9] if sq32 is None else sq32
        ps = psum()
        P.t(lambda e: e.matmul(ps[:, :], lhsT=avg[:], rhs=src[:], start=True, stop=True), r=[avg.tok, src.tok], w=[ps.tok])
        P.v(lambda e: e.tensor_tensor(out=cen[:], in0=src[:], in1=ps[:, :], op=ALU.subtract), r=[src.tok, ps.tok], w=[cen.tok])
        sqb = sq32.t[:].bitcast(BF16)[:, 0:TB]
        P.a(lambda e: e.activation(out=sqb, in_=cen[:], func=AF.Square), r=[cen.tok], w=[sq32.tok])
        ps2 = psum()
        P.t(lambda e: e.matmul(ps2[:, :], lhsT=avg_b[:], rhs=sqb, start=True, stop=True), r=[avg_b.tok, sq32.tok], w=[ps2.tok])
        P.a(lambda e: e.activation(out=sq32[:], in_=ps2[:, :], func=AF.Ln, bias=EPS), r=[ps2.tok], w=[sq32.tok])
        P.a(lambda e: e.activation(out=sq32[:], in_=sq32[:], func=AF.Exp, scale=-0.5), r=[sq32.tok], w=[sq32.tok])
        P.v(lambda e: e.tensor_tensor(out=cen[:], in0=cen[:], in1=sq32[:], op=ALU.mult), r=[cen.tok, sq32.tok], w=[cen.tok])
        P.v(lambda e: e.scalar_tensor_tensor(out=dst, in0=cen[:], scalar=gcol, in1=gate, op0=ALU.mult, op1=ALU.mult),
            r=[cen.tok, gn.tok, gtok], w=[mixp.tok if dtok is None else dtok])

    def outproj_part(wo, blk):
        t0 = blk * TB
        for dt_ in range(8):
            ps = psum()
            for kt in range(2):
                P.t(lambda e, ps=ps, kt=kt, dt_=dt_: e.matmul(ps[:, :], lhsT=wo[:, kt, dt_ * 128:(dt_ + 1) * 128], rhs=mixp[:, kt, :],
                                                              start=(kt == 0), stop=(kt == 1)), r=[wo.tok, mixp.tok], w=[ps.tok])
            P.v(lambda e, ps=ps, dt_=dt_: e.tensor_tensor(out=xT[:, dt_, t0:t0 + TB], in0=xT[:, dt_, t0:t0 + TB], in1=ps[:, :], op=ALU.add),
                r=[ps.tok, xtok[blk]], w=[xtok[blk]])

    Sret = [sb("Sret%d" % i, [128, 128]) for i in range(2)]
    Sret_b = [sb("Sretb%d" % i, [128, 128], BF16) for i in range(2)]

    def ret_phase(l):
        W = d_win[l]
        wA = wload([(W, 0, 512, 0)])
        wB = wload([(W, 512, 1024, 0)])
        wo = woload(l, 0)
        wR = wch[wrr[0] % NW]
        wrr[0] += 1
        src1 = AP(wA.t, 32, [[4096, 128], [512, 8], [64, 8], [1, 32]])
        src0 = AP(wA.t, 0, [[4096, 128], [512, 8], [64, 8], [1, 32]])
        dst0 = AP(wR.t, 0, [[4096, 128], [512, 8], [64, 8], [1, 32]])
        dst1 = AP(wR.t, 32, [[4096, 128], [512, 8], [64, 8], [1, 32]])
        P.g(lambda e: e.tensor_scalar(out=dst0, in0=src1, scalar1=-1.0, scalar2=None, op0=ALU.mult), r=[wA.tok], w=[wR.tok])
        P.g(lambda e: e.tensor_copy(out=dst1, in_=src0), r=[wA.tok], w=[wR.tok])
        rq, rk, rg, rvpad, oT = P0, P1, P2, VA, F[7]
        P.g(lambda e: e.memset(VA[:], 0.0), w=[VA.tok])
        for i in range(2):
            P.v(lambda e, i=i: e.memset(Sret[i][:], 0.0), w=[Sret[i].tok])
            P.v(lambda e, i=i: e.memset(Sret_b[i][:], 0.0), w=[Sret_b[i].tok])
        for blk in range(NBLK):
            rope_tables(blk)
            for i in range(2):
                rope_evac(fm_group(wA, i * 128, blk), fm_group(wR, i * 128, blk), i, rq, decq)
            for i in range(2):
                rope_evac(fm_group(wA, 256 + i * 128, blk), fm_group(wR, 256 + i * 128, blk), i, rk, deck)
            for c in range(4):
                ps = tm_group(wB, 0, blk, c)
                dst = AP(rvpad.t, c * 512, [[2048, 128], [256, 2], [192, 2], [1, 64]])
                P.a(lambda e, ps=ps, dst=dst: e.activation(out=dst, in_=ps[:, 0:256].rearrange("p (i h e) -> p i h e", i=2, h=2),
                                                           func=AF.Copy), r=[ps.tok], w=[rvpad.tok])
            for i in range(2):
                ps = fm_group(wB, 256 + i * 128, blk)
                P.a(lambda e, ps=ps, i=i: e.activation(out=rg[:, i, :], in_=ps[:, :], func=AF.Silu), r=[ps.tok], w=[rg.tok])
            mask4 = AP(cst["maskT"].t, 0, [[128, 128], [0, 4], [1, 128]])
            for i in range(2):
                pso = banks[6 + i]
                P.t(lambda e, pso=pso: e.matmul(pso[:, :], lhsT=zeros_b[:, 0:128], rhs=hnT[:, 0, 0:512], start=True, stop=False), r=[zeros_b.tok, htok[0]], w=[pso.tok])
                for h2 in range(2):
                    ps = psum()
                    hs = slice(64 * h2, 64 * h2 + 64)
                    for c in range(4):
                        cs = slice(c * 128, (c + 1) * 128)
                        P.t(lambda e, ps=ps, hs=hs, cs=cs, i=i: e.matmul(ps[:, cs], lhsT=rk[hs, i, cs], rhs=rq[hs, i, cs], start=True, stop=True),
                            r=[rk.tok, rq.tok], w=[ps.tok])
                    P.v(lambda e, ps=ps, i=i, h2=h2: e.tensor_tensor(out=ET[:, i * 2 + h2, :].rearrange("p (c l) -> p c l", c=4), in0=ps[:, :].rearrange("p (c l) -> p c l", c=4),
                                                                     in1=mask4, op=ALU.mult), r=[ps.tok, cst["maskT"].tok], w=[ET.tok[i * 2 + h2]])
                for c in range(4):
                    cs = slice(c * 128, (c + 1) * 128)
                    for h2 in range(2):
                        P.t(lambda e, c=c, i=i, h2=h2, cs=cs, pso=pso: e.matmul(pso[:, cs], lhsT=rvpad[:, c, i, h2, :], rhs=ET[:, i * 2 + h2, cs], start=False, stop=False),
                            r=[rvpad.tok, ET.tok[i * 2 + h2]], w=[pso.tok])
                pst = psum()
                pstb = pst.t[:].bitcast(BF16)
                for c in range(4):
                    cs = slice(c * 128, (c + 1) * 128)
                    P.t(lambda e, pstb=pstb, i=i, cs=cs: e.transpose(pstb[:, cs], rk[:, i, cs], ident_b[:]), r=[rk.tok, ident_b.tok], w=[pst.tok])
                ktm = sq2[i]
                P.a(lambda e, pstb=pstb, ktm=ktm: e.activation(out=ktm[:], in_=pstb[:, 0:512], func=AF.Copy), r=[pst.tok], w=[ktm.tok])
                psu = psum()
                for c in range(4):
                    cs = slice(c * 128, (c + 1) * 128)
                    vsl = AP(rvpad.t, c * 512 + i * 256, [[2048, 128], [192, 2], [1, 64]])
                    P.t(lambda e, psu=psu, ktm=ktm, vsl=vsl, cs=cs: e.matmul(psu[:, cs].rearrange("p (h e) -> p h e", h=2), lhsT=ktm[:, cs], rhs=vsl, start=True, stop=True),
                        r=[ktm.tok, rvpad.tok], w=[psu.tok])
                tt = F[4 + i]
                bdg = AP(bd2g.t, i * 128, [[256, 128], [0, 4], [1, 128]])
                P.v(lambda e, psu=psu, tt=tt, bdg=bdg: e.tensor_tensor(out=tt[:].rearrange("p (c x) -> p c x", c=4), in0=psu[:, :].rearrange("p (c x) -> p c x", c=4), in1=bdg, op=ALU.mult),
                    r=[psu.tok, bd2g.tok], w=[tt.tok])
            for c in range(4):
                cs = slice(c * 128, (c + 1) * 128)
                for i in range(2):
                    pso = banks[6 + i]
                    tt = F[4 + i]
                    P.t(lambda e, i=i, cs=cs, pso=pso, c=c: e.matmul(pso[:, cs], lhsT=Sret_b[i][:], rhs=rq[:, i, cs], start=False, stop=(c == 3)),
                        r=[Sret_b[i].tok, rq.tok], w=[pso.tok])
                    P.v(lambda e, i=i, tt=tt, cs=cs: e.scalar_tensor_tensor(out=Sret_b[i][:], in0=Sret[i][:], scalar=gL[:, i:i + 1], in1=tt[:, cs], op0=ALU.mult, op1=ALU.add),
                        r=[Sret[i].tok, gL.tok, tt.tok], w=[Sret_b[i].tok])
                    P.v(lambda e, i=i, tt=tt, cs=cs: e.scalar_tensor_tensor(out=Sret[i][:], in0=Sret[i][:], scalar=gL[:, i:i + 1], in1=tt[:, cs], op0=ALU.mult, op1=ALU.add),
                        r=[Sret[i].tok, gL.tok, tt.tok], w=[Sret[i].tok])
            for i in range(2):
                pso = banks[6 + i]
                oTi, ceni, sqi = (F[7], F[8], F[9]) if i == 0 else (F[0], F[1], F[2])
                P.a(lambda e, pso=pso, oTi=oTi: e.activation(out=oTi[:], in_=pso[:, :], func=AF.Copy), r=[pso.tok], w=[oTi.tok])
                head_norm_gate(oTi, rg[:, i, :], rg.tok, gn[:, l * 4 + i:l * 4 + i + 1], mixp[:, i, :], ceni, sqi, mixp.tok[i])
            outproj_part(wo, blk)
        for i in range(2):
            for h2 in range(2):
                P.dma("sync", o_ret[l, 2 * i + h2], Sret[i][64 * h2:64 * h2 + 64, 64 * h2:64 * h2 + 64],
                      r=[Sret[i].tok], is_output=True)
        if SAMPLE >= 1:
            ret_sample(l, wA, wB, wo)

    Cml = [sb("Cml%d" % i, [128, 256]) for i in range(2)]
    Cml_b = [sb("Cmlb%d" % i, [128, 256], BF16) for i in range(2)]
    Bcar = [sb("Bcar%d" % i, [128, 1]) for i in range(2)]
    Gcar = [sb("Gcar%d" % i, [128, 1]) for i in range(2)]
    E1t = sb("E1", [128, 128])
    mkt = [sb("mkt%d" % i, [128, 128], BF16) for i in range(2)]
    mkrr = [0]
    gs_col = sb("gs_col", [128, 4])
    ngs_col = sb("ngs_col", [128, 4])
    mlm_o = sb("mlm_o", [128, 2])

    def ml_phase(l):
        W = d_win[l]
        wC = wload([(W, 1024, 1536, 0)])
        wD = wload([(W, 1536, 2048, 0)])
        wE = wload([(W, 2048, 2312, 0)])
        wo = woload(l, 1)
        mq, mo, mg, mk_, mvp = P0, P1, P2, Pf, VA
        gates8 = Buf(F[6].t[0:8, :], 'gates8')
        gates8.tok = F[6].tok
        LFt, IGt, d0s, oT = F[0], F[1], F[5], F[7]
        ones_col = cst["l1"][:, 0:1].to_broadcast([128, TB])
        rot[0] = [0, 1, 2, 3, 4, 5]
        mask4 = AP(cst["maskT"].t, 0, [[128, 128], [0, 4], [1, 128]])
        for i in range(2):
            P.v(lambda e, i=i: e.memset(Cml[i][:], 0.0), w=[Cml[i].tok])
            P.v(lambda e, i=i: e.memset(Cml_b[i][:], 0.0), w=[Cml_b[i].tok])
            P.v(lambda e, i=i: e.memset(Bcar[i][:], 0.0), w=[Bcar[i].tok])
            P.v(lambda e, i=i: e.memset(Gcar[i][:], 0.0), w=[Gcar[i].tok])
        for blk in range(NBLK):
            for i in range(2 if KMASK & 1 else 0):
                ps = fm_group(wC, i * 128, blk)
                P.a(lambda e, ps=ps, i=i: e.activation(out=mq[:, i, :], in_=ps[:, :], func=AF.Copy), r=[ps.tok], w=[mq.tok])
            for i in range(2 if KMASK & 2 else 0):
                ps = fm_group(wC, 256 + i * 128, blk)
                P.a(lambda e, ps=ps, i=i: e.activation(out=mk_[:, i, :], in_=ps[:, :], func=AF.Copy), r=[ps.tok], w=[Pf.tok[i]])
            for c in range(4 if KMASK & 4 else 0):
                ps = tm_group(wD, 0, blk, c)
                dst = AP(mvp.t, c * 512, [[2048, 128], [256, 2], [192, 2], [1, 64]])
                P.a(lambda e, ps=ps, dst=dst: e.activation(out=dst, in_=ps[:, 0:256].rearrange("p (i h e) -> p i h e", i=2, h=2),
                                                           func=AF.Copy), r=[ps.tok], w=[mvp.tok])
            for i in range(2 if KMASK & 8 else 0):
                ps = fm_group(wD, 256 + i * 128, blk)
                P.a(lambda e, ps=ps, i=i: e.activation(out=mo[:, i, :], in_=ps[:, :], func=AF.Tanh, scale=0.5), r=[ps.tok], w=[mo.tok])
            for i in range(2 if KMASK & 16 else 0):
                ps = fm_group(wE, i * 128, blk)
                P.a(lambda e, ps=ps, i=i: e.activation(out=mg[:, i, :], in_=ps[:, :], func=AF.Silu), r=[ps.tok], w=[mg.tok])
            if KMASK & 32:
                ps = fm_group(wE, 256, blk, m=8)
                P.a(lambda e, ps=ps: e.activation(out=gates8[:, :], in_=ps[0:8, :], func=AF.Identity, bias=bif[:, l:l + 1]),
                    r=[ps.tok, bif.tok], w=[gates8.tok])
            for i in range(2 if KSUB >= 2 else 0):
                Gt = F[2] if i == 0 else F[4]
                MTt = F[3] if i == 0 else F[10]
                rt = Gt
                psi = psum()
                P.t(lambda e, psi=psi, i=i: e.matmul(psi[:, :], lhsT=cst["gexp"][:, (2 * i) * 128:(2 * i + 1) * 128], rhs=gates8[:, :],
                                                     start=True, stop=True), r=[cst["gexp"].tok, gates8.tok], w=[psi.tok])
                psf = psum()
                P.t(lambda e, psf=psf, i=i: e.matmul(psf[:, :], lhsT=cst["gexp"][:, (2 * i + 1) * 128:(2 * i + 2) * 128], rhs=gates8[:, :],
                                                     start=True, stop=True), r=[cst["gexp"].tok, gates8.tok], w=[psf.tok])
                P.a(lambda e, psf=psf: e.activation(out=LFt[:], in_=psf[:, :], func=AF.Exp, scale=-1.0), r=[psf.tok], w=[LFt.tok])
                P.a(lambda e: e.activation(out=LFt[:], in_=LFt[:], func=AF.Ln, bias=1.0), r=[LFt.tok], w=[LFt.tok])
                P.v(lambda e, i=i: e.tensor_tensor_scan(out=LFt[:], data0=ones_col, data1=LFt[:], initial=Bcar[i][:, 0:1],
                                                        op0=ALU.mult, op1=ALU.subtract), r=[LFt.tok, Bcar[i].tok, cst["l1"].tok], w=[LFt.tok])
                P.v(lambda e, psi=psi: e.tensor_tensor(out=IGt[:], in0=psi[:, :], in1=LFt[:], op=ALU.subtract), r=[psi.tok, LFt.tok], w=[IGt.tok])
                P.v(lambda e, i=i: e.tensor_tensor_scan(out=Gt[:], data0=ones_col, data1=IGt[:], initial=Gcar[i][:, 0:1],
                                                        op0=ALU.mult, op1=ALU.max), r=[IGt.tok, Gcar[i].tok, cst["l1"].tok], w=[Gt.tok])
                P.v(lambda e, i=i: e.tensor_copy(out=gs_col[:, 0:1], in_=Gcar[i][:, 0:1]), r=[Gcar[i].tok], w=[gs_col.tok])
                P.v(lambda e: e.tensor_copy(out=gs_col[:, 1:4], in_=AP(Gt.t, 127, [[TB, 128], [128, 3]])), r=[Gt.tok], w=[gs_col.tok])
                P.v(lambda e: e.tensor_scalar(out=ngs_col[:], in0=gs_col[:], scalar1=-1.0, scalar2=None, op0=ALU.mult), r=[gs_col.tok], w=[ngs_col.tok])
                P.v(lambda e: e.tensor_tensor(out=MTt[:], in0=LFt[:], in1=Gt[:], op=ALU.add), r=[LFt.tok, Gt.tok], w=[MTt.tok])
                if blk == NBLK - 1:
                    P.v(lambda e, i=i: e.tensor_copy(out=mlm_o[:, i:i + 1], in_=MTt[:, TB - 1:TB]), r=[MTt.tok], w=[mlm_o.tok])
                P.a(lambda e: e.activation(out=MTt[:], in_=MTt[:], func=AF.Exp, scale=-1.0), r=[MTt.tok], w=[MTt.tok])
                P.v(lambda e, i=i: e.tensor_copy(out=Bcar[i][:, 0:1], in_=LFt[:, TB - 1:TB]), r=[LFt.tok], w=[Bcar[i].tok])
                P.v(lambda e, i=i: e.tensor_copy(out=Gcar[i][:, 0:1], in_=Gt[:, TB - 1:TB]), r=[Gt.tok], w=[Gcar[i].tok])
                psn = banks[6]
                psd = banks[7]
                if KSUB < 3:
                    continue
                for c in range(4):
                    cs = slice(c * 128, (c + 1) * 128)
                    P.a(lambda e, c=c, cs=cs: e.activation(out=IGt[:, cs], in_=IGt[:, cs], func=AF.Exp, bias=ngs_col[:, c:c + 1]),
                        r=[IGt.tok, ngs_col.tok], w=[IGt.tok])
                    P.a(lambda e, c=c, cs=cs, Gt=Gt: e.activation(out=Gt[:, cs], in_=Gt[:, cs], func=AF.Exp, scale=-1.0, bias=gs_col[:, c:c + 1]),
                        r=[Gt.tok, gs_col.tok], w=[Gt.tok])
                ktil = mixp[:, i, :]
                mtk = mixp.tok[i]
                P.v(lambda e, i=i, ktil=ktil: e.scalar_tensor_tensor(out=ktil, in0=mk_[:, i, :], scalar=0.125, in1=IGt[:], op0=ALU.mult, op1=ALU.mult),
                    r=[Pf.tok[i], IGt.tok], w=[mtk])
                for bk in (psn, psd):
                    P.t(lambda e, bk=bk: e.matmul(bk[:, :], lhsT=zeros_b[:, 0:128], rhs=hnT[:, 0, 0:512], start=True, stop=False), r=[zeros_b.tok, htok[0]], w=[bk.tok])
                for h2 in range(2):
                    ps = psum()
                    hs = slice(64 * h2, 64 * h2 + 64)
                    ek = 2 * i + h2
                    for c in range(4):
                        cs = slice(c * 128, (c + 1) * 128)
                        P.t(lambda e, ps=ps, hs=hs, cs=cs, i=i: e.matmul(ps[:, cs], lhsT=mixp[hs, i, cs], rhs=mq[hs, i, cs], start=True, stop=True),
                            r=[mtk, mq.tok], w=[ps.tok])
                    P.v(lambda e, ps=ps, ek=ek: e.tensor_tensor(out=ET[:, ek, :].rearrange("p (c l) -> p c l", c=4), in0=ps[:, :].rearrange("p (c l) -> p c l", c=4),
                                                                in1=mask4, op=ALU.mult), r=[ps.tok, cst["maskT"].tok], w=[ET.tok[ek]])
                for c in range(4):
                    cs = slice(c * 128, (c + 1) * 128)
                    for h2 in range(2):
                        ek = 2 * i + h2
                        P.t(lambda e, c=c, i=i, h2=h2, cs=cs, ek=ek, psn=psn: e.matmul(psn[:, cs], lhsT=mvp[:, c, i, h2, :], rhs=ET[:, ek, cs], start=False, stop=False),
                            r=[mvp.tok, ET.tok[ek]], w=[psn.tok])
                        P.t(lambda e, h2=h2, cs=cs, ek=ek, psd=psd: e.matmul(psd[:, cs], lhsT=onespad[:, h2, :], rhs=ET[:, ek, cs], start=False, stop=False),
                            r=[onespad.tok, ET.tok[ek]], w=[psd.tok])
                pst = psum()
                pstb = pst.t[:].bitcast(BF16)
                for c in range(4):
                    cs = slice(c * 128, (c + 1) * 128)
                    P.t(lambda e, pstb=pstb, i=i, cs=cs: e.transpose(pstb[:, cs], mixp[:, i, cs], ident_b[:]), r=[mtk, ident_b.tok], w=[pst.tok])
                ktm = sq2[i]
                P.a(lambda e, pstb=pstb, ktm=ktm: e.activation(out=ktm[:], in_=pstb[:, 0:512], func=AF.Copy), r=[pst.tok], w=[ktm.tok])
                psuA = psum()
                psuB = psum()
                for c in range(4):
                    cs = slice(c * 128, (c + 1) * 128)
                    vsl = AP(mvp.t, c * 512 + i * 256, [[2048, 128], [192, 2], [1, 64]])
                    P.t(lambda e, ktm=ktm, vsl=vsl, cs=cs, psuA=psuA: e.matmul(psuA[:, cs].rearrange("p (h e) -> p h e", h=2), lhsT=ktm[:, cs], rhs=vsl, start=True, stop=True),
                        r=[ktm.tok, mvp.tok], w=[psuA.tok])
                    P.t(lambda e, ktm=ktm, cs=cs, c=c, psuB=psuB: e.matmul(psuB[:, c:c + 1], lhsT=ktm[:, cs], rhs=ones_b[:, 0:1], start=True, stop=True),
                        r=[ktm.tok, ones_b.tok], w=[psuB.tok])
                ttA = Pf[:, i, :]
                for c in range(4):
                    cs = slice(c * 128, (c + 1) * 128)
                    rcol = rt[:, c * 128 + 127:c * 128 + 128]
                    P.v(lambda e, cs=cs, rcol=rcol, psuA=psuA, i=i: e.scalar_tensor_tensor(out=Pf[:, i, cs], in0=psuA[:, cs], scalar=rcol, in1=cst["bd2"][:], op0=ALU.mult, op1=ALU.mult),
                        r=[psuA.tok, rt.tok, cst["bd2"].tok, mtk], w=[Pf.tok[i]])
                rcols = AP(rt.t, 127, [[TB, 128], [128, 4]])
                P.v(lambda e, psuB=psuB, i=i: e.tensor_tensor(out=ncs[:, i, :], in0=psuB[:, 0:4], in1=rcols, op=ALU.mult), r=[psuB.tok, rt.tok], w=[ncs.tok])
                for c in range(4):
                    cs = slice(c * 128, (c + 1) * 128)
                    P.t(lambda e, i=i, cs=cs, psn=psn: e.matmul(psn[:, cs], lhsT=Cml_b[i][:, 0:128], rhs=mq[:, i, cs], start=False, stop=(cs.stop == 512)),
                        r=[Cml_b[i].tok, mq.tok], w=[psn.tok])
                    P.t(lambda e, i=i, cs=cs, psd=psd: e.matmul(psd[:, cs], lhsT=Cml_b[i][:, 128:256], rhs=mq[:, i, cs], start=False, stop=(cs.stop == 512)),
                        r=[Cml_b[i].tok, mq.tok], w=[psd.tok])
                    tt = tts[ttr[0] % 2]
                    ttr[0] += 1
                    rcol = rt[:, c * 128 + 127:c * 128 + 128]
                    P.v(lambda e, tt=tt, c=c, i=i: e.tensor_scalar(out=tt[:, 0:128], in0=cst["bd2"][:], scalar1=ncs[:, i, c:c + 1], scalar2=None, op0=ALU.mult),
                        r=[ncs.tok, cst["bd2"].tok], w=[tt.tok])
                    P.v(lambda e, i=i, cs=cs, rcol=rcol: e.scalar_tensor_tensor(out=Cml_b[i][:, 0:128], in0=Cml[i][:, 0:128], scalar=rcol, in1=Pf[:, i, cs], op0=ALU.mult, op1=ALU.add),
                        r=[Cml[i].tok, rt.tok, Pf.tok[i]], w=[Cml_b[i].tok])
                    P.v(lambda e, tt=tt, i=i, rcol=rcol: e.scalar_tensor_tensor(out=Cml_b[i][:, 128:256], in0=Cml[i][:, 128:256], scalar=rcol, in1=tt[:, 0:128], op0=ALU.mult, op1=ALU.add),
                        r=[Cml[i].tok, rt.tok, tt.tok], w=[Cml_b[i].tok])
                    P.v(lambda e, i=i, cs=cs, rcol=rcol: e.scalar_tensor_tensor(out=Cml[i][:, 0:128], in0=Cml[i][:, 0:128], scalar=rcol, in1=Pf[:, i, cs], op0=ALU.mult, op1=ALU.add),
                        r=[Cml[i].tok, rt.tok, Pf.tok[i]], w=[Cml[i].tok])
                    P.v(lambda e, tt=tt, i=i, rcol=rcol: e.scalar_tensor_tensor(out=Cml[i][:, 128:256], in0=Cml[i][:, 128:256], scalar=rcol, in1=tt[:, 0:128], op0=ALU.mult, op1=ALU.add),
                        r=[Cml[i].tok, rt.tok, tt.tok], w=[Cml[i].tok])
                d0i, oTi, ceni, sqi = (F[5], F[7], F[8], F[9]) if i == 0 else (F[0], F[1], F[6], F[0])
                P.v(lambda e, psd=psd, d0i=d0i: e.tensor_tensor(out=d0i[:], in0=psd[:, :], in1=rt[:], op=ALU.mult), r=[psd.tok, rt.tok], w=[d0i.tok])
                P.a(lambda e, psn=psn, oTi=oTi: e.activation(out=oTi[:], in_=psn[:, :], func=AF.Copy), r=[psn.tok], w=[oTi.tok])
                P.a(lambda e, d0i=d0i: e.activation(out=d0i[:], in_=d0i[:], func=AF.Abs), r=[d0i.tok], w=[d0i.tok])
                P.v(lambda e, d0i=d0i: e.tensor_tensor(out=d0i[:], in0=d0i[:], in1=MTt[:], op=ALU.max), r=[d0i.tok, MTt.tok], w=[d0i.tok])
                P.a(lambda e, d0i=d0i: e.activation(out=d0i[:], in_=d0i[:], func=AF.Ln), r=[d0i.tok], w=[d0i.tok])
                P.a(lambda e, d0i=d0i: e.activation(out=d0i[:], in_=d0i[:], func=AF.Exp, scale=-1.0), r=[d0i.tok], w=[d0i.tok])
                P.v(lambda e, d0i=d0i: e.scalar_tensor_tensor(out=d0i[:], in0=d0i[:], scalar=0.5, in1=rt[:], op0=ALU.mult, op1=ALU.mult), r=[d0i.tok, rt.tok], w=[d0i.tok])
                P.v(lambda e, d0i=d0i, oTi=oTi: e.tensor_tensor(out=oTi[:], in0=oTi[:], in1=d0i[:], op=ALU.mult), r=[oTi.tok, d0i.tok], w=[oTi.tok])
                P.v(lambda e, i=i, oTi=oTi: e.scalar_tensor_tensor(out=oTi[:], in0=mo[:, i, :], scalar=1.0, in1=oTi[:], op0=ALU.add, op1=ALU.mult), r=[oTi.tok, mo.tok], w=[oTi.tok])
                head_norm_gate(oTi, mg[:, i, :], mg.tok, gn[:, l * 4 + 2 + i:l * 4 + 2 + i + 1], mixp[:, i, :], ceni, sqi, mixp.tok[i])
            if KSUB >= 3:
                outproj_part(wo, blk)
        for i in range(2 if KSUB >= 4 else 0):
            for h2 in range(2):
                hs = slice(64 * h2, 64 * h2 + 64)
                P.dma("sync", o_mlc[l, 2 * i + h2], Cml[i][hs, 64 * h2:64 * h2 + 64], r=[Cml[i].tok], is_output=True)
                P.dma("sync", o_mln[l, 2 * i + h2].rearrange("(a b) -> a b", b=1), Cml[i][hs, 128 + 64 * h2:128 + 64 * h2 + 1],
                      r=[Cml[i].tok], is_output=True)
                P.dma("sync", o_mlm[l, 2 * i + h2:2 * i + h2 + 1].rearrange("(a b) -> a b", b=1), mlm_o[64 * h2:64 * h2 + 1, i:i + 1],
                      r=[mlm_o.tok], is_output=True)
        rot[0] = [0, 1, 2, 3, 4, 5]
        if SAMPLE >= 2:
            ml_sample(l, wC, wD, wE, wo)

    def xa_phase(l):
        W = d_win[l]
        wG = wload([(W, 2824, 3336, 0)])
        wo = woload(l, 3)
        aq, ag = P0, P1
        ett = [[Buf(ET.t[:, k, :], "ET%d" % k) for k in range(4)],
               [Buf(F[1 + k].t[:].bitcast(BF16)[:, 0:TB], "ETb%d" % k) for k in range(4)]]
        for k in range(4):
            ett[0][k].tok = ET.tok[k]
            ett[1][k].tok = F[1 + k].tok
        for blk in range(NBLK):
            for i in range(2):
                ps = fm_group(wG, i * 128, blk)
                P.a(lambda e, ps=ps, i=i, aq=aq: e.activation(out=aq[:, i, :], in_=ps[:, :], func=AF.Copy), r=[ps.tok], w=[aq.tok])
            for i in range(2):
                ps = fm_group(wG, 256 + i * 128, blk)
                P.a(lambda e, ps=ps, i=i, ag=ag: e.activation(out=ag[:, i, :], in_=ps[:, :], func=AF.Silu), r=[ps.tok], w=[ag.tok])
            for i in range(2):
                for h2 in range(2):
                    hs = slice(64 * h2, 64 * h2 + 64)
                    for mt in range(2):
                        ps = psum()
                        P.t(lambda e, ps=ps, hs=hs, mt=mt, i=i: e.matmul(ps[:, :], lhsT=mkT[hs, i, mt * 128:(mt + 1) * 128], rhs=aq[hs, i, :],
                                                                         start=True, stop=True), r=[mkT.tok, aq.tok], w=[ps.tok])
                        et = ett[i][h2 * 2 + mt]
                        P.a(lambda e, ps=ps, et=et: e.activation(out=et.t, in_=ps[:, :], func=AF.Exp, scale=0.125), r=[ps.tok], w=[et.tok])
                pso = banks[6] if i == 0 else psum()
                psd = banks[7] if i == 0 else psum()
                recd = F[0] if i == 0 else F[5]
                n = 0
                for h2 in range(2):
                    for mt in range(2):
                        et = ett[i][h2 * 2 + mt]
                        P.t(lambda e, h2=h2, mt=mt, i=i, n=n, et=et, pso=pso: e.matmul(pso[:, :], lhsT=mvpad[:, mt, i, h2, :], rhs=et.t,
                                                                                     start=(n == 0), stop=(n == 3)), r=[mvpad.tok, et.tok], w=[pso.tok])
                        n += 1
                n = 0
                for h2 in range(2):
                    for mt in range(2):
                        et = ett[i][h2 * 2 + mt]
                        P.t(lambda e, h2=h2, mt=mt, n=n, et=et, psd=psd: e.matmul(psd[:, :], lhsT=onespad[:, h2, :], rhs=et.t,
                                                                                start=(n == 0), stop=(n == 3)), r=[onespad.tok, et.tok], w=[psd.tok])
                        n += 1
                P.a(lambda e, psd=psd, recd=recd: e.activation(out=recd[:], in_=psd[:, :], func=AF.Ln), r=[psd.tok], w=[recd.tok])
                P.a(lambda e, recd=recd: e.activation(out=recd[:], in_=recd[:], func=AF.Exp, scale=-1.0), r=[recd.tok], w=[recd.tok])
                P.v(lambda e, pso=pso, recd=recd: e.tensor_tensor(out=recd[:], in0=pso[:, :], in1=recd[:], op=ALU.mult), r=[pso.tok, recd.tok], w=[recd.tok])
                P.v(lambda e, i=i, recd=recd: e.tensor_tensor(out=mixp[:, i, :], in0=recd[:], in1=ag[:, i, :], op=ALU.mult),
                    r=[recd.tok, ag.tok], w=[mixp.tok[i]])
            outproj_part(wo, blk)
        if SAMPLE >= 3:
            xa_sample(l, wG, wo)


    d_s5par = din("s5par", [128, DEPTH * 3 * 8])
    d_s5b = din("s5b", [DEPTH, 128, 2 * 8 * 16])
    d_s5c = din("s5c", [DEPTH, 128, 2 * 8 * 16])
    d_s5d = din("s5d", [128, DEPTH * 2])
    d_wglu = din("w_glu", [DEPTH, 256, 256])
    o_s5 = dout("o_s5", [DEPTH, 128, 2, 8])
    s5par = sb("s5par", [128, DEPTH, 3, 8])
    s5b = sb("s5b", [128, 2, 8, 16])
    s5c = sb("s5c", [128, 2, 8, 16])
    s5d = sb("s5d", [128, DEPTH, 2])
    P.dma("sync", s5par[:], d_s5par.rearrange("p (l w g) -> p l w g", l=DEPTH, w=3), w=[s5par.tok])
    P.dma("sync", s5d[:], d_s5d.rearrange("p (l f) -> p l f", l=DEPTH), w=[s5d.tok])
    sA = sb("sA", [128, 6, 8])
    sK = sb("sK", [128, 5, 8, 9])
    sKi = sb("sKi", [128, 8, 9], I32)
    sBB = sb("sBB", [128, 2, 8, 16])
    sE = sb("sE", [128, 4, 9, 16])
    sT = Buf(F[9].t[:, 0:288].rearrange("p (a k c) -> p a k c", a=2, k=9), "sT")
    sT.tok = F[9].tok
    s5o = sb("s5o", [128, 2, 8])
    d_sx0 = din("sx0", [DEPTH, 128, 2 * 8 * NS])
    o_s5s = dout("o_s5s", [DEPTH, 128, 2 * 8 * NS])
    xs0 = sb("xs0", [128, 2, 8, NS])
    xs0b = sb("xs0b", [128, 2, 8, NS], BF16)
    usT = sb("usT", [128, 2, NS], BF16)
    sgs = sb("sgs", [128, 2, NS])
    ysacc = sb("ysacc", [128, 2, NS])
    sst = sb("sst", [128, 4, NS])

    def s5_phase(l):
        W = d_win[l]
        wF = wload([(W, 2312, 2824, 0)])
        wo = woload(l, 2)
        P.dma("sync", s5b[:], d_s5b[l].rearrange("p (r g c) -> p r g c", r=2, g=8), w=[s5b.tok])
        P.dma("sync", s5c[:], d_s5c[l].rearrange("p (r g c) -> p r g c", r=2, g=8), w=[s5c.tok])
        uT = [ET, VA]
        pfb = Pf.t[:].rearrange("p a b -> p (a b)").bitcast(BF16)

        def sgp(ft, blk):
            ix = ft * 4 + blk
            if ix < 4:
                return pfb[:, ix * TB:(ix + 1) * TB], Pf.tok
            if ix < 6:
                return P2[:, ix - 4, :], P2.tok
            return sq2[ix - 6][:, :], sq2[ix - 6].tok
        uTap = [ET.t[:].rearrange("p a b -> p (a b)"), VA.t[:].rearrange("p a b c d -> p (a b c d)")]
        for blk in range(NBLK):
            for ft in range(2):
                ps = fm_group(wF, ft * 128, blk)
                dsti = AP(uT[ft].t, blk * 64, [[T, 128], [1, 64], [256, 8]])
                P.a(lambda e, ps=ps, dsti=dsti: e.activation(out=dsti, in_=ps[:, :].rearrange("p (j s) -> p j s", s=8), func=AF.Copy),
                    r=[ps.tok], w=[uT[ft].tok])
            for ft in range(2):
                ps = fm_group(wF, 256 + ft * 128, blk)
                sga, sgtok = sgp(ft, blk)
                P.a(lambda e, ps=ps, sga=sga: e.activation(out=sga, in_=ps[:, :], func=AF.Silu), r=[ps.tok], w=[sgtok])
        if SAMPLE >= 4:
            for ft in range(2):
                ps = fm_sample(wF, ft * 128)
                P.a(lambda e, ps=ps, ft=ft: e.activation(out=usT[:, ft, :], in_=ps[:, 0:NS], func=AF.Copy), r=[ps.tok], w=[usT.tok])
                ps = fm_sample(wF, 256 + ft * 128)
                P.a(lambda e, ps=ps, ft=ft: e.activation(out=sgs[:, ft, :], in_=ps[:, 0:NS], func=AF.Silu), r=[ps.tok], w=[sgs.tok])
            P.dma("sync", xs0[:].rearrange("p a b c -> p (a b c)"), d_sx0[l], w=[xs0.tok])
            P.v(lambda e: e.tensor_copy(out=xs0b[:], in_=xs0[:]), r=[xs0.tok], w=[xs0b.tok])
            P.v(lambda e: e.memset(ysacc[:], 0.0), w=[ysacc.tok])
        are, aim, ldt = s5par[:, l, 0, :], s5par[:, l, 1, :], s5par[:, l, 2, :]
        dtc, ardt, th, fre, fim, tm8 = (sA[:, j, :] for j in range(6))
        P.a(lambda e: e.activation(out=dtc, in_=ldt, func=AF.Exp), r=[s5par.tok], w=[sA.tok])
        P.v(lambda e: e.tensor_tensor(out=ardt, in0=are, in1=dtc, op=ALU.mult), r=[s5par.tok, sA.tok], w=[sA.tok])
        P.v(lambda e: e.tensor_tensor(out=th, in0=aim, in1=dtc, op=ALU.mult), r=[s5par.tok, sA.tok], w=[sA.tok])
        P.v(lambda e: e.tensor_scalar(out=tm8, in0=th, scalar1=8.0, scalar2=None, op0=ALU.mult), r=[sA.tok], w=[sA.tok])
        kg = AP(cst["kgrid"].t, 0, [[9, 128], [0, 8], [1, 9]])
        arg, ang, Pr, Pi, scr = (sK[:, j, :, :] for j in range(5))
        P.v(lambda e: e.tensor_tensor(out=arg, in0=AP(sA.t, 8, [[48, 128], [1, 8], [0, 9]]), in1=kg, op=ALU.mult), r=[sA.tok, cst["kgrid"].tok], w=[sK.tok])
        P.a(lambda e: e.activation(out=arg, in_=arg, func=AF.Exp), r=[sK.tok], w=[sK.tok])
        P.v(lambda e: e.tensor_tensor(out=ang, in0=AP(sA.t, 16, [[48, 128], [1, 8], [0, 9]]), in1=kg, op=ALU.mult), r=[sA.tok, cst["kgrid"].tok], w=[sK.tok])
        sincos(ang, sK.tok, sKi[:], scr, sKi.tok, Pi, sK.tok, Pr, sK.tok)
        P.v(lambda e: e.tensor_tensor(out=Pr, in0=Pr, in1=arg, op=ALU.mult), r=[sK.tok], w=[sK.tok])
        P.v(lambda e: e.tensor_tensor(out=Pi, in0=Pi, in1=arg, op=ALU.mult), r=[sK.tok], w=[sK.tok])
        nr, ni, den = scr[:, :, 0], scr[:, :, 1], scr[:, :, 2]
        t_a, t_b = scr[:, :, 3], scr[:, :, 4]
        P.v(lambda e: e.tensor_scalar(out=nr, in0=sK[:, 2, :, 1], scalar1=-1.0, scalar2=None, op0=ALU.add), r=[sK.tok], w=[sK.tok])
        P.v(lambda e: e.tensor_copy(out=ni, in_=sK[:, 3, :, 1]), r=[sK.tok], w=[sK.tok])
        P.v(lambda e: e.tensor_tensor(out=den, in0=are, in1=are, op=ALU.mult), r=[s5par.tok], w=[sK.tok])
        P.v(lambda e: e.tensor_tensor(out=t_a, in0=aim, in1=aim, op=ALU.mult), r=[s5par.tok], w=[sK.tok])
        P.v(lambda e: e.tensor_tensor(out=den, in0=den, in1=t_a, op=ALU.add), r=[sK.tok], w=[sK.tok])
        P.v(lambda e: e.reciprocal(out=den, in_=den), r=[sK.tok], w=[sK.tok])
        P.v(lambda e: e.tensor_tensor(out=t_a, in0=nr, in1=are, op=ALU.mult), r=[sK.tok, s5par.tok], w=[sK.tok])
        P.v(lambda e: e.tensor_tensor(out=t_b, in0=ni, in1=aim, op=ALU.mult), r=[sK.tok, s5par.tok], w=[sK.tok])
        P.v(lambda e: e.tensor_tensor(out=t_a, in0=t_a, in1=t_b, op=ALU.add), r=[sK.tok], w=[sK.tok])
        P.v(lambda e: e.tensor_tensor(out=fre, in0=t_a, in1=den, op=ALU.mult), r=[sK.tok], w=[sA.tok])
        P.v(lambda e: e.tensor_tensor(out=t_a, in0=ni, in1=are, op=ALU.mult), r=[sK.tok, s5par.tok], w=[sK.tok])
        P.v(lambda e: e.tensor_tensor(out=t_b, in0=nr, in1=aim, op=ALU.mult), r=[sK.tok, s5par.tok], w=[sK.tok])
        P.v(lambda e: e.tensor_tensor(out=t_a, in0=t_a, in1=t_b, op=ALU.subtract), r=[sK.tok], w=[sK.tok])
        P.v(lambda e: e.tensor_tensor(out=fim, in0=t_a, in1=den, op=ALU.mult), r=[sK.tok], w=[sA.tok])
        freb = AP(sA.t, 24, [[48, 128], [1, 8], [0, 16]])
        fimb = AP(sA.t, 32, [[48, 128], [1, 8], [0, 16]])
        bre, bim = s5b[:, 0, :, :], s5b[:, 1, :, :]
        bbre, bbim = sBB[:, 0, :, :], sBB[:, 1, :, :]
        tq = F[9].t[:, 0:128].rearrange("p (g c) -> p g c", g=8)
        P.v(lambda e: e.tensor_tensor(out=bbre, in0=bre, in1=freb, op=ALU.mult), r=[s5b.tok, sA.tok], w=[sBB.tok])
        P.v(lambda e: e.tensor_tensor(out=tq, in0=bim, in1=fimb, op=ALU.mult), r=[s5b.tok, sA.tok], w=[sT.tok])
        P.v(lambda e: e.tensor_tensor(out=bbre, in0=bbre, in1=tq, op=ALU.subtract), r=[sBB.tok, sT.tok], w=[sBB.tok])
        P.v(lambda e: e.tensor_tensor(out=bbim, in0=bim, in1=freb, op=ALU.mult), r=[s5b.tok, sA.tok], w=[sBB.tok])
        P.v(lambda e: e.tensor_tensor(out=tq, in0=bre, in1=fimb, op=ALU.mult), r=[s5b.tok, sA.tok], w=[sT.tok])
        P.v(lambda e: e.tensor_tensor(out=bbim, in0=bbim, in1=tq, op=ALU.add), r=[sBB.tok, sT.tok], w=[sBB.tok])

        W0, W1b, WZ = wch[(wrr[0]) % NW], wch[(wrr[0] + 1) % NW], wF
        w0f = W0.t[:].rearrange("p a b -> p (a b)")
        w1f = W1b.t[:].rearrange("p a b -> p (a b)")
        wzf = WZ.t[:].rearrange("p a b -> p (a b)")
        ptok = [Tok("ptwA"), Tok("ptwB")]
        ytok = [W1b.tok, WZ.tok]
        ptw = [w0f[:, 0:2048], w0f[:, 2048:4096]]
        ymw = [w1f[:, 0:2304], wzf[:, 0:2304]]
        PTp = [[ptw[p_][:, 0:1024].rearrange("p (k c) -> p k c", k=8), ptw[p_][:, 1024:2048].rearrange("p (k c) -> p k c", k=8)] for p_ in range(2)]
        Ymp = [[ymw[p_][:, 0:1152].rearrange("p (k c) -> p k c", k=9), ymw[p_][:, 1152:2304].rearrange("p (k c) -> p k c", k=9)] for p_ in range(2)]
        wglu = Buf(w1f[:, 2304:2816].rearrange("p (k c) -> p k c", k=2), "wglu")
        wglu.tok = W1b.tok
        P.dma("gpsimd", wglu.t, d_wglu[l].rearrange("(k p) c -> p k c", p=128), w=[wglu.tok])
        first_use = [True, True]
        xprev = P0
        BDm = P1.t[:].rearrange("p a b -> p (a b)")[:, 0:1024].rearrange("p (k c) -> p k c", k=8)
        rot[0] = [6, 7]
        yacc = banks[0:4]
        bdacc = banks[4:6]
        cs_t, an_t, vt_t, wt_t, xt_t, t1_t, t2_t, nf_t = F[0], F[1], F[2], F[3], F[4], F[5], F[6], F[7]
        for ft in range(2):
            for bk in list(yacc) + list(bdacc):
                P.t(lambda e, bk=bk: e.matmul(bk[:, :], lhsT=zeros_b[:, 0:128], rhs=hnT[:, 0, 0:512], start=True, stop=False), r=[zeros_b.tok, htok[0]], w=[bk.tok])
            for pp in range(4):
                pi_ = ft * 4 + pp
                par = pi_ % 2
                PT, W1, Ym = PTp[par], PTp[par], Ymp[par]
                PTK, YTK = ptok[par], ytok[par]
                Prb = AP(sK.t, 2 * 72 + pi_ * 9, [[360, 128], [1, 9], [0, 16]])
                Pib = AP(sK.t, 3 * 72 + pi_ * 9, [[360, 128], [1, 9], [0, 16]])
                bbr = AP(sBB.t, pi_ * 16, [[256, 128], [0, 9], [1, 16]])
                bbi = AP(sBB.t, 128 + pi_ * 16, [[256, 128], [0, 9], [1, 16]])
                crb = AP(s5c.t, 0 * 128 + pi_ * 16, [[256, 128], [0, 9], [1, 16]])
                cib = AP(s5c.t, 1 * 128 + pi_ * 16, [[256, 128], [0, 9], [1, 16]])
                Ere, Eim, CAre, CAim = (sE[:, j, :, :] for j in range(4))
                ta, tb = sT[:, 0, :, :], sT[:, 1, :, :]
                for (o_, x1, y1, x2, y2, op_) in ((Ere, Prb, bbr, Pib, bbi, ALU.subtract), (Eim, Prb, bbi, Pib, bbr, ALU.add),
                                                   (CAre, Prb, crb, Pib, cib, ALU.subtract), (CAim, Pib, crb, Prb, cib, ALU.add)):
                    P.v(lambda e, o_=o_, x1=x1, y1=y1: e.tensor_tensor(out=o_, in0=x1, in1=y1, op=ALU.mult), r=[sK.tok, sBB.tok, s5c.tok], w=[sE.tok])
                    P.v(lambda e, x2=x2, y2=y2: e.tensor_tensor(out=ta, in0=x2, in1=y2, op=ALU.mult), r=[sK.tok, sBB.tok, s5c.tok], w=[sT.tok])
                    P.v(lambda e, o_=o_, op_=op_: e.tensor_tensor(out=o_, in0=o_, in1=ta, op=op_), r=[sE.tok, sT.tok], w=[sE.tok])
                P.g(lambda e, par=par: e.memset(ptw[par], 0.0), w=[PTK] + ([W0.tok] if first_use[par] else []))
                first_use[par] = False
                P.g(lambda e, par=par: e.memset(ymw[par], 0.0), w=[YTK])
                for g2 in range(2):
                    rs_ = slice(64 * g2, 64 * g2 + 64)
                    c0 = (2 * pp + g2) * 16
                    for ri in range(2):
                        P.a(lambda e, rs_=rs_, c0=c0, ri=ri: e.activation(out=PT[ri][rs_, :, c0:c0 + 16], in_=sE[rs_, ri, 0:8, :], func=AF.Copy), r=[sE.tok], w=[PTK])
                    P.a(lambda e, rs_=rs_, c0=c0: e.activation(out=Ym[0][rs_, :, c0:c0 + 16], in_=sE[rs_, 2, :, :], func=AF.Copy), r=[sE.tok], w=[YTK])
                    P.a(lambda e, rs_=rs_, c0=c0: e.activation(out=Ym[1][rs_, :, c0:c0 + 16], in_=sE[rs_, 3, :, :], func=AF.Copy, scale=-1.0), r=[sE.tok], w=[YTK])
                for k in range(8):
                    bk = bdacc[k // 4]
                    for ri in range(2):
                        P.t(lambda e, k=k, ri=ri, bk=bk, pp=pp: e.matmul(bk[:, (k % 4) * 128:(k % 4) * 128 + 128], lhsT=PT[ri][:, 0, :], rhs=Ym[ri][:, k, :],
                                                                         start=False, stop=(pp == 3 and ri == 1 and k % 4 == 3)),
                            r=[PTK, YTK], w=[bk.tok])
                for ri in range(2):
                    pst = psum()
                    pstb = pst.t[:].bitcast(BF16)
                    for k in range(8):
                        P.t(lambda e, pstb=pstb, ri=ri, k=k: e.transpose(pstb[:, k * 128:(k + 1) * 128], PT[ri][:, k, :], ident_b[:]),
                            r=[PTK, ident_b.tok], w=[pst.tok])
                    P.a(lambda e, pstb=pstb, ri=ri: e.activation(out=W1[ri], in_=pstb[:, :].rearrange("p (k c) -> p k c", k=8), func=AF.Copy),
                        r=[pst.tok], w=[PTK])
                if SAMPLE >= 4:
                    pvs = psum()
                    for ri in range(2):
                        P.t(lambda e, ri=ri, pvs=pvs, ft=ft: e.matmul(pvs[:, ri * NS:(ri + 1) * NS], lhsT=W1[ri][:, 0, :], rhs=usT[:, ft, :], start=True, stop=True),
                            r=[PTK, usT.tok], w=[pvs.tok])
                    pr1 = sK[:, 2, pi_, 1:2]
                    pi1 = sK[:, 3, pi_, 1:2]
                    x0r, x0i = xs0[:, 0, pi_, :], xs0[:, 1, pi_, :]
                    ta_, tb_ = sst[:, 0, :], sst[:, 1, :]
                    P.v(lambda e: e.tensor_scalar(out=ta_, in0=x0r, scalar1=pr1, scalar2=None, op0=ALU.mult), r=[xs0.tok, sK.tok], w=[sst.tok])
                    P.v(lambda e: e.tensor_scalar(out=tb_, in0=x0i, scalar1=pi1, scalar2=None, op0=ALU.mult), r=[xs0.tok, sK.tok], w=[sst.tok])
                    P.v(lambda e: e.tensor_tensor(out=ta_, in0=ta_, in1=tb_, op=ALU.subtract), r=[sst.tok], w=[sst.tok])
                    P.v(lambda e, pvs=pvs: e.tensor_tensor(out=sst[:, 2, :], in0=ta_, in1=pvs[:, 0:NS], op=ALU.add), r=[sst.tok, pvs.tok], w=[sst.tok])
                    P.v(lambda e: e.tensor_scalar(out=ta_, in0=x0i, scalar1=pr1, scalar2=None, op0=ALU.mult), r=[xs0.tok, sK.tok], w=[sst.tok])
                    P.v(lambda e: e.tensor_scalar(out=tb_, in0=x0r, scalar1=pi1, scalar2=None, op0=ALU.mult), r=[xs0.tok, sK.tok], w=[sst.tok])
                    P.v(lambda e: e.tensor_tensor(out=ta_, in0=ta_, in1=tb_, op=ALU.add), r=[sst.tok], w=[sst.tok])
                    P.v(lambda e, pvs=pvs, pi_=pi_: e.tensor_tensor(out=xs0[:, 1, pi_, :], in0=ta_, in1=pvs[:, NS:2 * NS], op=ALU.add), r=[sst.tok, pvs.tok], w=[xs0.tok])
                    P.v(lambda e, pi_=pi_: e.tensor_copy(out=xs0[:, 0, pi_, :], in_=sst[:, 2, :]), r=[sst.tok], w=[xs0.tok])
                    pys = psum()
                    for ri in range(2):
                        P.t(lambda e, ri=ri, pys=pys, pi_=pi_: e.matmul(pys[:, 0:NS], lhsT=Ym[ri][:, 1, :], rhs=xs0b[:, ri, pi_, :], start=(ri == 0), stop=(ri == 1)),
                            r=[YTK, xs0b.tok], w=[pys.tok])
                    P.v(lambda e, pys=pys, ft=ft: e.tensor_tensor(out=ysacc[:, ft, :], in0=ysacc[:, ft, :], in1=pys[:, 0:NS], op=ALU.add), r=[pys.tok, ysacc.tok], w=[ysacc.tok])
                psv = psum()
                for ri in range(2):
                    for s_ in range(8):
                        usl = AP(uT[ft].t, s_ * 256, [[T, 128], [1, 256]])
                        P.t(lambda e, ri=ri, s_=s_, usl=usl, psv=psv: e.matmul(psv[:, ri * 256:(ri + 1) * 256], lhsT=W1[ri][:, 7 - s_, :], rhs=usl,
                                                                              start=(s_ == 0), stop=(s_ == 7)), r=[PTK, uT[ft].tok], w=[psv.tok])
                P.v(lambda e, pi_=pi_: e.tensor_scalar(out=an_t[:, 0:128], in0=cst["l1"][:], scalar1=-1.0, scalar2=sA[:, 5, pi_:pi_ + 1], op0=ALU.add, op1=ALU.mult),
                    r=[cst["l1"].tok, sA.tok], w=[an_t.tok])
                P.v(lambda e, pi_=pi_: e.tensor_scalar(out=an_t[:, 128:256], in0=cst["l1"][:], scalar1=127.0, scalar2=sA[:, 5, pi_:pi_ + 1], op0=ALU.add, op1=ALU.mult),
                    r=[cst["l1"].tok, sA.tok], w=[an_t.tok])
                cb, sb_ = cs_t[:, 0:256], cs_t[:, 256:512]
                sincos(an_t[:, 0:256], an_t.tok, nf_t.t[:, 0:256].bitcast(I32), nf_t[:, 0:256], nf_t.tok, sb_, cs_t.tok, cb, cs_t.tok)
                vr, vi = psv[:, 0:256], psv[:, 256:512]
                vtr, vti = vt_t[:, 0:256], vt_t[:, 256:512]
                t1, t2 = t1_t[:, 0:256], t2_t[:, 0:256]
                P.v(lambda e: e.tensor_tensor(out=t1, in0=vr, in1=cb, op=ALU.mult), r=[psv.tok, cs_t.tok], w=[t1_t.tok])
                P.v(lambda e: e.tensor_tensor(out=t2, in0=vi, in1=sb_, op=ALU.mult), r=[psv.tok, cs_t.tok], w=[t2_t.tok])
                P.v(lambda e: e.tensor_tensor(out=vtr, in0=t1, in1=t2, op=ALU.add), r=[t1_t.tok, t2_t.tok], w=[vt_t.tok])
                P.v(lambda e: e.tensor_tensor(out=t1, in0=vi, in1=cb, op=ALU.mult), r=[psv.tok, cs_t.tok], w=[t1_t.tok])
                P.v(lambda e: e.tensor_tensor(out=t2, in0=vr, in1=sb_, op=ALU.mult), r=[psv.tok, cs_t.tok], w=[t2_t.tok])
                P.v(lambda e: e.tensor_tensor(out=vti, in0=t1, in1=t2, op=ALU.subtract), r=[t1_t.tok, t2_t.tok], w=[vt_t.tok])
                rho = AP(sK.t, pi_ * 9 + 8, [[360, 128], [0, 256]])
                wr_, wi_ = wt_t[:, 0:256], wt_t[:, 256:512]
                P.v(lambda e: e.tensor_tensor_scan(out=wr_, data0=rho, data1=vtr, initial=0.0, op0=ALU.mult, op1=ALU.add), r=[sK.tok, vt_t.tok], w=[wt_t.tok])
                P.v(lambda e: e.tensor_tensor_scan(out=wi_, data0=rho, data1=vti, initial=0.0, op0=ALU.mult, op1=ALU.add), r=[sK.tok, vt_t.tok], w=[wt_t.tok])
                xr_, xi_ = xt_t[:, 0:256], xt_t[:, 256:512]
                P.v(lambda e: e.tensor_tensor(out=t1, in0=wr_, in1=cb, op=ALU.mult), r=[wt_t.tok, cs_t.tok], w=[t1_t.tok])
                P.v(lambda e: e.tensor_tensor(out=t2, in0=wi_, in1=sb_, op=ALU.mult), r=[wt_t.tok, cs_t.tok], w=[t2_t.tok])
                P.v(lambda e: e.tensor_tensor(out=xr_, in0=t1, in1=t2, op=ALU.subtract), r=[t1_t.tok, t2_t.tok], w=[xt_t.tok])
                P.v(lambda e: e.tensor_tensor(out=t1, in0=wr_, in1=sb_, op=ALU.mult), r=[wt_t.tok, cs_t.tok], w=[t1_t.tok])
                P.v(lambda e: e.tensor_tensor(out=t2, in0=wi_, in1=cb, op=ALU.mult), r=[wt_t.tok, cs_t.tok], w=[t2_t.tok])
                P.v(lambda e: e.tensor_tensor(out=xi_, in0=t1, in1=t2, op=ALU.add), r=[t1_t.tok, t2_t.tok], w=[xt_t.tok])
                P.v(lambda e, pi_=pi_: e.tensor_copy(out=s5o[:, :, pi_], in_=AP(xt_t.t, 255, [[TB, 128], [256, 2]])), r=[xt_t.tok], w=[s5o.tok])
                P.v(lambda e: e.memset(xprev[:, :, 0:1], 0.0), w=[xprev.tok])
                P.a(lambda e: e.activation(out=xprev[:, :, 1:256], in_=AP(xt_t.t, 0, [[TB, 128], [256, 2], [1, 255]]), func=AF.Copy), r=[xt_t.tok], w=[xprev.tok])
                for t8 in range(8):
                    ya = yacc[t8 // 2]
                    for ri in range(2):
                        P.t(lambda e, t8=t8, ri=ri, ya=ya, pp=pp: e.matmul(ya[:, (t8 % 2) * 256:(t8 % 2) * 256 + 256], lhsT=Ym[ri][:, t8 + 1, :], rhs=xprev[:, ri, 0:256],
                                                                           start=False, stop=False), r=[YTK, xprev.tok], w=[ya.tok])
            for k in range(8):
                bk = bdacc[k // 4]
                if k == 0:
                    P.v(lambda e, bk=bk, ft=ft: e.scalar_tensor_tensor(out=BDm[:, 0, :], in0=cst["ident"][:], scalar=s5d[:, l, ft:ft + 1], in1=bk[:, 0:128],
                                                                       op0=ALU.mult, op1=ALU.add), r=[bk.tok, cst["ident"].tok, s5d.tok], w=[P1.tok])
                else:
                    P.v(lambda e, bk=bk, k=k: e.tensor_copy(out=BDm[:, k, :], in_=bk[:, (k % 4) * 128:(k % 4) * 128 + 128]), r=[bk.tok], w=[P1.tok])
            if SAMPLE >= 4:
                pbs = psum()
                P.t(lambda e, pbs=pbs, ft=ft: e.matmul(pbs[:, 0:NS], lhsT=BDm[:, 0, :], rhs=usT[:, ft, :], start=True, stop=True), r=[P1.tok, usT.tok], w=[pbs.tok])
                P.v(lambda e, pbs=pbs, ft=ft: e.tensor_tensor(out=ysacc[:, ft, :], in0=ysacc[:, ft, :], in1=pbs[:, 0:NS], op=ALU.add), r=[pbs.tok, ysacc.tok], w=[ysacc.tok])
            for t8 in range(8):
                ya = yacc[t8 // 2]
                for s_ in range(t8 + 1):
                    usl = AP(uT[ft].t, s_ * 256, [[T, 128], [1, 256]])
                    P.t(lambda e, t8=t8, s_=s_, ya=ya, usl=usl: e.matmul(ya[:, (t8 % 2) * 256:(t8 % 2) * 256 + 256], lhsT=BDm[:, t8 - s_, :], rhs=usl,
                                                                         start=False, stop=(s_ == t8 and t8 % 2 == 1)), r=[P1.tok, uT[ft].tok], w=[ya.tok])
            for t8 in range(8):
                ya = yacc[t8 // 2]
                ysl = ya[:, (t8 % 2) * 256:(t8 % 2) * 256 + 256]
                g1, g2_ = t1_t[:, 0:256], t2_t[:, 0:256]
                P.a(lambda e, ysl=ysl: e.activation(out=g1, in_=ysl, func=AF.Square), r=[ya.tok], w=[t1_t.tok])
                P.v(lambda e: e.tensor_scalar(out=g1, in0=g1, scalar1=0.044715, scalar2=1.0, op0=ALU.mult, op1=ALU.add), r=[t1_t.tok], w=[t1_t.tok])
                P.v(lambda e, ysl=ysl: e.tensor_tensor(out=g1, in0=g1, in1=ysl, op=ALU.mult), r=[t1_t.tok, ya.tok], w=[t1_t.tok])
                P.a(lambda e: e.activation(out=g2_, in_=g1, func=AF.Tanh, scale=0.79788456), r=[t1_t.tok], w=[t2_t.tok])
                usl = AP(uT[ft].t, t8 * 256, [[T, 128], [1, 256]])
                P.v(lambda e, ysl=ysl, usl=usl: e.scalar_tensor_tensor(out=usl, in0=g2_, scalar=1.0, in1=ysl, op0=ALU.add, op1=ALU.mult), r=[t2_t.tok, ya.tok], w=[uT[ft].tok])
        rot[0] = [0, 1, 2, 3, 4, 5]
        P.g(lambda e: e.memset(w0f[:, 0:2], 0.0), r=ptok, w=[W0.tok] + ptok)
        P.dma("sync", o_s5[l], s5o[:], r=[s5o.tok], is_output=True)
        for blk in range(NBLK):
            bs = slice(blk * TB, (blk + 1) * TB)
            for fo_ in range(2):
                ps = psum()
                for fi_ in range(2):
                    rsj = AP(uT[fi_].t, blk * 64, [[T, 128], [256, 8], [1, 64]])
                    P.t(lambda e, ps=ps, fi_=fi_, fo_=fo_, rsj=rsj: e.matmul(ps[:, :].rearrange("p (s j) -> p s j", s=8), lhsT=wglu[:, fi_, fo_ * 128:(fo_ + 1) * 128], rhs=rsj,
                                                                             start=(fi_ == 0), stop=(fi_ == 1)), r=[wglu.tok, uT[fi_].tok], w=[ps.tok])
                sg_ = F[10]
                sgn = AP(sg_.t, 0, [[TB, 128], [1, 8], [8, 64]])
                P.a(lambda e, ps=ps, sgn=sgn: e.activation(out=sgn, in_=ps[:, :].rearrange("p (s j) -> p s j", s=8), func=AF.Tanh, scale=0.25), r=[ps.tok], w=[sg_.tok])
                syn = AP(uT[fo_].t, blk * 64, [[T, 128], [1, 64], [256, 8]])
                P.v(lambda e, syn=syn: e.scalar_tensor_tensor(out=sg_[:].rearrange("p (j s) -> p j s", s=8), in0=sg_[:].rearrange("p (j s) -> p j s", s=8), scalar=1.0, in1=syn,
                                                            op0=ALU.add, op1=ALU.mult), r=[sg_.tok, uT[fo_].tok], w=[sg_.tok])
                sga, sgtok = sgp(fo_, blk)
                P.v(lambda e, fo_=fo_, sga=sga: e.scalar_tensor_tensor(out=mixp[:, fo_, :], in0=sg_[:], scalar=0.25, in1=sga, op0=ALU.mult, op1=ALU.mult),
                    r=[sg_.tok, sgtok], w=[mixp.tok[fo_]])
            outproj_part(wo, blk)
        if SAMPLE >= 4:
            P.dma("sync", o_s5s[l], xs0[:].rearrange("p a b c -> p (a b c)"), r=[xs0.tok], is_output=True)
            ysf = ysacc[:].rearrange("p a n -> p (a n)")
            g1 = sst[:, 0:2, :].rearrange("p a n -> p (a n)")
            g2_ = sst[:, 2:4, :].rearrange("p a n -> p (a n)")
            P.a(lambda e: e.activation(out=g1, in_=ysf, func=AF.Square), r=[ysacc.tok], w=[sst.tok])
            P.v(lambda e: e.tensor_scalar(out=g1, in0=g1, scalar1=0.044715, scalar2=1.0, op0=ALU.mult, op1=ALU.add), r=[sst.tok], w=[sst.tok])
            P.v(lambda e: e.tensor_tensor(out=g1, in0=g1, in1=ysf, op=ALU.mult), r=[sst.tok, ysacc.tok], w=[sst.tok])
            P.a(lambda e: e.activation(out=g2_, in_=g1, func=AF.Tanh, scale=0.79788456), r=[sst.tok], w=[sst.tok])
            P.v(lambda e: e.scalar_tensor_tensor(out=ysf, in0=g2_, scalar=1.0, in1=ysf, op0=ALU.add, op1=ALU.mult), r=[sst.tok, ysacc.tok], w=[ysacc.tok])
            P.v(lambda e: e.tensor_copy(out=usT[:], in_=ysacc[:]), r=[ysacc.tok], w=[usT.tok])
            for fo_ in range(2):
                ps = psum()
                for fi_ in range(2):
                    P.t(lambda e, ps=ps, fi_=fi_, fo_=fo_: e.matmul(ps[:, 0:NS], lhsT=wglu[:, fi_, fo_ * 128:(fo_ + 1) * 128], rhs=usT[:, fi_, :],
                                                                    start=(fi_ == 0), stop=(fi_ == 1)), r=[wglu.tok, usT.tok], w=[ps.tok])
                P.a(lambda e, ps=ps, fo_=fo_: e.activation(out=sst[:, fo_, :], in_=ps[:, 0:NS], func=AF.Tanh, scale=0.25), r=[ps.tok], w=[sst.tok])
                P.v(lambda e, fo_=fo_: e.scalar_tensor_tensor(out=sst[:, fo_, :], in0=sst[:, fo_, :], scalar=1.0, in1=ysacc[:, fo_, :], op0=ALU.add, op1=ALU.mult),
                    r=[sst.tok, ysacc.tok], w=[sst.tok])
                P.v(lambda e, fo_=fo_: e.scalar_tensor_tensor(out=mixs[:, fo_, :], in0=sst[:, fo_, :], scalar=0.25, in1=sgs[:, fo_, :], op0=ALU.mult, op1=ALU.mult),
                    r=[sst.tok, sgs.tok], w=[mixs.tok])
            outproj_s(wo)


    d_xsT = din("xsT", [1024, NS])
    d_sret = din("sret", [DEPTH, 64, 4096])
    d_gns = din("gns", [64, DEPTH * 2 * 64])
    o_ysT = dout("o_ysT", [1024, NS])
    o_sret = dout("o_sret", [DEPTH, 64, 4096])
    xsT = sb("xsT", [128, 8, NS])
    P.dma("sync", xsT[:], d_xsT.rearrange("(k p) n -> p k n", p=128), w=[xsT.tok])
    hnsT = sb("hnsT", [128, 8, NS], BF16)
    gns = sb("gns", [64, DEPTH * 2 * 64])
    P.dma("sync", gns[:], d_gns, w=[gns.tok])
    qk4 = sb("qk4", [64, 5, 64])
    ropes = sb("ropes", [64, 3, 32])
    ropei = sb("ropei", [64, 32], I32)
    sm = sb("sm", [64, 16])
    osn = sb("osn", [64, 4, 64])
    xpad = sb("xpad", [64, 2, 64])
    mixs = sb("mixs", [128, 2, NS], BF16)
    gam = sb("gam", [64, 1])
    P.a(lambda e: e.activation(out=gam[:], in_=cst["lg64"][:], func=AF.Exp), r=[cst["lg64"].tok], w=[gam.tok])
    P.v(lambda e: e.tensor_scalar(out=ropes[:, 2, :], in0=cst["invrow"][0:64, :], scalar1=PAST, scalar2=None, op0=ALU.mult),
        r=[cst["invrow"].tok], w=[ropes.tok])
    sincos(ropes[:, 2, :], ropes.tok, ropei[:], ropei.t[:].bitcast(F32), ropei.tok, ropes[:, 1, :], ropes.tok, ropes[:, 0, :], ropes.tok)

    def tm_sample(wb, c0, n):
        ps = psum()
        for kt in range(8):
            P.t(lambda e, kt=kt: e.matmul(ps[0:NS, 0:n], lhsT=hnsT[:, kt, :], rhs=wb[:, kt, c0:c0 + n], start=(kt == 0), stop=(kt == 7)),
                r=[wb.tok, hnsT.tok], w=[ps.tok])
        return ps

    def fm_sample(wb, c0):
        ps = psum()
        for kt in range(8):
            P.t(lambda e, kt=kt: e.matmul(ps[:, 0:NS], lhsT=wb[:, kt, c0:c0 + 128], rhs=hnsT[:, kt, :], start=(kt == 0), stop=(kt == 7)),
                r=[wb.tok, hnsT.tok], w=[ps.tok])
        return ps

    def to_hn(pr_ap, prtok, nsec, dst, dtok):
        ps = psum()
        for h in range(4):
            rhs = AP(pr_ap.tensor, pr_ap.offset + h * 64, [list(pr_ap.ap[0]), [256, nsec], [1, 64]])
            P.t(lambda e, h=h, rhs=rhs: e.matmul(ps[0:64, 0:nsec * 64].rearrange("p (s d) -> p s d", s=nsec), lhsT=cst["sel"][:, h * 64:(h + 1) * 64], rhs=rhs,
                                                 start=(h == 0), stop=(h == 3)), r=[cst["sel"].tok, prtok], w=[ps.tok])
        P.a(lambda e: e.activation(out=dst, in_=ps[0:64, 0:nsec * 64].rearrange("p (s d) -> p s d", s=nsec), func=AF.Copy), r=[ps.tok], w=[dtok])

    def rope_s(x4, xtok, secs):
        cosb = AP(ropes.t, 0, [[96, 64], [0, secs], [1, 32]])
        sinb = AP(ropes.t, 32, [[96, 64], [0, secs], [1, 32]])
        x1, x2 = x4[:, 0:secs, 0:32], x4[:, 0:secs, 32:64]
        ta = osn[:, 0, :].rearrange("p (s d) -> p s d", s=2)[:, 0:secs, :]
        tb = osn[:, 1, :].rearrange("p (s d) -> p s d", s=2)[:, 0:secs, :]
        tc = osn[:, 2, :].rearrange("p (s d) -> p s d", s=2)[:, 0:secs, :]
        P.v(lambda e: e.tensor_tensor(out=ta, in0=x1, in1=cosb, op=ALU.mult), r=[xtok, ropes.tok], w=[osn.tok])
        P.v(lambda e: e.tensor_tensor(out=tb, in0=x2, in1=sinb, op=ALU.mult), r=[xtok, ropes.tok], w=[osn.tok])
        P.v(lambda e: e.tensor_tensor(out=ta, in0=ta, in1=tb, op=ALU.subtract), r=[osn.tok], w=[osn.tok])
        P.v(lambda e: e.tensor_tensor(out=tb, in0=x1, in1=sinb, op=ALU.mult), r=[xtok, ropes.tok], w=[osn.tok])
        P.v(lambda e: e.tensor_tensor(out=tc, in0=x2, in1=cosb, op=ALU.mult), r=[xtok, ropes.tok], w=[osn.tok])
        P.v(lambda e: e.tensor_tensor(out=x2, in0=tb, in1=tc, op=ALU.add), r=[osn.tok], w=[xtok])
        P.v(lambda e: e.tensor_copy(out=x1, in_=ta), r=[osn.tok], w=[xtok])

    def headnorm_s(o_ap, g_ap, gn_ap):
        mean, var = sm[:, 0:1], sm[:, 1:2]
        cen = osn[:, 1, :]
        P.v(lambda e: e.tensor_reduce(out=mean, in_=o_ap, axis=AX.X, op=ALU.add), r=[osn.tok], w=[sm.tok])
        P.v(lambda e: e.tensor_scalar(out=mean, in0=mean, scalar1=-1.0 / 64, scalar2=None, op0=ALU.mult), r=[sm.tok], w=[sm.tok])
        P.v(lambda e: e.tensor_scalar(out=cen, in0=o_ap, scalar1=mean, scalar2=None, op0=ALU.add), r=[osn.tok, sm.tok], w=[osn.tok])
        P.v(lambda e: e.tensor_tensor(out=osn[:, 2, :], in0=cen, in1=cen, op=ALU.mult), r=[osn.tok], w=[osn.tok])
        P.v(lambda e: e.tensor_reduce(out=var, in_=osn[:, 2, :], axis=AX.X, op=ALU.add), r=[osn.tok], w=[sm.tok])
        P.a(lambda e: e.activation(out=var, in_=var, func=AF.Ln, scale=1.0 / 64, bias=EPS), r=[sm.tok], w=[sm.tok])
        P.a(lambda e: e.activation(out=var, in_=var, func=AF.Exp, scale=-0.5), r=[sm.tok], w=[sm.tok])
        P.v(lambda e: e.scalar_tensor_tensor(out=cen, in0=cen, scalar=var, in1=gn_ap, op0=ALU.mult, op1=ALU.mult), r=[osn.tok, sm.tok, gns.tok], w=[osn.tok])
        P.v(lambda e: e.tensor_tensor(out=cen, in0=cen, in1=g_ap, op=ALU.mult), r=[osn.tok], w=[osn.tok])
        return cen

    def place_mix(y_ap):
        for i in range(2):
            yb = AP(y_ap.tensor, y_ap.offset, [list(y_ap.ap[0]), [0, 2], [1, 64]])
            mk2 = AP(cst["mpair"].t, i * 2, [[4, 64], [1, 2], [0, 64]])
            P.v(lambda e, yb=yb, mk2=mk2: e.tensor_tensor(out=xpad[:], in0=yb, in1=mk2, op=ALU.mult), r=[osn.tok, cst["mpair"].tok], w=[xpad.tok])
            ps = psum()
            P.t(lambda e, ps=ps: e.matmul(ps[:, 0:NS], lhsT=xpad[:].rearrange("p a b -> p (a b)"), rhs=cst["sel2"][:], start=True, stop=True),
                r=[xpad.tok, cst["sel2"].tok], w=[ps.tok])
            P.a(lambda e, ps=ps, i=i: e.activation(out=mixs[:, i, :], in_=ps[:, 0:NS], func=AF.Copy), r=[ps.tok], w=[mixs.tok])

    def outproj_s(wo):
        ps = psum()
        for dt_ in range(8):
            for kt in range(2):
                P.t(lambda e, kt=kt, dt_=dt_: e.matmul(ps[:, dt_ * NS:(dt_ + 1) * NS], lhsT=wo[:, kt, dt_ * 128:(dt_ + 1) * 128], rhs=mixs[:, kt, :],
                                                       start=(kt == 0), stop=(kt == 1)), r=[wo.tok, mixs.tok], w=[ps.tok])
        P.v(lambda e: e.tensor_tensor(out=xsT[:].rearrange("p k n -> p (k n)"), in0=xsT[:].rearrange("p k n -> p (k n)"), in1=ps[:, 0:8 * NS], op=ALU.add),
            r=[ps.tok, xsT.tok], w=[xsT.tok])

    def ret_sample(l, wA, wB, wo):
        pr = Pf.t[:].rearrange("p a b -> p (a b)")
        ps = tm_sample(wA, 0, 512)
        P.a(lambda e: e.activation(out=pr[0:NS, 0:512], in_=ps[0:NS, 0:512], func=AF.Copy), r=[ps.tok], w=[Pf.tok])
        ps = tm_sample(wB, 0, 512)
        P.a(lambda e: e.activation(out=pr[0:NS, 512:1024], in_=ps[0:NS, 0:512], func=AF.Copy), r=[ps.tok], w=[Pf.tok])
        to_hn(pr[0:NS, :], Pf.tok, 4, qk4[:, 0:4, :], qk4.tok)
        rope_s(qk4, qk4.tok, 2)
        q, k, v, g = (qk4[:, j, :] for j in range(4))
        P.v(lambda e: e.tensor_scalar(out=k, in0=k, scalar1=0.125, scalar2=None, op0=ALU.mult), r=[qk4.tok], w=[qk4.tok])
        P.a(lambda e: e.activation(out=osn[:, 3, :], in_=g, func=AF.Silu), r=[qk4.tok], w=[osn.tok])
        o = osn[:, 0, :]
        for j in range(8):
            S = F[j]
            P.dma("sync", S[0:64, :], d_sret[l][:, j * 512:(j + 1) * 512], w=[S.tok])
            tmpk = F[8 + j % 2]
            kb = AP(qk4.t, 1 * 64 + j * 8, [[320, 64], [1, 8], [0, 64]])
            vb = AP(qk4.t, 2 * 64, [[320, 64], [0, 8], [1, 64]])
            qb = AP(qk4.t, 0 * 64 + j * 8, [[320, 64], [0, 64], [1, 8]])
            P.v(lambda e, tmpk=tmpk, kb=kb, vb=vb: e.tensor_tensor(out=tmpk[0:64, :].rearrange("p (d x) -> p d x", d=8), in0=kb, in1=vb, op=ALU.mult),
                r=[qk4.tok], w=[tmpk.tok])
            P.v(lambda e, S=S, tmpk=tmpk: e.scalar_tensor_tensor(out=S[0:64, :], in0=S[0:64, :], scalar=gam[:, 0:1], in1=tmpk[0:64, :], op0=ALU.mult, op1=ALU.add),
                r=[S.tok, tmpk.tok, gam.tok], w=[S.tok])
            P.dma("sync", o_sret[l][:, j * 512:(j + 1) * 512], S[0:64, :], r=[S.tok], is_output=True)
            Sv = AP(S.t, 0, [[TB, 64], [1, 64], [64, 8]])
            P.v(lambda e, tmpk=tmpk, Sv=Sv, qb=qb: e.tensor_tensor(out=tmpk[0:64, :].rearrange("p (x d) -> p x d", d=8), in0=Sv, in1=qb, op=ALU.mult),
                r=[S.tok, qk4.tok], w=[tmpk.tok])
            if j == 0:
                P.v(lambda e, tmpk=tmpk: e.tensor_reduce(out=o, in_=tmpk[0:64, :].rearrange("p (x d) -> p x d", d=8), axis=AX.X, op=ALU.add),
                    r=[tmpk.tok], w=[osn.tok])
            else:
                P.v(lambda e, tmpk=tmpk: e.tensor_reduce(out=osn[:, 2, :], in_=tmpk[0:64, :].rearrange("p (x d) -> p x d", d=8), axis=AX.X, op=ALU.add),
                    r=[tmpk.tok], w=[osn.tok])
                P.v(lambda e: e.tensor_tensor(out=o, in0=o, in1=osn[:, 2, :], op=ALU.add), r=[osn.tok], w=[osn.tok])
        y = headnorm_s(o, osn[:, 3, :], gns[:, (l * 2 + 0) * 64:(l * 2 + 1) * 64])
        place_mix(y)
        outproj_s(wo)


    d_smlc = din("smlc", [DEPTH, 64, 4096])
    d_smln = din("smln", [DEPTH, 64, 64])
    d_smlm = din("smlm", [DEPTH, 64, 1])
    d_bifs = din("bifs", [64, DEPTH * 2])
    o_smlc = dout("o_smlc", [DEPTH, 64, 4096])
    o_smln = dout("o_smln", [DEPTH, 64, 64])
    o_smlm = dout("o_smlm", [DEPTH, 64, 1])
    bifs = sb("bifs", [64, DEPTH * 2])
    P.dma("sync", bifs[:], d_bifs, w=[bifs.tok])
    n0s = sb("n0s", [64, 2, 64])

    def ml_sample(l, wC, wD, wE, wo):
        pr = Pf.t[:].rearrange("p a b -> p (a b)")
        pr2 = P2.t[:].rearrange("p a b -> p (a b)").bitcast(F32)
        ps = tm_sample(wC, 0, 512)
        P.a(lambda e: e.activation(out=pr[0:NS, 0:512], in_=ps[0:NS, 0:512], func=AF.Copy), r=[ps.tok], w=[Pf.tok])
        ps = tm_sample(wD, 0, 512)
        P.a(lambda e: e.activation(out=pr[0:NS, 512:1024], in_=ps[0:NS, 0:512], func=AF.Copy), r=[ps.tok], w=[Pf.tok])
        ps = tm_sample(wE, 0, 264)
        P.a(lambda e: e.activation(out=pr2[0:NS, 0:264], in_=ps[0:NS, 0:264], func=AF.Copy), r=[ps.tok], w=[P2.tok])
        to_hn(pr[0:NS, :], Pf.tok, 4, qk4[:, 0:4, :], qk4.tok)
        to_hn(pr2[0:NS, 0:256], P2.tok, 1, qk4[:, 4:5, :], qk4.tok)
        psg = psum()
        for h in range(4):
            rhs = AP(P2.t, 0, [[1024, NS], [8, 2]]).bitcast(F32) if False else AP(pr2.tensor, pr2.offset + 256 + h, [list(pr2.ap[0])[0:1] + [NS], [4, 2]])
            P.t(lambda e, h=h, rhs=rhs: e.matmul(psg[0:64, 0:2], lhsT=cst["sel"][:, h * 64:(h + 1) * 64], rhs=rhs, start=(h == 0), stop=(h == 3)),
                r=[cst["sel"].tok, P2.tok], w=[psg.tok])
        q, k, v, og, g = (qk4[:, j, :] for j in range(5))
        ig, fz, lf, a_, mt_, wi, ws, emt, qk_, nq, den, scol, rc = (sm[:, j:j + 1] for j in range(2, 15))
        P.v(lambda e: e.tensor_tensor(out=sm[:, 2:4], in0=psg[0:64, 0:2], in1=bifs[:, l * 2:l * 2 + 2], op=ALU.add), r=[psg.tok, bifs.tok], w=[sm.tok])
        P.dma("sync", n0s[:, 0, :], d_smln[l], w=[n0s.tok])
        P.dma("sync", sm[:, 15:16], d_smlm[l], w=[sm.tok])
        m0 = sm[:, 15:16]
        P.a(lambda e: e.activation(out=lf, in_=fz, func=AF.Exp, scale=-1.0), r=[sm.tok], w=[sm.tok])
        P.a(lambda e: e.activation(out=lf, in_=lf, func=AF.Ln, bias=1.0), r=[sm.tok], w=[sm.tok])
        P.v(lambda e: e.tensor_tensor(out=a_, in0=m0, in1=lf, op=ALU.subtract), r=[sm.tok], w=[sm.tok])
        P.v(lambda e: e.tensor_tensor(out=mt_, in0=a_, in1=ig, op=ALU.max), r=[sm.tok], w=[sm.tok])
        P.v(lambda e: e.tensor_tensor(out=wi, in0=ig, in1=mt_, op=ALU.subtract), r=[sm.tok], w=[sm.tok])
        P.a(lambda e: e.activation(out=wi, in_=wi, func=AF.Exp), r=[sm.tok], w=[sm.tok])
        P.v(lambda e: e.tensor_tensor(out=ws, in0=a_, in1=mt_, op=ALU.subtract), r=[sm.tok], w=[sm.tok])
        P.a(lambda e: e.activation(out=ws, in_=ws, func=AF.Exp), r=[sm.tok], w=[sm.tok])
        P.a(lambda e: e.activation(out=emt, in_=mt_, func=AF.Exp, scale=-1.0), r=[sm.tok], w=[sm.tok])
        P.dma("sync", o_smlm[l], mt_, r=[sm.tok], is_output=True)
        P.v(lambda e: e.tensor_scalar(out=k, in0=k, scalar1=0.125, scalar2=None, op0=ALU.mult), r=[qk4.tok], w=[qk4.tok])
        P.v(lambda e: e.tensor_tensor(out=osn[:, 2, :], in0=q, in1=k, op=ALU.mult), r=[qk4.tok], w=[osn.tok])
        P.v(lambda e: e.tensor_reduce(out=qk_, in_=osn[:, 2, :], axis=AX.X, op=ALU.add), r=[osn.tok], w=[sm.tok])
        P.v(lambda e: e.tensor_tensor(out=osn[:, 2, :], in0=q, in1=n0s[:, 0, :], op=ALU.mult), r=[qk4.tok, n0s.tok], w=[osn.tok])
        P.v(lambda e: e.tensor_reduce(out=nq, in_=osn[:, 2, :], axis=AX.X, op=ALU.add), r=[osn.tok], w=[sm.tok])
        P.v(lambda e: e.tensor_tensor(out=scol, in0=qk_, in1=wi, op=ALU.mult), r=[sm.tok], w=[sm.tok])
        P.v(lambda e: e.tensor_tensor(out=den, in0=nq, in1=ws, op=ALU.mult), r=[sm.tok], w=[sm.tok])
        P.v(lambda e: e.tensor_tensor(out=den, in0=den, in1=scol, op=ALU.add), r=[sm.tok], w=[sm.tok])
        P.a(lambda e: e.activation(out=den, in_=den, func=AF.Abs), r=[sm.tok], w=[sm.tok])
        P.v(lambda e: e.tensor_tensor(out=den, in0=den, in1=emt, op=ALU.max), r=[sm.tok], w=[sm.tok])
        P.v(lambda e: e.reciprocal(out=rc, in_=den), r=[sm.tok], w=[sm.tok])
        P.v(lambda e: e.tensor_scalar(out=n0s[:, 1, :], in0=k, scalar1=wi, scalar2=None, op0=ALU.mult), r=[qk4.tok, sm.tok], w=[n0s.tok])
        P.v(lambda e: e.scalar_tensor_tensor(out=n0s[:, 1, :], in0=n0s[:, 0, :], scalar=ws, in1=n0s[:, 1, :], op0=ALU.mult, op1=ALU.add),
            r=[n0s.tok, sm.tok], w=[n0s.tok])
        P.dma("sync", o_smln[l], n0s[:, 1, :], r=[n0s.tok], is_output=True)
        vp = osn[:, 3, :]
        P.v(lambda e: e.tensor_scalar(out=vp, in0=v, scalar1=wi, scalar2=None, op0=ALU.mult), r=[qk4.tok, sm.tok], w=[osn.tok])
        Cq = osn[:, 0, :]
        for j in range(8):
            C = F[j]
            P.dma("sync", C[0:64, :], d_smlc[l][:, j * 512:(j + 1) * 512], w=[C.tok])
            tmpk = F[8 + j % 2]
            qb = AP(qk4.t, 0, [[320, 64], [0, 8], [1, 64]])
            P.v(lambda e, tmpk=tmpk, C=C, qb=qb: e.tensor_tensor(out=tmpk[0:64, :].rearrange("p (x d) -> p x d", x=8), in0=C[0:64, :].rearrange("p (x d) -> p x d", x=8),
                                                                 in1=qb, op=ALU.mult), r=[C.tok, qk4.tok], w=[tmpk.tok])
            P.v(lambda e, tmpk=tmpk, j=j: e.tensor_reduce(out=Cq[:, j * 8:(j + 1) * 8], in_=tmpk[0:64, :].rearrange("p (x d) -> p x d", x=8), axis=AX.X, op=ALU.add),
                r=[tmpk.tok], w=[osn.tok])
            vb = AP(osn.t, 3 * 64 + j * 8, [[256, 64], [1, 8], [0, 64]])
            kb = AP(qk4.t, 1 * 64, [[320, 64], [0, 8], [1, 64]])
            P.v(lambda e, tmpk=tmpk, vb=vb, kb=kb: e.tensor_tensor(out=tmpk[0:64, :].rearrange("p (x d) -> p x d", x=8), in0=vb, in1=kb, op=ALU.mult),
                r=[osn.tok, qk4.tok], w=[tmpk.tok])
            P.v(lambda e, C=C, tmpk=tmpk: e.scalar_tensor_tensor(out=C[0:64, :], in0=C[0:64, :], scalar=ws, in1=tmpk[0:64, :], op0=ALU.mult, op1=ALU.add),
                r=[C.tok, tmpk.tok, sm.tok], w=[C.tok])
            P.dma("sync", o_smlc[l][:, j * 512:(j + 1) * 512], C[0:64, :], r=[C.tok], is_output=True)
        P.v(lambda e: e.tensor_scalar(out=osn[:, 2, :], in0=v, scalar1=scol, scalar2=None, op0=ALU.mult), r=[qk4.tok, sm.tok], w=[osn.tok])
        P.v(lambda e: e.scalar_tensor_tensor(out=Cq, in0=Cq, scalar=ws, in1=osn[:, 2, :], op0=ALU.mult, op1=ALU.add), r=[osn.tok, sm.tok], w=[osn.tok])
        P.a(lambda e: e.activation(out=osn[:, 2, :], in_=og, func=AF.Tanh, scale=0.5), r=[qk4.tok], w=[osn.tok])
        P.v(lambda e: e.tensor_scalar(out=osn[:, 2, :], in0=osn[:, 2, :], scalar1=0.5, scalar2=0.5, op0=ALU.mult, op1=ALU.add), r=[osn.tok], w=[osn.tok])
        P.v(lambda e: e.scalar_tensor_tensor(out=Cq, in0=Cq, scalar=rc, in1=osn[:, 2, :], op0=ALU.mult, op1=ALU.mult), r=[osn.tok, sm.tok], w=[osn.tok])
        P.a(lambda e: e.activation(out=osn[:, 3, :], in_=g, func=AF.Silu), r=[qk4.tok, osn.tok], w=[osn.tok])
        y = headnorm_s(Cq, osn[:, 3, :], gns[:, (l * 2 + 1) * 64:(l * 2 + 2) * 64])
        place_mix(y)
        outproj_s(wo)

    d_ck = din("ck", [DEPTH, NS, 256, 256])
    d_cv = din("cv", [DEPTH, NS, 256, 256])
    gsx = sb("gsx", [128, 2, NS])
    ones32 = sb("ones32", [128, 128])
    P.v(lambda e: e.memset(ones32[:], 1.0), w=[ones32.tok])

    def xa_sample(l, wG, wo):
        pr = Pf.t[:].rearrange("p a b -> p (a b)")
        ps = tm_sample(wG, 0, 256)
        P.a(lambda e: e.activation(out=pr[0:NS, 0:256], in_=ps[0:NS, 0:256], func=AF.Copy), r=[ps.tok], w=[Pf.tok])
        for i in range(2):
            ps = fm_sample(wG, 256 + i * 128)
            P.a(lambda e, ps=ps, i=i: e.activation(out=gsx[:, i, :], in_=ps[:, 0:NS], func=AF.Silu), r=[ps.tok], w=[gsx.tok])
        scT = F[10]
        for n in range(NS):
            qm = F[9]
            Kn = F[n % 4]
            P.dma("sync", Kn[:].rearrange("p (mt c) -> p mt c", mt=2), d_ck[l, n].rearrange("(mt p) c -> p mt c", p=128), w=[Kn.tok])
            P.v(lambda e, n=n: e.tensor_scalar(out=qm[0:NS, 0:256], in0=pr[0:NS, 0:256], scalar1=cst["ident"][0:NS, n:n + 1], scalar2=None, op0=ALU.mult),
                r=[Pf.tok, cst["ident"].tok], w=[qm.tok])
            psq = psum()
            P.t(lambda e, psq=psq: e.matmul(psq[:, 0:256], lhsT=ones32[0:NS, :], rhs=qm[0:NS, 0:256], start=True, stop=True), r=[ones32.tok, qm.tok], w=[psq.tok])
            tmpk = F[8]
            qrb = AP(psq.t, 0, [[512, 128], [0, 2], [1, 256]])
            P.v(lambda e, Kn=Kn, qrb=qrb: e.tensor_tensor(out=tmpk[:].rearrange("p (mt c) -> p mt c", mt=2), in0=Kn[:].rearrange("p (mt c) -> p mt c", mt=2), in1=qrb, op=ALU.mult),
                r=[Kn.tok, psq.tok], w=[tmpk.tok])
            dst = AP(scT.t, n * 4, [[TB, 128], [64, 2], [1, 4]])
            P.v(lambda e, dst=dst: e.tensor_reduce(out=dst, in_=tmpk[:].rearrange("p (mt h d) -> p mt h d", mt=2, h=4), axis=AX.X, op=ALU.add),
                r=[tmpk.tok], w=[scT.tok])
        pss = psum()
        for mt in range(2):
            P.t(lambda e, mt=mt: e.transpose(pss[0:64, mt * 128:(mt + 1) * 128], scT[:, mt * 64:(mt + 1) * 64], cst["ident"][:]),
                r=[scT.tok, cst["ident"].tok], w=[pss.tok])
        mx, sme, nb = sm[:, 0:1], sm[:, 1:2], sm[:, 2:3]
        Pm = F[9]
        P.v(lambda e: e.tensor_reduce(out=mx, in_=pss[0:64, 0:256], axis=AX.X, op=ALU.max), r=[pss.tok], w=[sm.tok])
        P.v(lambda e: e.tensor_scalar(out=nb, in0=mx, scalar1=-0.125, scalar2=None, op0=ALU.mult), r=[sm.tok], w=[sm.tok])
        P.a(lambda e: e.activation(out=Pm[0:64, 0:256], in_=pss[0:64, 0:256], func=AF.Exp, scale=0.125, bias=nb), r=[pss.tok, sm.tok], w=[Pm.tok])
        P.v(lambda e: e.tensor_reduce(out=sme, in_=Pm[0:64, 0:256], axis=AX.X, op=ALU.add), r=[Pm.tok], w=[sm.tok])
        P.v(lambda e: e.reciprocal(out=sme, in_=sme), r=[sm.tok], w=[sm.tok])
        P.v(lambda e: e.tensor_scalar(out=Pm[0:64, 0:256], in0=Pm[0:64, 0:256], scalar1=sme, scalar2=None, op0=ALU.mult), r=[Pm.tok, sm.tok], w=[Pm.tok])
        pst = psum()
        for mt in range(2):
            P.t(lambda e, mt=mt: e.transpose(pst[:, mt * 64:(mt + 1) * 64], Pm[0:64, mt * 128:(mt + 1) * 128], cst["ident"][0:64, 0:64]),
                r=[Pm.tok, cst["ident"].tok], w=[pst.tok])
        PTs = F[10]
        P.a(lambda e: e.activation(out=PTs[:, 128:256], in_=pst[:, 0:128], func=AF.Copy), r=[pst.tok], w=[PTs.tok])
        psx = psum()
        for n in range(NS):
            Vn = F[4 + n % 4]
            P.dma("sync", Vn[:].rearrange("p (mt c) -> p mt c", mt=2), d_cv[l, n].rearrange("(mt p) c -> p mt c", p=128), w=[Vn.tok])
            tmpk = F[8]
            pb = AP(PTs.t, 128 + n * 4, [[TB, 128], [64, 2], [1, 4], [0, 64]])
            P.v(lambda e, Vn=Vn, pb=pb: e.tensor_tensor(out=tmpk[:].rearrange("p (mt h d) -> p mt h d", mt=2, h=4), in0=Vn[:].rearrange("p (mt h d) -> p mt h d", mt=2, h=4),
                                                        in1=pb, op=ALU.mult), r=[Vn.tok, PTs.tok], w=[tmpk.tok])
            for i in range(2):
                for mt in range(2):
                    P.t(lambda e, i=i, mt=mt, n=n: e.matmul(psx[:, i * NS + n:i * NS + n + 1], lhsT=tmpk[:, mt * 256 + i * 128:mt * 256 + (i + 1) * 128], rhs=ones32[:, 0:1],
                                                            start=(mt == 0), stop=(mt == 1)), r=[tmpk.tok, ones32.tok], w=[psx.tok])
        P.v(lambda e: e.tensor_tensor(out=mixs[:].rearrange("p a n -> p (a n)"), in0=psx[:, 0:2 * NS], in1=gsx[:].rearrange("p a n -> p (a n)"), op=ALU.mult),
            r=[psx.tok, gsx.tok], w=[mixs.tok])
        outproj_s(wo)

    for l in range(DEPTH):
        for blk in range(NBLK):
            t0 = blk * TB
            rmsnorm(lambda kt: xT[:, kt, t0:t0 + TB], [xtok[blk]], TB, lambda kt: hnT[:, kt, t0:t0 + TB], lambda kt: [htok[blk]],
                    lambda kt: normw[:, l * 8 + kt:l * 8 + kt + 1])
        rmsnorm(lambda kt: xsT[:, kt, :], [xsT.tok], NS, lambda kt: hnsT[:, kt, :], lambda kt: [hnsT.tok],
                lambda kt: normw[:, l * 8 + kt:l * 8 + kt + 1])
        if STAGE >= 2:
            mem_kv(l)
        if STAGE >= 3:
            ret_phase(l)
        if STAGE >= 4:
            ml_phase(l)
        if STAGE >= 5:
            xa_phase(l)
        if STAGE >= 6:
            s5_phase(l)
    for blk in range(NBLK):
        t0 = blk * TB
        for kt in range(8):
            pass
        yo = [F[0], F[1], F[2], F[3], F[4], F[5], F[6], F[7]]
        rmsnorm(lambda kt: xT[:, kt, t0:t0 + TB], [xtok[blk]], TB, lambda kt: yo[kt][:], lambda kt: [yo[kt].tok], lambda kt: fnw[:, kt:kt + 1])
        for kt in range(8):
            P.dma("sync" if kt % 2 == 0 else "gpsimd", yTv[:, kt, t0:t0 + TB], yo[kt][:], r=[yo[kt].tok], is_output=True)

    yos = F[8]
    rmsnorm(lambda kt: xsT[:, kt, :], [xsT.tok], NS, lambda kt: yos[:, kt * NS:(kt + 1) * NS], lambda kt: [yos.tok], lambda kt: fnw[:, kt:kt + 1])
    P.dma("sync", o_ysT.rearrange("(k p) n -> p k n", p=128), yos[:, 0:8 * NS].rearrange("p (k n) -> p k n", k=8), r=[yos.tok], is_output=True)

    P.emit()
    es.close()
    return nc


_NC = None


def make_in_maps(inp, cores=range(NCORES)):
    cs = _consts()
    f = {k: np.asarray(v) for k, v in inp.items()}
    in_maps = []
    for b in cores:
        m = {}
        m["xT"] = _c(f["x_prompt"][b].T)
        m["memT"] = _c(f["mem_prompt"][b].T)
        m["w_in"] = _c(f["w_in"])
        m["w_out"] = _c(f["w_out"])
        m["w_mem_k"] = _c(f["w_mem_k"])
        m["w_mem_v"] = _c(f["w_mem_v"])
        m["normw"] = _c(f["norm_w"].reshape(DEPTH, 8, 128).transpose(2, 0, 1).reshape(128, DEPTH * 8))
        m["fnw"] = _c(f["final_norm_w"].reshape(8, 128).T)
        gnc = np.zeros((128, DEPTH, 2, 2), np.float32)
        for l in range(DEPTH):
            gnc[:, l, 0, :] = f["ret_gn"][l].reshape(2, 128).T
            gnc[:, l, 1, :] = f["ml_gn"][l].reshape(2, 128).T
        m["gn"] = _c(gnc.reshape(128, DEPTH * 4))
        m["bif"] = _c(np.concatenate([f["ml_b_i"].T, f["ml_b_f"].T], axis=0))
        def pairlay(a):
            a = np.asarray(a)
            L = a.shape[0]
            rest = a.shape[3:]
            a = a.reshape((L, 8, 2, 64) + rest)
            a = np.moveaxis(a, (2, 3), (0, 1))
            return a.reshape((128, L, 8) + rest)
        are = pairlay(f["s5_a_re"])
        aim = pairlay(f["s5_a_im"])
        ldt = pairlay(np.repeat(f["s5_log_dt"][:, :, None], 64, axis=2))
        m["s5par"] = _c(np.stack([are, aim, ldt], axis=2).reshape(128, -1))
        bre = pairlay(f["s5_b_re"])
        bim = pairlay(f["s5_b_im"])
        m["s5b"] = _c(np.stack([bre, bim], axis=2).transpose(1, 0, 2, 3, 4).reshape(DEPTH, 128, -1))
        cre = pairlay(np.swapaxes(f["s5_c_re"], 2, 3))
        cim = pairlay(np.swapaxes(f["s5_c_im"], 2, 3))
        m["s5c"] = _c(np.stack([cre, cim], axis=2).transpose(1, 0, 2, 3, 4).reshape(DEPTH, 128, -1))
        m["s5d"] = _c(f["s5_d"].reshape(DEPTH, 2, 128).transpose(2, 0, 1).reshape(128, -1))
        m["w_glu"] = _c(f["s5_w_glu"])
        ns = slice(NS * b, NS * (b + 1))
        m["xsT"] = _c(f["x_sample"][ns, 0, :].T)
        m["sret"] = _c(f["state_ret"][:, ns].transpose(0, 2, 1, 3, 4).reshape(DEPTH, 64, 4096))
        gs = np.zeros((4, NS, DEPTH, 2, 64), np.float32)
        for l in range(DEPTH):
            gs[:, :, l, 0, :] = f["ret_gn"][l].reshape(4, 1, 64)
            gs[:, :, l, 1, :] = f["ml_gn"][l].reshape(4, 1, 64)
        m["gns"] = _c(gs.reshape(64, -1))
        m["smlc"] = _c(f["state_mlstm_c"][:, ns].transpose(0, 2, 1, 3, 4).reshape(DEPTH, 64, 4096))
        m["smln"] = _c(f["state_mlstm_n"][:, ns].transpose(0, 2, 1, 3).reshape(DEPTH, 64, 64))
        m["smlm"] = _c(f["state_mlstm_m"][:, ns].transpose(0, 2, 1).reshape(DEPTH, 64, 1))
        bs_ = np.zeros((4, NS, DEPTH, 2), np.float32)
        for l in range(DEPTH):
            bs_[:, :, l, 0] = f["ml_b_i"][l].reshape(4, 1)
            bs_[:, :, l, 1] = f["ml_b_f"][l].reshape(4, 1)
        m["bifs"] = _c(bs_.reshape(64, -1))
        def spair(a):
            a = np.asarray(a)[:, ns]
            a = a.reshape(DEPTH, NS, 8, 2, 64).transpose(0, 3, 4, 2, 1)
            return a.reshape(DEPTH, 128, 8, NS)
        m["sx0"] = _c(np.stack([spair(f["state_s5_re"]), spair(f["state_s5_im"])], axis=2).reshape(DEPTH, 128, -1))
        m["ck"] = _c(f["cache_mem_k"][:, ns].reshape(DEPTH, NS, 256, 256))
        m["cv"] = _c(f["cache_mem_v"][:, ns].reshape(DEPTH, NS, 256, 256))
        for k, v in cs.items():
            m["c_" + k] = _c(v)
        in_maps.append(m)
    return in_maps


def kernel(**inp):
    global _NC
    if _NC is None:
        _NC = build()
    nc = _NC
    in_maps = make_in_maps(inp)
    res = run_bass_kernel_spmd(nc, in_maps, core_ids=list(range(NCORES)))
    return assemble(res.results)


def assemble(R):
    y_prompt = np.stack([R[b]["o_yT"].T for b in range(NCORES)]).astype(np.float32)
    memkv = np.stack([R[b]["o_memkv"] for b in range(NCORES)], axis=1)
    memk = np.ascontiguousarray(memkv[..., 0:256]).reshape(DEPTH, NCORES, 256, 4, 64)
    memv = np.ascontiguousarray(memkv[..., 256:512]).reshape(DEPTH, NCORES, 256, 4, 64)
    ret_p = np.stack([R[b]["o_ret"] for b in range(NCORES)], axis=1)
    mlc_p = np.stack([R[b]["o_mlc"].transpose(0, 1, 3, 2) for b in range(NCORES)], axis=1)
    mln_p = np.stack([R[b]["o_mln"] for b in range(NCORES)], axis=1)
    mlm_p = np.stack([R[b]["o_mlm"] for b in range(NCORES)], axis=1)
    s5 = np.stack([R[b]["o_s5"] for b in range(NCORES)], axis=1)
    s5 = s5.reshape(DEPTH, NCORES, 2, 64, 2, 8).transpose(0, 1, 4, 5, 2, 3).reshape(DEPTH, NCORES, 2, 16, 64)
    s5re_p = np.ascontiguousarray(s5[:, :, 0])
    s5im_p = np.ascontiguousarray(s5[:, :, 1])
    y_sample = np.concatenate([R[b]["o_ysT"].T for b in range(NCORES)], axis=0).reshape(NCORES * NS, 1, 1024)
    ret_s = np.concatenate([R[b]["o_sret"].reshape(DEPTH, 4, NS, 64, 64).transpose(0, 2, 1, 3, 4) for b in range(NCORES)], axis=1)
    mlc_s = np.concatenate([R[b]["o_smlc"].reshape(DEPTH, 4, NS, 64, 64).transpose(0, 2, 1, 3, 4) for b in range(NCORES)], axis=1)
    mln_s = np.concatenate([R[b]["o_smln"].reshape(DEPTH, 4, NS, 64).transpose(0, 2, 1, 3) for b in range(NCORES)], axis=1)
    mlm_s = np.concatenate([R[b]["o_smlm"].reshape(DEPTH, 4, NS).transpose(0, 2, 1) for b in range(NCORES)], axis=1)
    def unsp(b):
        a = R[b]["o_s5s"].reshape(DEPTH, 2, 64, 2, 8, NS).transpose(3, 0, 5, 4, 1, 2)
        return a.reshape(2, DEPTH, NS, 16, 64)
    s5s = np.concatenate([unsp(b) for b in range(NCORES)], axis=2)
    z = lambda *s: np.zeros(s, np.float32)
    outs = (y_prompt, y_sample, ret_p, ret_s, np.ascontiguousarray(mlc_p), mlc_s,
            mln_p, mln_s, mlm_p, mlm_s, s5re_p, s5s[0],
            s5im_p, s5s[1], memk, memv)
    return tuple(np.ascontiguousarray(o, dtype=np.float32) for o in outs)
```

```python
import contextlib
import numpy as np
import concourse.bass as bass
import concourse.mybir as mybir
from concourse.ap import AP
from concourse.bass_utils import run_bass_kernel_spmd

F32 = mybir.dt.float32
BF16 = mybir.dt.bfloat16
I32 = mybir.dt.int32
F32R = mybir.dt.float32r
AF = mybir.ActivationFunctionType
ALU = mybir.AluOpType
AX = mybir.AxisListType

NCORES = 8
T = 2048
TB = 512
NBLK = T // TB
NS = 16
DEPTH = 2
DIN = 3336
EPS = 1e-6
PAST = 16384.0
TWO_PI = 6.283185307179586
C1 = 6.28125
C2 = TWO_PI - C1


class Tok:
    __slots__ = ("w", "r", "name", "excl")

    def __init__(self, name=""):
        self.w = None
        self.r = []
        self.name = name
        self.excl = False


class Op:
    __slots__ = ("eng", "fn", "deps", "idx", "signal", "sem", "semval", "dma", "cost", "pos", "tab")

    def __init__(self, eng, fn, deps, idx, dma):
        self.cost = 100.0
        self.pos = idx
        self.tab = None
        self.eng = eng
        self.fn = fn
        self.deps = deps
        self.idx = idx
        self.signal = False
        self.sem = None
        self.semval = 0
        self.dma = dma


ENGS = ("sync", "scalar", "vector", "gpsimd", "tensor")
DMA_POOL = 20
WCHUNKS = [(0, 512), (512, 1024), (1024, 1536), (1536, 2048), (2048, 2312), (2824, 3336)]


class _Rec:
    def __init__(self):
        self.call = None

    def __getattr__(self, name):
        def f(*a, **kw):
            self.call = (name, a, kw)
            return self
        return f


def _est_cost(eng, name, a_, kw_, dma):
    try:
        out = kw_.get("out", None)
        if out is None:
            out = a_[0]
        shp = out.shape
        n = 1
        for d in shp[1:]:
            n *= int(d)
    except Exception:
        n = 128
    if dma:
        return 2000.0 + n * int(shp[0]) * 4 / 80.0
    if eng == "tensor":
        f32 = False
        try:
            f32 = (kw_.get("lhsT", None) is not None and kw_["lhsT"].dtype == F32)
        except Exception:
            pass
        return 60.0 + n * (4.0 if f32 else 1.0) / 1.7
    if eng == "vector":
        return 100.0 + n * (2.0 if name == "tensor_tensor_scan" else 1.0) / 0.78
    if eng == "scalar":
        return 230.0 + n / 1.2
    if eng == "gpsimd":
        return 150.0 + n / 0.5
    return 50.0


SCHED = True
_TABSET = {AF.Exp: "e", AF.Ln: "e", AF.Silu: "s", AF.Sin: "s", AF.Sigmoid: "g", AF.Tanh: "s"}
TAB_PEN = 0.0


def _list_schedule(ops):
    n = len(ops)
    succ = [[] for _ in range(n)]
    indeg = [0] * n
    for o in ops:
        for d in o.deps:
            succ[d].append(o.idx)
            indeg[o.idx] += 1
    bl = [0.0] * n
    for i in range(n - 1, -1, -1):
        m = 0.0
        for s_ in succ[i]:
            if bl[s_] > m:
                m = bl[s_]
        bl[i] = m + ops[i].cost + 100.0
    efree = {e: 0.0 for e in ENGS}
    fin = [0.0] * n
    rdy_t = [0.0] * n
    ready = [i for i in range(n) if indeg[i] == 0]
    order = []
    cur_tab = [None]
    WINDOW = 6000
    next_unsched = 0
    done = [False] * n
    while ready:
        best = None
        bk = None
        lim = next_unsched + WINDOW
        for i in ready:
            if i > lim:
                continue
            o = ops[i]
            st = efree[o.eng]
            if rdy_t[i] > st:
                st = rdy_t[i]
            if o.tab is not None and o.tab != cur_tab[0]:
                st = st + TAB_PEN
            key = (int(st / 400.0), -bl[i], i)
            if bk is None or key < bk:
                bk = key
                best = i
        if best is None:
            best = min(ready)
            o = ops[best]
            bk = (max(efree[o.eng], rdy_t[best]),)
        ready.remove(best)
        o = ops[best]
        st = max(efree[o.eng], rdy_t[best])
        if o.tab is not None:
            cur_tab[0] = o.tab
        if o.dma:
            efree[o.eng] = st + 60.0
            fin[best] = st + o.cost
        else:
            efree[o.eng] = st + o.cost
            fin[best] = st + o.cost
        order.append(best)
        done[best] = True
        while next_unsched < n and done[next_unsched]:
            next_unsched += 1
        for s_ in succ[best]:
            t = fin[best] + (110.0 if ops[s_].eng == o.eng else 220.0)
            if t > rdy_t[s_]:
                rdy_t[s_] = t
            indeg[s_] -= 1
            if indeg[s_] == 0:
                ready.append(s_)
    assert len(order) == n
    return order, max(fin)


class Prog:
    def __init__(self, nc):
        self.nc = nc
        self.ops = []
        self.out_ops = []

    @staticmethod
    def _flat(ts):
        out = []
        for t in ts:
            if isinstance(t, (list, tuple)):
                out.extend(Prog._flat(t))
            else:
                out.append(t)
        return out

    def op(self, eng, fn, r=(), w=(), dma=False):
        r = Prog._flat(r)
        w = Prog._flat(w)
        idx = len(self.ops)
        deps = set()
        for t in r:
            if t.w is not None:
                deps.add(t.w)
            if t.excl:
                for ri_ in t.r:
                    if self.ops[ri_].eng != eng:
                        deps.add(ri_)
        for t in w:
            if t.w is not None:
                deps.add(t.w)
            deps.update(t.r)
        rec = _Rec()
        fn(rec)
        name, a_, kw_ = rec.call
        o = Op(eng, (lambda e, name=name, a_=a_, kw_=kw_: getattr(e, name)(*a_, **kw_)), deps, idx, dma)
        o.cost = _est_cost(eng, name, a_, kw_, dma)
        if eng == "scalar" and name == "activation":
            o.tab = _TABSET.get(kw_.get("func", None), None)
        self.ops.append(o)
        for t in r:
            t.r.append(idx)
        for t in w:
            t.w = idx
            t.r = []
        return idx

    def v(self, fn, r=(), w=()):
        return self.op("vector", fn, r, w)

    def a(self, fn, r=(), w=()):
        return self.op("scalar", fn, r, w)

    def g(self, fn, r=(), w=()):
        return self.op("gpsimd", fn, r, w)

    def t(self, fn, r=(), w=()):
        return self.op("tensor", fn, r, w)

    def dma(self, q, out, in_, r=(), w=(), is_output=False):
        idx = self.op(q, lambda e, out=out, in_=in_: e.dma_start(out=out, in_=in_), r, w, dma=True)
        if is_output:
            self.out_ops.append(idx)
        return idx

    def emit(self):
        nc = self.nc
        ops = self.ops
        fin = Op("sync", lambda e: e.nop(), set(self.out_ops), len(ops), False)
        ops.append(fin)
        byidx = ops
        if SCHED:
            order, mk = _list_schedule(ops)
            self.est_makespan = mk
            ops = [byidx[i] for i in order]
            for p_, o in enumerate(ops):
                o.pos = p_
        with contextlib.ExitStack() as es:
            esem = {e: es.enter_context(nc.semaphore("s_" + e)) for e in ENGS}
            pools = {q: [es.enter_context(nc.semaphore("d_%s_%d" % (q, i))) for i in range(DMA_POOL)]
                     for q in ("sync", "gpsimd")}
            dcount = {"sync": 0, "gpsimd": 0}
            last_use = {}
            for o in ops:
                if o.dma:
                    i = dcount[o.eng]
                    dcount[o.eng] += 1
                    sem = pools[o.eng][i % DMA_POOL]
                    u = i // DMA_POOL
                    o.sem = sem
                    o.semval = 16 * (u + 1)
                    o.signal = True
                    key = (o.eng, i % DMA_POOL)
                    if key in last_use:
                        o.deps.add(last_use[key])
                    last_use[key] = o.idx
            for o in ops:
                for d in o.deps:
                    p = byidx[d]
                    if p.eng == "tensor" and o.eng == "tensor" and not p.dma:
                        continue
                    p.signal = True
            cnt = {e: 0 for e in ENGS}
            for o in ops:
                if not o.dma and o.signal:
                    cnt[o.eng] += 1
                    o.sem = esem[o.eng]
                    o.semval = cnt[o.eng]
            per = {e: [o for o in ops if o.eng == e] for e in ENGS}

            def run(eng, lst):
                waited = {}
                for o in lst:
                    need = {}
                    for d in o.deps:
                        p = byidx[d]
                        if p.eng == "tensor" and o.eng == "tensor" and not p.dma:
                            continue
                        k = id(p.sem)
                        if k not in need or need[k][1] < p.semval:
                            need[k] = (p.sem, p.semval)
                    for k, (sem, val) in need.items():
                        if waited.get(k, 0) >= val:
                            continue
                        eng.wait_ge(sem, val)
                        waited[k] = val
                    ins = o.fn(eng)
                    if o.signal:
                        ins.then_inc(o.sem, 16 if o.dma else 1)

            with nc.Block() as block:
                @block.sync
                def _(e):
                    run(e, per["sync"])

                @block.scalar
                def _(e):
                    run(e, per["scalar"])

                @block.vector
                def _(e):
                    run(e, per["vector"])

                @block.gpsimd
                def _(e):
                    run(e, per["gpsimd"])

                @block.tensor
                def _(e):
                    run(e, per["tensor"])


class Buf:
    def __init__(self, t, name=""):
        self.t = t
        self.tok = Tok(name)

    def __getitem__(self, k):
        return self.t[k]


def _c(a):
    return np.ascontiguousarray(a, dtype=np.float32)


def _consts():
    c = {}
    c["ident"] = np.eye(128, dtype=np.float32)
    c["maskT"] = np.triu(np.ones((128, 128), np.float32))
    bd = np.zeros((128, 128), np.float32)
    bd[:64, :64] = 1
    bd[64:, 64:] = 1
    c["bd2"] = bd
    pm = np.zeros((128, 128), np.float32)
    for h2 in range(2):
        for d in range(32):
            pm[h2 * 64 + d + 32, h2 * 64 + d] = -1.0
            pm[h2 * 64 + d, h2 * 64 + d + 32] = 1.0
    c["pm"] = pm
    inv = (10000.0 ** (-np.arange(32, dtype=np.float32) / 32)).astype(np.float32)
    c["invf"] = np.tile(inv, 4).reshape(128, 1).astype(np.float32)
    c["invrow"] = np.tile(inv.reshape(1, 32), (128, 1)).astype(np.float32)
    lg = np.log1p(-np.power(np.float32(2.0), -5.0 - np.arange(4, dtype=np.float32))).astype(np.float32)
    lgc = np.zeros((128, 2), np.float32)
    for i in range(2):
        for h2 in range(2):
            lgc[h2 * 64:(h2 + 1) * 64, i] = lg[2 * i + h2]
    c["lgcol"] = lgc
    c["l1"] = np.tile(np.arange(1, 129, dtype=np.float32).reshape(1, 128), (128, 1))
    ex = np.zeros((8, 4, 128), np.float32)
    for i in range(2):
        for h2 in range(2):
            ex[2 * i + h2, i * 2 + 0, h2 * 64:(h2 + 1) * 64] = 1.0
            ex[4 + 2 * i + h2, i * 2 + 1, h2 * 64:(h2 + 1) * 64] = 1.0
    c["gexp"] = ex.reshape(8, 512)
    sel = np.zeros((16, 4, 64), np.float32)
    for hh in range(4):
        for n in range(16):
            sel[n, hh, 16 * hh + n] = 1.0
    c["sel"] = sel.reshape(16, 256)
    sel2 = np.zeros((64, 16), np.float32)
    mpair = np.zeros((64, 2, 2), np.float32)
    lg64 = np.zeros((64, 1), np.float32)
    for hh in range(4):
        for n in range(16):
            sel2[16 * hh + n, n] = 1.0
            mpair[16 * hh + n, hh // 2, hh % 2] = 1.0
            lg64[16 * hh + n, 0] = lg[hh]
    c["sel2"] = sel2
    c["mpair"] = mpair.reshape(64, 4)
    c["lg64"] = lg64
    c["kgrid"] = np.tile(np.arange(9, dtype=np.float32).reshape(1, 9), (128, 1))
    return c


CONST_SHAPES = {"ident": [128, 128], "maskT": [128, 128], "bd2": [128, 128], "pm": [128, 128],
                "invf": [128, 1], "invrow": [128, 32], "lgcol": [128, 2], "l1": [128, 128],
                "gexp": [8, 512], "kgrid": [128, 9],
                "sel": [16, 256], "sel2": [64, 16], "mpair": [64, 4], "lg64": [64, 1]}


def build(stage=99):
    import os
    STAGE = int(os.environ.get('KSTAGE', '99'))
    KSUB = int(os.environ.get('KSUB', '99'))
    KMASK = int(os.environ.get('KMASK', '63'))
    SAMPLE = int(os.environ.get('KSAMPLE', '99'))
    nc = bass.Bass("TRN2", target_bir_lowering=False)
    P = Prog(nc)
    es = contextlib.ExitStack()

    def din(name, shape):
        return nc.dram_tensor(name, list(shape), F32, kind="ExternalInput").ap()

    def dout(name, shape):
        return nc.dram_tensor(name, list(shape), F32, kind="ExternalOutput").ap()

    def sb(name, shape, dt=F32):
        return Buf(es.enter_context(nc.sbuf_tensor("s_" + name, list(shape), dt)), name)

    d_xT = din("xT", [1024, T])
    d_memT = din("memT", [1024, 256])
    d_win = din("w_in", [DEPTH, 1024, DIN])
    d_wout = din("w_out", [DEPTH, 1024, 1024])
    d_wmk = din("w_mem_k", [DEPTH, 1024, 256])
    d_wmv = din("w_mem_v", [DEPTH, 1024, 256])
    d_normw = din("normw", [128, DEPTH * 8])
    d_fnw = din("fnw", [128, 8])
    d_gn = din("gn", [128, DEPTH * 4])
    d_bif = din("bif", [8, DEPTH])
    dc = {k: din("c_" + k, s) for k, s in CONST_SHAPES.items()}

    o_yT = dout("o_yT", [1024, T])
    o_memkv = dout("o_memkv", [DEPTH, 256, 512])
    o_ret = dout("o_ret", [DEPTH, 4, 64, 64])
    o_mlc = dout("o_mlc", [DEPTH, 4, 64, 64])
    o_mln = dout("o_mln", [DEPTH, 4, 64])
    o_mlm = dout("o_mlm", [DEPTH, 4])

    xTv = d_xT.rearrange("(k p) t -> p k t", p=128)
    yTv = o_yT.rearrange("(k p) t -> p k t", p=128)

    banks = [Buf(es.enter_context(nc.psum_tensor("ps%d" % i, [128, 512], F32)), "ps%d" % i) for i in range(8)]
    for b_ in banks:
        b_.tok.excl = True
    bank_rr = [0]
    rot = [[0, 1, 2, 3, 4, 5]]

    def psum():
        r_ = rot[0]
        b = banks[r_[bank_rr[0] % len(r_)]]
        bank_rr[0] += 1
        return b

    cst = {}
    for k, s in CONST_SHAPES.items():
        cst[k] = sb("k_" + k, s)
        P.dma("sync", cst[k].t[:], dc[k], w=[cst[k].tok])
    ident_b = sb("ident_b", [128, 128], BF16)
    P.v(lambda e: e.tensor_copy(out=ident_b[:], in_=cst["ident"][:]), r=[cst["ident"].tok], w=[ident_b.tok])
    zeros_b = sb("zeros_b", [128, 128], BF16)
    P.v(lambda e: e.memset(zeros_b[:], 0.0), w=[zeros_b.tok])
    ones_b = sb("ones_b", [128, 128], BF16)
    P.v(lambda e: e.memset(ones_b[:], 1.0), w=[ones_b.tok])
    onespad = sb("onespad", [128, 2, 128], BF16)
    P.v(lambda e: e.memset(onespad[:], 0.0), w=[onespad.tok])
    for h2 in range(2):
        P.v(lambda e, h2=h2: e.memset(onespad[:, h2, 64 * h2:64 * h2 + 64], 1.0), w=[onespad.tok])
    avg = sb("avg", [128, 128])
    P.v(lambda e: e.tensor_scalar(out=avg[:], in0=cst["bd2"][:], scalar1=1.0 / 64, scalar2=None, op0=ALU.mult),
        r=[cst["bd2"].tok], w=[avg.tok])
    avg_b = sb("avg_b", [128, 128], BF16)
    P.v(lambda e: e.tensor_copy(out=avg_b[:], in_=avg[:]), r=[avg.tok], w=[avg_b.tok])
    normw = sb("normw", [128, DEPTH * 8])
    P.dma("sync", normw[:], d_normw, w=[normw.tok])
    fnw = sb("fnw", [128, 8])
    P.dma("sync", fnw[:], d_fnw, w=[fnw.tok])
    gn = sb("gn", [128, DEPTH * 4])
    P.dma("sync", gn[:], d_gn, w=[gn.tok])
    bif = sb("bif", [8, DEPTH])
    P.dma("sync", bif[:], d_bif, w=[bif.tok])

    decq = sb("decq", [128, 2, 128])
    deck = sb("deck", [128, 2, 128])
    gL = sb("gL", [128, 2])
    for i in range(2):
        P.a(lambda e, i=i: e.activation(out=decq[:, i, :], in_=cst["l1"][:], func=AF.Exp, scale=cst["lgcol"][:, i:i + 1]),
            r=[cst["l1"].tok, cst["lgcol"].tok], w=[decq.tok])
    P.v(lambda e: e.reciprocal(out=deck[:], in_=decq[:]), r=[decq.tok], w=[deck.tok])
    P.v(lambda e: e.tensor_scalar(out=deck[:], in0=deck[:], scalar1=0.125, scalar2=None, op0=ALU.mult),
        r=[deck.tok], w=[deck.tok])
    P.v(lambda e: e.tensor_copy(out=gL[:], in_=decq[:, :, 127]), r=[decq.tok], w=[gL.tok])

    bd2g = sb("bd2g", [128, 2, 128])
    for i in range(2):
        P.v(lambda e, i=i: e.tensor_scalar(out=bd2g[:, i, :], in0=cst["bd2"][:], scalar1=gL[:, i:i + 1], scalar2=None, op0=ALU.mult),
            r=[cst["bd2"].tok, gL.tok], w=[bd2g.tok])

    xT = sb("xT", [128, 8, T])
    xtok = [Tok("xT%d" % b) for b in range(NBLK)]
    for b in range(NBLK):
        P.dma("sync", xT[:, :, b * TB:(b + 1) * TB], xTv[:, :, b * TB:(b + 1) * TB], w=[xtok[b]])
    hnT = sb("hnT", [128, 8, T], BF16)
    htok = [Tok("hnT%d" % b) for b in range(NBLK)]

    F = [sb("F%d" % i, [128, TB]) for i in range(11)]
    P0 = sb("P0", [128, 2, TB], BF16)
    P1 = sb("P1", [128, 2, TB], BF16)
    P2 = sb("P2", [128, 2, TB], BF16)
    Pf = sb("Pf", [128, 2, TB])
    VA = sb("VA", [128, 4, 2, 2, 128], BF16)
    ET = sb("ET", [128, 4, TB], BF16)
    mixp = sb("mixp", [128, 2, TB], BF16)
    mixp.tok = [Tok("mixp0"), Tok("mixp1")]
    ET.tok = [Tok("ET%d" % k_) for k_ in range(4)]
    Pf.tok = [Tok("Pf0"), Tok("Pf1")]
    ncs = sb("ncs", [128, 2, 4])
    sq2 = [sb("sq%d" % i, [128, TB], BF16) for i in range(2)]

    NW = 3
    wch = [sb("wch%d" % i, [128, 8, 512], BF16) for i in range(NW)]
    wrr = [0]
    wop = [sb("wop%d" % i, [128, 2, 1024], BF16) for i in range(1)]
    worr = [0]

    def wload(parts):
        b = wch[wrr[0] % NW]
        wrr[0] += 1
        for (s2, a0, a1, off) in parts:
            P.dma("gpsimd", b.t[:, :, off:off + (a1 - a0)], s2.rearrange("(k p) c -> p k c", p=128)[:, :, a0:a1], w=[b.tok])
        return b

    def woload(l, m):
        b = wop[0]
        worr[0] += 1
        P.dma("gpsimd", b.t[:], d_wout[l][m * 256:(m + 1) * 256, :].rearrange("(k p) c -> p k c", p=128), w=[b.tok])
        return b

    mkT = sb("mkT", [128, 2, 256], BF16)
    mvpad = sb("mvpad", [128, 2, 2, 2, 128], BF16)
    P.g(lambda e: e.memset(mvpad[:], 0.0), w=[mvpad.tok])

    def mem_kv(l):
        wb = wload([(d_wmk[l], 0, 256, 0), (d_wmv[l], 0, 256, 256)])
        mslot = wch[wrr[0] % NW]
        wrr[0] += 1
        memT = Buf(mslot.t[:].rearrange("p a b -> p (a b)")[:, 0:2048].rearrange("p (k m) -> p k m", k=8), "memT")
        memT.tok = mslot.tok
        P.dma("gpsimd", memT.t, d_memT.rearrange("(k p) m -> p k m", p=128), w=[memT.tok])
        for mt in range(2):
            ps = psum()
            for kt in range(8):
                P.t(lambda e, ps=ps, kt=kt, mt=mt: e.matmul(ps[:, :], lhsT=memT[:, kt, mt * 128:(mt + 1) * 128],
                                                            rhs=wb[:, kt, :], start=(kt == 0), stop=(kt == 7)),
                    r=[memT.tok, wb.tok], w=[ps.tok])
            fo = F[mt]
            P.a(lambda e, ps=ps, fo=fo: e.activation(out=fo[:], in_=ps[:, :], func=AF.Copy), r=[ps.tok], w=[fo.tok])
            dst = AP(mvpad.t, mt * 512 + 0, [[1024, 128], [256, 2], [192, 2], [1, 64]])
            P.v(lambda e, ps=ps, dst=dst: e.tensor_copy(out=dst, in_=ps[:, 256:512].rearrange("p (i h e) -> p i h e", i=2, h=2)),
                r=[ps.tok], w=[mvpad.tok])
            P.dma("sync", o_memkv[l, mt * 128:(mt + 1) * 128, :], fo[:], r=[fo.tok], is_output=True)
        for i in range(2):
            ps = psum()
            for kt in range(8):
                P.t(lambda e, ps=ps, kt=kt, i=i: e.matmul(ps[:, 0:256], lhsT=wb[:, kt, i * 128:(i + 1) * 128],
                                                          rhs=memT[:, kt, :], start=(kt == 0), stop=(kt == 7)),
                    r=[memT.tok, wb.tok], w=[ps.tok])
            P.v(lambda e, ps=ps, i=i: e.tensor_copy(out=mkT[:, i, :], in_=ps[:, 0:256]), r=[ps.tok], w=[mkT.tok])

    def rmsnorm(srcs, rtoks, ntok, dsts, wtoks, wcol):
        rs = F[10]
        ps = psum()
        for kt in range(8):
            sq = sq2[kt % 2]
            P.a(lambda e, kt=kt, sq=sq: e.activation(out=sq[:, 0:ntok], in_=srcs(kt), func=AF.Square), r=rtoks, w=[sq.tok])
            P.t(lambda e, kt=kt, sq=sq: e.matmul(ps[:, 0:ntok], lhsT=ones_b[:, :], rhs=sq[:, 0:ntok],
                                                 start=(kt == 0), stop=(kt == 7)), r=[sq.tok, ones_b.tok], w=[ps.tok])
        P.a(lambda e: e.activation(out=rs[:, 0:ntok], in_=ps[:, 0:ntok], func=AF.Ln, scale=1.0 / 1024, bias=EPS),
            r=[ps.tok], w=[rs.tok])
        P.a(lambda e: e.activation(out=rs[:, 0:ntok], in_=rs[:, 0:ntok], func=AF.Exp, scale=-0.5), r=[rs.tok], w=[rs.tok])
        for kt in range(8):
            P.v(lambda e, kt=kt: e.scalar_tensor_tensor(out=dsts(kt), in0=srcs(kt), scalar=wcol(kt), in1=rs[:, 0:ntok],
                                                        op0=ALU.mult, op1=ALU.mult), r=rtoks + [rs.tok, normw.tok, fnw.tok], w=wtoks(kt))

    def sincos(ang, atok, nfi, nff, ntok, sin_o, stok, cos_o, ctok):
        P.v(lambda e: e.tensor_scalar(out=nfi, in0=ang, scalar1=1.0 / TWO_PI, scalar2=None, op0=ALU.mult), r=[atok], w=[ntok])
        P.v(lambda e: e.tensor_copy(out=cos_o, in_=nfi), r=[ntok], w=[ctok])
        P.v(lambda e: e.scalar_tensor_tensor(out=ang, in0=cos_o, scalar=-C1, in1=ang, op0=ALU.mult, op1=ALU.add), r=[ctok, atok], w=[atok])
        P.v(lambda e: e.scalar_tensor_tensor(out=ang, in0=cos_o, scalar=-C2, in1=ang, op0=ALU.mult, op1=ALU.add), r=[ctok, atok], w=[atok])
        P.v(lambda e: e.tensor_scalar(out=ang, in0=ang, scalar1=3.1415925, scalar2=-3.1415925, op0=ALU.min, op1=ALU.max), r=[atok], w=[atok])
        P.a(lambda e: e.activation(out=sin_o, in_=ang, func=AF.Sin), r=[atok], w=[stok])
        P.a(lambda e: e.activation(out=ang, in_=ang, func=AF.Abs), r=[atok], w=[atok])
        P.a(lambda e: e.activation(out=cos_o, in_=ang, func=AF.Sin, scale=-1.0, bias=1.5707963), r=[atok], w=[ctok])

    def rope_tables(blk):
        angb, nf, cosb, sinb = F[0], F[1], F[2], F[3]
        t0 = float(blk * TB)
        for c in range(4):
            P.v(lambda e, c=c: e.tensor_scalar(out=angb[:, c * 128:(c + 1) * 128], in0=cst["l1"][:], scalar1=t0 + c * 128 - 1.0, scalar2=cst["invf"][:, 0:1],
                                               op0=ALU.add, op1=ALU.mult), r=[cst["l1"].tok, cst["invf"].tok], w=[angb.tok])
        sincos(angb[:], angb.tok, nf.t[:].bitcast(I32), nf[:], nf.tok, sinb[:], sinb.tok, cosb[:], cosb.tok)

    def fm_group(wb, c0, blk, m=128):
        ps = psum()
        for kt in range(8):
            P.t(lambda e, kt=kt: e.matmul(ps[0:m, :], lhsT=wb[:, kt, c0:c0 + m], rhs=hnT[:, kt, blk * TB:(blk + 1) * TB],
                                          start=(kt == 0), stop=(kt == 7)), r=[wb.tok, htok[blk]], w=[ps.tok])
        return ps

    def tm_group(wb, c0, blk, c, n=256):
        ps = psum()
        ts = slice(blk * TB + c * 128, blk * TB + (c + 1) * 128)
        for kt in range(8):
            P.t(lambda e, kt=kt: e.matmul(ps[:, 0:n], lhsT=hnT[:, kt, ts], rhs=wb[:, kt, c0:c0 + n],
                                          start=(kt == 0), stop=(kt == 7)), r=[wb.tok, htok[blk]], w=[ps.tok])
        return ps

    def rope_evac(ps, psr, i, dst, dec):
        cosb, sinb = F[2], F[3]
        t1, t2 = F[4 + i], F[6 + i] if False else F[6]
        P.v(lambda e: e.tensor_tensor(out=t1[:], in0=ps[:, :], in1=cosb[:], op=ALU.mult), r=[ps.tok, cosb.tok], w=[t1.tok])
        P.v(lambda e: e.tensor_tensor(out=t2[:], in0=psr[:, :], in1=sinb[:], op=ALU.mult), r=[psr.tok, sinb.tok], w=[t2.tok])
        P.v(lambda e: e.tensor_tensor(out=t1[:], in0=t1[:], in1=t2[:], op=ALU.add), r=[t1.tok, t2.tok], w=[t1.tok])
        decb = AP(dec.t, i * 128, [[256, 128], [0, 4], [1, 128]])
        P.v(lambda e: e.tensor_tensor(out=dst[:, i, :].rearrange("p (c l) -> p c l", c=4),
                                      in0=t1[:].rearrange("p (c l) -> p c l", c=4), in1=decb, op=ALU.mult),
            r=[t1.tok, dec.tok], w=[dst.tok])

    innT = [sb("innT%d" % i, [128, 2, 128], BF16) for i in range(2)]
    irr = [0]
    kTM = [sb("kTM%d" % i, [128, 128], BF16) for i in range(2)]
    krr = [0]
    tts = [sb("tt%d" % i, [128, 256]) for i in range(2)]
    ttr = [0]

    def head_norm_gate(src, gate, gtok, gcol, dst, cen=None, sq32=None, dtok=None):
        cen = F[8] if cen is None else cen
        sq32 = F[9] if sq32 is None else sq32
        ps = psum()
        P.t(lambda e: e.matmul(ps[:, :], lhsT=avg[:], rhs=src[:], start=True, stop=True), r=[avg.tok, src.tok], w=[ps.tok])
        P.v(lambda e: e.tensor_tensor(out=cen[:], in0=src[:], in1=ps[:, :], op=ALU.subtract), r=[src.tok, ps.tok], w=[cen.tok])
        sqb = sq32.t[:].bitcast(BF16)[:, 0:TB]
        P.a(lambda e: e.activation(out=sqb, in_=cen[:], func=AF.Square), r=[cen.tok], w=[sq32.tok])
        ps2 = psum()
        P.t(lambda e: e.matmul(ps2[:, :], lhsT=avg_b[:], rhs=sqb, start=True, stop=True), r=[avg_b.tok, sq32.tok], w=[ps2.tok])
        P.a(lambda e: e.activation(out=sq32[:], in_=ps2[:, :], func=AF.Ln, bias=EPS), r=[ps2.tok], w=[sq32.tok])
        P.a(lambda e: e.activation(out=sq32[:], in_=sq32[:], func=AF.Exp, scale=-0.5), r=[sq32.tok], w=[sq32.tok])
        P.v(lambda e: e.tensor_tensor(out=cen[:], in0=cen[:], in1=sq32[:], op=ALU.mult), r=[cen.tok, sq32.tok], w=[cen.tok])
        P.v(lambda e: e.scalar_tensor_tensor(out=dst, in0=cen[:], scalar=gcol, in1=gate, op0=ALU.mult, op1=ALU.mult),
            r=[cen.tok, gn.tok, gtok], w=[mixp.tok if dtok is None else dtok])

    def outproj_part(wo, blk):
        t0 = blk * TB
        for dt_ in range(8):
            ps = psum()
            for kt in range(2):
                P.t(lambda e, ps=ps, kt=kt, dt_=dt_: e.matmul(ps[:, :], lhsT=wo[:, kt, dt_ * 128:(dt_ + 1) * 128], rhs=mixp[:, kt, :],
                                                              start=(kt == 0), stop=(kt == 1)), r=[wo.tok, mixp.tok], w=[ps.tok])
            P.v(lambda e, ps=ps, dt_=dt_: e.tensor_tensor(out=xT[:, dt_, t0:t0 + TB], in0=xT[:, dt_, t0:t0 + TB], in1=ps[:, :], op=ALU.add),
                r=[ps.tok, xtok[blk]], w=[xtok[blk]])

    Sret = [sb("Sret%d" % i, [128, 128]) for i in range(2)]
    Sret_b = [sb("Sretb%d" % i, [128, 128], BF16) for i in range(2)]

    def ret_phase(l):
        W = d_win[l]
        wA = wload([(W, 0, 512, 0)])
        wB = wload([(W, 512, 1024, 0)])
        wo = woload(l, 0)
        wR = wch[wrr[0] % NW]
        wrr[0] += 1
        src1 = AP(wA.t, 32, [[4096, 128], [512, 8], [64, 8], [1, 32]])
        src0 = AP(wA.t, 0, [[4096, 128], [512, 8], [64, 8], [1, 32]])
        dst0 = AP(wR.t, 0, [[4096, 128], [512, 8], [64, 8], [1, 32]])
        dst1 = AP(wR.t, 32, [[4096, 128], [512, 8], [64, 8], [1, 32]])
        P.g(lambda e: e.tensor_scalar(out=dst0, in0=src1, scalar1=-1.0, scalar2=None, op0=ALU.mult), r=[wA.tok], w=[wR.tok])
        P.g(lambda e: e.tensor_copy(out=dst1, in_=src0), r=[wA.tok], w=[wR.tok])
        rq, rk, rg, rvpad, oT = P0, P1, P2, VA, F[7]
        P.g(lambda e: e.memset(VA[:], 0.0), w=[VA.tok])
        for i in range(2):
            P.v(lambda e, i=i: e.memset(Sret[i][:], 0.0), w=[Sret[i].tok])
            P.v(lambda e, i=i: e.memset(Sret_b[i][:], 0.0), w=[Sret_b[i].tok])
        for blk in range(NBLK):
            rope_tables(blk)
            for i in range(2):
                rope_evac(fm_group(wA, i * 128, blk), fm_group(wR, i * 128, blk), i, rq, decq)
            for i in range(2):
                rope_evac(fm_group(wA, 256 + i * 128, blk), fm_group(wR, 256 + i * 128, blk), i, rk, deck)
            for c in range(4):
                ps = tm_group(wB, 0, blk, c)
                dst = AP(rvpad.t, c * 512, [[2048, 128], [256, 2], [192, 2], [1, 64]])
                P.a(lambda e, ps=ps, dst=dst: e.activation(out=dst, in_=ps[:, 0:256].rearrange("p (i h e) -> p i h e", i=2, h=2),
                                                           func=AF.Copy), r=[ps.tok], w=[rvpad.tok])
            for i in range(2):
                ps = fm_group(wB, 256 + i * 128, blk)
                P.a(lambda e, ps=ps, i=i: e.activation(out=rg[:, i, :], in_=ps[:, :], func=AF.Silu), r=[ps.tok], w=[rg.tok])
            mask4 = AP(cst["maskT"].t, 0, [[128, 128], [0, 4], [1, 128]])
            for i in range(2):
                pso = banks[6 + i]
                P.t(lambda e, pso=pso: e.matmul(pso[:, :], lhsT=zeros_b[:, 0:128], rhs=hnT[:, 0, 0:512], start=True, stop=False), r=[zeros_b.tok, htok[0]], w=[pso.tok])
                for h2 in range(2):
                    ps = psum()
                    hs = slice(64 * h2, 64 * h2 + 64)
                    for c in range(4):
                        cs = slice(c * 128, (c + 1) * 128)
                        P.t(lambda e, ps=ps, hs=hs, cs=cs, i=i: e.matmul(ps[:, cs], lhsT=rk[hs, i, cs], rhs=rq[hs, i, cs], start=True, stop=True),
                            r=[rk.tok, rq.tok], w=[ps.tok])
                    P.v(lambda e, ps=ps, i=i, h2=h2: e.tensor_tensor(out=ET[:, i * 2 + h2, :].rearrange("p (c l) -> p c l", c=4), in0=ps[:, :].rearrange("p (c l) -> p c l", c=4),
                                                                     in1=mask4, op=ALU.mult), r=[ps.tok, cst["maskT"].tok], w=[ET.tok[i * 2 + h2]])
                for c in range(4):
                    cs = slice(c * 128, (c + 1) * 128)
                    for h2 in range(2):
                        P.t(lambda e, c=c, i=i, h2=h2, cs=cs, pso=pso: e.matmul(pso[:, cs], lhsT=rvpad[:, c, i, h2, :], rhs=ET[:, i * 2 + h2, cs], start=False, stop=False),
                            r=[rvpad.tok, ET.tok[i * 2 + h2]], w=[pso.tok])
                pst = psum()
                pstb = pst.t[:].bitcast(BF16)
                for c in range(4):
                    cs = slice(c * 128, (c + 1) * 128)
                    P.t(lambda e, pstb=pstb, i=i, cs=cs: e.transpose(pstb[:, cs], rk[:, i, cs], ident_b[:]), r=[rk.tok, ident_b.tok], w=[pst.tok])
                ktm = sq2[i]
                P.a(lambda e, pstb=pstb, ktm=ktm: e.activation(out=ktm[:], in_=pstb[:, 0:512], func=AF.Copy), r=[pst.tok], w=[ktm.tok])
                psu = psum()
                for c in range(4):
                    cs = slice(c * 128, (c + 1) * 128)
                    vsl = AP(rvpad.t, c * 512 + i * 256, [[2048, 128], [192, 2], [1, 64]])
                    P.t(lambda e, psu=psu, ktm=ktm, vsl=vsl, cs=cs: e.matmul(psu[:, cs].rearrange("p (h e) -> p h e", h=2), lhsT=ktm[:, cs], rhs=vsl, start=True, stop=True),
                        r=[ktm.tok, rvpad.tok], w=[psu.tok])
                tt = F[4 + i]
                bdg = AP(bd2g.t, i * 128, [[256, 128], [0, 4], [1, 128]])
                P.v(lambda e, psu=psu, tt=tt, bdg=bdg: e.tensor_tensor(out=tt[:].rearrange("p (c x) -> p c x", c=4), in0=psu[:, :].rearrange("p (c x) -> p c x", c=4), in1=bdg, op=ALU.mult),
                    r=[psu.tok, bd2g.tok], w=[tt.tok])
            for c in range(4):
                cs = slice(c * 128, (c + 1) * 128)
                for i in range(2):
                    pso = banks[6 + i]
                    tt = F[4 + i]
                    P.t(lambda e, i=i, cs=cs, pso=pso, c=c: e.matmul(pso[:, cs], lhsT=Sret_b[i][:], rhs=rq[:, i, cs], start=False, stop=(c == 3)),
                        r=[Sret_b[i].tok, rq.tok], w=[pso.tok])
                    P.v(lambda e, i=i, tt=tt, cs=cs: e.scalar_tensor_tensor(out=Sret_b[i][:], in0=Sret[i][:], scalar=gL[:, i:i + 1], in1=tt[:, cs], op0=ALU.mult, op1=ALU.add),
                        r=[Sret[i].tok, gL.tok, tt.tok], w=[Sret_b[i].tok])
                    P.v(lambda e, i=i, tt=tt, cs=cs: e.scalar_tensor_tensor(out=Sret[i][:], in0=Sret[i][:], scalar=gL[:, i:i + 1], in1=tt[:, cs], op0=ALU.mult, op1=ALU.add),
                        r=[Sret[i].tok, gL.tok, tt.tok], w=[Sret[i].tok])
            for i in range(2):
                pso = banks[6 + i]
                oTi, ceni, sqi = (F[7], F[8], F[9]) if i == 0 else (F[0], F[1], F[2])
                P.a(lambda e, pso=pso, oTi=oTi: e.activation(out=oTi[:], in_=pso[:, :], func=AF.Copy), r=[pso.tok], w=[oTi.tok])
                head_norm_gate(oTi, rg[:, i, :], rg.tok, gn[:, l * 4 + i:l * 4 + i + 1], mixp[:, i, :], ceni, sqi, mixp.tok[i])
            outproj_part(wo, blk)
        for i in range(2):
            for h2 in range(2):
                P.dma("sync", o_ret[l, 2 * i + h2], Sret[i][64 * h2:64 * h2 + 64, 64 * h2:64 * h2 + 64],
                      r=[Sret[i].tok], is_output=True)
        if SAMPLE >= 1:
            ret_sample(l, wA, wB, wo)

    Cml = [sb("Cml%d" % i, [128, 256]) for i in range(2)]
    Cml_b = [sb("Cmlb%d" % i, [128, 256], BF16) for i in range(2)]
    Bcar = [sb("Bcar%d" % i, [128, 1]) for i in range(2)]
    Gcar = [sb("Gcar%d" % i, [128, 1]) for i in range(2)]
    E1t = sb("E1", [128, 128])
    mkt = [sb("mkt%d" % i, [128, 128], BF16) for i in range(2)]
    mkrr = [0]
    gs_col = sb("gs_col", [128, 4])
    ngs_col = sb("ngs_col", [128, 4])
    mlm_o = sb("mlm_o", [128, 2])

    def ml_phase(l):
        W = d_win[l]
        wC = wload([(W, 1024, 1536, 0)])
        wD = wload([(W, 1536, 2048, 0)])
        wE = wload([(W, 2048, 2312, 0)])
        wo = woload(l, 1)
        mq, mo, mg, mk_, mvp = P0, P1, P2, Pf, VA
        gates8 = Buf(F[6].t[0:8, :], 'gates8')
        gates8.tok = F[6].tok
        LFt, IGt, d0s, oT = F[0], F[1], F[5], F[7]
        ones_col = cst["l1"][:, 0:1].to_broadcast([128, TB])
        rot[0] = [0, 1, 2, 3, 4, 5]
        mask4 = AP(cst["maskT"].t, 0, [[128, 128], [0, 4], [1, 128]])
        for i in range(2):
            P.v(lambda e, i=i: e.memset(Cml[i][:], 0.0), w=[Cml[i].tok])
            P.v(lambda e, i=i: e.memset(Cml_b[i][:], 0.0), w=[Cml_b[i].tok])
            P.v(lambda e, i=i: e.memset(Bcar[i][:], 0.0), w=[Bcar[i].tok])
            P.v(lambda e, i=i: e.memset(Gcar[i][:], 0.0), w=[Gcar[i].tok])
        for blk in range(NBLK):
            for i in range(2 if KMASK & 1 else 0):
                ps = fm_group(wC, i * 128, blk)
                P.a(lambda e, ps=ps, i=i: e.activation(out=mq[:, i, :], in_=ps[:, :], func=AF.Copy), r=[ps.tok], w=[mq.tok])
            for i in range(2 if KMASK & 2 else 0):
                ps = fm_group(wC, 256 + i * 128, blk)
                P.a(lambda e, ps=ps, i=i: e.activation(out=mk_[:, i, :], in_=ps[:, :], func=AF.Copy), r=[ps.tok], w=[Pf.tok[i]])
            for c in range(4 if KMASK & 4 else 0):
                ps = tm_group(wD, 0, blk, c)
                dst = AP(mvp.t, c * 512, [[2048, 128], [256, 2], [192, 2], [1, 64]])
                P.a(lambda e, ps=ps, dst=dst: e.activation(out=dst, in_=ps[:, 0:256].rearrange("p (i h e) -> p i h e", i=2, h=2),
                                                           func=AF.Copy), r=[ps.tok], w=[mvp.tok])
            for i in range(2 if KMASK & 8 else 0):
                ps = fm_group(wD, 256 + i * 128, blk)
                P.a(lambda e, ps=ps, i=i: e.activation(out=mo[:, i, :], in_=ps[:, :], func=AF.Tanh, scale=0.5), r=[ps.tok], w=[mo.tok])
            for i in range(2 if KMASK & 16 else 0):
                ps = fm_group(wE, i * 128, blk)
                P.a(lambda e, ps=ps, i=i: e.activation(out=mg[:, i, :], in_=ps[:, :], func=AF.Silu), r=[ps.tok], w=[mg.tok])
            if KMASK & 32:
                ps = fm_group(wE, 256, blk, m=8)
                P.a(lambda e, ps=ps: e.activation(out=gates8[:, :], in_=ps[0:8, :], func=AF.Identity, bias=bif[:, l:l + 1]),
                    r=[ps.tok, bif.tok], w=[gates8.tok])
            for i in range(2 if KSUB >= 2 else 0):
                Gt = F[2] if i == 0 else F[4]
                MTt = F[3] if i == 0 else F[10]
                rt = Gt
                psi = psum()
                P.t(lambda e, psi=psi, i=i: e.matmul(psi[:, :], lhsT=cst["gexp"][:, (2 * i) * 128:(2 * i + 1) * 128], rhs=gates8[:, :],
                                                     start=True, stop=True), r=[cst["gexp"].tok, gates8.tok], w=[psi.tok])
                psf = psum()
                P.t(lambda e, psf=psf, i=i: e.matmul(psf[:, :], lhsT=cst["gexp"][:, (2 * i + 1) * 128:(2 * i + 2) * 128], rhs=gates8[:, :],
                                                     start=True, stop=True), r=[cst["gexp"].tok, gates8.tok], w=[psf.tok])
                P.a(lambda e, psf=psf: e.activation(out=LFt[:], in_=psf[:, :], func=AF.Exp, scale=-1.0), r=[psf.tok], w=[LFt.tok])
                P.a(lambda e: e.activation(out=LFt[:], in_=LFt[:], func=AF.Ln, bias=1.0), r=[LFt.tok], w=[LFt.tok])
                P.v(lambda e, i=i: e.tensor_tensor_scan(out=LFt[:], data0=ones_col, data1=LFt[:], initial=Bcar[i][:, 0:1],
                                                        op0=ALU.mult, op1=ALU.subtract), r=[LFt.tok, Bcar[i].tok, cst["l1"].tok], w=[LFt.tok])
                P.v(lambda e, psi=psi: e.tensor_tensor(out=IGt[:], in0=psi[:, :], in1=LFt[:], op=ALU.subtract), r=[psi.tok, LFt.tok], w=[IGt.tok])
                P.v(lambda e, i=i: e.tensor_tensor_scan(out=Gt[:], data0=ones_col, data1=IGt[:], initial=Gcar[i][:, 0:1],
                                                        op0=ALU.mult, op1=ALU.max), r=[IGt.tok, Gcar[i].tok, cst["l1"].tok], w=[Gt.tok])
                P.v(lambda e, i=i: e.tensor_copy(out=gs_col[:, 0:1], in_=Gcar[i][:, 0:1]), r=[Gcar[i].tok], w=[gs_col.tok])
                P.v(lambda e: e.tensor_copy(out=gs_col[:, 1:4], in_=AP(Gt.t, 127, [[TB, 128], [128, 3]])), r=[Gt.tok], w=[gs_col.tok])
                P.v(lambda e: e.tensor_scalar(out=ngs_col[:], in0=gs_col[:], scalar1=-1.0, scalar2=None, op0=ALU.mult), r=[gs_col.tok], w=[ngs_col.tok])
                P.v(lambda e: e.tensor_tensor(out=MTt[:], in0=LFt[:], in1=Gt[:], op=ALU.add), r=[LFt.tok, Gt.tok], w=[MTt.tok])
                if blk == NBLK - 1:
                    P.v(lambda e, i=i: e.tensor_copy(out=mlm_o[:, i:i + 1], in_=MTt[:, TB - 1:TB]), r=[MTt.tok], w=[mlm_o.tok])
                P.a(lambda e: e.activation(out=MTt[:], in_=MTt[:], func=AF.Exp, scale=-1.0), r=[MTt.tok], w=[MTt.tok])
                P.v(lambda e, i=i: e.tensor_copy(out=Bcar[i][:, 0:1], in_=LFt[:, TB - 1:TB]), r=[LFt.tok], w=[Bcar[i].tok])
                P.v(lambda e, i=i: e.tensor_copy(out=Gcar[i][:, 0:1], in_=Gt[:, TB - 1:TB]), r=[Gt.tok], w=[Gcar[i].tok])
                psn = banks[6]
                psd = banks[7]
                if KSUB < 3:
                    continue
                for c in range(4):
                    cs = slice(c * 128, (c + 1) * 128)
                    P.a(lambda e, c=c, cs=cs: e.activation(out=IGt[:, cs], in_=IGt[:, cs], func=AF.Exp, bias=ngs_col[:, c:c + 1]),
                        r=[IGt.tok, ngs_col.tok], w=[IGt.tok])
                    P.a(lambda e, c=c, cs=cs, Gt=Gt: e.activation(out=Gt[:, cs], in_=Gt[:, cs], func=AF.Exp, scale=-1.0, bias=gs_col[:, c:c + 1]),
                        r=[Gt.tok, gs_col.tok], w=[Gt.tok])
                ktil = mixp[:, i, :]
                mtk = mixp.tok[i]
                P.v(lambda e, i=i, ktil=ktil: e.scalar_tensor_tensor(out=ktil, in0=mk_[:, i, :], scalar=0.125, in1=IGt[:], op0=ALU.mult, op1=ALU.mult),
                    r=[Pf.tok[i], IGt.tok], w=[mtk])
                for bk in (psn, psd):
                    P.t(lambda e, bk=bk: e.matmul(bk[:, :], lhsT=zeros_b[:, 0:128], rhs=hnT[:, 0, 0:512], start=True, stop=False), r=[zeros_b.tok, htok[0]], w=[bk.tok])
                for h2 in range(2):
                    ps = psum()
                    hs = slice(64 * h2, 64 * h2 + 64)
                    ek = 2 * i + h2
                    for c in range(4):
                        cs = slice(c * 128, (c + 1) * 128)
                        P.t(lambda e, ps=ps, hs=hs, cs=cs, i=i: e.matmul(ps[:, cs], lhsT=mixp[hs, i, cs], rhs=mq[hs, i, cs], start=True, stop=True),
                            r=[mtk, mq.tok], w=[ps.tok])
                    P.v(lambda e, ps=ps, ek=ek: e.tensor_tensor(out=ET[:, ek, :].rearrange("p (c l) -> p c l", c=4), in0=ps[:, :].rearrange("p (c l) -> p c l", c=4),
                                                                in1=mask4, op=ALU.mult), r=[ps.tok, cst["maskT"].tok], w=[ET.tok[ek]])
                for c in range(4):
                    cs = slice(c * 128, (c + 1) * 128)
                    for h2 in range(2):
                        ek = 2 * i + h2
                        P.t(lambda e, c=c, i=i, h2=h2, cs=cs, ek=ek, psn=psn: e.matmul(psn[:, cs], lhsT=mvp[:, c, i, h2, :], rhs=ET[:, ek, cs], start=False, stop=False),
                            r=[mvp.tok, ET.tok[ek]], w=[psn.tok])
                        P.t(lambda e, h2=h2, cs=cs, ek=ek, psd=psd: e.matmul(psd[:, cs], lhsT=onespad[:, h2, :], rhs=ET[:, ek, cs], start=False, stop=False),
                            r=[onespad.tok, ET.tok[ek]], w=[psd.tok])
                pst = psum()
                pstb = pst.t[:].bitcast(BF16)
                for c in range(4):
                    cs = slice(c * 128, (c + 1) * 128)
                    P.t(lambda e, pstb=pstb, i=i, cs=cs: e.transpose(pstb[:, cs], mixp[:, i, cs], ident_b[:]), r=[mtk, ident_b.tok], w=[pst.tok])
                ktm = sq2[i]
                P.a(lambda e, pstb=pstb, ktm=ktm: e.activation(out=ktm[:], in_=pstb[:, 0:512], func=AF.Copy), r=[pst.tok], w=[ktm.tok])
                psuA = psum()
                psuB = psum()
                for c in range(4):
                    cs = slice(c * 128, (c + 1) * 128)
                    vsl = AP(mvp.t, c * 512 + i * 256, [[2048, 128], [192, 2], [1, 64]])
                    P.t(lambda e, ktm=ktm, vsl=vsl, cs=cs, psuA=psuA: e.matmul(psuA[:, cs].rearrange("p (h e) -> p h e", h=2), lhsT=ktm[:, cs], rhs=vsl, start=True, stop=True),
                        r=[ktm.tok, mvp.tok], w=[psuA.tok])
                    P.t(lambda e, ktm=ktm, cs=cs, c=c, psuB=psuB: e.matmul(psuB[:, c:c + 1], lhsT=ktm[:, cs], rhs=ones_b[:, 0:1], start=True, stop=True),
                        r=[ktm.tok, ones_b.tok], w=[psuB.tok])
                ttA = Pf[:, i, :]
                for c in range(4):
                    cs = slice(c * 128, (c + 1) * 128)
                    rcol = rt[:, c * 128 + 127:c * 128 + 128]
                    P.v(lambda e, cs=cs, rcol=rcol, psuA=psuA, i=i: e.scalar_tensor_tensor(out=Pf[:, i, cs], in0=psuA[:, cs], scalar=rcol, in1=cst["bd2"][:], op0=ALU.mult, op1=ALU.mult),
                        r=[psuA.tok, rt.tok, cst["bd2"].tok, mtk], w=[Pf.tok[i]])
                rcols = AP(rt.t, 127, [[TB, 128], [128, 4]])
                P.v(lambda e, psuB=psuB, i=i: e.tensor_tensor(out=ncs[:, i, :], in0=psuB[:, 0:4], in1=rcols, op=ALU.mult), r=[psuB.tok, rt.tok], w=[ncs.tok])
                for c in range(4):
                    cs = slice(c * 128, (c + 1) * 128)
                    P.t(lambda e, i=i, cs=cs, psn=psn: e.matmul(psn[:, cs], lhsT=Cml_b[i][:, 0:128], rhs=mq[:, i, cs], start=False, stop=(cs.stop == 512)),
                        r=[Cml_b[i].tok, mq.tok], w=[psn.tok])
                    P.t(lambda e, i=i, cs=cs, psd=psd: e.matmul(psd[:, cs], lhsT=Cml_b[i][:, 128:256], rhs=mq[:, i, cs], start=False, stop=(cs.stop == 512)),
                        r=[Cml_b[i].tok, mq.tok], w=[psd.tok])
                    tt = tts[ttr[0] % 2]
                    ttr[0] += 1
                    rcol = rt[:, c * 128 + 127:c * 128 + 128]
                    P.v(lambda e, tt=tt, c=c, i=i: e.tensor_scalar(out=tt[:, 0:128], in0=cst["bd2"][:], scalar1=ncs[:, i, c:c + 1], scalar2=None, op0=ALU.mult),
                        r=[ncs.tok, cst["bd2"].tok], w=[tt.tok])
                    P.v(lambda e, i=i, cs=cs, rcol=rcol: e.scalar_tensor_tensor(out=Cml_b[i][:, 0:128], in0=Cml[i][:, 0:128], scalar=rcol, in1=Pf[:, i, cs], op0=ALU.mult, op1=ALU.add),
                        r=[Cml[i].tok, rt.tok, Pf.tok[i]], w=[Cml_b[i].tok])
                    P.v(lambda e, tt=tt, i=i, rcol=rcol: e.scalar_tensor_tensor(out=Cml_b[i][:, 128:256], in0=Cml[i][:, 128:256], scalar=rcol, in1=tt[:, 0:128], op0=ALU.mult, op1=ALU.add),
                        r=[Cml[i].tok, rt.tok, tt.tok], w=[Cml_b[i].tok])
                    P.v(lambda e, i=i, cs=cs, rcol=rcol: e.scalar_tensor_tensor(out=Cml[i][:, 0:128], in0=Cml[i][:, 0:128], scalar=rcol, in1=Pf[:, i, cs], op0=ALU.mult, op1=ALU.add),
                        r=[Cml[i].tok, rt.tok, Pf.tok[i]], w=[Cml[i].tok])
                    P.v(lambda e, tt=tt, i=i, rcol=rcol: e.scalar_tensor_tensor(out=Cml[i][:, 128:256], in0=Cml[i][:, 128:256], scalar=rcol, in1=tt[:, 0:128], op0=ALU.mult, op1=ALU.add),
                        r=[Cml[i].tok, rt.tok, tt.tok], w=[Cml[i].tok])
                d0i, oTi, ceni, sqi = (F[5], F[7], F[8], F[9]) if i == 0 else (F[0], F[1], F[6], F[0])
                P.v(lambda e, psd=psd, d0i=d0i: e.tensor_tensor(out=d0i[:], in0=psd[:, :], in1=rt[:], op=ALU.mult), r=[psd.tok, rt.tok], w=[d0i.tok])
                P.a(lambda e, psn=psn, oTi=oTi: e.activation(out=oTi[:], in_=psn[:, :], func=AF.Copy), r=[psn.tok], w=[oTi.tok])
                P.a(lambda e, d0i=d0i: e.activation(out=d0i[:], in_=d0i[:], func=AF.Abs), r=[d0i.tok], w=[d0i.tok])
                P.v(lambda e, d0i=d0i: e.tensor_tensor(out=d0i[:], in0=d0i[:], in1=MTt[:], op=ALU.max), r=[d0i.tok, MTt.tok], w=[d0i.tok])
                P.a(lambda e, d0i=d0i: e.activation(out=d0i[:], in_=d0i[:], func=AF.Ln), r=[d0i.tok], w=[d0i.tok])
                P.a(lambda e, d0i=d0i: e.activation(out=d0i[:], in_=d0i[:], func=AF.Exp, scale=-1.0), r=[d0i.tok], w=[d0i.tok])
                P.v(lambda e, d0i=d0i: e.scalar_tensor_tensor(out=d0i[:], in0=d0i[:], scalar=0.5, in1=rt[:], op0=ALU.mult, op1=ALU.mult), r=[d0i.tok, rt.tok], w=[d0i.tok])
                P.v(lambda e, d0i=d0i, oTi=oTi: e.tensor_tensor(out=oTi[:], in0=oTi[:], in1=d0i[:], op=ALU.mult), r=[oTi.tok, d0i.tok], w=[oTi.tok])
                P.v(lambda e, i=i, oTi=oTi: e.scalar_tensor_tensor(out=oTi[:], in0=mo[:, i, :], scalar=1.0, in1=oTi[:], op0=ALU.add, op1=ALU.mult), r=[oTi.tok, mo.tok], w=[oTi.tok])
                head_norm_gate(oTi, mg[:, i, :], mg.tok, gn[:, l * 4 + 2 + i:l * 4 + 2 + i + 1], mixp[:, i, :], ceni, sqi, mixp.tok[i])
            if KSUB >= 3:
                outproj_part(wo, blk)
        for i in range(2 if KSUB >= 4 else 0):
            for h2 in range(2):
                hs = slice(64 * h2, 64 * h2 + 64)
                P.dma("sync", o_mlc[l, 2 * i + h2], Cml[i][hs, 64 * h2:64 * h2 + 64], r=[Cml[i].tok], is_output=True)
                P.dma("sync", o_mln[l, 2 * i + h2].rearrange("(a b) -> a b", b=1), Cml[i][hs, 128 + 64 * h2:128 + 64 * h2 + 1],
                      r=[Cml[i].tok], is_output=True)
                P.dma("sync", o_mlm[l, 2 * i + h2:2 * i + h2 + 1].rearrange("(a b) -> a b", b=1), mlm_o[64 * h2:64 * h2 + 1, i:i + 1],
                      r=[mlm_o.tok], is_output=True)
        rot[0] = [0, 1, 2, 3, 4, 5]
        if SAMPLE >= 2:
            ml_sample(l, wC, wD, wE, wo)

    def xa_phase(l):
        W = d_win[l]
        wG = wload([(W, 2824, 3336, 0)])
        wo = woload(l, 3)
        aq, ag = P0, P1
        ett = [[Buf(ET.t[:, k, :], "ET%d" % k) for k in range(4)],
               [Buf(F[1 + k].t[:].bitcast(BF16)[:, 0:TB], "ETb%d" % k) for k in range(4)]]
        for k in range(4):
            ett[0][k].tok = ET.tok[k]
            ett[1][k].tok = F[1 + k].tok
        for blk in range(NBLK):
            for i in range(2):
                ps = fm_group(wG, i * 128, blk)
                P.a(lambda e, ps=ps, i=i, aq=aq: e.activation(out=aq[:, i, :], in_=ps[:, :], func=AF.Copy), r=[ps.tok], w=[aq.tok])
            for i in range(2):
                ps = fm_group(wG, 256 + i * 128, blk)
                P.a(lambda e, ps=ps, i=i, ag=ag: e.activation(out=ag[:, i, :], in_=ps[:, :], func=AF.Silu), r=[ps.tok], w=[ag.tok])
            for i in range(2):
                for h2 in range(2):
                    hs = slice(64 * h2, 64 * h2 + 64)
                    for mt in range(2):
                        ps = psum()
                        P.t(lambda e, ps=ps, hs=hs, mt=mt, i=i: e.matmul(ps[:, :], lhsT=mkT[hs, i, mt * 128:(mt + 1) * 128], rhs=aq[hs, i, :],
                                                                         start=True, stop=True), r=[mkT.tok, aq.tok], w=[ps.tok])
                        et = ett[i][h2 * 2 + mt]
                        P.a(lambda e, ps=ps, et=et: e.activation(out=et.t, in_=ps[:, :], func=AF.Exp, scale=0.125), r=[ps.tok], w=[et.tok])
                pso = banks[6] if i == 0 else psum()
                psd = banks[7] if i == 0 else psum()
                recd = F[0] if i == 0 else F[5]
                n = 0
                for h2 in range(2):
                    for mt in range(2):
                        et = ett[i][h2 * 2 + mt]
                        P.t(lambda e, h2=h2, mt=mt, i=i, n=n, et=et, pso=pso: e.matmul(pso[:, :], lhsT=mvpad[:, mt, i, h2, :], rhs=et.t,
                                                                                     start=(n == 0), stop=(n == 3)), r=[mvpad.tok, et.tok], w=[pso.tok])
                        n += 1
                n = 0
                for h2 in range(2):
                    for mt in range(2):
                        et = ett[i][h2 * 2 + mt]
                        P.t(lambda e, h2=h2, mt=mt, n=n, et=et, psd=psd: e.matmul(psd[:, :], lhsT=onespad[:, h2, :], rhs=et.t,
                                                                                start=(n == 0), stop=(n == 3)), r=[onespad.tok, et.tok], w=[psd.tok])
                        n += 1
                P.a(lambda e, psd=psd, recd=recd: e.activation(out=recd[:], in_=psd[:, :], func=AF.Ln), r=[psd.tok], w=[recd.tok])
                P.a(lambda e, recd=recd: e.activation(out=recd[:], in_=recd[:], func=AF.Exp, scale=-1.0), r=[recd.tok], w=[recd.tok])
                P.v(lambda e, pso=pso, recd=recd: e.tensor_tensor(out=recd[:], in0=pso[:, :], in1=recd[:], op=ALU.mult), r=[pso.tok, recd.tok], w=[recd.tok])
                P.v(lambda e, i=i, recd=recd: e.tensor_tensor(out=mixp[:, i, :], in0=recd[:], in1=ag[:, i, :], op=ALU.mult),
                    r=[recd.tok, ag.tok], w=[mixp.tok[i]])
            outproj_part(wo, blk)
        if SAMPLE >= 3:
            xa_sample(l, wG, wo)


    d_s5par = din("s5par", [128, DEPTH * 3 * 8])
    d_s5b = din("s5b", [DEPTH, 128, 2 * 8 * 16])
    d_s5c = din("s5c", [DEPTH, 128, 2 * 8 * 16])
    d_s5d = din("s5d", [128, DEPTH * 2])
    d_wglu = din("w_glu", [DEPTH, 256, 256])
    o_s5 = dout("o_s5", [DEPTH, 128, 2, 8])
    s5par = sb("s5par", [128, DEPTH, 3, 8])
    s5b = sb("s5b", [128, 2, 8, 16])
    s5c = sb("s5c", [128, 2, 8, 16])
    s5d = sb("s5d", [128, DEPTH, 2])
    P.dma("sync", s5par[:], d_s5par.rearrange("p (l w g) -> p l w g", l=DEPTH, w=3), w=[s5par.tok])
    P.dma("sync", s5d[:], d_s5d.rearrange("p (l f) -> p l f", l=DEPTH), w=[s5d.tok])
    sA = sb("sA", [128, 6, 8])
    sK = sb("sK", [128, 5, 8, 9])
    sKi = sb("sKi", [128, 8, 9], I32)
    sBB = sb("sBB", [128, 2, 8, 16])
    sE = sb("sE", [128, 4, 9, 16])
    sT = Buf(F[9].t[:, 0:288].rearrange("p (a k c) -> p a k c", a=2, k=9), "sT")
    sT.tok = F[9].tok
    s5o = sb("s5o", [128, 2, 8])
    d_sx0 = din("sx0", [DEPTH, 128, 2 * 8 * NS])
    o_s5s = dout("o_s5s", [DEPTH, 128, 2 * 8 * NS])
    xs0 = sb("xs0", [128, 2, 8, NS])
    xs0b = sb("xs0b", [128, 2, 8, NS], BF16)
    usT = sb("usT", [128, 2, NS], BF16)
    sgs = sb("sgs", [128, 2, NS])
    ysacc = sb("ysacc", [128, 2, NS])
    sst = sb("sst", [128, 4, NS])

    def s5_phase(l):
        W = d_win[l]
        wF = wload([(W, 2312, 2824, 0)])
        wo = woload(l, 2)
        P.dma("sync", s5b[:], d_s5b[l].rearrange("p (r g c) -> p r g c", r=2, g=8), w=[s5b.tok])
        P.dma("sync", s5c[:], d_s5c[l].rearrange("p (r g c) -> p r g c", r=2, g=8), w=[s5c.tok])
        uT = [ET, VA]
        pfb = Pf.t[:].rearrange("p a b -> p (a b)").bitcast(BF16)

        def sgp(ft, blk):
            ix = ft * 4 + blk
            if ix < 4:
                return pfb[:, ix * TB:(ix + 1) * TB], Pf.tok
            if ix < 6:
                return P2[:, ix - 4, :], P2.tok
            return sq2[ix - 6][:, :], sq2[ix - 6].tok
        uTap = [ET.t[:].rearrange("p a b -> p (a b)"), VA.t[:].rearrange("p a b c d -> p (a b c d)")]
        for blk in range(NBLK):
            for ft in range(2):
                ps = fm_group(wF, ft * 128, blk)
                dsti = AP(uT[ft].t, blk * 64, [[T, 128], [1, 64], [256, 8]])
                P.a(lambda e, ps=ps, dsti=dsti: e.activation(out=dsti, in_=ps[:, :].rearrange("p (j s) -> p j s", s=8), func=AF.Copy),
                    r=[ps.tok], w=[uT[ft].tok])
            for ft in range(2):
                ps = fm_group(wF, 256 + ft * 128, blk)
                sga, sgtok = sgp(ft, blk)
                P.a(lambda e, ps=ps, sga=sga: e.activation(out=sga, in_=ps[:, :], func=AF.Silu), r=[ps.tok], w=[sgtok])
        if SAMPLE >= 4:
            for ft in range(2):
                ps = fm_sample(wF, ft * 128)
                P.a(lambda e, ps=ps, ft=ft: e.activation(out=usT[:, ft, :], in_=ps[:, 0:NS], func=AF.Copy), r=[ps.tok], w=[usT.tok])
                ps = fm_sample(wF, 256 + ft * 128)
                P.a(lambda e, ps=ps, ft=ft: e.activation(out=sgs[:, ft, :], in_=ps[:, 0:NS], func=AF.Silu), r=[ps.tok], w=[sgs.tok])
            P.dma("sync", xs0[:].rearrange("p a b c -> p (a b c)"), d_sx0[l], w=[xs0.tok])
            P.v(lambda e: e.tensor_copy(out=xs0b[:], in_=xs0[:]), r=[xs0.tok], w=[xs0b.tok])
            P.v(lambda e: e.memset(ysacc[:], 0.0), w=[ysacc.tok])
        are, aim, ldt = s5par[:, l, 0, :], s5par[:, l, 1, :], s5par[:, l, 2, :]
        dtc, ardt, th, fre, fim, tm8 = (sA[:, j, :] for j in range(6))
        P.a(lambda e: e.activation(out=dtc, in_=ldt, func=AF.Exp), r=[s5par.tok], w=[sA.tok])
        P.v(lambda e: e.tensor_tensor(out=ardt, in0=are, in1=dtc, op=ALU.mult), r=[s5par.tok, sA.tok], w=[sA.tok])
        P.v(lambda e: e.tensor_tensor(out=th, in0=aim, in1=dtc, op=ALU.mult), r=[s5par.tok, sA.tok], w=[sA.tok])
        P.v(lambda e: e.tensor_scalar(out=tm8, in0=th, scalar1=8.0, scalar2=None, op0=ALU.mult), r=[sA.tok], w=[sA.tok])
        kg = AP(cst["kgrid"].t, 0, [[9, 128], [0, 8], [1, 9]])
        arg, ang, Pr, Pi, scr = (sK[:, j, :, :] for j in range(5))
        P.v(lambda e: e.tensor_tensor(out=arg, in0=AP(sA.t, 8, [[48, 128], [1, 8], [0, 9]]), in1=kg, op=ALU.mult), r=[sA.tok, cst["kgrid"].tok], w=[sK.tok])
        P.a(lambda e: e.activation(out=arg, in_=arg, func=AF.Exp), r=[sK.tok], w=[sK.tok])
        P.v(lambda e: e.tensor_tensor(out=ang, in0=AP(sA.t, 16, [[48, 128], [1, 8], [0, 9]]), in1=kg, op=ALU.mult), r=[sA.tok, cst["kgrid"].tok], w=[sK.tok])
        sincos(ang, sK.tok, sKi[:], scr, sKi.tok, Pi, sK.tok, Pr, sK.tok)
        P.v(lambda e: e.tensor_tensor(out=Pr, in0=Pr, in1=arg, op=ALU.mult), r=[sK.tok], w=[sK.tok])
        P.v(lambda e: e.tensor_tensor(out=Pi, in0=Pi, in1=arg, op=ALU.mult), r=[sK.tok], w=[sK.tok])
        nr, ni, den = scr[:, :, 0], scr[:, :, 1], scr[:, :, 2]
        t_a, t_b = scr[:, :, 3], scr[:, :, 4]
        P.v(lambda e: e.tensor_scalar(out=nr, in0=sK[:, 2, :, 1], scalar1=-1.0, scalar2=None, op0=ALU.add), r=[sK.tok], w=[sK.tok])
        P.v(lambda e: e.tensor_copy(out=ni, in_=sK[:, 3, :, 1]), r=[sK.tok], w=[sK.tok])
        P.v(lambda e: e.tensor_tensor(out=den, in0=are, in1=are, op=ALU.mult), r=[s5par.tok], w=[sK.tok])
        P.v(lambda e: e.tensor_tensor(out=t_a, in0=aim, in1=aim, op=ALU.mult), r=[s5par.tok], w=[sK.tok])
        P.v(lambda e: e.tensor_tensor(out=den, in0=den, in1=t_a, op=ALU.add), r=[sK.tok], w=[sK.tok])
        P.v(lambda e: e.reciprocal(out=den, in_=den), r=[sK.tok], w=[sK.tok])
        P.v(lambda e: e.tensor_tensor(out=t_a, in0=nr, in1=are, op=ALU.mult), r=[sK.tok, s5par.tok], w=[sK.tok])
        P.v(lambda e: e.tensor_tensor(out=t_b, in0=ni, in1=aim, op=ALU.mult), r=[sK.tok, s5par.tok], w=[sK.tok])
        P.v(lambda e: e.tensor_tensor(out=t_a, in0=t_a, in1=t_b, op=ALU.add), r=[sK.tok], w=[sK.tok])
        P.v(lambda e: e.tensor_tensor(out=fre, in0=t_a, in1=den, op=ALU.mult), r=[sK.tok], w=[sA.tok])
        P.v(lambda e: e.tensor_tensor(out=t_a, in0=ni, in1=are, op=ALU.mult), r=[sK.tok, s5par.tok], w=[sK.tok])
        P.v(lambda e: e.tensor_tensor(out=t_b, in0=nr, in1=aim, op=ALU.mult), r=[sK.tok, s5par.tok], w=[sK.tok])
        P.v(lambda e: e.tensor_tensor(out=t_a, in0=t_a, in1=t_b, op=ALU.subtract), r=[sK.tok], w=[sK.tok])
        P.v(lambda e: e.tensor_tensor(out=fim, in0=t_a, in1=den, op=ALU.mult), r=[sK.tok], w=[sA.tok])
        freb = AP(sA.t, 24, [[48, 128], [1, 8], [0, 16]])
        fimb = AP(sA.t, 32, [[48, 128], [1, 8], [0, 16]])
        bre, bim = s5b[:, 0, :, :], s5b[:, 1, :, :]
        bbre, bbim = sBB[:, 0, :, :], sBB[:, 1, :, :]
        tq = F[9].t[:, 0:128].rearrange("p (g c) -> p g c", g=8)
        P.v(lambda e: e.tensor_tensor(out=bbre, in0=bre, in1=freb, op=ALU.mult), r=[s5b.tok, sA.tok], w=[sBB.tok])
        P.v(lambda e: e.tensor_tensor(out=tq, in0=bim, in1=fimb, op=ALU.mult), r=[s5b.tok, sA.tok], w=[sT.tok])
        P.v(lambda e: e.tensor_tensor(out=bbre, in0=bbre, in1=tq, op=ALU.subtract), r=[sBB.tok, sT.tok], w=[sBB.tok])
        P.v(lambda e: e.tensor_tensor(out=bbim, in0=bim, in1=freb, op=ALU.mult), r=[s5b.tok, sA.tok], w=[sBB.tok])
        P.v(lambda e: e.tensor_tensor(out=tq, in0=bre, in1=fimb, op=ALU.mult), r=[s5b.tok, sA.tok], w=[sT.tok])
        P.v(lambda e: e.tensor_tensor(out=bbim, in0=bbim, in1=tq, op=ALU.add), r=[sBB.tok, sT.tok], w=[sBB.tok])

        W0, W1b, WZ = wch[(wrr[0]) % NW], wch[(wrr[0] + 1) % NW], wF
        w0f = W0.t[:].rearrange("p a b -> p (a b)")
        w1f = W1b.t[:].rearrange("p a b -> p (a b)")
        wzf = WZ.t[:].rearrange("p a b -> p (a b)")
        ptok = [Tok("ptwA"), Tok("ptwB")]
        ytok = [W1b.tok, WZ.tok]
        ptw = [w0f[:, 0:2048], w0f[:, 2048:4096]]
        ymw = [w1f[:, 0:2304], wzf[:, 0:2304]]
        PTp = [[ptw[p_][:, 0:1024].rearrange("p (k c) -> p k c", k=8), ptw[p_][:, 1024:2048].rearrange("p (k c) -> p k c", k=8)] for p_ in range(2)]
        Ymp = [[ymw[p_][:, 0:1152].rearrange("p (k c) -> p k c", k=9), ymw[p_][:, 1152:2304].rearrange("p (k c) -> p k c", k=9)] for p_ in range(2)]
        wglu = Buf(w1f[:, 2304:2816].rearrange("p (k c) -> p k c", k=2), "wglu")
        wglu.tok = W1b.tok
        P.dma("gpsimd", wglu.t, d_wglu[l].rearrange("(k p) c -> p k c", p=128), w=[wglu.tok])
        first_use = [True, True]
        xprev = P0
        BDm = P1.t[:].rearrange("p a b -> p (a b)")[:, 0:1024].rearrange("p (k c) -> p k c", k=8)
        rot[0] = [6, 7]
        yacc = banks[0:4]
        bdacc = banks[4:6]
        cs_t, an_t, vt_t, wt_t, xt_t, t1_t, t2_t, nf_t = F[0], F[1], F[2], F[3], F[4], F[5], F[6], F[7]
        for ft in range(2):
            for bk in list(yacc) + list(bdacc):
                P.t(lambda e, bk=bk: e.matmul(bk[:, :], lhsT=zeros_b[:, 0:128], rhs=hnT[:, 0, 0:512], start=True, stop=False), r=[zeros_b.tok, htok[0]], w=[bk.tok])
            for pp in range(4):
                pi_ = ft * 4 + pp
                par = pi_ % 2
                PT, W1, Ym = PTp[par], PTp[par], Ymp[par]
                PTK, YTK = ptok[par], ytok[par]
                Prb = AP(sK.t, 2 * 72 + pi_ * 9, [[360, 128], [1, 9], [0, 16]])
                Pib = AP(sK.t, 3 * 72 + pi_ * 9, [[360, 128], [1, 9], [0, 16]])
                bbr = AP(sBB.t, pi_ * 16, [[256, 128], [0, 9], [1, 16]])
                bbi = AP(sBB.t, 128 + pi_ * 16, [[256, 128], [0, 9], [1, 16]])
                crb = AP(s5c.t, 0 * 128 + pi_ * 16, [[256, 128], [0, 9], [1, 16]])
                cib = AP(s5c.t, 1 * 128 + pi_ * 16, [[256, 128], [0, 9], [1, 16]])
                Ere, Eim, CAre, CAim = (sE[:, j, :, :] for j in range(4))
                ta, tb = sT[:, 0, :, :], sT[:, 1, :, :]
                for (o_, x1, y1, x2, y2, op_) in ((Ere, Prb, bbr, Pib, bbi, ALU.subtract), (Eim, Prb, bbi, Pib, bbr, ALU.add),
                                                   (CAre, Prb, crb, Pib, cib, ALU.subtract), (CAim, Pib, crb, Prb, cib, ALU.add)):
                    P.v(lambda e, o_=o_, x1=x1, y1=y1: e.tensor_tensor(out=o_, in0=x1, in1=y1, op=ALU.mult), r=[sK.tok, sBB.tok, s5c.tok], w=[sE.tok])
                    P.v(lambda e, x2=x2, y2=y2: e.tensor_tensor(out=ta, in0=x2, in1=y2, op=ALU.mult), r=[sK.tok, sBB.tok, s5c.tok], w=[sT.tok])
                    P.v(lambda e, o_=o_, op_=op_: e.tensor_tensor(out=o_, in0=o_, in1=ta, op=op_), r=[sE.tok, sT.tok], w=[sE.tok])
                P.g(lambda e, par=par: e.memset(ptw[par], 0.0), w=[PTK] + ([W0.tok] if first_use[par] else []))
                first_use[par] = False
                P.g(lambda e, par=par: e.memset(ymw[par], 0.0), w=[YTK])
                for g2 in range(2):
                    rs_ = slice(64 * g2, 64 * g2 + 64)
                    c0 = (2 * pp + g2) * 16
                    for ri in range(2):
                        P.a(lambda e, rs_=rs_, c0=c0, ri=ri: e.activation(out=PT[ri][rs_, :, c0:c0 + 16], in_=sE[rs_, ri, 0:8, :], func=AF.Copy), r=[sE.tok], w=[PTK])
                    P.a(lambda e, rs_=rs_, c0=c0: e.activation(out=Ym[0][rs_, :, c0:c0 + 16], in_=sE[rs_, 2, :, :], func=AF.Copy), r=[sE.tok], w=[YTK])
                    P.a(lambda e, rs_=rs_, c0=c0: e.activation(out=Ym[1][rs_, :, c0:c0 + 16], in_=sE[rs_, 3, :, :], func=AF.Copy, scale=-1.0), r=[sE.tok], w=[YTK])
                for k in range(8):
                    bk = bdacc[k // 4]
                    for ri in range(2):
                        P.t(lambda e, k=k, ri=ri, bk=bk, pp=pp: e.matmul(bk[:, (k % 4) * 128:(k % 4) * 128 + 128], lhsT=PT[ri][:, 0, :], rhs=Ym[ri][:, k, :],
                                                                         start=False, stop=(pp == 3 and ri == 1 and k % 4 == 3)),
                            r=[PTK, YTK], w=[bk.tok])
                for ri in range(2):
                    pst = psum()
                    pstb = pst.t[:].bitcast(BF16)
                    for k in range(8):
                        P.t(lambda e, pstb=pstb, ri=ri, k=k: e.transpose(pstb[:, k * 128:(k + 1) * 128], PT[ri][:, k, :], ident_b[:]),
                            r=[PTK, ident_b.tok], w=[pst.tok])
                    P.a(lambda e, pstb=pstb, ri=ri: e.activation(out=W1[ri], in_=pstb[:, :].rearrange("p (k c) -> p k c", k=8), func=AF.Copy),
                        r=[pst.tok], w=[PTK])
                if SAMPLE >= 4:
                    pvs = psum()
                    for ri in range(2):
                        P.t(lambda e, ri=ri, pvs=pvs, ft=ft: e.matmul(pvs[:, ri * NS:(ri + 1) * NS], lhsT=W1[ri][:, 0, :], rhs=usT[:, ft, :], start=True, stop=True),
                            r=[PTK, usT.tok], w=[pvs.tok])
                    pr1 = sK[:, 2, pi_, 1:2]
                    pi1 = sK[:, 3, pi_, 1:2]
                    x0r, x0i = xs0[:, 0, pi_, :], xs0[:, 1, pi_, :]
                    ta_, tb_ = sst[:, 0, :], sst[:, 1, :]
                    P.v(lambda e: e.tensor_scalar(out=ta_, in0=x0r, scalar1=pr1, scalar2=None, op0=ALU.mult), r=[xs0.tok, sK.tok], w=[sst.tok])
                    P.v(lambda e: e.tensor_scalar(out=tb_, in0=x0i, scalar1=pi1, scalar2=None, op0=ALU.mult), r=[xs0.tok, sK.tok], w=[sst.tok])
                    P.v(lambda e: e.tensor_tensor(out=ta_, in0=ta_, in1=tb_, op=ALU.subtract), r=[sst.tok], w=[sst.tok])
                    P.v(lambda e, pvs=pvs: e.tensor_tensor(out=sst[:, 2, :], in0=ta_, in1=pvs[:, 0:NS], op=ALU.add), r=[sst.tok, pvs.tok], w=[sst.tok])
                    P.v(lambda e: e.tensor_scalar(out=ta_, in0=x0i, scalar1=pr1, scalar2=None, op0=ALU.mult), r=[xs0.tok, sK.tok], w=[sst.tok])
                    P.v(lambda e: e.tensor_scalar(out=tb_, in0=x0r, scalar1=pi1, scalar2=None, op0=ALU.mult), r=[xs0.tok, sK.tok], w=[sst.tok])
                    P.v(lambda e: e.tensor_tensor(out=ta_, in0=ta_, in1=tb_, op=ALU.add), r=[sst.tok], w=[sst.tok])
                    P.v(lambda e, pvs=pvs, pi_=pi_: e.tensor_tensor(out=xs0[:, 1, pi_, :], in0=ta_, in1=pvs[:, NS:2 * NS], op=ALU.add), r=[sst.tok, pvs.tok], w=[xs0.tok])
                    P.v(lambda e, pi_=pi_: e.tensor_copy(out=xs0[:, 0, pi_, :], in_=sst[:, 2, :]), r=[sst.tok], w=[xs0.tok])
                    pys = psum()
                    for ri in range(2):
                        P.t(lambda e, ri=ri, pys=pys, pi_=pi_: e.matmul(pys[:, 0:NS], lhsT=Ym[ri][:, 1, :], rhs=xs0b[:, ri, pi_, :], start=(ri == 0), stop=(ri == 1)),
                            r=[YTK, xs0b.tok], w=[pys.tok])
                    P.v(lambda e, pys=pys, ft=ft: e.tensor_tensor(out=ysacc[:, ft, :], in0=ysacc[:, ft, :], in1=pys[:, 0:NS], op=ALU.add), r=[pys.tok, ysacc.tok], w=[ysacc.tok])
                psv = psum()
                for ri in range(2):
                    for s_ in range(8):
                        usl = AP(uT[ft].t, s_ * 256, [[T, 128], [1, 256]])
                        P.t(lambda e, ri=ri, s_=s_, usl=usl, psv=psv: e.matmul(psv[:, ri * 256:(ri + 1) * 256], lhsT=W1[ri][:, 7 - s_, :], rhs=usl,
                                                                              start=(s_ == 0), stop=(s_ == 7)), r=[PTK, uT[ft].tok], w=[psv.tok])
                P.v(lambda e, pi_=pi_: e.tensor_scalar(out=an_t[:, 0:128], in0=cst["l1"][:], scalar1=-1.0, scalar2=sA[:, 5, pi_:pi_ + 1], op0=ALU.add, op1=ALU.mult),
                    r=[cst["l1"].tok, sA.tok], w=[an_t.tok])
                P.v(lambda e, pi_=pi_: e.tensor_scalar(out=an_t[:, 128:256], in0=cst["l1"][:], scalar1=127.0, scalar2=sA[:, 5, pi_:pi_ + 1], op0=ALU.add, op1=ALU.mult),
                    r=[cst["l1"].tok, sA.tok], w=[an_t.tok])
                cb, sb_ = cs_t[:, 0:256], cs_t[:, 256:512]
                sincos(an_t[:, 0:256], an_t.tok, nf_t.t[:, 0:256].bitcast(I32), nf_t[:, 0:256], nf_t.tok, sb_, cs_t.tok, cb, cs_t.tok)
                vr, vi = psv[:, 0:256], psv[:, 256:512]
                vtr, vti = vt_t[:, 0:256], vt_t[:, 256:512]
                t1, t2 = t1_t[:, 0:256], t2_t[:, 0:256]
                P.v(lambda e: e.tensor_tensor(out=t1, in0=vr, in1=cb, op=ALU.mult), r=[psv.tok, cs_t.tok], w=[t1_t.tok])
                P.v(lambda e: e.tensor_tensor(out=t2, in0=vi, in1=sb_, op=ALU.mult), r=[psv.tok, cs_t.tok], w=[t2_t.tok])
                P.v(lambda e: e.tensor_tensor(out=vtr, in0=t1, in1=t2, op=ALU.add), r=[t1_t.tok, t2_t.tok], w=[vt_t.tok])
                P.v(lambda e: e.tensor_tensor(out=t1, in0=vi, in1=cb, op=ALU.mult), r=[psv.tok, cs_t.tok], w=[t1_t.tok])
                P.v(lambda e: e.tensor_tensor(out=t2, in0=vr, in1=sb_, op=ALU.mult), r=[psv.tok, cs_t.tok], w=[t2_t.tok])
                P.v(lambda e: e.tensor_tensor(out=vti, in0=t1, in1=t2, op=ALU.subtract), r=[t1_t.tok, t2_t.tok], w=[vt_t.tok])
                rho = AP(sK.t, pi_ * 9 + 8, [[360, 128], [0, 256]])
                wr_, wi_ = wt_t[:, 0:256], wt_t[:, 256:512]
                P.v(lambda e: e.tensor_tensor_scan(out=wr_, data0=rho, data1=vtr, initial=0.0, op0=ALU.mult, op1=ALU.add), r=[sK.tok, vt_t.tok], w=[wt_t.tok])
                P.v(lambda e: e.tensor_tensor_scan(out=wi_, data0=rho, data1=vti, initial=0.0, op0=ALU.mult, op1=ALU.add), r=[sK.tok, vt_t.tok], w=[wt_t.tok])
                xr_, xi_ = xt_t[:, 0:256], xt_t[:, 256:512]
                P.v(lambda e: e.tensor_tensor(out=t1, in0=wr_, in1=cb, op=ALU.mult), r=[wt_t.tok, cs_t.tok], w=[t1_t.tok])
                P.v(lambda e: e.tensor_tensor(out=t2, in0=wi_, in1=sb_, op=ALU.mult), r=[wt_t.tok, cs_t.tok], w=[t2_t.tok])
                P.v(lambda e: e.tensor_tensor(out=xr_, in0=t1, in1=t2, op=ALU.subtract), r=[t1_t.tok, t2_t.tok], w=[xt_t.tok])
                P.v(lambda e: e.tensor_tensor(out=t1, in0=wr_, in1=sb_, op=ALU.mult), r=[wt_t.tok, cs_t.tok], w=[t1_t.tok])
                P.v(lambda e: e.tensor_tensor(out=t2, in0=wi_, in1=cb, op=ALU.mult), r=[wt_t.tok, cs_t.tok], w=[t2_t.tok])
                P.v(lambda e: e.tensor_tensor(out=xi_, in0=t1, in1=t2, op=ALU.add), r=[t1_t.tok, t2_t.tok], w=[xt_t.tok])
                P.v(lambda e, pi_=pi_: e.tensor_copy(out=s5o[:, :, pi_], in_=AP(xt_t.t, 255, [[TB, 128], [256, 2]])), r=[xt_t.tok], w=[s5o.tok])
                P.v(lambda e: e.memset(xprev[:, :, 0:1], 0.0), w=[xprev.tok])
                P.a(lambda e: e.activation(out=xprev[:, :, 1:256], in_=AP(xt_t.t, 0, [[TB, 128], [256, 2], [1, 255]]), func=AF.Copy), r=[xt_t.tok], w=[xprev.tok])
                for t8 in range(8):
                    ya = yacc[t8 // 2]
                    for ri in range(2):
                        P.t(lambda e, t8=t8, ri=ri, ya=ya, pp=pp: e.matmul(ya[:, (t8 % 2) * 256:(t8 % 2) * 256 + 256], lhsT=Ym[ri][:, t8 + 1, :], rhs=xprev[:, ri, 0:256],
                                                                           start=False, stop=False), r=[YTK, xprev.tok], w=[ya.tok])
            for k in range(8):
                bk = bdacc[k // 4]
                if k == 0:
                    P.v(lambda e, bk=bk, ft=ft: e.scalar_tensor_tensor(out=BDm[:, 0, :], in0=cst["ident"][:], scalar=s5d[:, l, ft:ft + 1], in1=bk[:, 0:128],
                                                                       op0=ALU.mult, op1=ALU.add), r=[bk.tok, cst["ident"].tok, s5d.tok], w=[P1.tok])
                else:
                    P.v(lambda e, bk=bk, k=k: e.tensor_copy(out=BDm[:, k, :], in_=bk[:, (k % 4) * 128:(k % 4) * 128 + 128]), r=[bk.tok], w=[P1.tok])
            if SAMPLE >= 4:
                pbs = psum()
                P.t(lambda e, pbs=pbs, ft=ft: e.matmul(pbs[:, 0:NS], lhsT=BDm[:, 0, :], rhs=usT[:, ft, :], start=True, stop=True), r=[P1.tok, usT.tok], w=[pbs.tok])
                P.v(lambda e, pbs=pbs, ft=ft: e.tensor_tensor(out=ysacc[:, ft, :], in0=ysacc[:, ft, :], in1=pbs[:, 0:NS], op=ALU.add), r=[pbs.tok, ysacc.tok], w=[ysacc.tok])
            for t8 in range(8):
                ya = yacc[t8 // 2]
                for s_ in range(t8 + 1):
                    usl = AP(uT[ft].t, s_ * 256, [[T, 128], [1, 256]])
                    P.t(lambda e, t8=t8, s_=s_, ya=ya, usl=usl: e.matmul(ya[:, (t8 % 2) * 256:(t8 % 2) * 256 + 256], lhsT=BDm[:, t8 - s_, :], rhs=usl,
                                                                         start=False, stop=(s_ == t8 and t8 % 2 == 1)), r=[P1.tok, uT[ft].tok], w=[ya.tok])
            for t8 in range(8):
                ya = yacc[t8 // 2]
                ysl = ya[:, (t8 % 2) * 256:(t8 % 2) * 256 + 256]
                g1, g2_ = t1_t[:, 0:256], t2_t[:, 0:256]
                P.a(lambda e, ysl=ysl: e.activation(out=g1, in_=ysl, func=AF.Square), r=[ya.tok], w=[t1_t.tok])
                P.v(lambda e: e.tensor_scalar(out=g1, in0=g1, scalar1=0.044715, scalar2=1.0, op0=ALU.mult, op1=ALU.add), r=[t1_t.tok], w=[t1_t.tok])
                P.v(lambda e, ysl=ysl: e.tensor_tensor(out=g1, in0=g1, in1=ysl, op=ALU.mult), r=[t1_t.tok, ya.tok], w=[t1_t.tok])
                P.a(lambda e: e.activation(out=g2_, in_=g1, func=AF.Tanh, scale=0.79788456), r=[t1_t.tok], w=[t2_t.tok])
                usl = AP(uT[ft].t, t8 * 256, [[T, 128], [1, 256]])
                P.v(lambda e, ysl=ysl, usl=usl: e.scalar_tensor_tensor(out=usl, in0=g2_, scalar=1.0, in1=ysl, op0=ALU.add, op1=ALU.mult), r=[t2_t.tok, ya.tok], w=[uT[ft].tok])
        rot[0] = [0, 1, 2, 3, 4, 5]
        P.g(lambda e: e.memset(w0f[:, 0:2], 0.0), r=ptok, w=[W0.tok] + ptok)
        P.dma("sync", o_s5[l], s5o[:], r=[s5o.tok], is_output=True)
        for blk in range(NBLK):
            bs = slice(blk * TB, (blk + 1) * TB)
            for fo_ in range(2):
                ps = psum()
                for fi_ in range(2):
                    rsj = AP(uT[fi_].t, blk * 64, [[T, 128], [256, 8], [1, 64]])
                    P.t(lambda e, ps=ps, fi_=fi_, fo_=fo_, rsj=rsj: e.matmul(ps[:, :].rearrange("p (s j) -> p s j", s=8), lhsT=wglu[:, fi_, fo_ * 128:(fo_ + 1) * 128], rhs=rsj,
                                                                             start=(fi_ == 0), stop=(fi_ == 1)), r=[wglu.tok, uT[fi_].tok], w=[ps.tok])
                sg_ = F[10]
                sgn = AP(sg_.t, 0, [[TB, 128], [1, 8], [8, 64]])
                P.a(lambda e, ps=ps, sgn=sgn: e.activation(out=sgn, in_=ps[:, :].rearrange("p (s j) -> p s j", s=8), func=AF.Tanh, scale=0.25), r=[ps.tok], w=[sg_.tok])
                syn = AP(uT[fo_].t, blk * 64, [[T, 128], [1, 64], [256, 8]])
                P.v(lambda e, syn=syn: e.scalar_tensor_tensor(out=sg_[:].rearrange("p (j s) -> p j s", s=8), in0=sg_[:].rearrange("p (j s) -> p j s", s=8), scalar=1.0, in1=syn,
                                                            op0=ALU.add, op1=ALU.mult), r=[sg_.tok, uT[fo_].tok], w=[sg_.tok])
                sga, sgtok = sgp(fo_, blk)
                P.v(lambda e, fo_=fo_, sga=sga: e.scalar_tensor_tensor(out=mixp[:, fo_, :], in0=sg_[:], scalar=0.25, in1=sga, op0=ALU.mult, op1=ALU.mult),
                    r=[sg_.tok, sgtok], w=[mixp.tok[fo_]])
            outproj_part(wo, blk)
        if SAMPLE >= 4:
            P.dma("sync", o_s5s[l], xs0[:].rearrange("p a b c -> p (a b c)"), r=[xs0.tok], is_output=True)
            ysf = ysacc[:].rearrange("p a n -> p (a n)")
            g1 = sst[:, 0:2, :].rearrange("p a n -> p (a n)")
            g2_ = sst[:, 2:4, :].rearrange("p a n -> p (a n)")
            P.a(lambda e: e.activation(out=g1, in_=ysf, func=AF.Square), r=[ysacc.tok], w=[sst.tok])
            P.v(lambda e: e.tensor_scalar(out=g1, in0=g1, scalar1=0.044715, scalar2=1.0, op0=ALU.mult, op1=ALU.add), r=[sst.tok], w=[sst.tok])
            P.v(lambda e: e.tensor_tensor(out=g1, in0=g1, in1=ysf, op=ALU.mult), r=[sst.tok, ysacc.tok], w=[sst.tok])
            P.a(lambda e: e.activation(out=g2_, in_=g1, func=AF.Tanh, scale=0.79788456), r=[sst.tok], w=[sst.tok])
            P.v(lambda e: e.scalar_tensor_tensor(out=ysf, in0=g2_, scalar=1.0, in1=ysf, op0=ALU.add, op1=ALU.mult), r=[sst.tok, ysacc.tok], w=[ysacc.tok])
            P.v(lambda e: e.tensor_copy(out=usT[:], in_=ysacc[:]), r=[ysacc.tok], w=[usT.tok])
            for fo_ in range(2):
                ps = psum()
                for fi_ in range(2):
                    P.t(lambda e, ps=ps, fi_=fi_, fo_=fo_: e.matmul(ps[:, 0:NS], lhsT=wglu[:, fi_, fo_ * 128:(fo_ + 1) * 128], rhs=usT[:, fi_, :],
                                                                    start=(fi_ == 0), stop=(fi_ == 1)), r=[wglu.tok, usT.tok], w=[ps.tok])
                P.a(lambda e, ps=ps, fo_=fo_: e.activation(out=sst[:, fo_, :], in_=ps[:, 0:NS], func=AF.Tanh, scale=0.25), r=[ps.tok], w=[sst.tok])
                P.v(lambda e, fo_=fo_: e.scalar_tensor_tensor(out=sst[:, fo_, :], in0=sst[:, fo_, :], scalar=1.0, in1=ysacc[:, fo_, :], op0=ALU.add, op1=ALU.mult),
                    r=[sst.tok, ysacc.tok], w=[sst.tok])
                P.v(lambda e, fo_=fo_: e.scalar_tensor_tensor(out=mixs[:, fo_, :], in0=sst[:, fo_, :], scalar=0.25, in1=sgs[:, fo_, :], op0=ALU.mult, op1=ALU.mult),
                    r=[sst.tok, sgs.tok], w=[mixs.tok])
            outproj_s(wo)


    d_xsT = din("xsT", [1024, NS])
    d_sret = din("sret", [DEPTH, 64, 4096])
    d_gns = din("gns", [64, DEPTH * 2 * 64])
    o_ysT = dout("o_ysT", [1024, NS])
    o_sret = dout("o_sret", [DEPTH, 64, 4096])
    xsT = sb("xsT", [128, 8, NS])
    P.dma("sync", xsT[:], d_xsT.rearrange("(k p) n -> p k n", p=128), w=[xsT.tok])
    hnsT = sb("hnsT", [128, 8, NS], BF16)
    gns = sb("gns", [64, DEPTH * 2 * 64])
    P.dma("sync", gns[:], d_gns, w=[gns.tok])
    qk4 = sb("qk4", [64, 5, 64])
    ropes = sb("ropes", [64, 3, 32])
    ropei = sb("ropei", [64, 32], I32)
    sm = sb("sm", [64, 16])
    osn = sb("osn", [64, 4, 64])
    xpad = sb("xpad", [64, 2, 64])
    mixs = sb("mixs", [128, 2, NS], BF16)
    gam = sb("gam", [64, 1])
    P.a(lambda e: e.activation(out=gam[:], in_=cst["lg64"][:], func=AF.Exp), r=[cst["lg64"].tok], w=[gam.tok])
    P.v(lambda e: e.tensor_scalar(out=ropes[:, 2, :], in0=cst["invrow"][0:64, :], scalar1=PAST, scalar2=None, op0=ALU.mult),
        r=[cst["invrow"].tok], w=[ropes.tok])
    sincos(ropes[:, 2, :], ropes.tok, ropei[:], ropei.t[:].bitcast(F32), ropei.tok, ropes[:, 1, :], ropes.tok, ropes[:, 0, :], ropes.tok)

    def tm_sample(wb, c0, n):
        ps = psum()
        for kt in range(8):
            P.t(lambda e, kt=kt: e.matmul(ps[0:NS, 0:n], lhsT=hnsT[:, kt, :], rhs=wb[:, kt, c0:c0 + n], start=(kt == 0), stop=(kt == 7)),
                r=[wb.tok, hnsT.tok], w=[ps.tok])
        return ps

    def fm_sample(wb, c0):
        ps = psum()
        for kt in range(8):
            P.t(lambda e, kt=kt: e.matmul(ps[:, 0:NS], lhsT=wb[:, kt, c0:c0 + 128], rhs=hnsT[:, kt, :], start=(kt == 0), stop=(kt == 7)),
                r=[wb.tok, hnsT.tok], w=[ps.tok])
        return ps

    def to_hn(pr_ap, prtok, nsec, dst, dtok):
        ps = psum()
        for h in range(4):
            rhs = AP(pr_ap.tensor, pr_ap.offset + h * 64, [list(pr_ap.ap[0]), [256, nsec], [1, 64]])
            P.t(lambda e, h=h, rhs=rhs: e.matmul(ps[0:64, 0:nsec * 64].rearrange("p (s d) -> p s d", s=nsec), lhsT=cst["sel"][:, h * 64:(h + 1) * 64], rhs=rhs,
                                                 start=(h == 0), stop=(h == 3)), r=[cst["sel"].tok, prtok], w=[ps.tok])
        P.a(lambda e: e.activation(out=dst, in_=ps[0:64, 0:nsec * 64].rearrange("p (s d) -> p s d", s=nsec), func=AF.Copy), r=[ps.tok], w=[dtok])

    def rope_s(x4, xtok, secs):
        cosb = AP(ropes.t, 0, [[96, 64], [0, secs], [1, 32]])
        sinb = AP(ropes.t, 32, [[96, 64], [0, secs], [1, 32]])
        x1, x2 = x4[:, 0:secs, 0:32], x4[:, 0:secs, 32:64]
        ta = osn[:, 0, :].rearrange("p (s d) -> p s d", s=2)[:, 0:secs, :]
        tb = osn[:, 1, :].rearrange("p (s d) -> p s d", s=2)[:, 0:secs, :]
        tc = osn[:, 2, :].rearrange("p (s d) -> p s d", s=2)[:, 0:secs, :]
        P.v(lambda e: e.tensor_tensor(out=ta, in0=x1, in1=cosb, op=ALU.mult), r=[xtok, ropes.tok], w=[osn.tok])
        P.v(lambda e: e.tensor_tensor(out=tb, in0=x2, in1=sinb, op=ALU.mult), r=[xtok, ropes.tok], w=[osn.tok])
        P.v(lambda e: e.tensor_tensor(out=ta, in0=ta, in1=tb, op=ALU.subtract), r=[osn.tok], w=[osn.tok])
        P.v(lambda e: e.tensor_tensor(out=tb, in0=x1, in1=sinb, op=ALU.mult), r=[xtok, ropes.tok], w=[osn.tok])
        P.v(lambda e: e.tensor_tensor(out=tc, in0=x2, in1=cosb, op=ALU.mult), r=[xtok, ropes.tok], w=[osn.tok])
        P.v(lambda e: e.tensor_tensor(out=x2, in0=tb, in1=tc, op=ALU.add), r=[osn.tok], w=[xtok])
        P.v(lambda e: e.tensor_copy(out=x1, in_=ta), r=[osn.tok], w=[xtok])

    def headnorm_s(o_ap, g_ap, gn_ap):
        mean, var = sm[:, 0:1], sm[:, 1:2]
        cen = osn[:, 1, :]
        P.v(lambda e: e.tensor_reduce(out=mean, in_=o_ap, axis=AX.X, op=ALU.add), r=[osn.tok], w=[sm.tok])
        P.v(lambda e: e.tensor_scalar(out=mean, in0=mean, scalar1=-1.0 / 64, scalar2=None, op0=ALU.mult), r=[sm.tok], w=[sm.tok])
        P.v(lambda e: e.tensor_scalar(out=cen, in0=o_ap, scalar1=mean, scalar2=None, op0=ALU.add), r=[osn.tok, sm.tok], w=[osn.tok])
        P.v(lambda e: e.tensor_tensor(out=osn[:, 2, :], in0=cen, in1=cen, op=ALU.mult), r=[osn.tok], w=[osn.tok])
        P.v(lambda e: e.tensor_reduce(out=var, in_=osn[:, 2, :], axis=AX.X, op=ALU.add), r=[osn.tok], w=[sm.tok])
        P.a(lambda e: e.activation(out=var, in_=var, func=AF.Ln, scale=1.0 / 64, bias=EPS), r=[sm.tok], w=[sm.tok])
        P.a(lambda e: e.activation(out=var, in_=var, func=AF.Exp, scale=-0.5), r=[sm.tok], w=[sm.tok])
        P.v(lambda e: e.scalar_tensor_tensor(out=cen, in0=cen, scalar=var, in1=gn_ap, op0=ALU.mult, op1=ALU.mult), r=[osn.tok, sm.tok, gns.tok], w=[osn.tok])
        P.v(lambda e: e.tensor_tensor(out=cen, in0=cen, in1=g_ap, op=ALU.mult), r=[osn.tok], w=[osn.tok])
        return cen

    def place_mix(y_ap):
        for i in range(2):
            yb = AP(y_ap.tensor, y_ap.offset, [list(y_ap.ap[0]), [0, 2], [1, 64]])
            mk2 = AP(cst["mpair"].t, i * 2, [[4, 64], [1, 2], [0, 64]])
            P.v(lambda e, yb=yb, mk2=mk2: e.tensor_tensor(out=xpad[:], in0=yb, in1=mk2, op=ALU.mult), r=[osn.tok, cst["mpair"].tok], w=[xpad.tok])
            ps = psum()
            P.t(lambda e, ps=ps: e.matmul(ps[:, 0:NS], lhsT=xpad[:].rearrange("p a b -> p (a b)"), rhs=cst["sel2"][:], start=True, stop=True),
                r=[xpad.tok, cst["sel2"].tok], w=[ps.tok])
            P.a(lambda e, ps=ps, i=i: e.activation(out=mixs[:, i, :], in_=ps[:, 0:NS], func=AF.Copy), r=[ps.tok], w=[mixs.tok])

    def outproj_s(wo):
        ps = psum()
        for dt_ in range(8):
            for kt in range(2):
                P.t(lambda e, kt=kt, dt_=dt_: e.matmul(ps[:, dt_ * NS:(dt_ + 1) * NS], lhsT=wo[:, kt, dt_ * 128:(dt_ + 1) * 128], rhs=mixs[:, kt, :],
                                                       start=(kt == 0), stop=(kt == 1)), r=[wo.tok, mixs.tok], w=[ps.tok])
        P.v(lambda e: e.tensor_tensor(out=xsT[:].rearrange("p k n -> p (k n)"), in0=xsT[:].rearrange("p k n -> p (k n)"), in1=ps[:, 0:8 * NS], op=ALU.add),
            r=[ps.tok, xsT.tok], w=[xsT.tok])

    def ret_sample(l, wA, wB, wo):
        pr = Pf.t[:].rearrange("p a b -> p (a b)")
        ps = tm_sample(wA, 0, 512)
        P.a(lambda e: e.activation(out=pr[0:NS, 0:512], in_=ps[0:NS, 0:512], func=AF.Copy), r=[ps.tok], w=[Pf.tok])
        ps = tm_sample(wB, 0, 512)
        P.a(lambda e: e.activation(out=pr[0:NS, 512:1024], in_=ps[0:NS, 0:512], func=AF.Copy), r=[ps.tok], w=[Pf.tok])
        to_hn(pr[0:NS, :], Pf.tok, 4, qk4[:, 0:4, :], qk4.tok)
        rope_s(qk4, qk4.tok, 2)
        q, k, v, g = (qk4[:, j, :] for j in range(4))
        P.v(lambda e: e.tensor_scalar(out=k, in0=k, scalar1=0.125, scalar2=None, op0=ALU.mult), r=[qk4.tok], w=[qk4.tok])
        P.a(lambda e: e.activation(out=osn[:, 3, :], in_=g, func=AF.Silu), r=[qk4.tok], w=[osn.tok])
        o = osn[:, 0, :]
        for j in range(8):
            S = F[j]
            P.dma("sync", S[0:64, :], d_sret[l][:, j * 512:(j + 1) * 512], w=[S.tok])
            tmpk = F[8 + j % 2]
            kb = AP(qk4.t, 1 * 64 + j * 8, [[320, 64], [1, 8], [0, 64]])
            vb = AP(qk4.t, 2 * 64, [[320, 64], [0, 8], [1, 64]])
            qb = AP(qk4.t, 0 * 64 + j * 8, [[320, 64], [0, 64], [1, 8]])
            P.v(lambda e, tmpk=tmpk, kb=kb, vb=vb: e.tensor_tensor(out=tmpk[0:64, :].rearrange("p (d x) -> p d x", d=8), in0=kb, in1=vb, op=ALU.mult),
                r=[qk4.tok], w=[tmpk.tok])
            P.v(lambda e, S=S, tmpk=tmpk: e.scalar_tensor_tensor(out=S[0:64, :], in0=S[0:64, :], scalar=gam[:, 0:1], in1=tmpk[0:64, :], op0=ALU.mult, op1=ALU.add),
                r=[S.tok, tmpk.tok, gam.tok], w=[S.tok])
            P.dma("sync", o_sret[l][:, j * 512:(j + 1) * 512], S[0:64, :], r=[S.tok], is_output=True)
            Sv = AP(S.t, 0, [[TB, 64], [1, 64], [64, 8]])
            P.v(lambda e, tmpk=tmpk, Sv=Sv, qb=qb: e.tensor_tensor(out=tmpk[0:64, :].rearrange("p (x d) -> p x d", d=8), in0=Sv, in1=qb, op=ALU.mult),
                r=[S.tok, qk4.tok], w=[tmpk.tok])
            if j == 0:
                P.v(lambda e, tmpk=tmpk: e.tensor_reduce(out=o, in_=tmpk[0:64, :].rearrange("p (x d) -> p x d", d=8), axis=AX.X, op=ALU.add),
                    r=[tmpk.tok], w=[osn.tok])
            else:
                P.v(lambda e, tmpk=tmpk: e.tensor_reduce(out=osn[:, 2, :], in_=tmpk[0:64, :].rearrange("p (x d) -> p x d", d=8), axis=AX.X, op=ALU.add),
                    r=[tmpk.tok], w=[osn.tok])
                P.v(lambda e: e.tensor_tensor(out=o, in0=o, in1=osn[:, 2, :], op=ALU.add), r=[osn.tok], w=[osn.tok])
        y = headnorm_s(o, osn[:, 3, :], gns[:, (l * 2 + 0) * 64:(l * 2 + 1) * 64])
        place_mix(y)
        outproj_s(wo)


    d_smlc = din("smlc", [DEPTH, 64, 4096])
    d_smln = din("smln", [DEPTH, 64, 64])
    d_smlm = din("smlm", [DEPTH, 64, 1])
    d_bifs = din("bifs", [64, DEPTH * 2])
    o_smlc = dout("o_smlc", [DEPTH, 64, 4096])
    o_smln = dout("o_smln", [DEPTH, 64, 64])
    o_smlm = dout("o_smlm", [DEPTH, 64, 1])
    bifs = sb("bifs", [64, DEPTH * 2])
    P.dma("sync", bifs[:], d_bifs, w=[bifs.tok])
    n0s = sb("n0s", [64, 2, 64])

    def ml_sample(l, wC, wD, wE, wo):
        pr = Pf.t[:].rearrange("p a b -> p (a b)")
        pr2 = P2.t[:].rearrange("p a b -> p (a b)").bitcast(F32)
        ps = tm_sample(wC, 0, 512)
        P.a(lambda e: e.activation(out=pr[0:NS, 0:512], in_=ps[0:NS, 0:512], func=AF.Copy), r=[ps.tok], w=[Pf.tok])
        ps = tm_sample(wD, 0, 512)
        P.a(lambda e: e.activation(out=pr[0:NS, 512:1024], in_=ps[0:NS, 0:512], func=AF.Copy), r=[ps.tok], w=[Pf.tok])
        ps = tm_sample(wE, 0, 264)
        P.a(lambda e: e.activation(out=pr2[0:NS, 0:264], in_=ps[0:NS, 0:264], func=AF.Copy), r=[ps.tok], w=[P2.tok])
        to_hn(pr[0:NS, :], Pf.tok, 4, qk4[:, 0:4, :], qk4.tok)
        to_hn(pr2[0:NS, 0:256], P2.tok, 1, qk4[:, 4:5, :], qk4.tok)
        psg = psum()
        for h in range(4):
            rhs = AP(P2.t, 0, [[1024, NS], [8, 2]]).bitcast(F32) if False else AP(pr2.tensor, pr2.offset + 256 + h, [list(pr2.ap[0])[0:1] + [NS], [4, 2]])
            P.t(lambda e, h=h, rhs=rhs: e.matmul(psg[0:64, 0:2], lhsT=cst["sel"][:, h * 64:(h + 1) * 64], rhs=rhs, start=(h == 0), stop=(h == 3)),
                r=[cst["sel"].tok, P2.tok], w=[psg.tok])
        q, k, v, og, g = (qk4[:, j, :] for j in range(5))
        ig, fz, lf, a_, mt_, wi, ws, emt, qk_, nq, den, scol, rc = (sm[:, j:j + 1] for j in range(2, 15))
        P.v(lambda e: e.tensor_tensor(out=sm[:, 2:4], in0=psg[0:64, 0:2], in1=bifs[:, l * 2:l * 2 + 2], op=ALU.add), r=[psg.tok, bifs.tok], w=[sm.tok])
        P.dma("sync", n0s[:, 0, :], d_smln[l], w=[n0s.tok])
        P.dma("sync", sm[:, 15:16], d_smlm[l], w=[sm.tok])
        m0 = sm[:, 15:16]
        P.a(lambda e: e.activation(out=lf, in_=fz, func=AF.Exp, scale=-1.0), r=[sm.tok], w=[sm.tok])
        P.a(lambda e: e.activation(out=lf, in_=lf, func=AF.Ln, bias=1.0), r=[sm.tok], w=[sm.tok])
        P.v(lambda e: e.tensor_tensor(out=a_, in0=m0, in1=lf, op=ALU.subtract), r=[sm.tok], w=[sm.tok])
        P.v(lambda e: e.tensor_tensor(out=mt_, in0=a_, in1=ig, op=ALU.max), r=[sm.tok], w=[sm.tok])
        P.v(lambda e: e.tensor_tensor(out=wi, in0=ig, in1=mt_, op=ALU.subtract), r=[sm.tok], w=[sm.tok])
        P.a(lambda e: e.activation(out=wi, in_=wi, func=AF.Exp), r=[sm.tok], w=[sm.tok])
        P.v(lambda e: e.tensor_tensor(out=ws, in0=a_, in1=mt_, op=ALU.subtract), r=[sm.tok], w=[sm.tok])
        P.a(lambda e: e.activation(out=ws, in_=ws, func=AF.Exp), r=[sm.tok], w=[sm.tok])
        P.a(lambda e: e.activation(out=emt, in_=mt_, func=AF.Exp, scale=-1.0), r=[sm.tok], w=[sm.tok])
        P.dma("sync", o_smlm[l], mt_, r=[sm.tok], is_output=True)
        P.v(lambda e: e.tensor_scalar(out=k, in0=k, scalar1=0.125, scalar2=None, op0=ALU.mult), r=[qk4.tok], w=[qk4.tok])
        P.v(lambda e: e.tensor_tensor(out=osn[:, 2, :], in0=q, in1=k, op=ALU.mult), r=[qk4.tok], w=[osn.tok])
        P.v(lambda e: e.tensor_reduce(out=qk_, in_=osn[:, 2, :], axis=AX.X, op=ALU.add), r=[osn.tok], w=[sm.tok])
        P.v(lambda e: e.tensor_tensor(out=osn[:, 2, :], in0=q, in1=n0s[:, 0, :], op=ALU.mult), r=[qk4.tok, n0s.tok], w=[osn.tok])
        P.v(lambda e: e.tensor_reduce(out=nq, in_=osn[:, 2, :], axis=AX.X, op=ALU.add), r=[osn.tok], w=[sm.tok])
        P.v(lambda e: e.tensor_tensor(out=scol, in0=qk_, in1=wi, op=ALU.mult), r=[sm.tok], w=[sm.tok])
        P.v(lambda e: e.tensor_tensor(out=den, in0=nq, in1=ws, op=ALU.mult), r=[sm.tok], w=[sm.tok])
        P.v(lambda e: e.tensor_tensor(out=den, in0=den, in1=scol, op=ALU.add), r=[sm.tok], w=[sm.tok])
        P.a(lambda e: e.activation(out=den, in_=den, func=AF.Abs), r=[sm.tok], w=[sm.tok])
        P.v(lambda e: e.tensor_tensor(out=den, in0=den, in1=emt, op=ALU.max), r=[sm.tok], w=[sm.tok])
        P.v(lambda e: e.reciprocal(out=rc, in_=den), r=[sm.tok], w=[sm.tok])
        P.v(lambda e: e.tensor_scalar(out=n0s[:, 1, :], in0=k, scalar1=wi, scalar2=None, op0=ALU.mult), r=[qk4.tok, sm.tok], w=[n0s.tok])
        P.v(lambda e: e.scalar_tensor_tensor(out=n0s[:, 1, :], in0=n0s[:, 0, :], scalar=ws, in1=n0s[:, 1, :], op0=ALU.mult, op1=ALU.add),
            r=[n0s.tok, sm.tok], w=[n0s.tok])
        P.dma("sync", o_smln[l], n0s[:, 1, :], r=[n0s.tok], is_output=True)
        vp = osn[:, 3, :]
        P.v(lambda e: e.tensor_scalar(out=vp, in0=v, scalar1=wi, scalar2=None, op0=ALU.mult), r=[qk4.tok, sm.tok], w=[osn.tok])
        Cq = osn[:, 0, :]
        for j in range(8):
            C = F[j]
            P.dma("sync", C[0:64, :], d_smlc[l][:, j * 512:(j + 1) * 512], w=[C.tok])
            tmpk = F[8 + j % 2]
            qb = AP(qk4.t, 0, [[320, 64], [0, 8], [1, 64]])
            P.v(lambda e, tmpk=tmpk, C=C, qb=qb: e.tensor_tensor(out=tmpk[0:64, :].rearrange("p (x d) -> p x d", x=8), in0=C[0:64, :].rearrange("p (x d) -> p x d", x=8),
                                                                 in1=qb, op=ALU.mult), r=[C.tok, qk4.tok], w=[tmpk.tok])
            P.v(lambda e, tmpk=tmpk, j=j: e.tensor_reduce(out=Cq[:, j * 8:(j + 1) * 8], in_=tmpk[0:64, :].rearrange("p (x d) -> p x d", x=8), axis=AX.X, op=ALU.add),
                r=[tmpk.tok], w=[osn.tok])
            vb = AP(osn.t, 3 * 64 + j * 8, [[256, 64], [1, 8], [0, 64]])
            kb = AP(qk4.t, 1 * 64, [[320, 64], [0, 8], [1, 64]])
            P.v(lambda e, tmpk=tmpk, vb=vb, kb=kb: e.tensor_tensor(out=tmpk[0:64, :].rearrange("p (x d) -> p x d", x=8), in0=vb, in1=kb, op=ALU.mult),
                r=[osn.tok, qk4.tok], w=[tmpk.tok])
            P.v(lambda e, C=C, tmpk=tmpk: e.scalar_tensor_tensor(out=C[0:64, :], in0=C[0:64, :], scalar=ws, in1=tmpk[0:64, :], op0=ALU.mult, op1=ALU.add),
                r=[C.tok, tmpk.tok, sm.tok], w=[C.tok])
            P.dma("sync", o_smlc[l][:, j * 512:(j + 1) * 512], C[0:64, :], r=[C.tok], is_output=True)
        P.v(lambda e: e.tensor_scalar(out=osn[:, 2, :], in0=v, scalar1=scol, scalar2=None, op0=ALU.mult), r=[qk4.tok, sm.tok], w=[osn.tok])
        P.v(lambda e: e.scalar_tensor_tensor(out=Cq, in0=Cq, scalar=ws, in1=osn[:, 2, :], op0=ALU.mult, op1=ALU.add), r=[osn.tok, sm.tok], w=[osn.tok])
        P.a(lambda e: e.activation(out=osn[:, 2, :], in_=og, func=AF.Tanh, scale=0.5), r=[qk4.tok], w=[osn.tok])
        P.v(lambda e: e.tensor_scalar(out=osn[:, 2, :], in0=osn[:, 2, :], scalar1=0.5, scalar2=0.5, op0=ALU.mult, op1=ALU.add), r=[osn.tok], w=[osn.tok])
        P.v(lambda e: e.scalar_tensor_tensor(out=Cq, in0=Cq, scalar=rc, in1=osn[:, 2, :], op0=ALU.mult, op1=ALU.mult), r=[osn.tok, sm.tok], w=[osn.tok])
        P.a(lambda e: e.activation(out=osn[:, 3, :], in_=g, func=AF.Silu), r=[qk4.tok, osn.tok], w=[osn.tok])
        y = headnorm_s(Cq, osn[:, 3, :], gns[:, (l * 2 + 1) * 64:(l * 2 + 2) * 64])
        place_mix(y)
        outproj_s(wo)

    d_ck = din("ck", [DEPTH, NS, 256, 256])
    d_cv = din("cv", [DEPTH, NS, 256, 256])
    gsx = sb("gsx", [128, 2, NS])
    ones32 = sb("ones32", [128, 128])
    P.v(lambda e: e.memset(ones32[:], 1.0), w=[ones32.tok])

    def xa_sample(l, wG, wo):
        pr = Pf.t[:].rearrange("p a b -> p (a b)")
        ps = tm_sample(wG, 0, 256)
        P.a(lambda e: e.activation(out=pr[0:NS, 0:256], in_=ps[0:NS, 0:256], func=AF.Copy), r=[ps.tok], w=[Pf.tok])
        for i in range(2):
            ps = fm_sample(wG, 256 + i * 128)
            P.a(lambda e, ps=ps, i=i: e.activation(out=gsx[:, i, :], in_=ps[:, 0:NS], func=AF.Silu), r=[ps.tok], w=[gsx.tok])
        scT = F[10]
        for n in range(NS):
            qm = F[9]
            Kn = F[n % 4]
            P.dma("sync", Kn[:].rearrange("p (mt c) -> p mt c", mt=2), d_ck[l, n].rearrange("(mt p) c -> p mt c", p=128), w=[Kn.tok])
            P.v(lambda e, n=n: e.tensor_scalar(out=qm[0:NS, 0:256], in0=pr[0:NS, 0:256], scalar1=cst["ident"][0:NS, n:n + 1], scalar2=None, op0=ALU.mult),
                r=[Pf.tok, cst["ident"].tok], w=[qm.tok])
            psq = psum()
            P.t(lambda e, psq=psq: e.matmul(psq[:, 0:256], lhsT=ones32[0:NS, :], rhs=qm[0:NS, 0:256], start=True, stop=True), r=[ones32.tok, qm.tok], w=[psq.tok])
            tmpk = F[8]
            qrb = AP(psq.t, 0, [[512, 128], [0, 2], [1, 256]])
            P.v(lambda e, Kn=Kn, qrb=qrb: e.tensor_tensor(out=tmpk[:].rearrange("p (mt c) -> p mt c", mt=2), in0=Kn[:].rearrange("p (mt c) -> p mt c", mt=2), in1=qrb, op=ALU.mult),
                r=[Kn.tok, psq.tok], w=[tmpk.tok])
            dst = AP(scT.t, n * 4, [[TB, 128], [64, 2], [1, 4]])
            P.v(lambda e, dst=dst: e.tensor_reduce(out=dst, in_=tmpk[:].rearrange("p (mt h d) -> p mt h d", mt=2, h=4), axis=AX.X, op=ALU.add),
                r=[tmpk.tok], w=[scT.tok])
        pss = psum()
        for mt in range(2):
            P.t(lambda e, mt=mt: e.transpose(pss[0:64, mt * 128:(mt + 1) * 128], scT[:, mt * 64:(mt + 1) * 64], cst["ident"][:]),
                r=[scT.tok, cst["ident"].tok], w=[pss.tok])
        mx, sme, nb = sm[:, 0:1], sm[:, 1:2], sm[:, 2:3]
        Pm = F[9]
        P.v(lambda e: e.tensor_reduce(out=mx, in_=pss[0:64, 0:256], axis=AX.X, op=ALU.max), r=[pss.tok], w=[sm.tok])
        P.v(lambda e: e.tensor_scalar(out=nb, in0=mx, scalar1=-0.125, scalar2=None, op0=ALU.mult), r=[sm.tok], w=[sm.tok])
        P.a(lambda e: e.activation(out=Pm[0:64, 0:256], in_=pss[0:64, 0:256], func=AF.Exp, scale=0.125, bias=nb), r=[pss.tok, sm.tok], w=[Pm.tok])
        P.v(lambda e: e.tensor_reduce(out=sme, in_=Pm[0:64, 0:256], axis=AX.X, op=ALU.add), r=[Pm.tok], w=[sm.tok])
        P.v(lambda e: e.reciprocal(out=sme, in_=sme), r=[sm.tok], w=[sm.tok])
        P.v(lambda e: e.tensor_scalar(out=Pm[0:64, 0:256], in0=Pm[0:64, 0:256], scalar1=sme, scalar2=None, op0=ALU.mult), r=[Pm.tok, sm.tok], w=[Pm.tok])
        pst = psum()
        for mt in range(2):
            P.t(lambda e, mt=mt: e.transpose(pst[:, mt * 64:(mt + 1) * 64], Pm[0:64, mt * 128:(mt + 1) * 128], cst["ident"][0:64, 0:64]),
                r=[Pm.tok, cst["ident"].tok], w=[pst.tok])
        PTs = F[10]
        P.a(lambda e: e.activation(out=PTs[:, 128:256], in_=pst[:, 0:128], func=AF.Copy), r=[pst.tok], w=[PTs.tok])
        psx = psum()
        for n in range(NS):
            Vn = F[4 + n % 4]
            P.dma("sync", Vn[:].rearrange("p (mt c) -> p mt c", mt=2), d_cv[l, n].rearrange("(mt p) c -> p mt c", p=128), w=[Vn.tok])
            tmpk = F[8]
            pb = AP(PTs.t, 128 + n * 4, [[TB, 128], [64, 2], [1, 4], [0, 64]])
            P.v(lambda e, Vn=Vn, pb=pb: e.tensor_tensor(out=tmpk[:].rearrange("p (mt h d) -> p mt h d", mt=2, h=4), in0=Vn[:].rearrange("p (mt h d) -> p mt h d", mt=2, h=4),
                                                        in1=pb, op=ALU.mult), r=[Vn.tok, PTs.tok], w=[tmpk.tok])
            for i in range(2):
                for mt in range(2):
                    P.t(lambda e, i=i, mt=mt, n=n: e.matmul(psx[:, i * NS + n:i * NS + n + 1], lhsT=tmpk[:, mt * 256 + i * 128:mt * 256 + (i + 1) * 128], rhs=ones32[:, 0:1],
                                                            start=(mt == 0), stop=(mt == 1)), r=[tmpk.tok, ones32.tok], w=[psx.tok])
        P.v(lambda e: e.tensor_tensor(out=mixs[:].rearrange("p a n -> p (a n)"), in0=psx[:, 0:2 * NS], in1=gsx[:].rearrange("p a n -> p (a n)"), op=ALU.mult),
            r=[psx.tok, gsx.tok], w=[mixs.tok])
        outproj_s(wo)

    for l in range(DEPTH):
        for blk in range(NBLK):
            t0 = blk * TB
            rmsnorm(lambda kt: xT[:, kt, t0:t0 + TB], [xtok[blk]], TB, lambda kt: hnT[:, kt, t0:t0 + TB], lambda kt: [htok[blk]],
                    lambda kt: normw[:, l * 8 + kt:l * 8 + kt + 1])
        rmsnorm(lambda kt: xsT[:, kt, :], [xsT.tok], NS, lambda kt: hnsT[:, kt, :], lambda kt: [hnsT.tok],
                lambda kt: normw[:, l * 8 + kt:l * 8 + kt + 1])
        if STAGE >= 2:
            mem_kv(l)
        if STAGE >= 3:
            ret_phase(l)
        if STAGE >= 4:
            ml_phase(l)
        if STAGE >= 5:
            xa_phase(l)
        if STAGE >= 6:
            s5_phase(l)
    for blk in range(NBLK):
        t0 = blk * TB
        for kt in range(8):
            pass
        yo = [F[0], F[1], F[2], F[3], F[4], F[5], F[6], F[7]]
        rmsnorm(lambda kt: xT[:, kt, t0:t0 + TB], [xtok[blk]], TB, lambda kt: yo[kt][:], lambda kt: [yo[kt].tok], lambda kt: fnw[:, kt:kt + 1])
        for kt in range(8):
            P.dma("sync" if kt % 2 == 0 else "gpsimd", yTv[:, kt, t0:t0 + TB], yo[kt][:], r=[yo[kt].tok], is_output=True)

    yos = F[8]
    rmsnorm(lambda kt: xsT[:, kt, :], [xsT.tok], NS, lambda kt: yos[:, kt * NS:(kt + 1) * NS], lambda kt: [yos.tok], lambda kt: fnw[:, kt:kt + 1])
    P.dma("sync", o_ysT.rearrange("(k p) n -> p k n", p=128), yos[:, 0:8 * NS].rearrange("p (k n) -> p k n", k=8), r=[yos.tok], is_output=True)

    P.emit()
    es.close()
    return nc


_NC = None


def make_in_maps(inp, cores=range(NCORES)):
    cs = _consts()
    f = {k: np.asarray(v) for k, v in inp.items()}
    in_maps = []
    for b in cores:
        m = {}
        m["xT"] = _c(f["x_prompt"][b].T)
        m["memT"] = _c(f["mem_prompt"][b].T)
        m["w_in"] = _c(f["w_in"])
        m["w_out"] = _c(f["w_out"])
        m["w_mem_k"] = _c(f["w_mem_k"])
        m["w_mem_v"] = _c(f["w_mem_v"])
        m["normw"] = _c(f["norm_w"].reshape(DEPTH, 8, 128).transpose(2, 0, 1).reshape(128, DEPTH * 8))
        m["fnw"] = _c(f["final_norm_w"].reshape(8, 128).T)
        gnc = np.zeros((128, DEPTH, 2, 2), np.float32)
        for l in range(DEPTH):
            gnc[:, l, 0, :] = f["ret_gn"][l].reshape(2, 128).T
            gnc[:, l, 1, :] = f["ml_gn"][l].reshape(2, 128).T
        m["gn"] = _c(gnc.reshape(128, DEPTH * 4))
        m["bif"] = _c(np.concatenate([f["ml_b_i"].T, f["ml_b_f"].T], axis=0))
        def pairlay(a):
            a = np.asarray(a)
            L = a.shape[0]
            rest = a.shape[3:]
            a = a.reshape((L, 8, 2, 64) + rest)
            a = np.moveaxis(a, (2, 3), (0, 1))
            return a.reshape((128, L, 8) + rest)
        are = pairlay(f["s5_a_re"])
        aim = pairlay(f["s5_a_im"])
        ldt = pairlay(np.repeat(f["s5_log_dt"][:, :, None], 64, axis=2))
        m["s5par"] = _c(np.stack([are, aim, ldt], axis=2).reshape(128, -1))
        bre = pairlay(f["s5_b_re"])
        bim = pairlay(f["s5_b_im"])
        m["s5b"] = _c(np.stack([bre, bim], axis=2).transpose(1, 0, 2, 3, 4).reshape(DEPTH, 128, -1))
        cre = pairlay(np.swapaxes(f["s5_c_re"], 2, 3))
        cim = pairlay(np.swapaxes(f["s5_c_im"], 2, 3))
        m["s5c"] = _c(np.stack([cre, cim], axis=2).transpose(1, 0, 2, 3, 4).reshape(DEPTH, 128, -1))
        m["s5d"] = _c(f["s5_d"].reshape(DEPTH, 2, 128).transpose(2, 0, 1).reshape(128, -1))
        m["w_glu"] = _c(f["s5_w_glu"])
        ns = slice(NS * b, NS * (b + 1))
        m["xsT"] = _c(f["x_sample"][ns, 0, :].T)
        m["sret"] = _c(f["state_ret"][:, ns].transpose(0, 2, 1, 3, 4).reshape(DEPTH, 64, 4096))
        gs = np.zeros((4, NS, DEPTH, 2, 64), np.float32)
        for l in range(DEPTH):
            gs[:, :, l, 0, :] = f["ret_gn"][l].reshape(4, 1, 64)
            gs[:, :, l, 1, :] = f["ml_gn"][l].reshape(4, 1, 64)
        m["gns"] = _c(gs.reshape(64, -1))
        m["smlc"] = _c(f["state_mlstm_c"][:, ns].transpose(0, 2, 1, 3, 4).reshape(DEPTH, 64, 4096))
        m["smln"] = _c(f["state_mlstm_n"][:, ns].transpose(0, 2, 1, 3).reshape(DEPTH, 64, 64))
        m["smlm"] = _c(f["state_mlstm_m"][:, ns].transpose(0, 2, 1).reshape(DEPTH, 64, 1))
        bs_ = np.zeros((4, NS, DEPTH, 2), np.float32)
        for l in range(DEPTH):
            bs_[:, :, l, 0] = f["ml_b_i"][l].reshape(4, 1)
            bs_[:, :, l, 1] = f["ml_b_f"][l].reshape(4, 1)
        m["bifs"] = _c(bs_.reshape(64, -1))
        def spair(a):
            a = np.asarray(a)[:, ns]
            a = a.reshape(DEPTH, NS, 8, 2, 64).transpose(0, 3, 4, 2, 1)
            return a.reshape(DEPTH, 128, 8, NS)
        m["sx0"] = _c(np.stack([spair(f["state_s5_re"]), spair(f["state_s5_im"])], axis=2).reshape(DEPTH, 128, -1))
        m["ck"] = _c(f["cache_mem_k"][:, ns].reshape(DEPTH, NS, 256, 256))
        m["cv"] = _c(f["cache_mem_v"][:, ns].reshape(DEPTH, NS, 256, 256))
        for k, v in cs.items():
            m["c_" + k] = _c(v)
        in_maps.append(m)
    return in_maps


def kernel(**inp):
    global _NC
    if _NC is None:
        _NC = build()
    nc = _NC
    in_maps = make_in_maps(inp)
    res = run_bass_kernel_spmd(nc, in_maps, core_ids=list(range(NCORES)))
    return assemble(res.results)


def assemble(R):
    y_prompt = np.stack([R[b]["o_yT"].T for b in range(NCORES)]).astype(np.float32)
    memkv = np.stack([R[b]["o_memkv"] for b in range(NCORES)], axis=1)
    memk = np.ascontiguousarray(memkv[..., 0:256]).reshape(DEPTH, NCORES, 256, 4, 64)
    memv = np.ascontiguousarray(memkv[..., 256:512]).reshape(DEPTH, NCORES, 256, 4, 64)
    ret_p = np.stack([R[b]["o_ret"] for b in range(NCORES)], axis=1)
    mlc_p = np.stack([R[b]["o_mlc"].transpose(0, 1, 3, 2) for b in range(NCORES)], axis=1)
    mln_p = np.stack([R[b]["o_mln"] for b in range(NCORES)], axis=1)
    mlm_p = np.stack([R[b]["o_mlm"] for b in range(NCORES)], axis=1)
    s5 = np.stack([R[b]["o_s5"] for b in range(NCORES)], axis=1)
    s5 = s5.reshape(DEPTH, NCORES, 2, 64, 2, 8).transpose(0, 1, 4, 5, 2, 3).reshape(DEPTH, NCORES, 2, 16, 64)
    s5re_p = np.ascontiguousarray(s5[:, :, 0])
    s5im_p = np.ascontiguousarray(s5[:, :, 1])
    y_sample = np.concatenate([R[b]["o_ysT"].T for b in range(NCORES)], axis=0).reshape(NCORES * NS, 1, 1024)
    ret_s = np.concatenate([R[b]["o_sret"].reshape(DEPTH, 4, NS, 64, 64).transpose(0, 2, 1, 3, 4) for b in range(NCORES)], axis=1)
    mlc_s = np.concatenate([R[b]["o_smlc"].reshape(DEPTH, 4, NS, 64, 64).transpose(0, 2, 1, 3, 4) for b in range(NCORES)], axis=1)
    mln_s = np.concatenate([R[b]["o_smln"].reshape(DEPTH, 4, NS, 64).transpose(0, 2, 1, 3) for b in range(NCORES)], axis=1)
    mlm_s = np.concatenate([R[b]["o_smlm"].reshape(DEPTH, 4, NS).transpose(0, 2, 1) for b in range(NCORES)], axis=1)
    def unsp(b):
        a = R[b]["o_s5s"].reshape(DEPTH, 2, 64, 2, 8, NS).transpose(3, 0, 5, 4, 1, 2)
        return a.reshape(2, DEPTH, NS, 16, 64)
    s5s = np.concatenate([unsp(b) for b in range(NCORES)], axis=2)
    z = lambda *s: np.zeros(s, np.float32)
    outs = (y_prompt, y_sample, ret_p, ret_s, np.ascontiguousarray(mlc_p), mlc_s,
            mln_p, mln_s, mlm_p, mlm_s, s5re_p, s5s[0],
            s5im_p, s5s[1], memk, memv)
    return tuple(np.ascontiguousarray(o, dtype=np.float32) for o in outs)
```
